# Optimizing a Trainium2 kernel written in Bass

```python
import jax, jax.numpy as jnp
from jax import lax
import numpy as np

D_MODEL = 1024
BATCH = 16
SEQ = 2048
DEPTH = 1
DEC_BATCH = 32
DEC_SEQ = 4
PAST_LEN = 16384
PAGE_SIZE = 128

R_HEAD_DIM = 64
R_WIDTH = D_MODEL // 2
R_HEADS = R_WIDTH // R_HEAD_DIM
DECAY_LORA = 64
AAA_LORA = 64
R_SHIFT_WIDTH = 3 * R_WIDTH + DECAY_LORA + AAA_LORA
N_HEAD_DIM = 64
N_WIDTH = D_MODEL - R_WIDTH
N_HEADS = N_WIDTH // N_HEAD_DIM
N_KV_HEADS = 2
N_GROUP = N_HEADS // N_KV_HEADS
N_BRANCH = 3
KV_SLOTS = 2 * N_BRANCH
CMP_BLOCK = 32
CMP_STRIDE = 16
SEL_BLOCK = 64
SEL_TOPK = 16
WINDOW = 512
SEL_QUERY_BLOCK = 32
WIN_QUERY_BLOCK = 128
MIX_WIDTH = R_WIDTH + N_WIDTH
PROJ_SPLITS = (R_SHIFT_WIDTH, R_WIDTH, N_WIDTH, N_WIDTH, KV_SLOTS * N_KV_HEADS * N_HEAD_DIM, N_BRANCH * N_HEADS)
PROJ_WIDTH = sum(PROJ_SPLITS)
RMS_EPS = 1e-6
GN_EPS = 64e-5
MASK_NEG = -1e30
SEL_BIAS = 1e4
ATTN_SCALE = N_HEAD_DIM ** -0.5

kernel_name = 'hymba_rwkv7_nsa_decode_step'


def rms_norm(x, g):
    xf = x.astype(jnp.float32)
    y = xf * lax.rsqrt(jnp.mean(xf * xf, axis=-1, keepdims=True) + RMS_EPS)
    return (y * g.astype(jnp.float32)).astype(x.dtype)


def masked_softmax(s, mask):
    s = jnp.where(mask, s.astype(jnp.float32), MASK_NEG)
    p = jax.nn.softmax(s, axis=-1)
    return jnp.where(mask, p, 0.0)


def split_proj(p):
    offs = np.cumsum((0,) + PROJ_SPLITS)
    return tuple(p[..., int(offs[i]):int(offs[i + 1])] for i in range(len(PROJ_SPLITS)))


def to_heads(q):
    B, T, _ = q.shape
    return q.reshape(B, T, N_KV_HEADS, N_GROUP, N_HEAD_DIM).transpose(0, 2, 3, 1, 4)


def rwkv_mix(z, z_prev, s0, gate, mu_shift, w0, w_decay_up, a0, w_aaa_up, k_k, k_a, r_k, gn_w, gn_b):
    B, T, _ = z.shape
    f32 = jnp.float32
    zp = jnp.concatenate([z_prev[:, None].astype(z.dtype), z[:, :-1]], axis=1)
    zs = z + (zp - z) * mu_shift
    r, k, v, wd, ad = jnp.split(zs, [R_WIDTH, 2 * R_WIDTH, 3 * R_WIDTH, 3 * R_WIDTH + DECAY_LORA], axis=-1)
    w_log = -jax.nn.softplus(-(w0 + jnp.tanh(wd) @ w_decay_up).astype(f32)) - 0.5
    decay = jnp.exp(-jnp.exp(w_log))
    a = jax.nn.sigmoid((a0 + ad @ w_aaa_up).astype(f32))
    hs = (B, T, R_HEADS, R_HEAD_DIM)
    r, k, v, decay, a = (t.astype(f32).reshape(hs) for t in (r, k, v, decay, a))
    hshape = (R_HEADS, R_HEAD_DIM)
    kk = k * k_k.astype(f32).reshape(hshape)
    kk = kk * lax.rsqrt(jnp.maximum(jnp.sum(kk * kk, axis=-1, keepdims=True), 1e-24))
    k = k * (1.0 + (a - 1.0) * k_a.astype(f32).reshape(hshape))

    def step(S, inp):
        r_t, w_t, k_t, v_t, kk_t, a_t = inp
        sa = jnp.einsum('bhij,bhj->bhi', S, kk_t)
        S = (S * w_t[:, :, None, :] - sa[..., None] * (kk_t * a_t)[:, :, None, :]
             + v_t[..., None] * k_t[:, :, None, :])
        return S, jnp.einsum('bhij,bhj->bhi', S, r_t)

    s_fin, y = lax.scan(step, s0.astype(f32), tuple(jnp.moveaxis(t, 1, 0) for t in (r, decay, k, v, kk, a)))
    y = jnp.moveaxis(y, 0, 1)
    yc = y - jnp.mean(y, axis=-1, keepdims=True)
    y = yc * lax.rsqrt(jnp.mean(yc * yc, axis=-1, keepdims=True) + GN_EPS)
    y = y * gn_w.astype(f32).reshape(hshape) + gn_b.astype(f32).reshape(hshape)
    y = y + jnp.sum(r * k * r_k.astype(f32).reshape(hshape), axis=-1, keepdims=True) * v
    out = y.reshape(B, T, R_WIDTH).astype(z.dtype) * jax.nn.silu(gate)
    return out, s_fin, z[:, -1]


def compress(kv, w_pos, w_mix):
    B, L = kv.shape[:2]
    n_sub = CMP_BLOCK // CMP_STRIDE
    n_chunk = L // CMP_STRIDE
    n_cmp = n_chunk - n_sub + 1
    c = kv[:, :n_chunk * CMP_STRIDE].reshape(B, n_chunk, CMP_STRIDE, 2, N_KV_HEADS, N_HEAD_DIM)
    wp = w_pos.reshape(2, n_sub, CMP_STRIDE, N_HEAD_DIM)
    pooled = jnp.einsum('bjpekd,epd->bjekd', c[:, 0:n_cmp], wp[:, 0])
    for m in range(1, n_sub):
        pooled = pooled + jnp.einsum('bjpekd,epd->bjekd', c[:, m:m + n_cmp], wp[:, m])
    return jnp.einsum('bjekd,edf->bjekf', pooled, w_mix)


def compressed_attn(q, kc, qpos):
    n_cmp = kc.shape[1]
    kend = jnp.arange(n_cmp) * CMP_STRIDE + (CMP_BLOCK - 1)
    mask = kend[None, :] <= qpos[:, None]
    s = jnp.einsum('bkgtd,bjkd->bkgtj', q, kc[:, :, 0]) * ATTN_SCALE
    p = masked_softmax(s, mask)
    o = jnp.einsum('bkgtj,bjkd->bkgtd', p.astype(q.dtype), kc[:, :, 1])
    return o, jnp.sum(p, axis=2)


def select_blocks(imp, qpos, total_len):
    n_cmp = imp.shape[-1]
    n_sel = -(-total_len // SEL_BLOCK)
    cs = jnp.arange(n_cmp) * CMP_STRIDE
    ss = jnp.arange(n_sel) * SEL_BLOCK
    overlap = ((cs[:, None] < ss[None, :] + SEL_BLOCK) & (cs[:, None] + CMP_BLOCK > ss[None, :])).astype(jnp.float32)
    score = jnp.einsum('bktj,js->bkts', imp, overlap)
    cur = (qpos // SEL_BLOCK)[:, None]
    sid = jnp.arange(n_sel)[None, :]
    valid = sid <= cur
    forced = (sid == 0) | (sid == cur) | (sid == cur - 1)
    score = jnp.where(valid, score + jnp.where(forced, SEL_BIAS, 0.0), -SEL_BIAS)
    vals, idx = lax.top_k(score, min(SEL_TOPK, n_sel))
    return idx, vals > -0.5 * SEL_BIAS


def selected_attn(q, ks, vs, idx, valid, qpos):
    B, KV, T, K = idx.shape
    kpos = idx[..., None] * SEL_BLOCK + jnp.arange(SEL_BLOCK)
    mask = (valid[..., None] & (kpos <= qpos[:, None, None])).reshape(B, KV, 1, T, K * SEL_BLOCK)
    ks = ks.reshape(B, KV, T, K * SEL_BLOCK, N_HEAD_DIM)
    vs = vs.reshape(B, KV, T, K * SEL_BLOCK, N_HEAD_DIM)
    s = jnp.einsum('bkgtd,bktnd->bkgtn', q, ks) * ATTN_SCALE
    p = masked_softmax(s, mask)
    return jnp.einsum('bkgtn,bktnd->bkgtd', p.astype(q.dtype), vs)


def selected_prompt_attn(q, kv_sel, idx, valid, qpos):
    B, T = kv_sel.shape[:2]
    n_sel = T // SEL_BLOCK
    blocks = kv_sel.reshape(B, n_sel, SEL_BLOCK, 2, N_KV_HEADS, N_HEAD_DIM).transpose(0, 4, 1, 2, 3, 5)
    bk, bv = blocks[..., 0, :], blocks[..., 1, :]
    bi = jnp.arange(B)[:, None, None, None]
    gi = jnp.arange(N_KV_HEADS)[None, :, None, None]
    nq = T // SEL_QUERY_BLOCK

    def split(t, ax):
        return jnp.moveaxis(t.reshape(t.shape[:ax] + (nq, SEL_QUERY_BLOCK) + t.shape[ax + 1:]), ax, 0)

    def chunk(args):
        qc, ic, vc, pc = args
        return selected_attn(qc, bk[bi, gi, ic], bv[bi, gi, ic], ic, vc, pc)

    out = lax.map(chunk, (split(q, 3), split(idx, 2), split(valid, 2), qpos.reshape(nq, SEL_QUERY_BLOCK)))
    return jnp.moveaxis(out, 0, 3).reshape(B, N_KV_HEADS, N_GROUP, T, N_HEAD_DIM)


def selected_sample_attn(q, kv_new, cache, page_table, idx, valid, qpos):
    B, T = kv_new.shape[:2]
    bpp = PAGE_SIZE // SEL_BLOCK
    n_past = page_table.shape[1] * bpp
    pool = cache.reshape(cache.shape[0] * bpp, SEL_BLOCK, cache.shape[2], N_KV_HEADS, N_HEAD_DIM)
    n_tail = -(-T // SEL_BLOCK)
    tail = jnp.pad(kv_new, ((0, 0), (0, n_tail * SEL_BLOCK - T), (0, 0), (0, 0), (0, 0)))
    tail = tail.reshape(B, n_tail, SEL_BLOCK, 2, N_KV_HEADS, N_HEAD_DIM).transpose(0, 4, 1, 2, 3, 5)
    bi = jnp.arange(B)[:, None, None, None]
    gi = jnp.arange(N_KV_HEADS)[None, :, None, None]
    ip = jnp.minimum(idx, n_past - 1)
    phys = page_table[bi, ip // bpp] * bpp + ip % bpp
    past_rows = pool[phys, :, 2:4, gi]
    tail_rows = tail[bi, gi, jnp.clip(idx - n_past, 0, n_tail - 1)]
    rows = jnp.where((idx < n_past)[..., None, None, None], past_rows.astype(tail_rows.dtype), tail_rows)
    return selected_attn(q, rows[..., 0, :], rows[..., 1, :], idx, valid, qpos)


def window_attn(q, kw, qpos, kpos):
    mask = (kpos[None, :] <= qpos[:, None]) & (kpos[None, :] > qpos[:, None] - WINDOW) & (kpos[None, :] >= 0)
    s = jnp.einsum('bkgtd,bskd->bkgts', q, kw[:, :, 0]) * ATTN_SCALE
    p = masked_softmax(s, mask)
    return jnp.einsum('bkgts,bskd->bkgtd', p.astype(q.dtype), kw[:, :, 1])


def window_prompt_attn(q, kv_win):
    B, T = kv_win.shape[:2]
    padded = jnp.pad(kv_win, ((0, 0), (WINDOW, 0), (0, 0), (0, 0), (0, 0)))
    nq = T // WIN_QUERY_BLOCK

    def chunk(i):
        start = i * WIN_QUERY_BLOCK
        kw = lax.dynamic_slice_in_dim(padded, start, WINDOW + WIN_QUERY_BLOCK, axis=1)
        qc = lax.dynamic_slice_in_dim(q, start, WIN_QUERY_BLOCK, axis=3)
        qpos = start + jnp.arange(WIN_QUERY_BLOCK)
        kpos = start - WINDOW + jnp.arange(WINDOW + WIN_QUERY_BLOCK)
        return window_attn(qc, kw, qpos, kpos)

    out = lax.map(chunk, jnp.arange(nq))
    return jnp.moveaxis(out, 0, 3).reshape(B, N_KV_HEADS, N_GROUP, T, N_HEAD_DIM)


def mixer_out(y_r, o_c, o_s, o_w, gl, gate_n, w_out):
    B, T, _ = y_r.shape
    g = jax.nn.sigmoid(gl.astype(jnp.float32)).reshape(B, T, N_BRANCH, N_KV_HEADS, N_GROUP)
    g = g.transpose(2, 0, 3, 4, 1)[..., None]
    o = g[0] * o_c + g[1] * o_s + g[2] * o_w
    o = o.transpose(0, 3, 1, 2, 4).reshape(B, T, N_WIDTH).astype(y_r.dtype) * jax.nn.silu(gate_n)
    return jnp.concatenate([y_r, o], axis=-1) @ w_out


def prompt_layer(x, norm_g, w_in, mu_shift, w0, w_decay_up, a0, w_aaa_up, k_k, k_a, r_k, gn_w, gn_b,
                 w_cmp_pos, w_cmp_mix, w_out):
    B, T, _ = x.shape
    zr, gate_r, q, gate_n, kv, gl = split_proj(rms_norm(x, norm_g) @ w_in)
    y_r, wkv, shift = rwkv_mix(zr, jnp.zeros((B, R_SHIFT_WIDTH), zr.dtype),
                               jnp.zeros((B, R_HEADS, R_HEAD_DIM, R_HEAD_DIM), jnp.float32), gate_r,
                               mu_shift, w0, w_decay_up, a0, w_aaa_up, k_k, k_a, r_k, gn_w, gn_b)
    kv = kv.reshape(B, T, KV_SLOTS, N_KV_HEADS, N_HEAD_DIM)
    qh = to_heads(q)
    qpos = jnp.arange(T)
    o_c, imp = compressed_attn(qh, compress(kv[:, :, 0:2], w_cmp_pos, w_cmp_mix), qpos)
    idx, valid = select_blocks(imp, qpos, T)
    o_s = selected_prompt_attn(qh, kv[:, :, 2:4], idx, valid, qpos)
    o_w = window_prompt_attn(qh, kv[:, :, 4:6])
    y = x + mixer_out(y_r, o_c, o_s, o_w, gl, gate_n, w_out)
    return y, kv[:, :, 0:4], kv[:, T - min(WINDOW, T):, 4:6], wkv, shift


def sample_layer(x, cache, win_buf, wkv0, shift0, page_table, norm_g, w_in, mu_shift, w0, w_decay_up, a0,
                 w_aaa_up, k_k, k_a, r_k, gn_w, gn_b, w_cmp_pos, w_cmp_mix, w_out):
    B, T, _ = x.shape
    past = page_table.shape[1] * PAGE_SIZE
    zr, gate_r, q, gate_n, kv, gl = split_proj(rms_norm(x, norm_g) @ w_in)
    y_r, wkv, shift = rwkv_mix(zr, shift0, wkv0, gate_r,
                               mu_shift, w0, w_decay_up, a0, w_aaa_up, k_k, k_a, r_k, gn_w, gn_b)
    kv = kv.reshape(B, T, KV_SLOTS, N_KV_HEADS, N_HEAD_DIM)
    qh = to_heads(q)
    qpos = past + jnp.arange(T)
    past_cmp = cache[page_table, :, 0:2].reshape(B, past, 2, N_KV_HEADS, N_HEAD_DIM)
    full_cmp = jnp.concatenate([past_cmp.astype(kv.dtype), kv[:, :, 0:2]], axis=1)
    o_c, imp = compressed_attn(qh, compress(full_cmp, w_cmp_pos, w_cmp_mix), qpos)
    idx, valid = select_blocks(imp, qpos, past + T)
    o_s = selected_sample_attn(qh, kv[:, :, 2:4], cache, page_table, idx, valid, qpos)
    nb = win_buf.shape[1]
    keys_w = jnp.concatenate([win_buf.astype(kv.dtype), kv[:, :, 4:6]], axis=1)
    o_w = window_attn(qh, keys_w, qpos, past - nb + jnp.arange(nb + T))
    y = x + mixer_out(y_r, o_c, o_s, o_w, gl, gate_n, w_out)
    n_keep = min(WINDOW, nb + T)
    return y, kv[:, :, 0:4], keys_w[:, nb + T - n_keep:], wkv, shift


def setup_inputs(seed: int = 0) -> dict:
    key = jax.random.key(seed)
    k = jax.random.split(key, 24)
    f32 = jnp.float32

    def nrm(i, shape, scale):
        return scale * jax.random.normal(k[i], shape, f32)

    n_pages = PAST_LEN // PAGE_SIZE
    n_phys = (5 * DEC_BATCH * n_pages) // 4
    win_len = min(WINDOW, PAST_LEN)
    hd = N_HEAD_DIM
    page_table = jax.random.permutation(k[6], n_phys)[:DEC_BATCH * n_pages].reshape(DEC_BATCH, n_pages).astype(jnp.int32)
    return {
        'x_prompt': nrm(0, (BATCH, SEQ, D_MODEL), 1.0),
        'x_sample': nrm(1, (DEC_BATCH, DEC_SEQ, D_MODEL), 1.0),
        'cache_kv': nrm(2, (DEPTH, n_phys, PAGE_SIZE, 4, N_KV_HEADS, hd), 1.0),
        'cache_kv_win': nrm(3, (DEPTH, DEC_BATCH, win_len, 2, N_KV_HEADS, hd), 1.0),
        'state_wkv': nrm(4, (DEPTH, DEC_BATCH, R_HEADS, R_HEAD_DIM, R_HEAD_DIM), 0.5),
        'state_shift': nrm(5, (DEPTH, DEC_BATCH, R_SHIFT_WIDTH), 1.0),
        'page_table': page_table,
        'norm_g': 1.0 + nrm(7, (DEPTH, D_MODEL), 0.02),
        'w_in': nrm(8, (DEPTH, D_MODEL, PROJ_WIDTH), D_MODEL ** -0.5),
        'mu_shift': jax.random.uniform(k[9], (DEPTH, R_SHIFT_WIDTH), f32),
        'w0': nrm(10, (DEPTH, R_WIDTH), 0.5),
        'w_decay_up': nrm(11, (DEPTH, DECAY_LORA, R_WIDTH), 0.1),
        'a0': nrm(12, (DEPTH, R_WIDTH), 0.1),
        'w_aaa_up': nrm(13, (DEPTH, AAA_LORA, R_WIDTH), AAA_LORA ** -0.5),
        'k_k': 0.85 + nrm(14, (DEPTH, R_WIDTH), 0.02),
        'k_a': 1.0 + nrm(15, (DEPTH, R_WIDTH), 0.02),
        'r_k': nrm(16, (DEPTH, R_WIDTH), 0.1),
        'gn_w': 1.0 + nrm(17, (DEPTH, R_WIDTH), 0.02),
        'gn_b': nrm(18, (DEPTH, R_WIDTH), 0.02),
        'w_cmp_pos': (1.0 + nrm(19, (DEPTH, 2, CMP_BLOCK, hd), 0.1)) * CMP_BLOCK ** -0.5,
        'w_cmp_mix': nrm(20, (DEPTH, 2, hd, hd), hd ** -0.5),
        'w_out': nrm(21, (DEPTH, MIX_WIDTH, D_MODEL), MIX_WIDTH ** -0.5),
        'final_g': 1.0 + nrm(22, (D_MODEL,), 0.02),
    }


def reference(x_prompt, x_sample, cache_kv, cache_kv_win, state_wkv, state_shift, page_table,
              norm_g, w_in, mu_shift, w0, w_decay_up, a0, w_aaa_up, k_k, k_a, r_k, gn_w, gn_b,
              w_cmp_pos, w_cmp_mix, w_out, final_g):
    y_p, y_s = x_prompt, x_sample
    kvp, kvs, wnp, wns, skp, sks, shp, shs = [], [], [], [], [], [], [], []
    for layer in range(DEPTH):
        lw = (norm_g[layer], w_in[layer], mu_shift[layer], w0[layer], w_decay_up[layer], a0[layer],
              w_aaa_up[layer], k_k[layer], k_a[layer], r_k[layer], gn_w[layer], gn_b[layer],
              w_cmp_pos[layer], w_cmp_mix[layer], w_out[layer])
        y_p, p_kv, p_win, p_wkv, p_shift = prompt_layer(y_p, *lw)
        y_s, s_kv, s_win, s_wkv, s_shift = sample_layer(y_s, cache_kv[layer], cache_kv_win[layer],
                                                        state_wkv[layer], state_shift[layer], page_table, *lw)
        kvp.append(p_kv); kvs.append(s_kv); wnp.append(p_win); wns.append(s_win)
        skp.append(p_wkv); sks.append(s_wkv); shp.append(p_shift); shs.append(s_shift)
    y_prompt = rms_norm(y_p, final_g)
    y_sample = rms_norm(y_s, final_g)
    kv_prompt = jnp.stack(kvp)
    kv_sample = jnp.stack(kvs)
    win_prompt = jnp.stack(wnp)
    win_sample = jnp.stack(wns)
    wkv_prompt = jnp.stack(skp)
    wkv_sample = jnp.stack(sks)
    shift_prompt = jnp.stack(shp)
    shift_sample = jnp.stack(shs)
    return (y_prompt, y_sample, kv_prompt, kv_sample, win_prompt, win_sample, wkv_prompt, wkv_sample, shift_prompt, shift_sample)
```

```python
import contextlib
import numpy as np
import ml_dtypes
import concourse.bass as bass
import concourse.mybir as mybir
from concourse.bass_utils import run_bass_kernel_spmd

F32 = mybir.dt.float32
BF16 = mybir.dt.bfloat16
I32 = mybir.dt.int32
AF = mybir.ActivationFunctionType
ALU = mybir.AluOpType
AX = mybir.AxisListType

ENGS = ("pe", "act", "dve", "pool", "sp")
NDMASEM = 8
BIG = 30000.0
T = 2048
TB = 256
NTT = TB // 128
NB = T // TB
DW = 3992
EPS = 1e-6
GN_EPS = 64e-5
DEC_C = 0.6065306597126334


class _Stop(Exception):
    pass


import os
KSTOP = float(os.environ.get("KSTOP", "999"))


KSKIP = [int(os.environ.get("KSKIP", "0"))]


def chk(stage):
    if stage >= KSTOP:
        if KSKIP[0] > 0 and stage == KSTOP:
            KSKIP[0] -= 1
            return
        raise _Stop()


class Prog:
    def __init__(self, nc):
        self.nc = nc
        self.ops = []

    def op(self, eng, fn, reads=(), writes=()):
        import sys
        f = sys._getframe(2)
        self.ops.append(dict(eng=eng, fn=fn, reads=tuple(reads), writes=tuple(writes), dma=False, line=f.f_lineno))

    def dma(self, eng, out, in_, reads=(), writes=(), **kw):
        self.ops.append(dict(eng=eng, fn=lambda e: e.dma_start(out=out, in_=in_, **kw),
                             reads=tuple(reads), writes=tuple(writes), dma=True))

    def dma_fn(self, eng, fn, reads=(), writes=()):
        self.ops.append(dict(eng=eng, fn=fn, reads=tuple(reads), writes=tuple(writes), dma=True))

    def build(self, sems, block):
        ops = self.ops
        n = len(ops)
        last_w, readers = {}, {}
        deps = [None] * n
        needed = [False] * n
        for i, o in enumerate(ops):
            d = set()
            for r in o["reads"]:
                if r in last_w:
                    d.add(last_w[r])
            for w in o["writes"]:
                if w in last_w:
                    d.add(last_w[w])
                d.update(readers.get(w, ()))
            d.discard(i)
            d = {j for j in d if not (ops[j]["eng"] == "pe" and o["eng"] == "pe"
                                      and not ops[j]["dma"] and not o["dma"])}
            deps[i] = d
            for j in d:
                needed[j] = True
            for w in o["writes"]:
                last_w[w] = i
                readers[w] = []
            for r in o["reads"]:
                readers.setdefault(r, []).append(i)
        cnt = {e: 0 for e in ENGS}
        dcnt = {e: 0 for e in ENGS}
        ev = [None] * n
        prevdma = [None] * n
        for i, o in enumerate(ops):
            e = o["eng"]
            if o["dma"]:
                k = dcnt[e]
                dcnt[e] += 1
                s = k % NDMASEM
                ev[i] = ((e, s), 16 * (k // NDMASEM + 1))
                if k >= NDMASEM:
                    prevdma[i] = ((e, s), 16 * (k // NDMASEM))
            elif needed[i]:
                cnt[e] += 1
                ev[i] = (e, cnt[e])
        per_eng = {e: [i for i, o in enumerate(ops) if o["eng"] == e] for e in ENGS}
        final_dma = {}
        for i, o in enumerate(ops):
            if o["dma"]:
                final_dma[ev[i][0]] = max(final_dma.get(ev[i][0], 0), ev[i][1])

        def emit(e, eng, idxs, last):
            waited = {}
            for i in idxs:
                o = ops[i]
                want = {}
                for j in deps[i]:
                    sk, v = ev[j]
                    want[sk] = max(want.get(sk, 0), v)
                if prevdma[i] is not None:
                    sk, v = prevdma[i]
                    want[sk] = max(want.get(sk, 0), v)
                for sk, v in want.items():
                    if waited.get(sk, 0) < v:
                        eng.wait_ge(sems[sk], v)
                        waited[sk] = v
                if os.environ.get("KTRACE") and i >= n - 60:
                    print("OP", i, e, "line", o.get("line"), "R", o["reads"], "W", o["writes"], "waits", want, "ev", ev[i], flush=True)
                ins = o["fn"](eng)
                if o["dma"]:
                    ins.then_inc(sems[ev[i][0]], 16)
                elif ev[i] is not None:
                    ins.then_inc(sems[e], 1)
            if last:
                for sk, v in final_dma.items():
                    if waited.get(sk, 0) < v:
                        eng.wait_ge(sems[sk], v)

        block.sync(lambda eng: emit("sp", eng, per_eng["sp"], True))
        block.tensor(lambda eng: emit("pe", eng, per_eng["pe"], False))
        block.scalar(lambda eng: emit("act", eng, per_eng["act"], False))
        block.vector(lambda eng: emit("dve", eng, per_eng["dve"], False))
        block.gpsimd(lambda eng: emit("pool", eng, per_eng["pool"], False))


def _consts():
    bf = ml_dtypes.bfloat16
    c = {}
    p = np.arange(128)[:, None]
    q = np.arange(128)[None, :]
    c["identf"] = np.eye(128, dtype=np.float32)
    c["identb"] = np.eye(128, dtype=np.float32).astype(bf)
    su = (p < q).astype(np.float32)
    ui = (p <= q).astype(np.float32)
    c["mask4"] = np.concatenate([su, ui, su, ui], axis=1)
    c["maskL"] = np.concatenate([(p > q).astype(np.float32)] * 2, axis=1)
    tq = np.arange(TB)[None, :]
    causb = np.zeros((128, NTT, TB), np.float32)
    for r in range(NTT):
        causb[:, r, :] = np.where(p + 128 * r > tq, -BIG, 0.0)
    c["causb"] = causb.astype(bf)
    winlow = np.zeros((128, 2, TB), np.float32)
    winlow[:, 0, :] = np.where(p <= tq, -BIG, 0.0)
    winlow[:, 1, :] = np.where(p <= tq - 128, -BIG, 0.0)
    c["winlow"] = winlow.astype(bf)
    tt = np.arange(T)[None, :]
    c["cmpbias"] = np.where((16 * p + 31 > tt) | (p >= 127), -BIG, 0.0).astype(bf)
    s32 = np.arange(32)[:, None]
    c["zp"] = (tt // 64 == s32).astype(np.float32).astype(bf)
    addc = np.zeros((128, 16, 32), np.float32)
    for t16 in range(16):
        t = 128 * t16 + np.arange(128)[:, None]
        cur = t // 64
        s = np.arange(32)[None, :]
        valid = s <= cur
        forced = (s == 0) | (s == cur) | (s == cur - 1)
        addc[:, t16, :] = np.where(valid, np.where(forced, 1e4, 0.0), -1e4)
    c["addc"] = addc
    j = np.arange(128)[:, None]
    s = np.arange(32)[None, :]
    ov = ((16 * j < 64 * s + 64) & (16 * j + 32 > 64 * s) & (j < 127)).astype(np.float32)
    c["ov32"] = ov
    col = np.arange(248)[None, :]
    c["zs"] = (col - 120 == p // 16).astype(np.float32).astype(bf)
    c["resetm"] = np.tile((np.arange(TB)[None, :] % 128 != 0).astype(np.float32), (128, 1))
    c["bd64"] = ((p // 64) == (q // 64)).astype(np.float32)
    c["hsel"] = np.stack([(np.arange(128) < 64), (np.arange(128) >= 64)], axis=1).astype(np.float32)
    c["zeros"] = np.zeros((128, 512), np.float32).astype(bf)
    p4 = np.arange(4)[:, None]
    q4 = np.arange(4)[None, :]
    su4 = (p4 < q4).astype(np.float32)
    ui4 = (p4 <= q4).astype(np.float32)
    c["mask4s"] = np.concatenate([su4, ui4, su4, ui4], axis=1)
    c["maskLs"] = np.concatenate([(p4 > q4).astype(np.float32)] * 2, axis=1)
    jj = np.arange(128)[:, None]
    s33 = np.arange(33)[None, :]
    c["ovc"] = ((4 * s33 - 1 <= jj) & (jj <= 4 * s33 + 3)).astype(np.float32).astype(bf)
    lastb = np.zeros((128, 16), np.float32)
    lastb[127, :] = -BIG
    c["lastb"] = lastb.astype(bf)
    gsel = np.zeros((16, 4), np.float32)
    for g in range(4):
        for t in range(4):
            gsel[g * 4 + t, t] = 1.0
    c["gsel"] = gsel
    addcs = np.zeros((4, 257), np.float32)
    addcs[:, [0, 255, 256]] = 1e4
    c["addcs"] = addcs
    hl = np.zeros((1, 2, 128), np.float32)
    hl[0, 0, :64] = 1.0
    hl[0, 1, 64:] = 1.0
    c["hl"] = hl.astype(bf)
    tq16 = np.tile(np.arange(4), 4)[None, :]
    c["tailb"] = np.where(np.arange(4)[:, None] > tq16, -BIG, 0.0).astype(np.float32).astype(bf)
    winms = np.ones((128, 4, 2, 4), np.float32)
    winms[:, 0, :, :] = (np.arange(128)[:, None, None] > np.arange(4)[None, None, :]).astype(np.float32)
    c["winms"] = winms.reshape(128, 32).astype(bf)
    c["iota"] = np.stack([2.0 * np.arange(128), 2.0 * np.arange(128) + 1.0], axis=1).astype(np.float32)
    return c


CONST_SPECS = None


def _const_specs():
    global CONST_SPECS
    if CONST_SPECS is None:
        CONST_SPECS = _consts()
    return CONST_SPECS


def _groups():
    g = []
    g.append(("T0", [(1664, 512)], False))
    g.append(("T1", [(2688, 512)], False))
    g.append(("T2", [(3200, 512)], False))
    g.append(("T3", [(3712, 280)], False))
    g.append(("F0", [(1536, 128)], False))
    for fc in range(4):
        g.append((f"R{fc}", [(128 * fc, 128), (512 + 128 * fc, 128), (1024 + 128 * fc, 128)], False))
    g.append(("Q", [(2176, 512)], True))
    g.append(("K", [(3456, 128), (3712, 128)], False))
    return g


GROUPS = _groups()
GIDX = {g[0]: i for i, g in enumerate(GROUPS)}


def build_nc(n_cache_rows, with_sample=True):
    nc = bass.Bass("TRN2", target_bir_lowering=False)
    C = _const_specs()
    dt_of = lambda a: BF16 if a.dtype == ml_dtypes.bfloat16 else F32

    def din(name, shape, dt=F32):
        return nc.dram_tensor(name, list(shape), dt, kind="ExternalInput").ap()

    def dout(name, shape, dt=F32):
        return nc.dram_tensor(name, list(shape), dt, kind="ExternalOutput").ap()

    xp = din("xp", [2, T, 1024])
    w_in = din("w_in", [1024, DW])
    w_out = din("w_out", [1024, 1024])
    norm_g = din("norm_g", [1024])
    final_g = din("final_g", [1024])
    mu = din("mu", [1664])
    vecs = {k: din(k, [512]) for k in ("w0", "a0", "k_k", "k_a", "r_k", "gn_w", "gn_b")}
    w_dup = din("w_dup", [64, 512])
    w_aup = din("w_aup", [64, 512])
    w_cpos = din("w_cpos", [2, 32, 64])
    w_cmix = din("w_cmix", [2, 64, 64])
    cd = {k: din("c_" + k, v.shape, dt_of(v)) for k, v in C.items()}

    yp = dout("yp", [2, T, 1024])
    kvp = dout("kvp", [2, T, 512])
    winp = dout("winp", [2, 512, 256])
    wkvp = dout("wkvp", [2, 8, 64, 64])
    shp = dout("shp", [2, 1664])

    xs = din("xs", [16, 1024])
    cache = din("cache", [n_cache_rows, 512])
    cwin = din("cwin", [4, 512, 256])
    swkv = din("swkv", [4, 8, 64, 64])
    sshift = din("sshift", [4, 1664])
    pt = din("pt", [4, 128], I32)
    ys = dout("ys", [16, 1024])
    kvs = dout("kvs", [16, 512])
    wins = dout("wins", [4, 512, 256])
    wkvs = dout("wkvs", [4, 8, 64, 64])
    shs = dout("shs", [4, 1664])

    wscr = nc.dram_tensor("wscr", [len(GROUPS), 128, 8 * 512], BF16, kind="Internal").ap()

    P = Prog(nc)
    with contextlib.ExitStack() as st:
        def sb(name, shape, dt=F32):
            return st.enter_context(nc.sbuf_tensor(name, list(shape), dt))

        def pst(name, shape, dt=F32):
            return st.enter_context(nc.psum_tensor(name, list(shape), dt))

        def mm(out, lhsT, rhs, start, stop, R, W):
            P.op("pe", lambda e: e.matmul(out, lhsT=lhsT, rhs=rhs, start=start, stop=stop,
                                          skip_group_check=True), R, W)

        def tr(out, in_, ident, R, W):
            P.op("pe", lambda e: e.transpose(out, in_, ident), R, W)

        def act(out, in_, func, R, W, eng="act", **kw):
            P.op("act", lambda e: e.activation(out=out, in_=in_, func=func, **kw), R, W)

        def vop(eng, name, R, W, **kw):
            P.op(eng, lambda e: getattr(e, name)(**kw), R, W)

        def cp(eng, out, in_, R, W):
            if eng == "act":
                P.op("act", lambda e: e.copy(out=out, in_=in_), R, W)
            else:
                P.op(eng, lambda e: e.tensor_copy(out=out, in_=in_), R, W)

        def ld(out, in_, W, R=(), eng="sp", **kw):
            P.dma(eng, out, in_, reads=R, writes=W, **kw)

        def stq(out, in_, R, eng="pool"):
            P.dma(eng, out, in_, reads=R, writes=())

        rr = {"n": 0}

        def evac_eng():
            rr["n"] += 1
            return "act" if rr["n"] % 2 else "dve"

        ct = {}
        for k, v in C.items():
            ct[k] = sb("k_" + k, v.shape, dt_of(v))
            ld(ct[k][:], cd[k], ["k_" + k])
        identf, identb = ct["identf"], ct["identb"]

        tmpf = [sb(f"tmpf{i}", [128, TB]) for i in range(8)]
        stw_t = [sb(f"stw{i}", [128, 512]) for i in range(1)]
        stw = [stw_t[0], stw_t[0]]
        wup_f = stw_t[0]
        g8 = sb("g8", [128, 8])
        gq8 = sb("gq8", [128, 8])
        mu13 = sb("mu13", [128, 13])
        pv = {k: sb("pv_" + k, [128, 4]) for k in ("w0", "a0", "k_k", "k_a", "r_k")}
        omka = sb("omka", [128, 4])
        gnw = sb("gnw", [128, 512])
        gnb = sb("gnb", [128, 512])
        fgb = sb("fgb", [128, 1024])
        wup = sb("wup", [128, 2, 512], BF16)
        wt = sb("wt", [128, 2, 256])
        wmix_f = sb("wmix_f", [128, 2, 64])
        wmix = sb("wmix", [128, 2, 2, 64], BF16)
        xres = sb("xres", [128, 1024])
        wout_f = xres
        wout = sb("wout", [128, 8, 1024], BF16)
        with nc.allow_non_contiguous_dma(reason="small per-feature vectors"):
            ld(g8[:], norm_g.rearrange("(kt p) -> p kt", p=128), ["g8"], allow_slow_non_contiguous=True)
            ld(mu13[:], mu.rearrange("(c p) -> p c", p=128), ["mu13"], allow_slow_non_contiguous=True)
            for k in pv:
                ld(pv[k][:], vecs[k].rearrange("(c p) -> p c", p=128), ["pv_" + k], allow_slow_non_contiguous=True)
        ld(gnw[:], vecs["gn_w"].rearrange("(o f) -> o f", o=1).to_broadcast([128, 512]), ["gnw"])
        ld(gnb[:], vecs["gn_b"].rearrange("(o f) -> o f", o=1).to_broadcast([128, 512]), ["gnb"])
        ld(fgb[:], final_g.rearrange("(o f) -> o f", o=1).to_broadcast([128, 1024]), ["fgb"])
        ld(wup_f[0:64, :], w_dup, ["stw0"])
        ld(wup_f[64:128, :], w_aup, ["stw0"])
        vop("pool", "memset", [], ["wup"], ap=wup[:], constant=0.0)
        cp("dve", wup[0:64, 0, :], wup_f[0:64, :], ["stw0"], ["wup"])
        cp("dve", wup[64:128, 1, :], wup_f[64:128, :], ["stw0"], ["wup"])
        vop("dve", "tensor_scalar", ["g8"], ["gq8"], out=gq8[:], in0=g8[:], scalar1=0.125, scalar2=None, op0=ALU.mult)
        vop("dve", "tensor_scalar", ["pv_k_a"], ["omka"], out=omka[:], in0=pv["k_a"][:], scalar1=-1.0, scalar2=1.0,
            op0=ALU.mult, op1=ALU.add)
        with nc.allow_non_contiguous_dma(reason="compress weights"):
            for m in range(2):
                for e in range(2):
                    for k in range(2):
                        for blk in range(8):
                            ld(wt[16 * blk:16 * blk + 16, m, e * 128 + k * 64: e * 128 + k * 64 + 64],
                               w_cpos[e, 16 * m:16 * m + 16, :], ["wt"])
            for e in range(2):
                ld(wmix_f[0:64, e, :], w_cmix[e], ["wmix_f"])
                ld(wmix_f[64:128, e, :], w_cmix[e], ["wmix_f"])
        vop("pool", "memset", [], ["wmix"], ap=wmix[:], constant=0.0)
        cp("dve", wmix[0:64, 0, :, :], wmix_f[0:64, :, :], ["wmix_f"], ["wmix"])
        cp("dve", wmix[64:128, 1, :, :], wmix_f[64:128, :, :], ["wmix_f"], ["wmix"])
        for kt in range(8):
            ld(wout_f[:], w_out[kt * 128:(kt + 1) * 128, :], ["xres"])
            cp("act" if kt % 2 else "dve", wout[:, kt, :], wout_f[:], ["xres"], ["wout"])

        wbuf = [sb(f"wbuf{i}", [128, 8, 512], BF16) for i in range(2)]
        for gi, (gname, segs, qs) in enumerate(GROUPS):
            sbuf = wbuf[gi % 2]
            key = f"wbuf{gi % 2}"
            for kt in range(8):
                s = stw[0]
                skey = "stw0"
                off = 0
                for (c0, ncol) in segs:
                    ld(s[:, off:off + ncol], w_in[kt * 128:(kt + 1) * 128, c0:c0 + ncol], [skey])
                    off += ncol
                gsrc = gq8 if qs else g8
                act(sbuf[:, kt, 0:off], s[:, 0:off], AF.Copy, [skey, "g8", "gq8"], [key], scale=gsrc[:, kt:kt + 1])
            P.dma("sp", wscr[gi].rearrange("p (k c) -> p k c", k=8)[:, :, 0:off], sbuf[:, :, 0:off],
                  reads=[key], writes=["wscr"])

        xt = [sb(f"xt{i}", [128, 1024]) for i in range(2)]
        ss = sb("ss", [128, 1])
        rstd = sb("rstd", [128, 1])
        xn = [sb(f"xn{i}", [128, 1024], BF16) for i in range(2)]
        xnT = sb("xnT", [128, 8, TB], BF16)
        gsr = sb("gsr", [128, NTT, 512], BF16)
        gsn = sb("gsn", [128, NTT, 512], BF16)
        kv0 = [sb(f"kv0_{i}", [128, 512]) for i in range(2)]
        kv1 = [sb(f"kv1_{i}", [128, 280]) for i in range(2)]
        gates = sb("gates", [128, NTT, 24])
        ym = [sb(f"ym{i}", [128, 256], BF16) for i in range(2)]
        QT = sb("QT", [64, 8, TB], BF16)
        KTs = sb("KTs", [64, 2, T], BF16)
        KTw = sb("KTw", [64, 2, T], BF16)
        Vs = sb("Vs", [128, 16, 2, 65], BF16)
        Vw = sb("Vw", [128, 16, 2, 65], BF16)
        pooled = sb("pooled", [128, 2, 128])
        pooledb = sb("pooledb", [128, 2, 128], BF16)
        kcT = sb("kcT", [64, 2, 128], BF16)
        vcx = sb("vcx", [128, 2, 97], BF16)
        zsb12 = sb("zsb12", [128, TB + 1])
        zsb = sb("zsb", [128, 3, TB + 1])
        zs12 = sb("zs12", [128, TB])
        zs = sb("zs", [128, 3, TB])
        lora = sb("lora", [128, TB], BF16)
        AR = sb("AR", [128, NTT, 2, 128])
        BK = sb("BK", [128, 2, TB])
        Wc = sb("Wc", [128, NTT * 8])
        prod = sb("prod", [128, TB])
        STz = sb("STz", [128, 4, 2, 64])
        BKz = sb("BKz", [128, 2, 2, TB])
        wkv2 = sb("wkv2", [128, 64])
        Ssb = sb("Ssb", [128, 128])
        tok = sb("tok", [128, 3, 2, 64])
        Mt = [sb(f"Mt{i}", [128, 512]) for i in range(2)]
        PT0 = sb("PT0", [128, 2, 128])
        Pk = [sb(f"Pk{i}", [128, 2, 2, 128]) for i in range(6)]
        Xa = [sb(f"Xa{i}", [128, 2, 64]) for i in range(2)]
        ytok = sb("ytok", [128, NTT, 2, 64])
        vtok = sb("vtok", [128, NTT, 2, 64])
        rk = sb("rk", [128, NTT, 2])
        st4 = sb("st4", [128, 4])
        gtmp = [sb(f"gtmp{i}", [128, 2, 64]) for i in range(3)]
        mix = sb("mix", [128, NTT, 1024], BF16)
        mixT = sb("mixT", [128, 8, 128], BF16)
        sq_junk = mixT[:].rearrange("p k t -> p (k t)")
        PTb = [sb(f"PTb{i}", [128, TB], BF16) for i in range(3)]
        oacc = sb("oacc", [128, NTT, 8, 64])
        otmp = sb("otmp", [128, NTT, 64])
        rden = sb("rden", [128, NTT])
        scg = sb("scg", [128, NTT])
        sc = sb("sc", [128, NTT, 2, 32])
        scw = sb("scw", [128, 32])
        m8a = sb("m8a", [128, 8])
        m8b = sb("m8b", [128, 8])
        thr = sb("thr", [128, 1])
        selb = sb("selb", [128, 32], BF16)
        selbT = sb("selbT", [32, 2, TB], BF16)
        yout = sb("yout", [128, 1024])
        zc = sb("zc", [128, 13])

        Vn = sb("Vn", [4, 2, 2, 65], BF16)
        QTs = sb("QTs", [64, 8, 4], BF16)
        QTz = sb("QTz", [128, 2, 16], BF16)
        KTn = sb("KTn", [64, 2, 2, 4], BF16)
        ptb = sb("ptb", [128, 128], I32)
        ptf = sb("ptf", [128, 128])
        pidx = sb("pidx", [128, 2, 128], I32)
        cache_h = cache.rearrange("n (h c) -> (n h) c", h=2)
        pg4 = [xres[:].rearrange("p (g c) -> p g c", g=4), yout[:].rearrange("p (g c) -> p g c", g=4)]
        pall = Vs[:].rearrange("p a k c -> p (a k c)")[:, 0:2048].rearrange("p (c j) -> p c j", c=2)
        kcTs = Vw[0:64, :, :, :].rearrange("p a k c -> p (a k c)")[:, 0:2048].rearrange("p (c j) -> p c j", c=2)
        vcs = ct["cmpbias"][:, 0:1040].rearrange("p (j k c) -> p j k c", j=8, k=2)
        selflat4 = ct["zp"][0:1, 0:2048].rearrange("o (h g k t) -> o h g k t", h=2, g=128, k=2)
        ocs1 = stw_t[0][0:16, 0:322]
        obr = Mt[1][0:16, 0:384].rearrange("p (b k d) -> p b k d", b=3, k=2)
        scs = kv1[1][0:4, 0:257]
        scs2 = kv0[1][0:4, 0:257]
        KTpg = sb("KTpg", [128, 4, 128], BF16)
        Vpg = sb("Vpg", [128, 4, 2, 65], BF16)
        mexp = sb("mexp", [128, 32], BF16)

        psA = pst("psA", [128, 512])
        psB = pst("psB", [128, 512])
        psT = pst("psT", [128, 1024], BF16)
        psP = pst("psP", [128, 512])
        psR = [pst(f"psR{i}", [128, 512]) for i in range(4)]

        sems = {}
        for e in ENGS:
            sems[e] = st.enter_context(nc.semaphore("s_" + e))
            for i in range(NDMASEM):
                sems[(e, i)] = st.enter_context(nc.semaphore(f"d_{e}_{i}"))
        block = st.enter_context(nc.Block())

        wb_n = {"n": 0}

        def load_group(gname):
            gi = GIDX[gname]
            i = wb_n["n"] % 2
            wb_n["n"] += 1
            ld(wbuf[i][:], wscr[gi].rearrange("p (k c) -> p k c", k=8), [f"wbuf{i}"], R=["wscr"])
            return wbuf[i], f"wbuf{i}"

        vop("pool", "memset", [], ["Vs"], ap=Vs[:], constant=1.0)
        vop("pool", "memset", [], ["Vw"], ap=Vw[:], constant=1.0)

        psAB = [(psA, "psA"), (psB, "psB")]
        pn = {"n": 0}

        def nextps():
            pn["n"] += 1
            return psAB[pn["n"] % 2]

        R0, R1, R2, R3 = psR
        sps = [(R1, "psR1"), (R2, "psR2")]
        sn = {"n": 0}

        def rmsnorm_T(xsrc, npart, tt):
            x_ = xt[tt % 2]
            xk = f"xt{tt % 2}"
            pp = slice(0, npart)
            ld(x_[pp, :], xsrc, [xk])
            act(sq_junk[pp, :], x_[pp, :], AF.Square, [xk], ["mixT", "ss"], accum_out=ss[pp, :])
            act(rstd[pp, :], ss[pp, :], AF.Sqrt, ["ss"], ["rstd"], scale=1.0 / 1024, bias=EPS)
            vop("dve", "reciprocal", ["rstd"], ["rstd"], out=rstd[pp, :], in_=rstd[pp, :])
            xn_ = xn[tt % 2]
            xnk = f"xn{tt % 2}"
            vop("dve", "tensor_scalar", [xk, "rstd"], [xnk], out=xn_[pp, :], in0=x_[pp, :], scalar1=rstd[pp, 0:1],
                scalar2=None, op0=ALU.mult)
            for kt in range(8):
                tr(psT[:, kt * 128:kt * 128 + npart], xn_[pp, kt * 128:(kt + 1) * 128], identb[pp, pp],
                   [xnk, "k_identb"], ["psT"])
            cp("act", xnT[:, :, tt * 128:tt * 128 + npart], psT[:].rearrange("p (k t) -> p k t", k=8)[:, :, 0:npart],
               ["psT"], ["xnT"])

        def rwkv_block(n, C, nlev, npart_of_chunk, gate_cols):
            nch = n // C
            wb, wk = load_group("F0")
            ps_, pk = nextps()
            for kt in range(8):
                mm(ps_[:, 0:n], wb[:, kt, 0:128], xnT[:, kt, 0:n], kt == 0, kt == 7, ["xnT", wk], [pk])
            cp("dve", zsb12[:, 0:1], zc[:, 12:13], ["zc"], ["zsb12"])
            cp("act", zsb12[:, 1:n + 1], ps_[:, 0:n], [pk], ["zsb12"])
            cp("dve", zc[:, 12:13], zsb12[:, n:n + 1], ["zsb12"], ["zc"])
            vop("dve", "tensor_tensor", ["zsb12"], ["tmpf5"], out=tmpf[5][:, 0:n], in0=zsb12[:, 0:n],
                in1=zsb12[:, 1:n + 1], op=ALU.subtract)
            vop("dve", "scalar_tensor_tensor", ["tmpf5", "mu13", "zsb12"], ["zs12"], out=zs12[:, 0:n], in0=tmpf[5][:, 0:n],
                scalar=mu13[:, 12:13], in1=zsb12[:, 1:n + 1], op0=ALU.mult, op1=ALU.add)
            act(lora[0:64, 0:n], zs12[0:64, 0:n], AF.Tanh, ["zs12"], ["lora"])
            cp("dve", lora[64:128, 0:n], zs12[64:128, 0:n], ["zs12"], ["lora"])
            m4 = ct["mask4"] if C == 128 else ct["mask4s"]
            mL = ct["maskL"] if C == 128 else ct["maskLs"]
            m4k = "k_mask4" if C == 128 else "k_mask4s"
            mLk = "k_maskL" if C == 128 else "k_maskLs"
            cpp = slice(0, C)
            for fc in range(4):
                wb, wk = load_group(f"R{fc}")
                for j3 in range(3):
                    ps_, pk = nextps()
                    cidx = j3 * 4 + fc
                    for kt in range(8):
                        mm(ps_[:, 0:n], wb[:, kt, j3 * 128:(j3 + 1) * 128], xnT[:, kt, 0:n], kt == 0, kt == 7,
                           ["xnT", wk], [pk])
                    cp("dve", zsb[:, j3, 0:1], zc[:, cidx:cidx + 1], ["zc"], ["zsb"])
                    cp("act", zsb[:, j3, 1:n + 1], ps_[:, 0:n], [pk], ["zsb"])
                    cp("dve", zc[:, cidx:cidx + 1], zsb[:, j3, n:n + 1], ["zsb"], ["zc"])
                    vop("dve", "tensor_tensor", ["zsb"], [f"tmpf{5 + j3}"], out=tmpf[5 + j3][:, 0:n], in0=zsb[:, j3, 0:n],
                        in1=zsb[:, j3, 1:n + 1], op=ALU.subtract)
                    vop("dve", "scalar_tensor_tensor", [f"tmpf{5 + j3}", "mu13", "zsb"], ["zs"], out=zs[:, j3, 0:n],
                        in0=tmpf[5 + j3][:, 0:n], scalar=mu13[:, cidx:cidx + 1], in1=zsb[:, j3, 1:n + 1],
                        op0=ALU.mult, op1=ALU.add)
                r_, k_, v_ = zs[:, 0, 0:n], zs[:, 1, 0:n], zs[:, 2, 0:n]
                sg, al, lw, cum, t4, t5, t6, t7 = [t[:, 0:n] for t in tmpf]
                K = lambda *i: [f"tmpf{j}" for j in i]
                ps_, pk = nextps()
                mm(ps_[:, 0:n], wup[:, 0, fc * 128:(fc + 1) * 128], lora[:, 0:n], True, True, ["wup", "lora"], [pk])
                act(sg, ps_[:, 0:n], AF.Sigmoid, [pk, "pv_w0"], K(0), bias=pv["w0"][:, fc:fc + 1])
                vop("dve", "tensor_scalar", K(0), K(2), out=lw, in0=sg, scalar1=-DEC_C, scalar2=None, op0=ALU.mult)
                ps_, pk = nextps()
                mm(ps_[:, 0:n], wup[:, 1, fc * 128:(fc + 1) * 128], lora[:, 0:n], True, True, ["wup", "lora"], [pk])
                act(al, ps_[:, 0:n], AF.Sigmoid, [pk, "pv_a0"], K(1), bias=pv["a0"][:, fc:fc + 1])
                vop("dve", "tensor_scalar", ["zs", "pv_k_k"], K(4), out=t4, in0=k_, scalar1=pv["k_k"][:, fc:fc + 1],
                    scalar2=None, op0=ALU.mult)
                vop("dve", "tensor_tensor", K(4), K(5), out=t5, in0=t4, in1=t4, op=ALU.mult)
                ps_, pk = nextps()
                mm(ps_[:, 0:n], ct["bd64"][:], t5, True, True, ["k_bd64"] + K(5), [pk])
                vop("dve", "tensor_scalar", [pk], K(5), out=t5, in0=ps_[:, 0:n], scalar1=1e-24, scalar2=None, op0=ALU.max)
                act(t5, t5, AF.Sqrt, K(5), K(5))
                vop("dve", "reciprocal", K(5), K(5), out=t5, in_=t5)
                vop("dve", "tensor_tensor", K(4, 5), K(4), out=t4, in0=t4, in1=t5, op=ALU.mult)
                vop("dve", "tensor_scalar", K(1) + ["pv_k_a", "omka"], K(5), out=t5, in0=al,
                    scalar1=pv["k_a"][:, fc:fc + 1], scalar2=omka[:, fc:fc + 1], op0=ALU.mult, op1=ALU.add)
                vop("dve", "tensor_tensor", ["zs"] + K(5), K(5), out=t5, in0=k_, in1=t5, op=ALU.mult)
                vop("dve", "tensor_tensor_scan", ["k_resetm"] + K(2), K(3), out=cum, data0=ct["resetm"][:, 0:n] if C == 128 else ct["resetm"][:, 0:n],
                    data1=lw, initial=0.0, op0=ALU.mult, op1=ALU.add)
                act(t6, cum, AF.Exp, K(3), K(6))
                for ch in range(nch):
                    cs = slice(ch * C, (ch + 1) * C)
                    cp("dve", Wc[:, 8 * ch:8 * ch + 1], t6[:, ch * C + C - 1: ch * C + C], K(6), ["Wc"])
                    vop("dve", "tensor_tensor", ["zs"] + K(6), ["AR"], out=AR[:, ch, 1, 0:C], in0=r_[:, cs], in1=t6[:, cs], op=ALU.mult)
                vop("dve", "scalar_tensor_tensor", ["zs", "pv_r_k"] + K(5), ["prod"], out=prod[:, 0:n], in0=r_,
                    scalar=pv["r_k"][:, fc:fc + 1], in1=t5, op0=ALU.mult, op1=ALU.mult)
                act(t7, cum, AF.Exp, K(3), K(7), scale=-1.0)
                vop("dve", "tensor_tensor", K(5, 7), ["BK"], out=BK[:, 1, 0:n], in0=t5, in1=t7, op=ALU.mult)
                for hh in range(2):
                    rws = slice(hh * 64, hh * 64 + 64)
                    cp("pool", BKz[rws, hh, 1, 0:n], BK[rws, 1, 0:n], ["BK"], ["BKz"])
                vop("dve", "tensor_tensor", K(4, 1), K(5), out=t5, in0=t4, in1=al, op=ALU.mult)
                vop("dve", "tensor_tensor", K(5, 7), ["BK"], out=BK[:, 0, 0:n], in0=t5, in1=t7, op=ALU.mult)
                for hh in range(2):
                    rws = slice(hh * 64, hh * 64 + 64)
                    cp("pool", BKz[rws, hh, 0, 0:n], BK[rws, 0, 0:n], ["BK"], ["BKz"])
                vop("dve", "tensor_tensor", K(3, 2), K(6), out=t6, in0=cum, in1=lw, op=ALU.subtract)
                act(t6, t6, AF.Exp, K(6), K(6))
                for ch in range(nch):
                    cs = slice(ch * C, (ch + 1) * C)
                    vop("dve", "scalar_tensor_tensor", K(4, 6), ["AR"], out=AR[:, ch, 0, 0:C], in0=t4[:, cs], scalar=-1.0,
                        in1=t6[:, cs], op0=ALU.mult, op1=ALU.mult)
                chk(3)
                for ch in range(nch):
                    cs = slice(ch * C, (ch + 1) * C)
                    ARf = AR[:, ch, :, 0:C]
                    tr(R0[cpp, 0:128], BK[:, 0, cs], identf[:], ["BK", "k_identf"], ["psR0"])
                    tr(R0[cpp, 128:256], BK[:, 1, cs], identf[:], ["BK", "k_identf"], ["psR0"])
                    tr(R0[cpp, 256:384], zs[:, 2, cs], identf[:], ["zs", "k_identf"], ["psR0"])
                    cp("act", tok[cpp].rearrange("p a h d -> p (a h d)"), R0[cpp, 0:384], ["psR0"], ["tok"])
                    cp("dve", vtok[cpp, ch, :, :], tok[cpp, 2, :, :], ["tok"], ["vtok"])
                    for hh in range(2):
                        for a2 in range(2):
                            for a3 in range(2):
                                mm(R1[cpp, (2 * a2 + a3) * C:(2 * a2 + a3 + 1) * C], BKz[:, hh, a2, cs], AR[:, ch, a3, 0:C], True, True,
                                   ["BKz", "AR"], ["psR1"])
                        vop("dve", "tensor_tensor", ["psR1", m4k], [f"Mt{hh}"], out=Mt[hh][cpp, 0:4 * C], in0=R1[cpp, 0:4 * C],
                            in1=m4[cpp, 0:4 * C], op=ALU.mult)
                        mm(R3[cpp, 256 + hh * C: 256 + (hh + 1) * C], AR[:, ch, 0, 0:C], BKz[:, hh, 0, cs], True, True,
                           ["AR", "BKz"], ["psR3b"])
                    vop("dve", "tensor_tensor", ["psR3b", mLk], ["PT0"], out=PT0[cpp, :, 0:C],
                        in0=R3[cpp, 256:256 + 2 * C].rearrange("p (h t) -> p h t", h=2), in1=mL[cpp, 0:2 * C].rearrange("p (h t) -> p h t", h=2), op=ALU.mult)
                    for lv in range(nlev - 1):
                        for hh in range(2):
                            if lv == 0:
                                Pm, PTm, kk_ = Mt[hh][cpp, 0:C], PT0[cpp, hh, 0:C], [f"Mt{hh}", "PT0"]
                            else:
                                Pm, PTm, kk_ = Pk[lv - 1][cpp, hh, 0, 0:C], Pk[lv - 1][cpp, hh, 1, 0:C], [f"Pk{lv - 1}"]
                            mm(R2[cpp, (2 * hh) * C:(2 * hh + 1) * C], PTm, Pm, True, True, kk_, ["psR2"])
                            mm(R2[cpp, (2 * hh + 1) * C:(2 * hh + 2) * C], Pm, PTm, True, True, kk_, ["psR2"])
                        cp(evac_eng(), Pk[lv][cpp, :, :, 0:C], R2[cpp, 0:4 * C].rearrange("p (h a t) -> p h a t", h=2, a=2), ["psR2"], [f"Pk{lv}"])
                    for hh in range(2):
                        mm(R3[cpp, hh * 64:(hh + 1) * 64], AR[:, ch, 0, 0:C], STz[:, fc, hh, :], True, False, ["AR", "STz"], ["psR3a"])
                        mm(R3[cpp, hh * 64:(hh + 1) * 64], Mt[hh][cpp, 2 * C:3 * C], tok[cpp, 2, hh, :], False, True,
                           [f"Mt{hh}", "tok"], ["psR3a"])
                    xi = 0
                    cp("act", Xa[0][cpp].rearrange("p h d -> p (h d)"), R3[cpp, 0:128], ["psR3a"], ["Xa0"])
                    for lv in range(nlev):
                        for hh in range(2):
                            if lv == 0:
                                Pm, kk_ = Mt[hh][cpp, 0:C], [f"Mt{hh}"]
                            else:
                                Pm, kk_ = Pk[lv - 1][cpp, hh, 0, 0:C], [f"Pk{lv - 1}"]
                            mm(R3[cpp, hh * 64:(hh + 1) * 64], identf[cpp, cpp], Xa[xi][cpp, hh, :], True, False,
                               ["k_identf", f"Xa{xi}"], ["psR3a"])
                            mm(R3[cpp, hh * 64:(hh + 1) * 64], Pm, Xa[xi][cpp, hh, :], False, True, kk_ + [f"Xa{xi}"], ["psR3a"])
                        cp(evac_eng(), Xa[1 - xi][cpp].rearrange("p h d -> p (h d)"), R3[cpp, 0:128], ["psR3a"], [f"Xa{1 - xi}"])
                        xi = 1 - xi
                    E = Xa[xi]
                    ek = f"Xa{xi}"
                    for hh in range(2):
                        mm(R3[cpp, hh * 64:(hh + 1) * 64], AR[:, ch, 1, 0:C], STz[:, fc, hh, :], True, False, ["AR", "STz"], ["psR3a"])
                        mm(R3[cpp, hh * 64:(hh + 1) * 64], Mt[hh][cpp, C:2 * C], E[cpp, hh, :], False, False, [f"Mt{hh}", ek], ["psR3a"])
                        mm(R3[cpp, hh * 64:(hh + 1) * 64], Mt[hh][cpp, 3 * C:4 * C], tok[cpp, 2, hh, :], False, True,
                           [f"Mt{hh}", "tok"], ["psR3a"])
                    cp("act", ytok[cpp, ch, :, :].rearrange("p h d -> p (h d)"), R3[cpp, 0:128], ["psR3a"], ["ytok"])
                    SU = psB[:, 0:128]
                    mm(SU, identf[:], STz[:, fc, :, :].rearrange("p h d -> p (h d)"), True, False, ["k_identf", "STz"], ["psB"])
                    mm(SU, tok[cpp, 0, :, :].rearrange("p h d -> p (h d)"), E[cpp].rearrange("p h d -> p (h d)"), False, False, ["tok", ek], ["psB"])
                    mm(SU, tok[cpp, 1, :, :].rearrange("p h d -> p (h d)"), tok[cpp, 2, :, :].rearrange("p h d -> p (h d)"), False, True, ["tok"], ["psB"])
                    cp("dve", Ssb[:], psB[:, 0:128], ["psB"], ["Ssb"])
                    for hh in range(2):
                        rows = slice(hh * 64, hh * 64 + 64)
                        act(STz[rows, fc, hh, :], Ssb[rows, hh * 64:(hh + 1) * 64], AF.Copy, ["Ssb", "Wc"], ["STz"],
                            scale=Wc[rows, 8 * ch:8 * ch + 1])
                chk(4)
                for ch in range(nch):
                    cs = slice(ch * C, (ch + 1) * C)
                    mm(psB[cpp, 0:2], prod[:, cs], ct["hsel"][:], True, True, ["prod", "k_hsel"], ["psB"])
                    cp("dve", rk[cpp, ch, :], psB[cpp, 0:2], ["psB"], ["rk"])
                    y2 = ytok[cpp, ch, :, :]
                    g0, g1, g2 = [g[cpp] for g in gtmp]
                    s4 = st4[cpp]
                    vop("dve", "tensor_reduce", ["ytok"], ["st4"], out=s4[:, 0:2], in_=y2, axis=AX.X, op=ALU.add)
                    vop("dve", "tensor_tensor", ["ytok"], ["gtmp0"], out=g0, in0=y2, in1=y2, op=ALU.mult)
                    vop("dve", "tensor_reduce", ["gtmp0"], ["st4"], out=s4[:, 2:4], in_=g0, axis=AX.X, op=ALU.add)
                    vop("dve", "tensor_scalar", ["st4"], ["st4"], out=s4[:, 0:2], in0=s4[:, 0:2], scalar1=1.0 / 64, scalar2=None, op0=ALU.mult)
                    vop("dve", "tensor_tensor", ["st4"], ["gtmp1"], out=g1[:, :, 0], in0=s4[:, 0:2], in1=s4[:, 0:2], op=ALU.mult)
                    vop("dve", "scalar_tensor_tensor", ["st4", "gtmp1"], ["st4"], out=s4[:, 2:4], in0=s4[:, 2:4], scalar=1.0 / 64,
                        in1=g1[:, :, 0], op0=ALU.mult, op1=ALU.subtract)
                    act(s4[:, 2:4], s4[:, 2:4], AF.Sqrt, ["st4"], ["st4"], bias=GN_EPS)
                    vop("dve", "reciprocal", ["st4"], ["st4"], out=s4[:, 2:4], in_=s4[:, 2:4])
                    vop("dve", "tensor_tensor", ["ytok", "st4"], ["gtmp0"], out=g0, in0=y2,
                        in1=s4[:, 0:2].rearrange("p (a o) -> p a o", o=1).to_broadcast([C, 2, 64]), op=ALU.subtract)
                    vop("dve", "tensor_tensor", ["gtmp0", "st4"], ["gtmp0"], out=g0, in0=g0,
                        in1=s4[:, 2:4].rearrange("p (a o) -> p a o", o=1).to_broadcast([C, 2, 64]), op=ALU.mult)
                    gw = gnw[cpp, fc * 128:(fc + 1) * 128].rearrange("p (h d) -> p h d", h=2)
                    gb = gnb[cpp, fc * 128:(fc + 1) * 128].rearrange("p (h d) -> p h d", h=2)
                    vop("dve", "tensor_tensor", ["gtmp0", "gnw"], ["gtmp0"], out=g0, in0=g0, in1=gw, op=ALU.mult)
                    vop("dve", "tensor_tensor", ["gtmp0", "gnb"], ["gtmp0"], out=g0, in0=g0, in1=gb, op=ALU.add)
                    vop("dve", "tensor_tensor", ["vtok", "rk"], ["gtmp1"], out=g1, in0=vtok[cpp, ch, :, :],
                        in1=rk[cpp, ch, :].rearrange("p (a o) -> p a o", o=1).to_broadcast([C, 2, 64]), op=ALU.mult)
                    vop("dve", "tensor_tensor", ["gtmp0", "gtmp1"], ["gtmp0"], out=g0, in0=g0, in1=g1, op=ALU.add)
                    vop("dve", "tensor_tensor", ["gtmp0", "gsr"], ["mix"],
                        out=mix[cpp, ch, fc * 128:(fc + 1) * 128].rearrange("p (h d) -> p h d", h=2), in0=g0,
                        in1=gsr[cpp, ch, fc * 128:(fc + 1) * 128].rearrange("p (h d) -> p h d", h=2), op=ALU.mult)
                chk(4.5)

        def shift_and_state_out(shdst, wkvdst):
            for rnd in range(4):
                ncs = 4 if rnd < 3 else 1
                for ci in range(ncs):
                    c13 = rnd * 4 + ci
                    tr(R0[0:1, ci * 128:(ci + 1) * 128], zc[:, c13:c13 + 1], identf[:], ["zc", "k_identf"], ["psR0"])
                dst = xres if rnd < 2 else yout
                dk = "xres" if rnd < 2 else "yout"
                o_ = (rnd % 2) * 512
                cp("dve", dst[0:1, o_:o_ + ncs * 128], R0[0:1, 0:ncs * 128], ["psR0"], [dk])
            stq(shdst[:, 0:1024], xres[0:1, 0:1024], ["xres"])
            stq(shdst[:, 1024:1664], yout[0:1, 0:640], ["yout"])
            for fc in range(4):
                tr(R0[:, 0:128], STz[:, fc, :, :].rearrange("p h d -> p (h d)"), identf[:], ["STz", "k_identf"], ["psR0"])
                cp("dve", Ssb[:], R0[:, 0:128], ["psR0"], ["Ssb"])
                for hh in range(2):
                    rws = slice(hh * 64, hh * 64 + 64)
                    cp("act", wkv2[rws, :], Ssb[rws, hh * 64:(hh + 1) * 64], ["Ssb"], ["wkv2"])
                stq(wkvdst[2 * fc:2 * fc + 2, :, :].rearrange("h i j -> (h i) j"), wkv2[:], ["wkv2"])

        def proj_qk(n, kdst):
            wb, wk = load_group("Q")
            for h in range(8):
                ps_, pk = nextps()
                for kt in range(8):
                    mm(ps_[0:64, 0:n], wb[:, kt, h * 64:(h + 1) * 64], xnT[:, kt, 0:n], kt == 0, kt == 7, ["xnT", wk], [pk])
                cp(evac_eng(), QT[:, h, 0:n], ps_[0:64, 0:n], [pk], ["QT"])
            wb, wk = load_group("K")
            for si in range(2):
                for kvh in range(2):
                    ps_, pk = nextps()
                    c0 = si * 128 + kvh * 64
                    for kt in range(8):
                        mm(ps_[0:64, 0:n], wb[:, kt, c0:c0 + 64], xnT[:, kt, 0:n], kt == 0, kt == 7, ["xnT", wk], [pk])
                    dst, dk = kdst(si, kvh)
                    cp(evac_eng(), dst, ps_[0:64, 0:n], [pk], [dk])

        def finish(h, br, ps_, stride, first, npart, ntt):
            pp = slice(0, npart)
            view = ps_[pp, 0:ntt * stride].rearrange("p (t c) -> p t c", t=ntt)
            vop("dve", "tensor_scalar", ["psR3a"], ["rden"], out=rden[pp, 0:ntt], in0=view[:, :, 64], scalar1=1e-30, scalar2=None,
                op0=ALU.max)
            vop("dve", "reciprocal", ["rden"], ["rden"], out=rden[pp, 0:ntt], in_=rden[pp, 0:ntt])
            vop("dve", "tensor_tensor", ["rden", "gates"], ["scg"], out=scg[pp, 0:ntt], in0=rden[pp, 0:ntt], in1=gates[pp, 0:ntt, br * 8 + h], op=ALU.mult)
            bc = scg[pp, 0:ntt].rearrange("p (a o) -> p a o", o=1).to_broadcast([npart, ntt, 64])
            if first:
                vop("dve", "tensor_tensor", ["psR3a", "scg"], ["oacc"], out=oacc[pp, 0:ntt, h, :], in0=view[:, :, 0:64], in1=bc, op=ALU.mult)
            else:
                vop("dve", "tensor_tensor", ["psR3a", "scg"], ["otmp"], out=otmp[pp, 0:ntt, :], in0=view[:, :, 0:64], in1=bc, op=ALU.mult)
                vop("dve", "tensor_tensor", ["otmp", "oacc"], ["oacc"], out=oacc[pp, 0:ntt, h, :], in0=oacc[pp, 0:ntt, h, :], in1=otmp[pp, 0:ntt, :], op=ALU.add)

        def out_proj(npart, tt, xsrc, ydst):
            pp = slice(0, npart)
            vop("dve", "tensor_tensor", ["oacc", "gsn"], ["mix"], out=mix[pp, tt, 512:1024],
                in0=oacc[pp, tt, :, :].rearrange("p h d -> p (h d)"), in1=gsn[pp, tt, :], op=ALU.mult)
            for kt in range(8):
                tr(psT[:, kt * 128:kt * 128 + npart], mix[pp, tt, kt * 128:(kt + 1) * 128], identb[pp, pp], ["mix", "k_identb"], ["psT"])
            cp("act", mixT[:, :, 0:npart], psT[:].rearrange("p (k t) -> p k t", k=8)[:, :, 0:npart], ["psT"], ["mixT"])
            ld(xres[pp, :], xsrc, ["xres"])
            for nchunk, (ps_, pk) in enumerate(psAB):
                for kt in range(8):
                    mm(ps_[pp, 0:512], mixT[:, kt, 0:npart], wout[:, kt, nchunk * 512:(nchunk + 1) * 512], kt == 0, kt == 7,
                       ["mixT", "wout"], [pk])
                vop("dve", "tensor_tensor", [pk, "xres"], ["yout"], out=yout[pp, nchunk * 512:(nchunk + 1) * 512], in0=ps_[pp, 0:512],
                    in1=xres[pp, nchunk * 512:(nchunk + 1) * 512], op=ALU.add)
            act(sq_junk[pp, :], yout[pp, :], AF.Square, ["yout"], ["mixT", "ss"], accum_out=ss[pp, :])
            act(rstd[pp, :], ss[pp, :], AF.Sqrt, ["ss"], ["rstd"], scale=1.0 / 1024, bias=EPS)
            vop("dve", "reciprocal", ["rstd"], ["rstd"], out=rstd[pp, :], in_=rstd[pp, :])
            vop("dve", "scalar_tensor_tensor", ["yout", "rstd", "fgb"], ["yout"], out=yout[pp, :], in0=yout[pp, :], scalar=rstd[pp, 0:1],
                in1=fgb[pp, :], op0=ALU.mult, op1=ALU.mult)
            stq(ydst, yout[pp, :], ["yout"])

        def sample_jobs():
            ovc = ct["ovc"]
            vop("pool", "memset", [], ["k_cmpbias"], ap=ct["cmpbias"][:, 0:1040], constant=1.0)
            vop("pool", "memset", [], ["Vn"], ap=Vn[:], constant=1.0)
            vop("pool", "memset", [], ["Vpg"], ap=Vpg[:], constant=1.0)
            vop("pool", "memset", [], ["QTz"], ap=QTz[:], constant=0.0)
            for bs in range(4):
                ld(xres[0:1, 0:1024], sshift[bs:bs + 1, 0:1024], ["xres"])
                ld(yout[0:1, 0:640], sshift[bs:bs + 1, 1024:1664], ["yout"])
                for c13 in range(13):
                    src = xres[0:1, c13 * 128:(c13 + 1) * 128] if c13 < 8 else yout[0:1, (c13 - 8) * 128:(c13 - 7) * 128]
                    tr(R0[:, c13:c13 + 1], src, identf[0:1, 0:1], ["xres", "yout", "k_identf"], ["psR0"])
                cp("dve", zc[:], R0[:, 0:13], ["psR0"], ["zc"])
                vop("dve", "memset", [], ["STz"], ap=STz[:], constant=0.0)
                for fc in range(4):
                    ld(tmpf[0][0:64, 0:128].rearrange("i (h j) -> i h j", h=2), swkv[bs, 2 * fc:2 * fc + 2, :, :].rearrange("h i j -> i h j"), ["tmpf0"])
                    tr(R0[:, 0:64], tmpf[0][0:64, 0:128], identf[0:64, 0:64], ["tmpf0", "k_identf"], ["psR0"])
                    cp("dve", Ssb[:, 0:64], R0[:, 0:64], ["psR0"], ["Ssb"])
                    for hh in range(2):
                        rws = slice(hh * 64, hh * 64 + 64)
                        cp("act", STz[rws, fc, hh, :], Ssb[rws, 0:64], ["Ssb"], ["STz"])
                chk(20)
                rmsnorm_T(xs[4 * bs:4 * bs + 4, :], 4, 0)
                p4 = slice(0, 4)
                for gname, ncol in (("T0", 512), ("T1", 512), ("T2", 512), ("T3", 280)):
                    wb, wk = load_group(gname)
                    ps_, pk = nextps()
                    for kt in range(8):
                        mm(ps_[p4, 0:ncol], xnT[:, kt, 0:4], wb[:, kt, 0:ncol], kt == 0, kt == 7, ["xnT", wk], [pk])
                    if gname == "T0":
                        act(gsr[p4, 0, :], ps_[p4, 0:512], AF.Silu, [pk], ["gsr"])
                    elif gname == "T1":
                        act(gsn[p4, 0, :], ps_[p4, 0:512], AF.Silu, [pk], ["gsn"])
                    elif gname == "T2":
                        cp("act", kv0[0][p4, :], ps_[p4, 0:512], [pk], ["kv0_0"])
                        stq(kvs[4 * bs:4 * bs + 4, :], kv0[0][p4, :], ["kv0_0"])
                        cp("dve", Vn[p4, 0, :, 0:64], kv0[0][p4, 384:512].rearrange("p (k d) -> p k d", k=2), ["kv0_0"], ["Vn"])
                    else:
                        cp("act", kv1[0][p4, :], ps_[p4, 0:280], [pk], ["kv1_0"])
                        stq(wins[bs, 508:512, :], kv1[0][p4, 0:256], ["kv1_0"])
                        cp("dve", Vn[p4, 1, :, 0:64], kv1[0][p4, 128:256].rearrange("p (k d) -> p k d", k=2), ["kv1_0"], ["Vn"])
                        act(gates[p4, 0, :], kv1[0][p4, 256:280], AF.Sigmoid, ["kv1_0"], ["gates"])
                chk(21)
                rwkv_block(4, 4, 2, 4, None)
                chk(22)
                shift_and_state_out(shs[bs:bs + 1, :], wkvs[bs])
                chk(23)
                proj_qk(4, lambda si, kvh: (KTn[:, si, kvh, :], "KTn"))
                cp("dve", QTs[:], QT[:, :, 0:4], ["QT"], ["QTs"])
                wb, wk = load_group("Q")
                for g in range(4):
                    h = 4 + g
                    ps_, pk = nextps()
                    for kt in range(8):
                        mm(ps_[:, 0:4], wb[:, kt, (h - 1) * 64:(h + 1) * 64], xnT[:, kt, 0:4], kt == 0, kt == 7, ["xnT", wk], [pk])
                    cp("dve", Ssb[:, g * 4:(g + 1) * 4], ps_[:, 0:4], [pk], ["Ssb"])
                cp("act", QTz[64:128, 1, :], Ssb[64:128, 0:16], ["Ssb"], ["QTz"])
                cp("dve", QTz[0:64, 0, :], QTs[:, 0:4, :].rearrange("p g t -> p (g t)"), ["QTs"], ["QTz"])
                chk(24)
                ld(ptb[:], pt[bs:bs + 1, :].to_broadcast([128, 128]), ["ptb"])
                cp("dve", ptf[:], ptb[:], ["ptb"], ["ptf"])
                vop("dve", "tensor_scalar", ["ptf", "k_iota"], ["pidx"], out=pidx[:, 0, :], in0=ptf[:], scalar1=256.0, scalar2=ct["iota"][:, 0:1],
                    op0=ALU.mult, op1=ALU.add)
                vop("dve", "tensor_scalar", ["ptf", "k_iota"], ["pidx"], out=pidx[:, 1, :], in0=ptf[:], scalar1=256.0, scalar2=ct["iota"][:, 1:2],
                    op0=ALU.mult, op1=ALU.add)

                def gather4(g0, col0, key):
                    dst = pg4[key]
                    kname = ("xres", "yout")[key]
                    for i in range(4):
                        P.dma_fn("pool", (lambda d_, gi: (lambda e: e.indirect_dma_start(
                            out=d_, out_offset=None, in_=cache_h,
                            in_offset=bass.IndirectOffsetOnAxis(ap=pidx[:, col0 // 256, gi:gi + 1], axis=0))))(dst[:, i, :], g0 + i),
                            reads=["pidx"], writes=[kname])
                    return dst, kname

                chk(25)
                gi_ = 0
                for G in range(8):
                    mm(psP[:, 0:258], ct["zeros"][:, 0:128], ct["zeros"][:, 0:258], True, True, ["k_zeros"], ["psP"])
                    for pq in range(4):
                        src, sk = gather4(16 * G + 4 * pq, 0, gi_ % 2)
                        gi_ += 1
                        for i in range(4):
                            ti = 4 * pq + i
                            lo, hi = 8 * ti - 1, 8 * ti + 8
                            for m in range(2):
                                vop("pool" if m else "dve", "tensor_tensor", [sk, "wt"], [f"ym{m}"], out=ym[m][:], in0=src[:, i, :],
                                    in1=wt[:, m, :], op=ALU.mult)
                            for cc in range(2):
                                for m in range(2):
                                    zoff = m - 8 * ti + 120
                                    mm(psP[:, cc * 129 + 1 + lo: cc * 129 + 1 + hi], ym[m][:, cc * 128:(cc + 1) * 128],
                                       ct["zs"][:, zoff + lo: zoff + hi], False, True, [f"ym{m}", "k_zs"], ["psP"])
                    for cc in range(2):
                        cp("act", pall[:, cc, 128 * G:128 * G + 128], psP[:, cc * 129 + 1:cc * 129 + 129], ["psP"], ["Vs"])
                        if G > 0:
                            vop("dve", "tensor_tensor", ["psP", "Vs"], ["Vs"], out=pall[:, cc, 128 * G - 1:128 * G],
                                in0=psP[:, cc * 129:cc * 129 + 1], in1=pall[:, cc, 128 * G - 1:128 * G], op=ALU.add)
                chk(26)
                for kvh in range(2):
                    for half in range(2):
                        js = slice(half * 512, (half + 1) * 512)
                        mm(psA[0:64, 0:512], wmix[:, kvh, 0, :], pall[:, 0, js], True, True, ["wmix", "Vs"], ["psA"])
                        cp("act", kcTs[:, kvh, js], psA[0:64, 0:512], ["psA"], ["Vw"])
                    for jt in range(8):
                        mm(psB[:, 0:64], pall[:, 1, jt * 128:(jt + 1) * 128], wmix[:, kvh, 1, :], True, True, ["wmix", "Vs"], ["psB"])
                        cp("dve", vcs[:, jt, kvh, 0:64], psB[:, 0:64], ["psB"], ["k_cmpbias"])
                chk(27)
                p16 = slice(0, 16)
                for kvh in range(2):
                    mm(R3[p16, 0:322], ct["zeros"][:, 0:16], ct["zeros"][:, 0:322], True, False, ["k_zeros"], ["psR3a"])
                    for jt in range(8):
                        sp_, spk = sps[sn["n"] % 2]
                        pt_ = PTb[sn["n"] % 3]
                        ptk = f"PTb{sn['n'] % 3}"
                        sn["n"] += 1
                        mm(sp_[:, 0:16], kcTs[:, kvh, jt * 128:(jt + 1) * 128], QTs[:, 4 * kvh:4 * kvh + 4, :].rearrange("p g t -> p (g t)"),
                           True, jt != 7, ["Vw", "QTs"], [spk])
                        if jt == 7:
                            mm(sp_[:, 0:16], identb[:], ct["lastb"][:], False, True, ["k_identb", "k_lastb"], [spk])
                        act(pt_[:, 0:16], sp_[:, 0:16], AF.Exp, [spk], [ptk])
                        mm(R3[p16, 0:65], pt_[:, 0:16], vcs[:, jt, kvh, :], False, False, [ptk, "k_cmpbias"], ["psR3a"])
                        mm(R3[p16, 65 + 32 * jt:65 + 32 * jt + 33], pt_[:, 0:16], ovc[:, 0:33], False, jt == 7, [ptk, "k_ovc"], ["psR3a"])
                    chk(27.1)
                    cp("act", ocs1[p16, :], R3[p16, 0:322], ["psR3a"], ["stw0"])
                    vop("dve", "reciprocal", ["stw0"], ["rden"], out=rden[p16, 0:1], in_=ocs1[p16, 64:65])
                    vop("dve", "tensor_scalar", ["stw0", "rden"], ["stw0"], out=ocs1[p16, :], in0=ocs1[p16, :], scalar1=rden[p16, 0:1],
                        scalar2=None, op0=ALU.mult)
                    cp("dve", obr[p16, 0, kvh, :], ocs1[p16, 0:64], ["stw0"], ["Mt1"])
                    chk(27.2)
                    mm(psA[p4, 0:257], ct["gsel"][p16, :], ocs1[p16, 65:322], True, True, ["k_gsel", "stw0"], ["psA"])
                    vop("dve", "tensor_tensor", ["psA", "k_addcs"], ["kv1_1"], out=scs[p4, :], in0=psA[p4, 0:257], in1=ct["addcs"][p4, :], op=ALU.add)
                    vop("dve", "max", ["kv1_1"], ["m8a"], out=m8a[p4, :], in_=scs[p4, :])
                    vop("dve", "match_replace", ["kv1_1", "m8a"], ["kv0_1"], out=scs2[p4, :], in_to_replace=m8a[p4, :], in_values=scs[p4, :], imm_value=-3e4)
                    vop("dve", "max", ["kv0_1"], ["m8b"], out=m8b[p4, :], in_=scs2[p4, :])
                    vop("dve", "tensor_scalar", ["kv1_1", "m8b"], ["kv0_1"], out=scs2[p4, :], in0=scs[p4, :], scalar1=m8b[p4, 7:8], scalar2=-BIG,
                        op0=ALU.is_lt, op1=ALU.mult)
                    chk(27.3)
                    for t_ in range(4):
                        mm(psB[0:1, 0:257], ct["identf"][p4, t_:t_ + 1], scs2[p4, :], True, True, ["k_identf", "kv0_1"], ["psB"])
                        cp("dve" if t_ % 2 else "act", selflat4[0:1, :, :, kvh, t_],
                           psB[0:1, 0:256].rearrange("o (pg hf) -> o hf pg", hf=2), ["psB"], ["k_zp"])

                chk(28)
                def page_attn(src, sk, npg, mask_mm, mexp_const, first_group):
                    for i in range(npg):
                        tr(R0[:, i * 128:(i + 1) * 128], src[:, i, 0:128], identf[:], [sk, "k_identf"], ["psR0"])
                    cp("act", KTpg[:, 0:npg, :].rearrange("p g n -> p (g n)"), R0[:, 0:npg * 128], ["psR0"], ["KTpg"])
                    cp("dve", Vpg[:, 0:npg, :, 0:64], src[:, 0:npg, 128:256].rearrange("p g (k d) -> p g k d", k=2), [sk], ["Vpg"])
                    sp_, spk = sps[sn["n"] % 2]
                    pt_ = PTb[sn["n"] % 3]
                    ptk = f"PTb{sn['n'] % 3}"
                    sn["n"] += 1
                    for i in range(npg):
                        for kvh in range(2):
                            c0 = (i * 2 + kvh) * 16
                            mm(sp_[:, c0:c0 + 16], KTpg[:, i, :], QTz[:, kvh, :], True, True, ["KTpg", "QTz"], [spk])
                    if mask_mm is not None:
                        for hf in range(2):
                            mm(sp_[:, 256:256 + npg * 8], ct["hl"][0:1, hf, :], mask_mm(hf), hf == 0, hf == 1, ["k_hl", "k_zp"], [spk])
                        act(mexp[:, 0:npg * 8], sp_[:, 256:256 + npg * 8], AF.Exp, [spk], ["mexp"])
                        mk = mexp[:, 0:npg * 8]
                        mkk = "mexp"
                    else:
                        mk = mexp_const
                        mkk = "k_winms"
                    act(pt_[:, 0:npg * 32], sp_[:, 0:npg * 32], AF.Exp, [spk], [ptk])
                    vop("dve", "tensor_tensor", [ptk, mkk], [ptk], out=pt_[:, 0:npg * 32].rearrange("p (a g t) -> p a g t", g=4, t=4),
                        in0=pt_[:, 0:npg * 32].rearrange("p (a g t) -> p a g t", g=4, t=4),
                        in1=mk.rearrange("p (a o t) -> p a o t", o=1, t=4).to_broadcast([128, npg * 2, 4, 4]), op=ALU.mult)
                    for i in range(npg):
                        for kvh in range(2):
                            c0 = (i * 2 + kvh) * 16
                            mm(R3[p16, kvh * 65:(kvh + 1) * 65], pt_[:, c0:c0 + 16], Vpg[:, i, kvh, :], first_group and i == 0 and kvh == 0,
                               False, [ptk, "Vpg"], ["psR3a"])

                def tail_attn(si, last):
                    sp_, spk = sps[sn["n"] % 2]
                    pt_ = PTb[sn["n"] % 3]
                    ptk = f"PTb{sn['n'] % 3}"
                    sn["n"] += 1
                    for kvh in range(2):
                        mm(sp_[p4, kvh * 16:(kvh + 1) * 16], KTn[:, si, kvh, :], QTs[:, 4 * kvh:4 * kvh + 4, :].rearrange("p g t -> p (g t)"),
                           True, False, ["KTn", "QTs"], [spk])
                        mm(sp_[p4, kvh * 16:(kvh + 1) * 16], identb[p4, p4], ct["tailb"][p4, :], False, True, ["k_identb", "k_tailb"], [spk])
                    act(pt_[p4, 0:32], sp_[p4, 0:32], AF.Exp, [spk], [ptk])
                    for kvh in range(2):
                        mm(R3[p16, kvh * 65:(kvh + 1) * 65], pt_[p4, kvh * 16:(kvh + 1) * 16], Vn[p4, si, kvh, :], False, last and kvh == 1,
                           [ptk, "Vn"], ["psR3a"])

                def fold(br, first):
                    for kvh in range(2):
                        cp("act", obr[p16, br, kvh, :], R3[p16, kvh * 65:kvh * 65 + 64], ["psR3a"], ["Mt1"])
                        cp("act", rden[p16, 1:2], R3[p16, kvh * 65 + 64:kvh * 65 + 65], ["psR3a"], ["rden"])
                        vop("dve", "reciprocal", ["rden"], ["rden"], out=rden[p16, 0:1], in_=rden[p16, 1:2])
                        vop("dve", "tensor_scalar", ["Mt1", "rden"], ["Mt1"], out=obr[p16, br, kvh, :], in0=obr[p16, br, kvh, :],
                            scalar1=rden[p16, 0:1], scalar2=None, op0=ALU.mult)

                chk(29)
                for grp in range(32):
                    src, sk = gather4(4 * grp, 256, gi_ % 2)
                    gi_ += 1
                    page_attn(src, sk, 4, (lambda hf, grp=grp: selflat4[0:1, hf, 4 * grp:4 * grp + 4, :, :].rearrange("o a k t -> o (a k t)")),
                              None, grp == 0)
                tail_attn(0, True)
                fold(1, False)
                chk(30)
                ld(pg4[0][:], cwin[bs].rearrange("(g p) c -> p g c", p=128), ["xres"])
                for tl in range(4):
                    lo_ = 4 if tl == 0 else 0
                    stq(wins[bs, 128 * tl + lo_ - 4:128 * tl + 124, :], pg4[0][lo_:128, tl, :], ["xres"])
                page_attn(pg4[0], "xres", 4, None, ct["winms"][:], True)
                tail_attn(1, True)
                fold(2, False)
                chk(31)
                for kvh in range(2):
                    for g in range(4):
                        h = 4 * kvh + g
                        for br in range(3):
                            mm(psA[p4, br * 64:(br + 1) * 64], ct["identf"][p16, g * 4:g * 4 + 4], obr[p16, br, kvh, :], True, True,
                               ["k_identf", "Mt1"], ["psA"])
                        for br in range(3):
                            if br == 0:
                                vop("dve", "tensor_scalar", ["psA", "gates"], ["oacc"], out=oacc[p4, 0, h, :], in0=psA[p4, 0:64],
                                    scalar1=gates[p4, 0, h:h + 1], scalar2=None, op0=ALU.mult)
                            else:
                                vop("dve", "scalar_tensor_tensor", ["psA", "gates", "oacc"], ["oacc"], out=oacc[p4, 0, h, :], in0=psA[p4, br * 64:(br + 1) * 64],
                                    scalar=gates[p4, 0, br * 8 + h:br * 8 + h + 1], in1=oacc[p4, 0, h, :], op0=ALU.mult, op1=ALU.add)
                chk(32)
                out_proj(4, 0, xs[4 * bs:4 * bs + 4, :], ys[4 * bs:4 * bs + 4, :])

        try:
          chk(0)
          vop("pool", "memset", [], ["BKz"], ap=BKz[:], constant=0.0)
          for b in range(2 if os.environ.get("KNOPROMPT") is None else 0):
            vop("dve", "memset", [], ["zc"], ap=zc[:], constant=0.0)
            vop("dve", "memset", [], ["STz"], ap=STz[:], constant=0.0)
            mm(psP[:, 0:256], ct["zeros"][:, 0:128], ct["zeros"][:, 0:256], True, True, ["k_zeros"], ["psP"])
            for tb in range(NB):
                t0 = tb * TB
                for tt in range(NTT):
                    rmsnorm_T(xp[b, t0 + tt * 128:t0 + (tt + 1) * 128, :], 128, tt)
                chk(1)
                for gname, ncol in (("T0", 512), ("T1", 512), ("T2", 512), ("T3", 280)):
                    wb, wk = load_group(gname)
                    for tt in range(NTT):
                        ps_, pk = nextps()
                        for kt in range(8):
                            mm(ps_[:, 0:ncol], xnT[:, kt, tt * 128:(tt + 1) * 128], wb[:, kt, 0:ncol],
                               kt == 0, kt == 7, ["xnT", wk], [pk])
                        tile_i = tb * NTT + tt
                        if gname == "T0":
                            act(gsr[:, tt, :], ps_[:, 0:512], AF.Silu, [pk], ["gsr"])
                        elif gname == "T1":
                            act(gsn[:, tt, :], ps_[:, 0:512], AF.Silu, [pk], ["gsn"])
                        elif gname == "T2":
                            k0 = kv0[tt % 2]
                            kk0 = f"kv0_{tt % 2}"
                            cp("act", k0[:], ps_[:, 0:512], [pk], [kk0])
                            stq(kvp[b, t0 + tt * 128:t0 + (tt + 1) * 128, :], k0[:], [kk0])
                            cp("dve", Vs[:, tile_i, :, 0:64], k0[:, 384:512].rearrange("p (k d) -> p k d", k=2),
                               [kk0], ["Vs"])
                            for m in range(2):
                                vop("pool", "tensor_tensor", [kk0, "wt"], [f"ym{m}"], out=ym[m][:], in0=k0[:, 0:256],
                                    in1=wt[:, m, :], op=ALU.mult)
                            for cc in range(2):
                                for m in range(2):
                                    j0 = 8 * tile_i - 1
                                    lo = max(j0, 0)
                                    hi = min(8 * tile_i + 8, 127)
                                    zoff = m - 8 * tile_i + 120
                                    mm(psP[:, cc * 128 + lo: cc * 128 + hi], ym[m][:, cc * 128:(cc + 1) * 128],
                                       ct["zs"][:, zoff + lo: zoff + hi], False, True, [f"ym{m}", "k_zs"], ["psP"])
                        else:
                            k1 = kv1[tt % 2]
                            kk1 = f"kv1_{tt % 2}"
                            cp("act", k1[:], ps_[:, 0:280], [pk], [kk1])
                            if t0 + tt * 128 >= T - 512:
                                stq(winp[b, t0 + tt * 128 - (T - 512): t0 + (tt + 1) * 128 - (T - 512), :],
                                    k1[:, 0:256], [kk1])
                            cp("dve", Vw[:, tile_i, :, 0:64], k1[:, 128:256].rearrange("p (k d) -> p k d", k=2),
                               [kk1], ["Vw"])
                            act(gates[:, tt, :], k1[:, 256:280], AF.Sigmoid, [kk1], ["gates"])
                chk(2)
                rwkv_block(TB, 128, 7, 128, None)
                chk(5)
                if tb == NB - 1:
                    shift_and_state_out(shp[b:b + 1, :], wkvp[b])
                chk(6)
                proj_qk(TB, lambda si, kvh: ((KTs, KTw)[si][:, kvh, t0:t0 + TB], ("KTs", "KTw")[si]))
                chk(6.2)
                cp("dve", pooled[:].rearrange("p a j -> p (a j)"), psP[:, 0:256], ["psP"], ["pooled"])
                cp("act", pooledb[:], pooled[:], ["pooled"], ["pooledb"])
                chk(6.3)
                for kvh in range(2):
                    mm(psA[0:64, 0:128], wmix[:, kvh, 0, :], pooledb[:, 0, :], True, True, ["wmix", "pooledb"], ["psA"])
                    cp("act", kcT[:, kvh, :], psA[0:64, 0:128], ["psA"], ["kcT"])
                    mm(psB[:, 0:64], pooledb[:, 1, :], wmix[:, kvh, 1, :], True, True, ["wmix", "pooledb"], ["psB"])
                    cp("dve", vcx[:, kvh, 0:64], psB[:, 0:64], ["psB"], ["vcx"])
                chk(6.4)
                if tb == 0 and b == 0:
                    vop("pool", "memset", [], ["vcx"], ap=vcx[:, :, 64:65], constant=1.0)
                    for kvh in range(2):
                        cp("dve", vcx[:, kvh, 65:97], ct["ov32"][:], ["k_ov32"], ["vcx"])

                def attend(h, br, tiles, Kt_, kkey, Vt_, vkey, first):
                    kvh = h // 4
                    nt = len(tiles)
                    for ti, (kt, biases) in enumerate(tiles):
                        sp_, spk = sps[sn["n"] % 2]
                        pt_ = PTb[sn["n"] % 3]
                        ptk = f"PTb{sn['n'] % 3}"
                        sn["n"] += 1
                        mm(sp_[:, 0:TB], Kt_[:, kvh, kt * 128:(kt + 1) * 128], QT[:, h, :], True, len(biases) == 0,
                           [kkey, "QT"], [spk])
                        for bi, (bl, br_, bkeys) in enumerate(biases):
                            mm(sp_[:, 0:TB], bl, br_, False, bi == len(biases) - 1, bkeys, [spk])
                        act(pt_[:], sp_[:, 0:TB], AF.Exp, [spk], [ptk])
                        for tt in range(NTT):
                            mm(R3[:, tt * 65:(tt + 1) * 65], pt_[:, tt * 128:(tt + 1) * 128], Vt_[:, kt, kvh, :],
                               ti == 0 and tt == 0, ti == nt - 1 and tt == NTT - 1, [ptk, vkey], ["psR3a"])
                    finish(h, br, R3, 65, first, 128, NTT)

                chk(7)
                for h in range(8):
                    kvh = h // 4
                    sp_, spk = sps[sn["n"] % 2]
                    pt_ = PTb[sn["n"] % 3]
                    ptk = f"PTb{sn['n'] % 3}"
                    sn["n"] += 1
                    mm(sp_[:, 0:TB], kcT[:, kvh, :], QT[:, h, :], True, False, ["kcT", "QT"], [spk])
                    mm(sp_[:, 0:TB], identb[:], ct["cmpbias"][:, t0:t0 + TB], False, True, ["k_identb", "k_cmpbias"], [spk])
                    act(pt_[:], sp_[:, 0:TB], AF.Exp, [spk], [ptk])
                    for tt in range(NTT):
                        mm(R3[:, tt * 97:(tt + 1) * 97], pt_[:, tt * 128:(tt + 1) * 128], vcx[:, kvh, :], tt == 0, tt == NTT - 1,
                           [ptk, "vcx"], ["psR3a"])
                    finish(h, 0, R3, 97, True, 128, NTT)
                    view = R3[:, 0:NTT * 97].rearrange("p (t c) -> p t c", t=NTT)
                    if h % 4 == 0:
                        vop("dve", "tensor_tensor", ["psR3a", "rden"], ["sc"], out=sc[:, :, kvh, :], in0=view[:, :, 65:97],
                            in1=rden[:].rearrange("p (a o) -> p a o", o=1).to_broadcast([128, NTT, 32]), op=ALU.mult)
                    else:
                        vop("dve", "tensor_tensor", ["psR3a", "rden"], ["otmp"], out=otmp[:, :, 0:32], in0=view[:, :, 65:97],
                            in1=rden[:].rearrange("p (a o) -> p a o", o=1).to_broadcast([128, NTT, 32]), op=ALU.mult)
                        vop("dve", "tensor_tensor", ["otmp", "sc"], ["sc"], out=sc[:, :, kvh, :], in0=sc[:, :, kvh, :], in1=otmp[:, :, 0:32], op=ALU.add)
                chk(8)
                for tt in range(NTT):
                    for kvh in range(2):
                        vop("dve", "tensor_tensor", ["sc", "k_addc"], ["scw"], out=scw[:], in0=sc[:, tt, kvh, :],
                            in1=ct["addc"][:, tb * NTT + tt, :], op=ALU.add)
                        vop("dve", "max", ["scw"], ["m8a"], out=m8a[:], in_=scw[:])
                        vop("dve", "match_replace", ["scw", "m8a"], ["otmp"], out=otmp[:, 0, 0:32], in_to_replace=m8a[:], in_values=scw[:],
                            imm_value=-3e4)
                        vop("dve", "max", ["otmp"], ["m8b"], out=m8b[:], in_=otmp[:, 0, 0:32])
                        vop("dve", "tensor_scalar", ["m8b"], ["thr"], out=thr[:], in0=m8b[:, 7:8], scalar1=-5000.0, scalar2=None, op0=ALU.max)
                        vop("dve", "tensor_scalar", ["scw", "thr"], ["selb"], out=selb[:], in0=scw[:], scalar1=thr[:, 0:1], scalar2=-BIG,
                            op0=ALU.is_lt, op1=ALU.mult)
                        tr(psT[0:32, 0:128], selb[:], identb[:], ["selb", "k_identb"], ["psT"])
                        cp("act", selbT[:, kvh, tt * 128:(tt + 1) * 128], psT[0:32, 0:128], ["psT"], ["selbT"])
                chk(9)
                for h in range(8):
                    kvh = h // 4
                    tiles = []
                    for kt in range(0, tb * NTT + NTT):
                        bs = [(ct["zp"][:, kt * 128:(kt + 1) * 128], selbT[:, kvh, :], ["k_zp", "selbT"])]
                        if kt >= tb * NTT:
                            bs.append((identb[:], ct["causb"][:, kt - tb * NTT, :], ["k_identb", "k_causb"]))
                        tiles.append((kt, bs))
                    attend(h, 1, tiles, KTs, "KTs", Vs, "Vs", False)
                chk(10)
                for h in range(8):
                    tiles = []
                    q0 = tb * NTT
                    for kt in range(max(0, q0 - 4), q0 + NTT):
                        bs = []
                        if kt >= q0:
                            bs.append((identb[:], ct["causb"][:, kt - q0, :], ["k_identb", "k_causb"]))
                        elif kt - q0 + 4 < 2:
                            bs.append((identb[:], ct["winlow"][:, kt - q0 + 4, :], ["k_identb", "k_winlow"]))
                        tiles.append((kt, bs))
                    attend(h, 2, tiles, KTw, "KTw", Vw, "Vw", False)
                chk(11)
                for tt in range(NTT):
                    out_proj(128, tt, xp[b, t0 + tt * 128:t0 + (tt + 1) * 128, :], yp[b, t0 + tt * 128:t0 + (tt + 1) * 128, :])
                chk(12)

          if with_sample:
            sample_jobs()
        except _Stop:
            for _i in range(int(os.environ.get("KPAD", "0"))):
                eng_ = os.environ.get("KPADENG", "act")
                if eng_ == "act":
                    act(thr[:], thr[:], AF.Copy, ["thr"], ["thr"])
                else:
                    cp(eng_, thr[:], m8a[:, 0:1], ["m8a"], ["thr"])
            if os.environ.get("KDBG"):
                dbg = dout("dbg", [128, 4096])
                o = 0
                for nm, tl, ncol in (("STz", STz, 512), ("Mt0", Mt[0], 512), ("Pk5", Pk[5], 512), ("Xa0", Xa[0], 128), ("Xa1", Xa[1], 128),
                                     ("tok", tok, 384), ("AR", AR, 512), ("BK", BK, 512), ("ytok", ytok, 256), ("Wc", Wc, 16)):
                    flat = tl[:]
                    shp_ = list(tl.shape) if hasattr(tl, "shape") else None
                    names = "abcdefg"[:len(shp_) - 1]
                    if len(shp_) > 2:
                        flat = tl[:].rearrange("p " + " ".join(names) + " -> p (" + " ".join(names) + ")")
                    stq(dbg[:, o:o + ncol], flat, [nm if nm not in ("Mt0", "Pk5", "Xa0", "Xa1") else nm])
                    o += ncol
        P.build(sems, block)
    return nc


def _core_inputs(c, inp, consts, cache2d=None, pt_override=None):
    d = {}
    d["xp"] = np.ascontiguousarray(inp["x_prompt"][2 * c:2 * c + 2])
    d["w_in"] = np.ascontiguousarray(inp["w_in"][0])
    d["w_out"] = np.ascontiguousarray(inp["w_out"][0])
    d["norm_g"] = np.ascontiguousarray(inp["norm_g"][0])
    d["final_g"] = np.ascontiguousarray(inp["final_g"])
    d["mu"] = np.ascontiguousarray(inp["mu_shift"][0])
    for k in ("w0", "a0", "k_k", "k_a", "r_k", "gn_w", "gn_b"):
        d[k] = np.ascontiguousarray(inp[k][0])
    d["w_dup"] = np.ascontiguousarray(inp["w_decay_up"][0])
    d["w_aup"] = np.ascontiguousarray(inp["w_aaa_up"][0])
    d["w_cpos"] = np.ascontiguousarray(inp["w_cmp_pos"][0])
    d["w_cmix"] = np.ascontiguousarray(inp["w_cmp_mix"][0])
    d["xs"] = np.ascontiguousarray(inp["x_sample"][4 * c:4 * c + 4]).reshape(16, 1024)
    d["cache"] = cache2d
    d["cwin"] = np.ascontiguousarray(inp["cache_kv_win"][0, 4 * c:4 * c + 4]).reshape(4, 512, 256)
    d["swkv"] = np.ascontiguousarray(inp["state_wkv"][0, 4 * c:4 * c + 4])
    d["sshift"] = np.ascontiguousarray(inp["state_shift"][0, 4 * c:4 * c + 4])
    d["pt"] = np.ascontiguousarray(inp["page_table"][4 * c:4 * c + 4] if pt_override is None else pt_override).astype(np.int32)
    for k, v in consts.items():
        d["c_" + k] = v
    return d


def kernel(**inp):
    inp = {k: np.asarray(v) for k, v in inp.items()}
    consts = _const_specs()
    cache2d = np.ascontiguousarray(inp["cache_kv"][0]).reshape(-1, 512)
    nc = build_nc(cache2d.shape[0])
    in_maps = [_core_inputs(c, inp, consts, cache2d) for c in range(8)]
    res = run_bass_kernel_spmd(nc, in_maps, core_ids=list(range(8))).results
    cat = lambda k: np.concatenate([r[k] for r in res], axis=0)
    y_prompt = cat("yp")
    y_sample = cat("ys").reshape(32, 4, 1024)
    kv_prompt = cat("kvp").reshape(1, 16, T, 4, 2, 64)
    kv_sample = cat("kvs").reshape(1, 32, 4, 4, 2, 64)
    win_prompt = cat("winp").reshape(1, 16, 512, 2, 2, 64)
    win_sample = cat("wins").reshape(1, 32, 512, 2, 2, 64)
    wkv_prompt = cat("wkvp").reshape(1, 16, 8, 64, 64)
    wkv_sample = cat("wkvs").reshape(1, 32, 8, 64, 64)
    shift_prompt = cat("shp").reshape(1, 16, 1664)
    shift_sample = cat("shs").reshape(1, 32, 1664)
    return (y_prompt, y_sample, kv_prompt, kv_sample, win_prompt, win_sample, wkv_prompt, wkv_sample,
            shift_prompt, shift_sample)
```

```python
import contextlib
import numpy as np
import ml_dtypes
import concourse.bass as bass
import concourse.mybir as mybir
from concourse.bass_utils import run_bass_kernel_spmd

F32 = mybir.dt.float32
BF16 = mybir.dt.bfloat16
I32 = mybir.dt.int32
F32R = mybir.dt.float32r
AF = mybir.ActivationFunctionType
ALU = mybir.AluOpType
AX = mybir.AxisListType

ENGS = ("pe", "act", "dve", "pool", "sp")
NDMASEM = 8
BIG = 30000.0
T = 2048
TB = 256
NTT = TB // 128
NB = T // TB
DW = 3992
EPS = 1e-6
GN_EPS = 64e-5
DEC_C = 0.6065306597126334


class _Stop(Exception):
    pass


import os
KSTOP = float(os.environ.get("KSTOP", "999"))


KSKIP = [int(os.environ.get("KSKIP", "0"))]


def chk(stage):
    if stage >= KSTOP:
        if KSKIP[0] > 0 and stage == KSTOP:
            KSKIP[0] -= 1
            return
        raise _Stop()


class Prog:
    def __init__(self, nc):
        self.nc = nc
        self.ops = []

    def op(self, eng, fn, reads=(), writes=()):
        import sys
        f = sys._getframe(2)
        self.ops.append(dict(eng=eng, fn=fn, reads=tuple(reads), writes=tuple(writes), dma=False, line=f.f_lineno))

    def dma(self, eng, out, in_, reads=(), writes=(), **kw):
        self.ops.append(dict(eng=eng, fn=lambda e: e.dma_start(out=out, in_=in_, **kw),
                             reads=tuple(reads), writes=tuple(writes), dma=True))

    def dma_fn(self, eng, fn, reads=(), writes=()):
        self.ops.append(dict(eng=eng, fn=fn, reads=tuple(reads), writes=tuple(writes), dma=True))

    def build(self, sems, block):
        ops = self.ops
        n = len(ops)
        last_w, readers = {}, {}
        deps = [None] * n
        needed = [False] * n
        for i, o in enumerate(ops):
            d = set()
            for r in o["reads"]:
                if r in last_w:
                    d.add(last_w[r])
            for w in o["writes"]:
                if w in last_w:
                    d.add(last_w[w])
                d.update(readers.get(w, ()))
            d.discard(i)
            d = {j for j in d if not (ops[j]["eng"] == "pe" and o["eng"] == "pe"
                                      and not ops[j]["dma"] and not o["dma"])}
            deps[i] = d
            for j in d:
                needed[j] = True
            for w in o["writes"]:
                last_w[w] = i
                readers[w] = []
            for r in o["reads"]:
                readers.setdefault(r, []).append(i)
        cnt = {e: 0 for e in ENGS}
        dcnt = {e: 0 for e in ENGS}
        ev = [None] * n
        prevdma = [None] * n
        for i, o in enumerate(ops):
            e = o["eng"]
            if o["dma"]:
                k = dcnt[e]
                dcnt[e] += 1
                s = k % NDMASEM
                ev[i] = ((e, s), 16 * (k // NDMASEM + 1))
                if k >= NDMASEM:
                    prevdma[i] = ((e, s), 16 * (k // NDMASEM))
            elif needed[i]:
                cnt[e] += 1
                ev[i] = (e, cnt[e])
        per_eng = {e: [i for i, o in enumerate(ops) if o["eng"] == e] for e in ENGS}
        final_dma = {}
        for i, o in enumerate(ops):
            if o["dma"]:
                final_dma[ev[i][0]] = max(final_dma.get(ev[i][0], 0), ev[i][1])

        def emit(e, eng, idxs, last):
            waited = {}
            for i in idxs:
                o = ops[i]
                want = {}
                for j in deps[i]:
                    sk, v = ev[j]
                    want[sk] = max(want.get(sk, 0), v)
                if prevdma[i] is not None:
                    sk, v = prevdma[i]
                    want[sk] = max(want.get(sk, 0), v)
                for sk, v in want.items():
                    if waited.get(sk, 0) < v:
                        eng.wait_ge(sems[sk], v)
                        waited[sk] = v
                if os.environ.get("KTRACE") and i >= n - 60:
                    print("OP", i, e, "line", o.get("line"), "R", o["reads"], "W", o["writes"], "waits", want, "ev", ev[i], flush=True)
                ins = o["fn"](eng)
                if o["dma"]:
                    ins.then_inc(sems[ev[i][0]], 16)
                elif ev[i] is not None:
                    ins.then_inc(sems[e], 1)
            if last:
                for sk, v in final_dma.items():
                    if waited.get(sk, 0) < v:
                        eng.wait_ge(sems[sk], v)

        block.sync(lambda eng: emit("sp", eng, per_eng["sp"], True))
        block.tensor(lambda eng: emit("pe", eng, per_eng["pe"], False))
        block.scalar(lambda eng: emit("act", eng, per_eng["act"], False))
        block.vector(lambda eng: emit("dve", eng, per_eng["dve"], False))
        block.gpsimd(lambda eng: emit("pool", eng, per_eng["pool"], False))


def _consts():
    bf = ml_dtypes.bfloat16
    c = {}
    p = np.arange(128)[:, None]
    q = np.arange(128)[None, :]
    c["identf"] = np.eye(128, dtype=np.float32)
    c["identb"] = np.eye(128, dtype=np.float32).astype(bf)
    su = (p < q).astype(np.float32)
    ui = (p <= q).astype(np.float32)
    c["mask4"] = np.concatenate([su, ui, su, ui], axis=1)
    c["maskL"] = np.concatenate([(p > q).astype(np.float32)] * 2, axis=1)
    tq = np.arange(TB)[None, :]
    causb = np.zeros((128, NTT, TB), np.float32)
    for r in range(NTT):
        causb[:, r, :] = np.where(p + 128 * r > tq, -BIG, 0.0)
    c["causb"] = causb.astype(bf)
    winlow = np.zeros((128, 2, TB), np.float32)
    winlow[:, 0, :] = np.where(p <= tq, -BIG, 0.0)
    winlow[:, 1, :] = np.where(p <= tq - 128, -BIG, 0.0)
    c["winlow"] = winlow.astype(bf)
    tt = np.arange(T)[None, :]
    c["cmpbias"] = np.where((16 * p + 31 > tt) | (p >= 127), -BIG, 0.0).astype(bf)
    s32 = np.arange(32)[:, None]
    c["zp"] = (tt // 64 == s32).astype(np.float32).astype(bf)
    addc = np.zeros((128, 16, 32), np.float32)
    for t16 in range(16):
        t = 128 * t16 + np.arange(128)[:, None]
        cur = t // 64
        s = np.arange(32)[None, :]
        valid = s <= cur
        forced = (s == 0) | (s == cur) | (s == cur - 1)
        addc[:, t16, :] = np.where(valid, np.where(forced, 1e4, 0.0), -1e4)
    c["addc"] = addc
    j = np.arange(128)[:, None]
    s = np.arange(32)[None, :]
    ov = ((16 * j < 64 * s + 64) & (16 * j + 32 > 64 * s) & (j < 127)).astype(np.float32)
    c["ov32"] = ov
    col = np.arange(248)[None, :]
    c["zs"] = (col - 120 == p // 16).astype(np.float32).astype(bf)
    c["resetm"] = np.tile((np.arange(TB)[None, :] % 128 != 0).astype(np.float32), (128, 1))
    c["bd64"] = ((p // 64) == (q // 64)).astype(np.float32)
    c["hsel"] = np.stack([(np.arange(128) < 64), (np.arange(128) >= 64)], axis=1).astype(np.float32)
    c["zeros"] = np.zeros((128, 512), np.float32).astype(bf)
    p4 = np.arange(4)[:, None]
    q4 = np.arange(4)[None, :]
    su4 = (p4 < q4).astype(np.float32)
    ui4 = (p4 <= q4).astype(np.float32)
    c["mask4s"] = np.concatenate([su4, ui4, su4, ui4], axis=1)
    c["maskLs"] = np.concatenate([(p4 > q4).astype(np.float32)] * 2, axis=1)
    jj = np.arange(128)[:, None]
    s33 = np.arange(33)[None, :]
    c["ovc"] = ((4 * s33 - 1 <= jj) & (jj <= 4 * s33 + 3)).astype(np.float32).astype(bf)
    lastb = np.zeros((128, 16), np.float32)
    lastb[127, :] = -BIG
    c["lastb"] = lastb.astype(bf)
    gsel = np.zeros((16, 4), np.float32)
    for g in range(4):
        for t in range(4):
            gsel[g * 4 + t, t] = 1.0
    c["gsel"] = gsel
    addcs = np.zeros((4, 257), np.float32)
    addcs[:, [0, 255, 256]] = 1e4
    c["addcs"] = addcs
    hl = np.zeros((1, 2, 128), np.float32)
    hl[0, 0, :64] = 1.0
    hl[0, 1, 64:] = 1.0
    c["hl"] = hl.astype(bf)
    tq16 = np.tile(np.arange(4), 4)[None, :]
    c["tailb"] = np.where(np.arange(4)[:, None] > tq16, -BIG, 0.0).astype(np.float32).astype(bf)
    winms = np.ones((128, 4, 2, 4), np.float32)
    winms[:, 0, :, :] = (np.arange(128)[:, None, None] > np.arange(4)[None, None, :]).astype(np.float32)
    c["winms"] = winms.reshape(128, 32).astype(bf)
    c["iota"] = np.stack([2.0 * np.arange(128), 2.0 * np.arange(128) + 1.0], axis=1).astype(np.float32)
    return c


CONST_SPECS = None


def _const_specs():
    global CONST_SPECS
    if CONST_SPECS is None:
        CONST_SPECS = _consts()
    return CONST_SPECS


def _groups():
    g = []
    g.append(("T0", [(1664, 512)], False))
    g.append(("T1", [(2688, 512)], False))
    g.append(("T2", [(3200, 512)], False))
    g.append(("T3", [(3712, 280)], False))
    g.append(("F0", [(1536, 128)], False))
    for fc in range(4):
        g.append((f"R{fc}", [(128 * fc, 128), (512 + 128 * fc, 128), (1024 + 128 * fc, 128)], False))
    g.append(("Q", [(2176, 512)], True))
    g.append(("K", [(3456, 128), (3712, 128)], False))
    return g


GROUPS = _groups()
GIDX = {g[0]: i for i, g in enumerate(GROUPS)}


def build_nc(n_cache_rows, with_sample=True):
    nc = bass.Bass("TRN2", target_bir_lowering=False)
    C = _const_specs()
    dt_of = lambda a: BF16 if a.dtype == ml_dtypes.bfloat16 else F32

    def din(name, shape, dt=F32):
        return nc.dram_tensor(name, list(shape), dt, kind="ExternalInput").ap()

    def dout(name, shape, dt=F32):
        return nc.dram_tensor(name, list(shape), dt, kind="ExternalOutput").ap()

    xp = din("xp", [2, T, 1024])
    w_in = din("w_in", [1024, DW])
    w_out = din("w_out", [1024, 1024])
    norm_g = din("norm_g", [1024])
    final_g = din("final_g", [1024])
    mu = din("mu", [1664])
    vecs = {k: din(k, [512]) for k in ("w0", "a0", "k_k", "k_a", "r_k", "gn_w", "gn_b")}
    w_dup = din("w_dup", [64, 512])
    w_aup = din("w_aup", [64, 512])
    w_cpos = din("w_cpos", [2, 32, 64])
    w_cmix = din("w_cmix", [2, 64, 64])
    cd = {k: din("c_" + k, v.shape, dt_of(v)) for k, v in C.items()}

    yp = dout("yp", [2, T, 1024])
    kvp = dout("kvp", [2, T, 512])
    winp = dout("winp", [2, 512, 256])
    wkvp = dout("wkvp", [2, 8, 64, 64])
    shp = dout("shp", [2, 1664])

    xs = din("xs", [16, 1024])
    cache = din("cache", [n_cache_rows, 512])
    cwin = din("cwin", [4, 512, 256])
    swkv = din("swkv", [4, 8, 64, 64])
    sshift = din("sshift", [4, 1664])
    pt = din("pt", [4, 128], I32)
    ys = dout("ys", [16, 1024])
    kvs = dout("kvs", [16, 512])
    wins = dout("wins", [4, 512, 256])
    wkvs = dout("wkvs", [4, 8, 64, 64])
    shs = dout("shs", [4, 1664])

    wscr = nc.dram_tensor("wscr", [len(GROUPS), 128, 8 * 512], BF16, kind="Internal").ap()

    P = Prog(nc)
    with contextlib.ExitStack() as st:
        def sb(name, shape, dt=F32):
            return st.enter_context(nc.sbuf_tensor(name, list(shape), dt))

        def pst(name, shape, dt=F32):
            return st.enter_context(nc.psum_tensor(name, list(shape), dt))

        def mm(out, lhsT, rhs, start, stop, R, W):
            P.op("pe", lambda e: e.matmul(out, lhsT=lhsT, rhs=rhs, start=start, stop=stop,
                                          skip_group_check=True), R, W)

        def tr(out, in_, ident, R, W):
            P.op("pe", lambda e: e.transpose(out, in_, ident), R, W)

        def act(out, in_, func, R, W, eng="act", **kw):
            P.op("act", lambda e: e.activation(out=out, in_=in_, func=func, **kw), R, W)

        def vop(eng, name, R, W, **kw):
            P.op(eng, lambda e: getattr(e, name)(**kw), R, W)

        def cp(eng, out, in_, R, W):
            if eng == "act":
                P.op("act", lambda e: e.copy(out=out, in_=in_), R, W)
            else:
                P.op(eng, lambda e: e.tensor_copy(out=out, in_=in_), R, W)

        def ld(out, in_, W, R=(), eng="sp", **kw):
            P.dma(eng, out, in_, reads=R, writes=W, **kw)

        def stq(out, in_, R, eng="pool"):
            P.dma(eng, out, in_, reads=R, writes=())

        rr = {"n": 0}

        def evac_eng():
            rr["n"] += 1
            return "act" if rr["n"] % 2 else "dve"

        ct = {}
        for k, v in C.items():
            ct[k] = sb("k_" + k, v.shape, dt_of(v))
            ld(ct[k][:], cd[k], ["k_" + k])
        identf, identb = ct["identf"], ct["identb"]

        tmpf = [sb(f"tmpf{i}", [128, TB]) for i in range(8)]
        stw_t = [sb(f"stw{i}", [128, 512]) for i in range(1)]
        stw = [stw_t[0], stw_t[0]]
        wup_f = stw_t[0]
        g8 = sb("g8", [128, 8])
        gq8 = sb("gq8", [128, 8])
        mu13 = sb("mu13", [128, 13])
        pv = {k: sb("pv_" + k, [128, 4]) for k in ("w0", "a0", "k_k", "k_a", "r_k")}
        omka = sb("omka", [128, 4])
        gnw = sb("gnw", [128, 512])
        gnb = sb("gnb", [128, 512])
        fgb = sb("fgb", [128, 1024])
        wup = sb("wup", [128, 2, 512], BF16)
        wt = sb("wt", [128, 2, 256])
        wmix_f = sb("wmix_f", [128, 2, 64])
        wmix = sb("wmix", [128, 2, 2, 64], BF16)
        xres = sb("xres", [128, 1024])
        wout_f = xres
        wout = sb("wout", [128, 8, 1024], BF16)
        with nc.allow_non_contiguous_dma(reason="small per-feature vectors"):
            ld(g8[:], norm_g.rearrange("(kt p) -> p kt", p=128), ["g8"], allow_slow_non_contiguous=True)
            ld(mu13[:], mu.rearrange("(c p) -> p c", p=128), ["mu13"], allow_slow_non_contiguous=True)
            for k in pv:
                ld(pv[k][:], vecs[k].rearrange("(c p) -> p c", p=128), ["pv_" + k], allow_slow_non_contiguous=True)
        ld(gnw[:], vecs["gn_w"].rearrange("(o f) -> o f", o=1).to_broadcast([128, 512]), ["gnw"])
        ld(gnb[:], vecs["gn_b"].rearrange("(o f) -> o f", o=1).to_broadcast([128, 512]), ["gnb"])
        ld(fgb[:], final_g.rearrange("(o f) -> o f", o=1).to_broadcast([128, 1024]), ["fgb"])
        ld(wup_f[0:64, :], w_dup, ["stw0"])
        ld(wup_f[64:128, :], w_aup, ["stw0"])
        vop("pool", "memset", [], ["wup"], ap=wup[:], constant=0.0)
        cp("dve", wup[0:64, 0, :], wup_f[0:64, :], ["stw0"], ["wup"])
        cp("dve", wup[64:128, 1, :], wup_f[64:128, :], ["stw0"], ["wup"])
        vop("dve", "tensor_scalar", ["g8"], ["gq8"], out=gq8[:], in0=g8[:], scalar1=0.125, scalar2=None, op0=ALU.mult)
        vop("dve", "tensor_scalar", ["pv_k_a"], ["omka"], out=omka[:], in0=pv["k_a"][:], scalar1=-1.0, scalar2=1.0,
            op0=ALU.mult, op1=ALU.add)
        with nc.allow_non_contiguous_dma(reason="compress weights"):
            for m in range(2):
                for e in range(2):
                    for k in range(2):
                        for blk in range(8):
                            ld(wt[16 * blk:16 * blk + 16, m, e * 128 + k * 64: e * 128 + k * 64 + 64],
                               w_cpos[e, 16 * m:16 * m + 16, :], ["wt"])
            for e in range(2):
                ld(wmix_f[0:64, e, :], w_cmix[e], ["wmix_f"])
                ld(wmix_f[64:128, e, :], w_cmix[e], ["wmix_f"])
        vop("pool", "memset", [], ["wmix"], ap=wmix[:], constant=0.0)
        cp("dve", wmix[0:64, 0, :, :], wmix_f[0:64, :, :], ["wmix_f"], ["wmix"])
        cp("dve", wmix[64:128, 1, :, :], wmix_f[64:128, :, :], ["wmix_f"], ["wmix"])
        for kt in range(8):
            ld(wout_f[:], w_out[kt * 128:(kt + 1) * 128, :], ["xres"])
            cp("act" if kt % 2 else "dve", wout[:, kt, :], wout_f[:], ["xres"], ["wout"])

        wbuf = [sb(f"wbuf{i}", [128, 8, 512], BF16) for i in range(2)]
        for gi, (gname, segs, qs) in enumerate(GROUPS):
            sbuf = wbuf[gi % 2]
            key = f"wbuf{gi % 2}"
            for kt in range(8):
                s = stw[0]
                skey = "stw0"
                off = 0
                for (c0, ncol) in segs:
                    ld(s[:, off:off + ncol], w_in[kt * 128:(kt + 1) * 128, c0:c0 + ncol], [skey])
                    off += ncol
                gsrc = gq8 if qs else g8
                act(sbuf[:, kt, 0:off], s[:, 0:off], AF.Copy, [skey, "g8", "gq8"], [key], scale=gsrc[:, kt:kt + 1])
            P.dma("sp", wscr[gi].rearrange("p (k c) -> p k c", k=8)[:, :, 0:off], sbuf[:, :, 0:off],
                  reads=[key], writes=["wscr"])

        xt = [sb(f"xt{i}", [128, 1024]) for i in range(2)]
        ss = sb("ss", [128, 1])
        rstd = sb("rstd", [128, 1])
        xn = [sb(f"xn{i}", [128, 1024], BF16) for i in range(2)]
        xnT = sb("xnT", [128, 8, TB], BF16)
        gsr = sb("gsr", [128, NTT, 512], BF16)
        gsn = sb("gsn", [128, NTT, 512], BF16)
        kv0 = [sb(f"kv0_{i}", [128, 512]) for i in range(2)]
        kv1 = [sb(f"kv1_{i}", [128, 280]) for i in range(2)]
        gates = sb("gates", [128, NTT, 24])
        ym = [sb(f"ym{i}", [128, 256], BF16) for i in range(2)]
        QT = sb("QT", [64, 8, TB], BF16)
        KTs = sb("KTs", [64, 2, T], BF16)
        KTw = sb("KTw", [64, 2, T], BF16)
        Vs = sb("Vs", [128, 16, 2, 65], BF16)
        Vw = sb("Vw", [128, 16, 2, 65], BF16)
        pooled = sb("pooled", [128, 2, 128])
        pooledb = sb("pooledb", [128, 2, 128], BF16)
        kcT = sb("kcT", [64, 2, 128], BF16)
        vcx = sb("vcx", [128, 2, 97], BF16)
        zsb12 = sb("zsb12", [128, TB + 1])
        zsb = sb("zsb", [128, 3, TB + 1])
        zs12 = sb("zs12", [128, TB])
        zs = sb("zs", [128, 3, TB])
        lora = sb("lora", [128, TB], BF16)
        AR = sb("AR", [128, NTT, 2, 128])
        BK = sb("BK", [128, 2, TB])
        Wc = sb("Wc", [128, NTT * 8])
        prod = sb("prod", [128, TB])
        STz = sb("STz", [128, 4, 2, 64])
        BKz = sb("BKz", [128, 2, 2, TB])
        wkv2 = sb("wkv2", [128, 64])
        Ssb = sb("Ssb", [128, 128])
        tok = sb("tok", [128, 3, 2, 64])
        Mt = [sb(f"Mt{i}", [128, 512]) for i in range(2)]
        PT0 = sb("PT0", [128, 2, 128])
        Pk = [sb(f"Pk{i}", [128, 2, 2, 128]) for i in range(6)]
        Xa = [sb(f"Xa{i}", [128, 2, 64]) for i in range(2)]
        ytok = sb("ytok", [128, NTT, 2, 64])
        vtok = sb("vtok", [128, NTT, 2, 64])
        rk = sb("rk", [128, NTT, 2])
        st4 = sb("st4", [128, 4])
        gtmp = [sb(f"gtmp{i}", [128, 2, 64]) for i in range(3)]
        mix = sb("mix", [128, NTT, 1024], BF16)
        mixT = sb("mixT", [128, 8, 128], BF16)
        sq_junk = mixT[:].rearrange("p k t -> p (k t)")
        PTb = [sb(f"PTb{i}", [128, TB], BF16) for i in range(3)]
        oacc = sb("oacc", [128, NTT, 8, 64])
        otmp = sb("otmp", [128, NTT, 64])
        rden = sb("rden", [128, NTT])
        scg = sb("scg", [128, NTT])
        sc = sb("sc", [128, NTT, 2, 32])
        scw = sb("scw", [128, 32])
        m8a = sb("m8a", [128, 8])
        m8b = sb("m8b", [128, 8])
        thr = sb("thr", [128, 1])
        selb = sb("selb", [128, 32], BF16)
        selbT = sb("selbT", [32, 2, TB], BF16)
        yout = sb("yout", [128, 1024])
        zc = sb("zc", [128, 13])

        Vn = sb("Vn", [4, 2, 2, 65], BF16)
        QTs = sb("QTs", [64, 8, 4], BF16)
        QTz = sb("QTz", [128, 2, 16], BF16)
        KTn = sb("KTn", [64, 2, 2, 4], BF16)
        ptb = sb("ptb", [128, 128], I32)
        ptf = sb("ptf", [128, 128])
        pidx = sb("pidx", [128, 2, 128], I32)
        cache_h = cache.rearrange("n (h c) -> (n h) c", h=2)
        pg4 = [xres[:].rearrange("p (g c) -> p g c", g=4), yout[:].rearrange("p (g c) -> p g c", g=4)]
        pall = Vs[:].rearrange("p a k c -> p (a k c)")[:, 0:2048].rearrange("p (c j) -> p c j", c=2)
        kcTs = Vw[0:64, :, :, :].rearrange("p a k c -> p (a k c)")[:, 0:2048].rearrange("p (c j) -> p c j", c=2)
        vcs = ct["cmpbias"][:, 0:1040].rearrange("p (j k c) -> p j k c", j=8, k=2)
        selflat4 = ct["zp"][0:1, 0:2048].rearrange("o (h g k t) -> o h g k t", h=2, g=128, k=2)
        ocs1 = stw_t[0][0:16, 0:322]
        obr = sb("obr", [16, 3, 2, 64])
        scs = kv1[1][0:4, 0:257]
        scs2 = kv0[1][0:4, 0:257]
        KTpg = sb("KTpg", [128, 4, 128], BF16)
        Vpg = sb("Vpg", [128, 4, 2, 65], BF16)
        mexp = sb("mexp", [128, 32], BF16)

        psA = pst("psA", [128, 512])
        psB = pst("psB", [128, 512])
        psT = pst("psT", [128, 1024], BF16)
        psP = pst("psP", [128, 512])
        psR = [pst(f"psR{i}", [128, 512]) for i in range(4)]

        sems = {}
        for e in ENGS:
            sems[e] = st.enter_context(nc.semaphore("s_" + e))
            for i in range(NDMASEM):
                sems[(e, i)] = st.enter_context(nc.semaphore(f"d_{e}_{i}"))
        block = st.enter_context(nc.Block())

        wb_n = {"n": 0}

        def load_group(gname):
            gi = GIDX[gname]
            i = wb_n["n"] % 2
            wb_n["n"] += 1
            ld(wbuf[i][:], wscr[gi].rearrange("p (k c) -> p k c", k=8), [f"wbuf{i}"], R=["wscr"])
            return wbuf[i], f"wbuf{i}"

        vop("pool", "memset", [], ["Vs"], ap=Vs[:], constant=1.0)
        vop("pool", "memset", [], ["Vw"], ap=Vw[:], constant=1.0)

        identr_t = sb("identr", [128, 128])
        identr = identr_t[:].bitcast(F32R)
        cp("dve", identr, identf[:], ["k_identf"], ["identr"])
        psAB = [(psA, "psA"), (psB, "psB")]
        pn = {"n": 0}

        def nextps():
            pn["n"] += 1
            return psAB[pn["n"] % 2]

        R0, R1, R2, R3 = psR
        sps = [(R1, "psR1"), (R2, "psR2")]
        sn = {"n": 0}

        def rmsnorm_T(xsrc, npart, tt):
            x_ = xt[tt % 2]
            xk = f"xt{tt % 2}"
            pp = slice(0, npart)
            ld(x_[pp, :], xsrc, [xk])
            act(sq_junk[pp, :], x_[pp, :], AF.Square, [xk], ["mixT", "ss"], accum_out=ss[pp, :])
            act(rstd[pp, :], ss[pp, :], AF.Sqrt, ["ss"], ["rstd"], scale=1.0 / 1024, bias=EPS)
            vop("dve", "reciprocal", ["rstd"], ["rstd"], out=rstd[pp, :], in_=rstd[pp, :])
            xn_ = xn[tt % 2]
            xnk = f"xn{tt % 2}"
            vop("dve", "tensor_scalar", [xk, "rstd"], [xnk], out=xn_[pp, :], in0=x_[pp, :], scalar1=rstd[pp, 0:1],
                scalar2=None, op0=ALU.mult)
            for kt in range(8):
                tr(psT[:, kt * 128:kt * 128 + npart], xn_[pp, kt * 128:(kt + 1) * 128], identb[pp, pp],
                   [xnk, "k_identb"], ["psT"])
            cp("act", xnT[:, :, tt * 128:tt * 128 + npart], psT[:].rearrange("p (k t) -> p k t", k=8)[:, :, 0:npart],
               ["psT"], ["xnT"])

        def rwkv_block(n, C, nlev, npart_of_chunk, gate_cols):
            nch = n // C
            use_r = (C == 128)
            Rm = (lambda ap: ap.bitcast(F32R)) if use_r else (lambda ap: ap)
            Ro = lambda ap: ap.bitcast(F32R)
            idm = identr if use_r else identf
            idk = "identr" if use_r else "k_identf"
            wb, wk = load_group("F0")
            ps_, pk = nextps()
            for kt in range(8):
                mm(ps_[:, 0:n], wb[:, kt, 0:128], xnT[:, kt, 0:n], kt == 0, kt == 7, ["xnT", wk], [pk])
            cp("dve", zsb12[:, 0:1], zc[:, 12:13], ["zc"], ["zsb12"])
            cp("act", zsb12[:, 1:n + 1], ps_[:, 0:n], [pk], ["zsb12"])
            cp("dve", zc[:, 12:13], zsb12[:, n:n + 1], ["zsb12"], ["zc"])
            vop("dve", "tensor_tensor", ["zsb12"], ["tmpf5"], out=tmpf[5][:, 0:n], in0=zsb12[:, 0:n],
                in1=zsb12[:, 1:n + 1], op=ALU.subtract)
            vop("dve", "scalar_tensor_tensor", ["tmpf5", "mu13", "zsb12"], ["zs12"], out=zs12[:, 0:n], in0=tmpf[5][:, 0:n],
                scalar=mu13[:, 12:13], in1=zsb12[:, 1:n + 1], op0=ALU.mult, op1=ALU.add)
            act(lora[0:64, 0:n], zs12[0:64, 0:n], AF.Tanh, ["zs12"], ["lora"])
            cp("dve", lora[64:128, 0:n], zs12[64:128, 0:n], ["zs12"], ["lora"])
            m4 = ct["mask4"] if C == 128 else ct["mask4s"]
            mL = ct["maskL"] if C == 128 else ct["maskLs"]
            m4k = "k_mask4" if C == 128 else "k_mask4s"
            mLk = "k_maskL" if C == 128 else "k_maskLs"
            cpp = slice(0, C)
            for fc in range(4):
                wb, wk = load_group(f"R{fc}")
                for j3 in range(3):
                    ps_, pk = nextps()
                    cidx = j3 * 4 + fc
                    for kt in range(8):
                        mm(ps_[:, 0:n], wb[:, kt, j3 * 128:(j3 + 1) * 128], xnT[:, kt, 0:n], kt == 0, kt == 7,
                           ["xnT", wk], [pk])
                    cp("dve", zsb[:, j3, 0:1], zc[:, cidx:cidx + 1], ["zc"], ["zsb"])
                    cp("act", zsb[:, j3, 1:n + 1], ps_[:, 0:n], [pk], ["zsb"])
                    cp("dve", zc[:, cidx:cidx + 1], zsb[:, j3, n:n + 1], ["zsb"], ["zc"])
                    vop("dve", "tensor_tensor", ["zsb"], [f"tmpf{5 + j3}"], out=tmpf[5 + j3][:, 0:n], in0=zsb[:, j3, 0:n],
                        in1=zsb[:, j3, 1:n + 1], op=ALU.subtract)
                    vop("dve", "scalar_tensor_tensor", [f"tmpf{5 + j3}", "mu13", "zsb"], ["zs"], out=zs[:, j3, 0:n],
                        in0=tmpf[5 + j3][:, 0:n], scalar=mu13[:, cidx:cidx + 1], in1=zsb[:, j3, 1:n + 1],
                        op0=ALU.mult, op1=ALU.add)
                r_, k_, v_ = zs[:, 0, 0:n], zs[:, 1, 0:n], zs[:, 2, 0:n]
                sg, al, lw, cum, t4, t5, t6, t7 = [t[:, 0:n] for t in tmpf]
                K = lambda *i: [f"tmpf{j}" for j in i]
                ps_, pk = nextps()
                mm(ps_[:, 0:n], wup[:, 0, fc * 128:(fc + 1) * 128], lora[:, 0:n], True, True, ["wup", "lora"], [pk])
                act(sg, ps_[:, 0:n], AF.Sigmoid, [pk, "pv_w0"], K(0), bias=pv["w0"][:, fc:fc + 1])
                vop("dve", "tensor_scalar", K(0), K(2), out=lw, in0=sg, scalar1=-DEC_C, scalar2=None, op0=ALU.mult)
                ps_, pk = nextps()
                mm(ps_[:, 0:n], wup[:, 1, fc * 128:(fc + 1) * 128], lora[:, 0:n], True, True, ["wup", "lora"], [pk])
                act(al, ps_[:, 0:n], AF.Sigmoid, [pk, "pv_a0"], K(1), bias=pv["a0"][:, fc:fc + 1])
                vop("dve", "tensor_scalar", ["zs", "pv_k_k"], K(4), out=t4, in0=k_, scalar1=pv["k_k"][:, fc:fc + 1],
                    scalar2=None, op0=ALU.mult)
                vop("dve", "tensor_tensor", K(4), K(5), out=t5, in0=t4, in1=t4, op=ALU.mult)
                ps_, pk = nextps()
                mm(ps_[:, 0:n], ct["bd64"][:], t5, True, True, ["k_bd64"] + K(5), [pk])
                vop("dve", "tensor_scalar", [pk], K(5), out=t5, in0=ps_[:, 0:n], scalar1=1e-24, scalar2=None, op0=ALU.max)
                act(t5, t5, AF.Sqrt, K(5), K(5))
                vop("dve", "reciprocal", K(5), K(5), out=t5, in_=t5)
                vop("dve", "tensor_tensor", K(4, 5), K(4), out=t4, in0=t4, in1=t5, op=ALU.mult)
                vop("dve", "tensor_scalar", K(1) + ["pv_k_a", "omka"], K(5), out=t5, in0=al,
                    scalar1=pv["k_a"][:, fc:fc + 1], scalar2=omka[:, fc:fc + 1], op0=ALU.mult, op1=ALU.add)
                vop("dve", "tensor_tensor", ["zs"] + K(5), K(5), out=t5, in0=k_, in1=t5, op=ALU.mult)
                vop("dve", "tensor_tensor_scan", ["k_resetm"] + K(2), K(3), out=cum, data0=ct["resetm"][:, 0:n] if C == 128 else ct["resetm"][:, 0:n],
                    data1=lw, initial=0.0, op0=ALU.mult, op1=ALU.add)
                act(t6, cum, AF.Exp, K(3), K(6))
                for ch in range(nch):
                    cs = slice(ch * C, (ch + 1) * C)
                    cp("dve", Wc[:, 8 * ch:8 * ch + 1], t6[:, ch * C + C - 1: ch * C + C], K(6), ["Wc"])
                    vop("dve", "tensor_tensor", ["zs"] + K(6), ["AR"], out=Ro(AR[:, ch, 1, 0:C]), in0=r_[:, cs], in1=t6[:, cs], op=ALU.mult)
                vop("dve", "scalar_tensor_tensor", ["zs", "pv_r_k"] + K(5), ["prod"], out=prod[:, 0:n], in0=r_,
                    scalar=pv["r_k"][:, fc:fc + 1], in1=t5, op0=ALU.mult, op1=ALU.mult)
                act(t7, cum, AF.Exp, K(3), K(7), scale=-1.0)
                vop("dve", "tensor_tensor", K(5, 7), ["BK"], out=Ro(BK[:, 1, 0:n]), in0=t5, in1=t7, op=ALU.mult)
                for hh in range(2):
                    rws = slice(hh * 64, hh * 64 + 64)
                    cp("act", Ro(BKz[rws, hh, 1, 0:n]), BK[rws, 1, 0:n], ["BK"], ["BKz"])
                vop("dve", "tensor_tensor", K(4, 1), K(5), out=t5, in0=t4, in1=al, op=ALU.mult)
                vop("dve", "tensor_tensor", K(5, 7), ["BK"], out=Ro(BK[:, 0, 0:n]), in0=t5, in1=t7, op=ALU.mult)
                for hh in range(2):
                    rws = slice(hh * 64, hh * 64 + 64)
                    cp("act", Ro(BKz[rws, hh, 0, 0:n]), BK[rws, 0, 0:n], ["BK"], ["BKz"])
                vop("dve", "tensor_tensor", K(3, 2), K(6), out=t6, in0=cum, in1=lw, op=ALU.subtract)
                act(t6, t6, AF.Exp, K(6), K(6))
                for ch in range(nch):
                    cs = slice(ch * C, (ch + 1) * C)
                    vop("dve", "scalar_tensor_tensor", K(4, 6), ["AR"], out=Ro(AR[:, ch, 0, 0:C]), in0=t4[:, cs], scalar=-1.0,
                        in1=t6[:, cs], op0=ALU.mult, op1=ALU.mult)
                chk(3)
                for ch in range(nch):
                    cs = slice(ch * C, (ch + 1) * C)
                    ARf = AR[:, ch, :, 0:C]
                    tr(R0[cpp, 0:128], BK[:, 0, cs], identf[:], ["BK", "k_identf"], ["psR0"])
                    tr(R0[cpp, 128:256], BK[:, 1, cs], identf[:], ["BK", "k_identf"], ["psR0"])
                    tr(R0[cpp, 256:384], zs[:, 2, cs], identf[:], ["zs", "k_identf"], ["psR0"])
                    cp("act", Ro(tok[cpp].rearrange("p a h d -> p (a h d)")), R0[cpp, 0:384], ["psR0"], ["tok"])
                    cp("dve", vtok[cpp, ch, :, :], tok[cpp, 2, :, :], ["tok"], ["vtok"])
                    for hh in range(2):
                        for a2 in range(2):
                            for a3 in range(2):
                                mm(R1[cpp, (2 * a2 + a3) * C:(2 * a2 + a3 + 1) * C], Rm(BKz[:, hh, a2, cs]), Rm(AR[:, ch, a3, 0:C]), True, True,
                                   ["BKz", "AR"], ["psR1"])
                        vop("dve", "tensor_tensor", ["psR1", m4k], [f"Mt{hh}"], out=Ro(Mt[hh][cpp, 0:4 * C]), in0=R1[cpp, 0:4 * C],
                            in1=m4[cpp, 0:4 * C], op=ALU.mult)
                        mm(R3[cpp, 256 + hh * C: 256 + (hh + 1) * C], Rm(AR[:, ch, 0, 0:C]), Rm(BKz[:, hh, 0, cs]), True, True,
                           ["AR", "BKz"], ["psR3b"])
                    vop("dve", "tensor_tensor", ["psR3b", mLk], ["PT0"], out=Ro(PT0[cpp, :, 0:C]),
                        in0=R3[cpp, 256:256 + 2 * C].rearrange("p (h t) -> p h t", h=2), in1=mL[cpp, 0:2 * C].rearrange("p (h t) -> p h t", h=2), op=ALU.mult)
                    for lv in range(nlev - 1):
                        for hh in range(2):
                            if lv == 0:
                                Pm, PTm, kk_ = Mt[hh][cpp, 0:C], PT0[cpp, hh, 0:C], [f"Mt{hh}", "PT0"]
                            else:
                                Pm, PTm, kk_ = Pk[lv - 1][cpp, hh, 0, 0:C], Pk[lv - 1][cpp, hh, 1, 0:C], [f"Pk{lv - 1}"]
                            mm(R2[cpp, (2 * hh) * C:(2 * hh + 1) * C], Rm(PTm), Rm(Pm), True, True, kk_, ["psR2"])
                            mm(R2[cpp, (2 * hh + 1) * C:(2 * hh + 2) * C], Rm(Pm), Rm(PTm), True, True, kk_, ["psR2"])
                        cp(evac_eng(), Ro(Pk[lv][cpp, :, :, 0:C]), R2[cpp, 0:4 * C].rearrange("p (h a t) -> p h a t", h=2, a=2), ["psR2"], [f"Pk{lv}"])
                    for hh in range(2):
                        mm(R3[cpp, hh * 64:(hh + 1) * 64], Rm(AR[:, ch, 0, 0:C]), Rm(STz[:, fc, hh, :]), True, False, ["AR", "STz"], ["psR3a"])
                        mm(R3[cpp, hh * 64:(hh + 1) * 64], Rm(Mt[hh][cpp, 2 * C:3 * C]), Rm(tok[cpp, 2, hh, :]), False, True,
                           [f"Mt{hh}", "tok"], ["psR3a"])
                    xi = 0
                    cp("act", Ro(Xa[0][cpp].rearrange("p h d -> p (h d)")), R3[cpp, 0:128], ["psR3a"], ["Xa0"])
                    for lv in range(nlev):
                        for hh in range(2):
                            if lv == 0:
                                Pm, kk_ = Mt[hh][cpp, 0:C], [f"Mt{hh}"]
                            else:
                                Pm, kk_ = Pk[lv - 1][cpp, hh, 0, 0:C], [f"Pk{lv - 1}"]
                            mm(R3[cpp, hh * 64:(hh + 1) * 64], idm[cpp, cpp], Rm(Xa[xi][cpp, hh, :]), True, False,
                               [idk, f"Xa{xi}"], ["psR3a"])
                            mm(R3[cpp, hh * 64:(hh + 1) * 64], Rm(Pm), Rm(Xa[xi][cpp, hh, :]), False, True, kk_ + [f"Xa{xi}"], ["psR3a"])
                        cp(evac_eng(), Ro(Xa[1 - xi][cpp].rearrange("p h d -> p (h d)")), R3[cpp, 0:128], ["psR3a"], [f"Xa{1 - xi}"])
                        xi = 1 - xi
                    E = Xa[xi]
                    ek = f"Xa{xi}"
                    for hh in range(2):
                        mm(R3[cpp, hh * 64:(hh + 1) * 64], Rm(AR[:, ch, 1, 0:C]), Rm(STz[:, fc, hh, :]), True, False, ["AR", "STz"], ["psR3a"])
                        mm(R3[cpp, hh * 64:(hh + 1) * 64], Rm(Mt[hh][cpp, C:2 * C]), Rm(E[cpp, hh, :]), False, False, [f"Mt{hh}", ek], ["psR3a"])
                        mm(R3[cpp, hh * 64:(hh + 1) * 64], Rm(Mt[hh][cpp, 3 * C:4 * C]), Rm(tok[cpp, 2, hh, :]), False, True,
                           [f"Mt{hh}", "tok"], ["psR3a"])
                    cp("act", ytok[cpp, ch, :, :].rearrange("p h d -> p (h d)"), R3[cpp, 0:128], ["psR3a"], ["ytok"])
                    SU = psB[:, 0:128]
                    mm(SU, idm[:], Rm(STz[:, fc, :, :].rearrange("p h d -> p (h d)")), True, False, [idk, "STz"], ["psB"])
                    mm(SU, Rm(tok[cpp, 0, :, :].rearrange("p h d -> p (h d)")), Rm(E[cpp].rearrange("p h d -> p (h d)")), False, False, ["tok", ek], ["psB"])
                    mm(SU, Rm(tok[cpp, 1, :, :].rearrange("p h d -> p (h d)")), Rm(tok[cpp, 2, :, :].rearrange("p h d -> p (h d)")), False, True, ["tok"], ["psB"])
                    cp("dve", Ssb[:], psB[:, 0:128], ["psB"], ["Ssb"])
                    for hh in range(2):
                        rows = slice(hh * 64, hh * 64 + 64)
                        act(Ro(STz[rows, fc, hh, :]), Ssb[rows, hh * 64:(hh + 1) * 64], AF.Copy, ["Ssb", "Wc"], ["STz"],
                            scale=Wc[rows, 8 * ch:8 * ch + 1])
                chk(4)
                for ch in range(nch):
                    cs = slice(ch * C, (ch + 1) * C)
                    mm(psB[cpp, 0:2], prod[:, cs], ct["hsel"][:], True, True, ["prod", "k_hsel"], ["psB"])
                    cp("dve", rk[cpp, ch, :], psB[cpp, 0:2], ["psB"], ["rk"])
                    y2 = ytok[cpp, ch, :, :]
                    g0, g1, g2 = [g[cpp] for g in gtmp]
                    s4 = st4[cpp]
                    vop("dve", "tensor_reduce", ["ytok"], ["st4"], out=s4[:, 0:2], in_=y2, axis=AX.X, op=ALU.add)
                    vop("dve", "tensor_tensor", ["ytok"], ["gtmp0"], out=g0, in0=y2, in1=y2, op=ALU.mult)
                    vop("dve", "tensor_reduce", ["gtmp0"], ["st4"], out=s4[:, 2:4], in_=g0, axis=AX.X, op=ALU.add)
                    vop("dve", "tensor_scalar", ["st4"], ["st4"], out=s4[:, 0:2], in0=s4[:, 0:2], scalar1=1.0 / 64, scalar2=None, op0=ALU.mult)
                    vop("dve", "tensor_tensor", ["st4"], ["gtmp1"], out=g1[:, :, 0], in0=s4[:, 0:2], in1=s4[:, 0:2], op=ALU.mult)
                    vop("dve", "scalar_tensor_tensor", ["st4", "gtmp1"], ["st4"], out=s4[:, 2:4], in0=s4[:, 2:4], scalar=1.0 / 64,
                        in1=g1[:, :, 0], op0=ALU.mult, op1=ALU.subtract)
                    act(s4[:, 2:4], s4[:, 2:4], AF.Sqrt, ["st4"], ["st4"], bias=GN_EPS)
                    vop("dve", "reciprocal", ["st4"], ["st4"], out=s4[:, 2:4], in_=s4[:, 2:4])
                    vop("dve", "tensor_tensor", ["ytok", "st4"], ["gtmp0"], out=g0, in0=y2,
                        in1=s4[:, 0:2].rearrange("p (a o) -> p a o", o=1).to_broadcast([C, 2, 64]), op=ALU.subtract)
                    vop("dve", "tensor_tensor", ["gtmp0", "st4"], ["gtmp0"], out=g0, in0=g0,
                        in1=s4[:, 2:4].rearrange("p (a o) -> p a o", o=1).to_broadcast([C, 2, 64]), op=ALU.mult)
                    gw = gnw[cpp, fc * 128:(fc + 1) * 128].rearrange("p (h d) -> p h d", h=2)
                    gb = gnb[cpp, fc * 128:(fc + 1) * 128].rearrange("p (h d) -> p h d", h=2)
                    vop("dve", "tensor_tensor", ["gtmp0", "gnw"], ["gtmp0"], out=g0, in0=g0, in1=gw, op=ALU.mult)
                    vop("dve", "tensor_tensor", ["gtmp0", "gnb"], ["gtmp0"], out=g0, in0=g0, in1=gb, op=ALU.add)
                    vop("dve", "tensor_tensor", ["vtok", "rk"], ["gtmp1"], out=g1, in0=vtok[cpp, ch, :, :],
                        in1=rk[cpp, ch, :].rearrange("p (a o) -> p a o", o=1).to_broadcast([C, 2, 64]), op=ALU.mult)
                    vop("dve", "tensor_tensor", ["gtmp0", "gtmp1"], ["gtmp0"], out=g0, in0=g0, in1=g1, op=ALU.add)
                    vop("dve", "tensor_tensor", ["gtmp0", "gsr"], ["mix"],
                        out=mix[cpp, ch, fc * 128:(fc + 1) * 128].rearrange("p (h d) -> p h d", h=2), in0=g0,
                        in1=gsr[cpp, ch, fc * 128:(fc + 1) * 128].rearrange("p (h d) -> p h d", h=2), op=ALU.mult)
                chk(4.5)

        def shift_and_state_out(shdst, wkvdst):
            for rnd in range(4):
                ncs = 4 if rnd < 3 else 1
                for ci in range(ncs):
                    c13 = rnd * 4 + ci
                    tr(R0[0:1, ci * 128:(ci + 1) * 128], zc[:, c13:c13 + 1], identf[:], ["zc", "k_identf"], ["psR0"])
                dst = xres if rnd < 2 else yout
                dk = "xres" if rnd < 2 else "yout"
                o_ = (rnd % 2) * 512
                cp("dve", dst[0:1, o_:o_ + ncs * 128], R0[0:1, 0:ncs * 128], ["psR0"], [dk])
            stq(shdst[:, 0:1024], xres[0:1, 0:1024], ["xres"])
            stq(shdst[:, 1024:1664], yout[0:1, 0:640], ["yout"])
            for fc in range(4):
                tr(R0[:, 0:128], STz[:, fc, :, :].rearrange("p h d -> p (h d)"), identf[:], ["STz", "k_identf"], ["psR0"])
                cp("dve", Ssb[:], R0[:, 0:128], ["psR0"], ["Ssb"])
                for hh in range(2):
                    rws = slice(hh * 64, hh * 64 + 64)
                    cp("act", wkv2[rws, :], Ssb[rws, hh * 64:(hh + 1) * 64], ["Ssb"], ["wkv2"])
                stq(wkvdst[2 * fc:2 * fc + 2, :, :].rearrange("h i j -> (h i) j"), wkv2[:], ["wkv2"])

        def proj_qk(n, kdst):
            wb, wk = load_group("Q")
            for h in range(8):
                ps_, pk = nextps()
                for kt in range(8):
                    mm(ps_[0:64, 0:n], wb[:, kt, h * 64:(h + 1) * 64], xnT[:, kt, 0:n], kt == 0, kt == 7, ["xnT", wk], [pk])
                cp(evac_eng(), QT[:, h, 0:n], ps_[0:64, 0:n], [pk], ["QT"])
            wb, wk = load_group("K")
            for si in range(2):
                for kvh in range(2):
                    ps_, pk = nextps()
                    c0 = si * 128 + kvh * 64
                    for kt in range(8):
                        mm(ps_[0:64, 0:n], wb[:, kt, c0:c0 + 64], xnT[:, kt, 0:n], kt == 0, kt == 7, ["xnT", wk], [pk])
                    dst, dk = kdst(si, kvh)
                    cp(evac_eng(), dst, ps_[0:64, 0:n], [pk], [dk])

        def finish(h, br, ps_, stride, first, npart, ntt):
            pp = slice(0, npart)
            view = ps_[pp, 0:ntt * stride].rearrange("p (t c) -> p t c", t=ntt)
            vop("dve", "tensor_scalar", ["psR3a"], ["rden"], out=rden[pp, 0:ntt], in0=view[:, :, 64], scalar1=1e-30, scalar2=None,
                op0=ALU.max)
            vop("dve", "reciprocal", ["rden"], ["rden"], out=rden[pp, 0:ntt], in_=rden[pp, 0:ntt])
            vop("dve", "tensor_tensor", ["rden", "gates"], ["scg"], out=scg[pp, 0:ntt], in0=rden[pp, 0:ntt], in1=gates[pp, 0:ntt, br * 8 + h], op=ALU.mult)
            bc = scg[pp, 0:ntt].rearrange("p (a o) -> p a o", o=1).to_broadcast([npart, ntt, 64])
            if first:
                vop("dve", "tensor_tensor", ["psR3a", "scg"], ["oacc"], out=oacc[pp, 0:ntt, h, :], in0=view[:, :, 0:64], in1=bc, op=ALU.mult)
            else:
                vop("dve", "tensor_tensor", ["psR3a", "scg"], ["otmp"], out=otmp[pp, 0:ntt, :], in0=view[:, :, 0:64], in1=bc, op=ALU.mult)
                vop("dve", "tensor_tensor", ["otmp", "oacc"], ["oacc"], out=oacc[pp, 0:ntt, h, :], in0=oacc[pp, 0:ntt, h, :], in1=otmp[pp, 0:ntt, :], op=ALU.add)

        def out_proj(npart, tt, xsrc, ydst):
            pp = slice(0, npart)
            vop("dve", "tensor_tensor", ["oacc", "gsn"], ["mix"], out=mix[pp, tt, 512:1024],
                in0=oacc[pp, tt, :, :].rearrange("p h d -> p (h d)"), in1=gsn[pp, tt, :], op=ALU.mult)
            for kt in range(8):
                tr(psT[:, kt * 128:kt * 128 + npart], mix[pp, tt, kt * 128:(kt + 1) * 128], identb[pp, pp], ["mix", "k_identb"], ["psT"])
            cp("act", mixT[:, :, 0:npart], psT[:].rearrange("p (k t) -> p k t", k=8)[:, :, 0:npart], ["psT"], ["mixT"])
            ld(xres[pp, :], xsrc, ["xres"])
            for nchunk, (ps_, pk) in enumerate(psAB):
                for kt in range(8):
                    mm(ps_[pp, 0:512], mixT[:, kt, 0:npart], wout[:, kt, nchunk * 512:(nchunk + 1) * 512], kt == 0, kt == 7,
                       ["mixT", "wout"], [pk])
                vop("dve", "tensor_tensor", [pk, "xres"], ["yout"], out=yout[pp, nchunk * 512:(nchunk + 1) * 512], in0=ps_[pp, 0:512],
                    in1=xres[pp, nchunk * 512:(nchunk + 1) * 512], op=ALU.add)
            act(sq_junk[pp, :], yout[pp, :], AF.Square, ["yout"], ["mixT", "ss"], accum_out=ss[pp, :])
            act(rstd[pp, :], ss[pp, :], AF.Sqrt, ["ss"], ["rstd"], scale=1.0 / 1024, bias=EPS)
            vop("dve", "reciprocal", ["rstd"], ["rstd"], out=rstd[pp, :], in_=rstd[pp, :])
            vop("dve", "scalar_tensor_tensor", ["yout", "rstd", "fgb"], ["yout"], out=yout[pp, :], in0=yout[pp, :], scalar=rstd[pp, 0:1],
                in1=fgb[pp, :], op0=ALU.mult, op1=ALU.mult)
            stq(ydst, yout[pp, :], ["yout"])

        def sample_jobs():
            ovc = ct["ovc"]
            vop("pool", "memset", [], ["k_cmpbias"], ap=ct["cmpbias"][:, 0:1040], constant=1.0)
            vop("pool", "memset", [], ["Vn"], ap=Vn[:], constant=1.0)
            vop("pool", "memset", [], ["Vpg"], ap=Vpg[:], constant=1.0)
            vop("pool", "memset", [], ["QTz"], ap=QTz[:], constant=0.0)
            for bs in range(4):
                ld(xres[0:1, 0:1024], sshift[bs:bs + 1, 0:1024], ["xres"])
                ld(yout[0:1, 0:640], sshift[bs:bs + 1, 1024:1664], ["yout"])
                for c13 in range(13):
                    src = xres[0:1, c13 * 128:(c13 + 1) * 128] if c13 < 8 else yout[0:1, (c13 - 8) * 128:(c13 - 7) * 128]
                    tr(R0[:, c13:c13 + 1], src, identf[0:1, 0:1], ["xres", "yout", "k_identf"], ["psR0"])
                cp("dve", zc[:], R0[:, 0:13], ["psR0"], ["zc"])
                cp("dve", STz[:].rearrange("p f h d -> p (f h d)").bitcast(F32R), ct["zeros"][:, 0:512], ["k_zeros"], ["STz"])
                for fc in range(4):
                    ld(tmpf[0][0:64, 0:128].rearrange("i (h j) -> i h j", h=2), swkv[bs, 2 * fc:2 * fc + 2, :, :].rearrange("h i j -> i h j"), ["tmpf0"])
                    tr(R0[:, 0:64], tmpf[0][0:64, 0:128], identf[0:64, 0:64], ["tmpf0", "k_identf"], ["psR0"])
                    cp("dve", Ssb[:, 0:64], R0[:, 0:64], ["psR0"], ["Ssb"])
                    for hh in range(2):
                        rws = slice(hh * 64, hh * 64 + 64)
                        cp("act", STz[rws, fc, hh, :].bitcast(F32R), Ssb[rws, 0:64], ["Ssb"], ["STz"])
                chk(20)
                rmsnorm_T(xs[4 * bs:4 * bs + 4, :], 4, 0)
                p4 = slice(0, 4)
                for gname, ncol in (("T0", 512), ("T1", 512), ("T2", 512), ("T3", 280)):
                    wb, wk = load_group(gname)
                    ps_, pk = nextps()
                    for kt in range(8):
                        mm(ps_[p4, 0:ncol], xnT[:, kt, 0:4], wb[:, kt, 0:ncol], kt == 0, kt == 7, ["xnT", wk], [pk])
                    if gname == "T0":
                        act(gsr[p4, 0, :], ps_[p4, 0:512], AF.Silu, [pk], ["gsr"])
                    elif gname == "T1":
                        act(gsn[p4, 0, :], ps_[p4, 0:512], AF.Silu, [pk], ["gsn"])
                    elif gname == "T2":
                        cp("act", kv0[0][p4, :], ps_[p4, 0:512], [pk], ["kv0_0"])
                        stq(kvs[4 * bs:4 * bs + 4, :], kv0[0][p4, :], ["kv0_0"])
                        cp("dve", Vn[p4, 0, :, 0:64], kv0[0][p4, 384:512].rearrange("p (k d) -> p k d", k=2), ["kv0_0"], ["Vn"])
                    else:
                        cp("act", kv1[0][p4, :], ps_[p4, 0:280], [pk], ["kv1_0"])
                        stq(wins[bs, 508:512, :], kv1[0][p4, 0:256], ["kv1_0"])
                        cp("dve", Vn[p4, 1, :, 0:64], kv1[0][p4, 128:256].rearrange("p (k d) -> p k d", k=2), ["kv1_0"], ["Vn"])
                        act(gates[p4, 0, :], kv1[0][p4, 256:280], AF.Sigmoid, ["kv1_0"], ["gates"])
                chk(21)
                rwkv_block(4, 4, 2, 4, None)
                chk(22)
                shift_and_state_out(shs[bs:bs + 1, :], wkvs[bs])
                chk(23)
                proj_qk(4, lambda si, kvh: (KTn[:, si, kvh, :], "KTn"))
                cp("dve", QTs[:], QT[:, :, 0:4], ["QT"], ["QTs"])
                wb, wk = load_group("Q")
                for g in range(4):
                    h = 4 + g
                    ps_, pk = nextps()
                    for kt in range(8):
                        mm(ps_[:, 0:4], wb[:, kt, (h - 1) * 64:(h + 1) * 64], xnT[:, kt, 0:4], kt == 0, kt == 7, ["xnT", wk], [pk])
                    cp("dve", Ssb[:, g * 4:(g + 1) * 4], ps_[:, 0:4], [pk], ["Ssb"])
                cp("act", QTz[64:128, 1, :], Ssb[64:128, 0:16], ["Ssb"], ["QTz"])
                cp("dve", QTz[0:64, 0, :], QTs[:, 0:4, :].rearrange("p g t -> p (g t)"), ["QTs"], ["QTz"])
                chk(24)
                ld(ptb[:], pt[bs:bs + 1, :].to_broadcast([128, 128]), ["ptb"])
                cp("dve", ptf[:], ptb[:], ["ptb"], ["ptf"])
                vop("dve", "tensor_scalar", ["ptf", "k_iota"], ["pidx"], out=pidx[:, 0, :], in0=ptf[:], scalar1=256.0, scalar2=ct["iota"][:, 0:1],
                    op0=ALU.mult, op1=ALU.add)
                vop("dve", "tensor_scalar", ["ptf", "k_iota"], ["pidx"], out=pidx[:, 1, :], in0=ptf[:], scalar1=256.0, scalar2=ct["iota"][:, 1:2],
                    op0=ALU.mult, op1=ALU.add)

                def gather4(g0, col0, key):
                    dst = pg4[key]
                    kname = ("xres", "yout")[key]
                    for i in range(4):
                        P.dma_fn("pool", (lambda d_, gi: (lambda e: e.indirect_dma_start(
                            out=d_, out_offset=None, in_=cache_h,
                            in_offset=bass.IndirectOffsetOnAxis(ap=pidx[:, col0 // 256, gi:gi + 1], axis=0))))(dst[:, i, :], g0 + i),
                            reads=["pidx"], writes=[kname])
                    return dst, kname

                chk(25)
                gi_ = 0
                for G in range(8):
                    mm(psP[:, 0:258], ct["zeros"][:, 0:128], ct["zeros"][:, 0:258], True, True, ["k_zeros"], ["psP"])
                    for pq in range(4):
                        src, sk = gather4(16 * G + 4 * pq, 0, gi_ % 2)
                        gi_ += 1
                        for i in range(4):
                            ti = 4 * pq + i
                            lo, hi = 8 * ti - 1, 8 * ti + 8
                            for m in range(2):
                                vop("pool" if m else "dve", "tensor_tensor", [sk, "wt"], [f"ym{m}"], out=ym[m][:], in0=src[:, i, :],
                                    in1=wt[:, m, :], op=ALU.mult)
                            for cc in range(2):
                                for m in range(2):
                                    zoff = m - 8 * ti + 120
                                    mm(psP[:, cc * 129 + 1 + lo: cc * 129 + 1 + hi], ym[m][:, cc * 128:(cc + 1) * 128],
                                       ct["zs"][:, zoff + lo: zoff + hi], False, True, [f"ym{m}", "k_zs"], ["psP"])
                    for cc in range(2):
                        cp("act", pall[:, cc, 128 * G:128 * G + 128], psP[:, cc * 129 + 1:cc * 129 + 129], ["psP"], ["Vs"])
                        if G > 0:
                            vop("dve", "tensor_tensor", ["psP", "Vs"], ["Vs"], out=pall[:, cc, 128 * G - 1:128 * G],
                                in0=psP[:, cc * 129:cc * 129 + 1], in1=pall[:, cc, 128 * G - 1:128 * G], op=ALU.add)
                chk(26)
                for kvh in range(2):
                    for half in range(2):
                        js = slice(half * 512, (half + 1) * 512)
                        mm(psA[0:64, 0:512], wmix[:, kvh, 0, :], pall[:, 0, js], True, True, ["wmix", "Vs"], ["psA"])
                        cp("act", kcTs[:, kvh, js], psA[0:64, 0:512], ["psA"], ["Vw"])
                    for jt in range(8):
                        mm(psB[:, 0:64], pall[:, 1, jt * 128:(jt + 1) * 128], wmix[:, kvh, 1, :], True, True, ["wmix", "Vs"], ["psB"])
                        cp("dve", vcs[:, jt, kvh, 0:64], psB[:, 0:64], ["psB"], ["k_cmpbias"])
                chk(27)
                p16 = slice(0, 16)
                for kvh in range(2):
                    mm(R3[p16, 0:322], ct["zeros"][:, 0:16], ct["zeros"][:, 0:322], True, False, ["k_zeros"], ["psR3a"])
                    for jt in range(8):
                        sp_, spk = sps[sn["n"] % 2]
                        pt_ = PTb[sn["n"] % 3]
                        ptk = f"PTb{sn['n'] % 3}"
                        sn["n"] += 1
                        mm(sp_[:, 0:16], kcTs[:, kvh, jt * 128:(jt + 1) * 128], QTs[:, 4 * kvh:4 * kvh + 4, :].rearrange("p g t -> p (g t)"),
                           True, jt != 7, ["Vw", "QTs"], [spk])
                        if jt == 7:
                            mm(sp_[:, 0:16], identb[:], ct["lastb"][:], False, True, ["k_identb", "k_lastb"], [spk])
                        act(pt_[:, 0:16], sp_[:, 0:16], AF.Exp, [spk], [ptk])
                        mm(R3[p16, 0:65], pt_[:, 0:16], vcs[:, jt, kvh, :], False, False, [ptk, "k_cmpbias"], ["psR3a"])
                        mm(R3[p16, 65 + 32 * jt:65 + 32 * jt + 33], pt_[:, 0:16], ovc[:, 0:33], False, jt == 7, [ptk, "k_ovc"], ["psR3a"])
                    chk(27.1)
                    cp("act", ocs1[p16, :], R3[p16, 0:322], ["psR3a"], ["stw0"])
                    vop("dve", "reciprocal", ["stw0"], ["rden"], out=rden[p16, 0:1], in_=ocs1[p16, 64:65])
                    vop("dve", "tensor_scalar", ["stw0", "rden"], ["stw0"], out=ocs1[p16, :], in0=ocs1[p16, :], scalar1=rden[p16, 0:1],
                        scalar2=None, op0=ALU.mult)
                    cp("dve", obr[p16, 0, kvh, :], ocs1[p16, 0:64], ["stw0"], ["Mt1"])
                    chk(27.2)
                    mm(psA[p4, 0:257], ct["gsel"][p16, :], ocs1[p16, 65:322], True, True, ["k_gsel", "stw0"], ["psA"])
                    vop("dve", "tensor_tensor", ["psA", "k_addcs"], ["kv1_1"], out=scs[p4, :], in0=psA[p4, 0:257], in1=ct["addcs"][p4, :], op=ALU.add)
                    vop("dve", "max", ["kv1_1"], ["m8a"], out=m8a[p4, :], in_=scs[p4, :])
                    vop("dve", "match_replace", ["kv1_1", "m8a"], ["kv0_1"], out=scs2[p4, :], in_to_replace=m8a[p4, :], in_values=scs[p4, :], imm_value=-3e4)
                    vop("dve", "max", ["kv0_1"], ["m8b"], out=m8b[p4, :], in_=scs2[p4, :])
                    vop("dve", "tensor_scalar", ["kv1_1", "m8b"], ["kv0_1"], out=scs2[p4, :], in0=scs[p4, :], scalar1=m8b[p4, 7:8], scalar2=-BIG,
                        op0=ALU.is_lt, op1=ALU.mult)
                    chk(27.3)
                    for t_ in range(4):
                        mm(psB[0:1, 0:257], ct["identf"][p4, t_:t_ + 1], scs2[p4, :], True, True, ["k_identf", "kv0_1"], ["psB"])
                        cp("dve" if t_ % 2 else "act", selflat4[0:1, :, :, kvh, t_],
                           psB[0:1, 0:256].rearrange("o (pg hf) -> o hf pg", hf=2), ["psB"], ["k_zp"])

                chk(28)
                def page_attn(src, sk, npg, mask_mm, mexp_const, first_group):
                    for i in range(npg):
                        tr(R0[:, i * 128:(i + 1) * 128], src[:, i, 0:128], identf[:], [sk, "k_identf"], ["psR0"])
                    cp("act", KTpg[:, 0:npg, :].rearrange("p g n -> p (g n)"), R0[:, 0:npg * 128], ["psR0"], ["KTpg"])
                    cp("dve", Vpg[:, 0:npg, :, 0:64], src[:, 0:npg, 128:256].rearrange("p g (k d) -> p g k d", k=2), [sk], ["Vpg"])
                    sp_, spk = sps[sn["n"] % 2]
                    pt_ = PTb[sn["n"] % 3]
                    ptk = f"PTb{sn['n'] % 3}"
                    sn["n"] += 1
                    for i in range(npg):
                        for kvh in range(2):
                            c0 = (i * 2 + kvh) * 16
                            mm(sp_[:, c0:c0 + 16], KTpg[:, i, :], QTz[:, kvh, :], True, True, ["KTpg", "QTz"], [spk])
                    if mask_mm is not None:
                        for hf in range(2):
                            mm(sp_[:, 256:256 + npg * 8], ct["hl"][0:1, hf, :], mask_mm(hf), hf == 0, hf == 1, ["k_hl", "k_zp"], [spk])
                        act(mexp[:, 0:npg * 8], sp_[:, 256:256 + npg * 8], AF.Exp, [spk], ["mexp"])
                        mk = mexp[:, 0:npg * 8]
                        mkk = "mexp"
                    else:
                        mk = mexp_const
                        mkk = "k_winms"
                    act(pt_[:, 0:npg * 32], sp_[:, 0:npg * 32], AF.Exp, [spk], [ptk])
                    vop("dve", "tensor_tensor", [ptk, mkk], [ptk], out=pt_[:, 0:npg * 32].rearrange("p (a g t) -> p a g t", g=4, t=4),
                        in0=pt_[:, 0:npg * 32].rearrange("p (a g t) -> p a g t", g=4, t=4),
                        in1=mk.rearrange("p (a o t) -> p a o t", o=1, t=4).to_broadcast([128, npg * 2, 4, 4]), op=ALU.mult)
                    for i in range(npg):
                        for kvh in range(2):
                            c0 = (i * 2 + kvh) * 16
                            mm(R3[p16, kvh * 65:(kvh + 1) * 65], pt_[:, c0:c0 + 16], Vpg[:, i, kvh, :], first_group and i == 0 and kvh == 0,
                               False, [ptk, "Vpg"], ["psR3a"])

                def tail_attn(si, last):
                    sp_, spk = sps[sn["n"] % 2]
                    pt_ = PTb[sn["n"] % 3]
                    ptk = f"PTb{sn['n'] % 3}"
                    sn["n"] += 1
                    for kvh in range(2):
                        mm(sp_[p4, kvh * 16:(kvh + 1) * 16], KTn[:, si, kvh, :], QTs[:, 4 * kvh:4 * kvh + 4, :].rearrange("p g t -> p (g t)"),
                           True, False, ["KTn", "QTs"], [spk])
                        mm(sp_[p4, kvh * 16:(kvh + 1) * 16], identb[p4, p4], ct["tailb"][p4, :], False, True, ["k_identb", "k_tailb"], [spk])
                    act(pt_[p4, 0:32], sp_[p4, 0:32], AF.Exp, [spk], [ptk])
                    for kvh in range(2):
                        mm(R3[p16, kvh * 65:(kvh + 1) * 65], pt_[p4, kvh * 16:(kvh + 1) * 16], Vn[p4, si, kvh, :], False, last and kvh == 1,
                           [ptk, "Vn"], ["psR3a"])

                def fold(br, first):
                    for kvh in range(2):
                        cp("act", obr[p16, br, kvh, :], R3[p16, kvh * 65:kvh * 65 + 64], ["psR3a"], ["Mt1"])
                        cp("act", rden[p16, 1:2], R3[p16, kvh * 65 + 64:kvh * 65 + 65], ["psR3a"], ["rden"])
                        vop("dve", "reciprocal", ["rden"], ["rden"], out=rden[p16, 0:1], in_=rden[p16, 1:2])
                        vop("dve", "tensor_scalar", ["Mt1", "rden"], ["Mt1"], out=obr[p16, br, kvh, :], in0=obr[p16, br, kvh, :],
                            scalar1=rden[p16, 0:1], scalar2=None, op0=ALU.mult)

                chk(29)
                for grp in range(32):
                    src, sk = gather4(4 * grp, 256, gi_ % 2)
                    gi_ += 1
                    page_attn(src, sk, 4, (lambda hf, grp=grp: selflat4[0:1, hf, 4 * grp:4 * grp + 4, :, :].rearrange("o a k t -> o (a k t)")),
                              None, grp == 0)
                tail_attn(0, True)
                fold(1, False)
                chk(30)
                ld(pg4[0][:], cwin[bs].rearrange("(g p) c -> p g c", p=128), ["xres"])
                for tl in range(4):
                    lo_ = 4 if tl == 0 else 0
                    stq(wins[bs, 128 * tl + lo_ - 4:128 * tl + 124, :], pg4[0][lo_:128, tl, :], ["xres"])
                page_attn(pg4[0], "xres", 4, None, ct["winms"][:], True)
                tail_attn(1, True)
                fold(2, False)
                chk(31)
                for kvh in range(2):
                    for g in range(4):
                        h = 4 * kvh + g
                        for br in range(3):
                            mm(psA[p4, br * 64:(br + 1) * 64], ct["identf"][p16, g * 4:g * 4 + 4], obr[p16, br, kvh, :], True, True,
                               ["k_identf", "Mt1"], ["psA"])
                        for br in range(3):
                            if br == 0:
                                vop("dve", "tensor_scalar", ["psA", "gates"], ["oacc"], out=oacc[p4, 0, h, :], in0=psA[p4, 0:64],
                                    scalar1=gates[p4, 0, h:h + 1], scalar2=None, op0=ALU.mult)
                            else:
                                vop("dve", "scalar_tensor_tensor", ["psA", "gates", "oacc"], ["oacc"], out=oacc[p4, 0, h, :], in0=psA[p4, br * 64:(br + 1) * 64],
                                    scalar=gates[p4, 0, br * 8 + h:br * 8 + h + 1], in1=oacc[p4, 0, h, :], op0=ALU.mult, op1=ALU.add)
                chk(32)
                out_proj(4, 0, xs[4 * bs:4 * bs + 4, :], ys[4 * bs:4 * bs + 4, :])

        try:
          chk(0)
          for hh_ in range(2):
              cp("dve", BKz[:, hh_, :, :].rearrange("p a t -> p (a t)").bitcast(F32R), ct["zeros"][:, 0:2 * TB], ["k_zeros"], ["BKz"])
          for b in range(2 if os.environ.get("KNOPROMPT") is None else 0):
            vop("dve", "memset", [], ["zc"], ap=zc[:], constant=0.0)
            cp("dve", STz[:].rearrange("p f h d -> p (f h d)").bitcast(F32R), ct["zeros"][:, 0:512], ["k_zeros"], ["STz"])
            mm(psP[:, 0:256], ct["zeros"][:, 0:128], ct["zeros"][:, 0:256], True, True, ["k_zeros"], ["psP"])
            for tb in range(NB):
                t0 = tb * TB
                for tt in range(NTT):
                    rmsnorm_T(xp[b, t0 + tt * 128:t0 + (tt + 1) * 128, :], 128, tt)
                chk(1)
                for gname, ncol in (("T0", 512), ("T1", 512), ("T2", 512), ("T3", 280)):
                    wb, wk = load_group(gname)
                    for tt in range(NTT):
                        ps_, pk = nextps()
                        for kt in range(8):
                            mm(ps_[:, 0:ncol], xnT[:, kt, tt * 128:(tt + 1) * 128], wb[:, kt, 0:ncol],
                               kt == 0, kt == 7, ["xnT", wk], [pk])
                        tile_i = tb * NTT + tt
                        if gname == "T0":
                            act(gsr[:, tt, :], ps_[:, 0:512], AF.Silu, [pk], ["gsr"])
                        elif gname == "T1":
                            act(gsn[:, tt, :], ps_[:, 0:512], AF.Silu, [pk], ["gsn"])
                        elif gname == "T2":
                            k0 = kv0[tt % 2]
                            kk0 = f"kv0_{tt % 2}"
                            cp("act", k0[:], ps_[:, 0:512], [pk], [kk0])
                            stq(kvp[b, t0 + tt * 128:t0 + (tt + 1) * 128, :], k0[:], [kk0])
                            cp("dve", Vs[:, tile_i, :, 0:64], k0[:, 384:512].rearrange("p (k d) -> p k d", k=2),
                               [kk0], ["Vs"])
                            for m in range(2):
                                vop("pool", "tensor_tensor", [kk0, "wt"], [f"ym{m}"], out=ym[m][:], in0=k0[:, 0:256],
                                    in1=wt[:, m, :], op=ALU.mult)
                            for cc in range(2):
                                for m in range(2):
                                    j0 = 8 * tile_i - 1
                                    lo = max(j0, 0)
                                    hi = min(8 * tile_i + 8, 127)
                                    zoff = m - 8 * tile_i + 120
                                    mm(psP[:, cc * 128 + lo: cc * 128 + hi], ym[m][:, cc * 128:(cc + 1) * 128],
                                       ct["zs"][:, zoff + lo: zoff + hi], False, True, [f"ym{m}", "k_zs"], ["psP"])
                        else:
                            k1 = kv1[tt % 2]
                            kk1 = f"kv1_{tt % 2}"
                            cp("act", k1[:], ps_[:, 0:280], [pk], [kk1])
                            if t0 + tt * 128 >= T - 512:
                                stq(winp[b, t0 + tt * 128 - (T - 512): t0 + (tt + 1) * 128 - (T - 512), :],
                                    k1[:, 0:256], [kk1])
                            cp("dve", Vw[:, tile_i, :, 0:64], k1[:, 128:256].rearrange("p (k d) -> p k d", k=2),
                               [kk1], ["Vw"])
                            act(gates[:, tt, :], k1[:, 256:280], AF.Sigmoid, [kk1], ["gates"])
                chk(2)
                rwkv_block(TB, 128, 7, 128, None)
                chk(5)
                if tb == NB - 1:
                    shift_and_state_out(shp[b:b + 1, :], wkvp[b])
                chk(6)
                proj_qk(TB, lambda si, kvh: ((KTs, KTw)[si][:, kvh, t0:t0 + TB], ("KTs", "KTw")[si]))
                chk(6.2)
                cp("dve", pooled[:].rearrange("p a j -> p (a j)"), psP[:, 0:256], ["psP"], ["pooled"])
                cp("act", pooledb[:], pooled[:], ["pooled"], ["pooledb"])
                chk(6.3)
                for kvh in range(2):
                    mm(psA[0:64, 0:128], wmix[:, kvh, 0, :], pooledb[:, 0, :], True, True, ["wmix", "pooledb"], ["psA"])
                    cp("act", kcT[:, kvh, :], psA[0:64, 0:128], ["psA"], ["kcT"])
                    mm(psB[:, 0:64], pooledb[:, 1, :], wmix[:, kvh, 1, :], True, True, ["wmix", "pooledb"], ["psB"])
                    cp("dve", vcx[:, kvh, 0:64], psB[:, 0:64], ["psB"], ["vcx"])
                chk(6.4)
                if tb == 0 and b == 0:
                    vop("pool", "memset", [], ["vcx"], ap=vcx[:, :, 64:65], constant=1.0)
                    for kvh in range(2):
                        cp("dve", vcx[:, kvh, 65:97], ct["ov32"][:], ["k_ov32"], ["vcx"])

                def attend(h, br, tiles, Kt_, kkey, Vt_, vkey, first):
                    kvh = h // 4
                    nt = len(tiles)
                    for ti, (kt, biases) in enumerate(tiles):
                        sp_, spk = sps[sn["n"] % 2]
                        pt_ = PTb[sn["n"] % 3]
                        ptk = f"PTb{sn['n'] % 3}"
                        sn["n"] += 1
                        mm(sp_[:, 0:TB], Kt_[:, kvh, kt * 128:(kt + 1) * 128], QT[:, h, :], True, len(biases) == 0,
                           [kkey, "QT"], [spk])
                        for bi, (bl, br_, bkeys) in enumerate(biases):
                            mm(sp_[:, 0:TB], bl, br_, False, bi == len(biases) - 1, bkeys, [spk])
                        act(pt_[:], sp_[:, 0:TB], AF.Exp, [spk], [ptk])
                        for tt in range(NTT):
                            mm(R3[:, tt * 65:(tt + 1) * 65], pt_[:, tt * 128:(tt + 1) * 128], Vt_[:, kt, kvh, :],
                               ti == 0 and tt == 0, ti == nt - 1 and tt == NTT - 1, [ptk, vkey], ["psR3a"])
                    finish(h, br, R3, 65, first, 128, NTT)

                chk(7)
                for h in range(8):
                    kvh = h // 4
                    sp_, spk = sps[sn["n"] % 2]
                    pt_ = PTb[sn["n"] % 3]
                    ptk = f"PTb{sn['n'] % 3}"
                    sn["n"] += 1
                    mm(sp_[:, 0:TB], kcT[:, kvh, :], QT[:, h, :], True, False, ["kcT", "QT"], [spk])
                    mm(sp_[:, 0:TB], identb[:], ct["cmpbias"][:, t0:t0 + TB], False, True, ["k_identb", "k_cmpbias"], [spk])
                    act(pt_[:], sp_[:, 0:TB], AF.Exp, [spk], [ptk])
                    for tt in range(NTT):
                        mm(R3[:, tt * 97:(tt + 1) * 97], pt_[:, tt * 128:(tt + 1) * 128], vcx[:, kvh, :], tt == 0, tt == NTT - 1,
                           [ptk, "vcx"], ["psR3a"])
                    finish(h, 0, R3, 97, True, 128, NTT)
                    view = R3[:, 0:NTT * 97].rearrange("p (t c) -> p t c", t=NTT)
                    if h % 4 == 0:
                        vop("dve", "tensor_tensor", ["psR3a", "rden"], ["sc"], out=sc[:, :, kvh, :], in0=view[:, :, 65:97],
                            in1=rden[:].rearrange("p (a o) -> p a o", o=1).to_broadcast([128, NTT, 32]), op=ALU.mult)
                    else:
                        vop("dve", "tensor_tensor", ["psR3a", "rden"], ["otmp"], out=otmp[:, :, 0:32], in0=view[:, :, 65:97],
                            in1=rden[:].rearrange("p (a o) -> p a o", o=1).to_broadcast([128, NTT, 32]), op=ALU.mult)
                        vop("dve", "tensor_tensor", ["otmp", "sc"], ["sc"], out=sc[:, :, kvh, :], in0=sc[:, :, kvh, :], in1=otmp[:, :, 0:32], op=ALU.add)
                chk(8)
                for tt in range(NTT):
                    for kvh in range(2):
                        vop("dve", "tensor_tensor", ["sc", "k_addc"], ["scw"], out=scw[:], in0=sc[:, tt, kvh, :],
                            in1=ct["addc"][:, tb * NTT + tt, :], op=ALU.add)
                        vop("dve", "max", ["scw"], ["m8a"], out=m8a[:], in_=scw[:])
                        vop("dve", "match_replace", ["scw", "m8a"], ["otmp"], out=otmp[:, 0, 0:32], in_to_replace=m8a[:], in_values=scw[:],
                            imm_value=-3e4)
                        vop("dve", "max", ["otmp"], ["m8b"], out=m8b[:], in_=otmp[:, 0, 0:32])
                        vop("dve", "tensor_scalar", ["m8b"], ["thr"], out=thr[:], in0=m8b[:, 7:8], scalar1=-5000.0, scalar2=None, op0=ALU.max)
                        vop("dve", "tensor_scalar", ["scw", "thr"], ["selb"], out=selb[:], in0=scw[:], scalar1=thr[:, 0:1], scalar2=-BIG,
                            op0=ALU.is_lt, op1=ALU.mult)
                        tr(psT[0:32, 0:128], selb[:], identb[:], ["selb", "k_identb"], ["psT"])
                        cp("act", selbT[:, kvh, tt * 128:(tt + 1) * 128], psT[0:32, 0:128], ["psT"], ["selbT"])
                chk(9)
                for h in range(8):
                    kvh = h // 4
                    tiles = []
                    for kt in range(0, tb * NTT + NTT):
                        bs = [(ct["zp"][:, kt * 128:(kt + 1) * 128], selbT[:, kvh, :], ["k_zp", "selbT"])]
                        if kt >= tb * NTT:
                            bs.append((identb[:], ct["causb"][:, kt - tb * NTT, :], ["k_identb", "k_causb"]))
                        tiles.append((kt, bs))
                    attend(h, 1, tiles, KTs, "KTs", Vs, "Vs", False)
                chk(10)
                for h in range(8):
                    tiles = []
                    q0 = tb * NTT
                    for kt in range(max(0, q0 - 4), q0 + NTT):
                        bs = []
                        if kt >= q0:
                            bs.append((identb[:], ct["causb"][:, kt - q0, :], ["k_identb", "k_causb"]))
                        elif kt - q0 + 4 < 2:
                            bs.append((identb[:], ct["winlow"][:, kt - q0 + 4, :], ["k_identb", "k_winlow"]))
                        tiles.append((kt, bs))
                    attend(h, 2, tiles, KTw, "KTw", Vw, "Vw", False)
                chk(11)
                for tt in range(NTT):
                    out_proj(128, tt, xp[b, t0 + tt * 128:t0 + (tt + 1) * 128, :], yp[b, t0 + tt * 128:t0 + (tt + 1) * 128, :])
                chk(12)

          if with_sample:
            sample_jobs()
        except _Stop:
            for _i in range(int(os.environ.get("KPAD", "0"))):
                eng_ = os.environ.get("KPADENG", "act")
                if eng_ == "act":
                    act(thr[:], thr[:], AF.Copy, ["thr"], ["thr"])
                else:
                    cp(eng_, thr[:], m8a[:, 0:1], ["m8a"], ["thr"])
            if os.environ.get("KDBG"):
                dbg = dout("dbg", [128, 4096])
                o = 0
                for nm, tl, ncol in (("STz", STz, 512), ("Mt0", Mt[0], 512), ("Pk5", Pk[5], 512), ("Xa0", Xa[0], 128), ("Xa1", Xa[1], 128),
                                     ("tok", tok, 384), ("AR", AR, 512), ("BK", BK, 512), ("ytok", ytok, 256), ("Wc", Wc, 16)):
                    flat = tl[:]
                    shp_ = list(tl.shape) if hasattr(tl, "shape") else None
                    names = "abcdefg"[:len(shp_) - 1]
                    if len(shp_) > 2:
                        flat = tl[:].rearrange("p " + " ".join(names) + " -> p (" + " ".join(names) + ")")
                    stq(dbg[:, o:o + ncol], flat, [nm if nm not in ("Mt0", "Pk5", "Xa0", "Xa1") else nm])
                    o += ncol
        P.build(sems, block)
    return nc


def _core_inputs(c, inp, consts, cache2d=None, pt_override=None):
    d = {}
    d["xp"] = np.ascontiguousarray(inp["x_prompt"][2 * c:2 * c + 2])
    d["w_in"] = np.ascontiguousarray(inp["w_in"][0])
    d["w_out"] = np.ascontiguousarray(inp["w_out"][0])
    d["norm_g"] = np.ascontiguousarray(inp["norm_g"][0])
    d["final_g"] = np.ascontiguousarray(inp["final_g"])
    d["mu"] = np.ascontiguousarray(inp["mu_shift"][0])
    for k in ("w0", "a0", "k_k", "k_a", "r_k", "gn_w", "gn_b"):
        d[k] = np.ascontiguousarray(inp[k][0])
    d["w_dup"] = np.ascontiguousarray(inp["w_decay_up"][0])
    d["w_aup"] = np.ascontiguousarray(inp["w_aaa_up"][0])
    d["w_cpos"] = np.ascontiguousarray(inp["w_cmp_pos"][0])
    d["w_cmix"] = np.ascontiguousarray(inp["w_cmp_mix"][0])
    d["xs"] = np.ascontiguousarray(inp["x_sample"][4 * c:4 * c + 4]).reshape(16, 1024)
    d["cache"] = cache2d
    d["cwin"] = np.ascontiguousarray(inp["cache_kv_win"][0, 4 * c:4 * c + 4]).reshape(4, 512, 256)
    d["swkv"] = np.ascontiguousarray(inp["state_wkv"][0, 4 * c:4 * c + 4])
    d["sshift"] = np.ascontiguousarray(inp["state_shift"][0, 4 * c:4 * c + 4])
    d["pt"] = np.ascontiguousarray(inp["page_table"][4 * c:4 * c + 4] if pt_override is None else pt_override).astype(np.int32)
    for k, v in consts.items():
        d["c_" + k] = v
    return d


def kernel(**inp):
    inp = {k: np.asarray(v) for k, v in inp.items()}
    consts = _const_specs()
    cache2d = np.ascontiguousarray(inp["cache_kv"][0]).reshape(-1, 512)
    nc = build_nc(cache2d.shape[0])
    in_maps = [_core_inputs(c, inp, consts, cache2d) for c in range(8)]
    res = run_bass_kernel_spmd(nc, in_maps, core_ids=list(range(8))).results
    cat = lambda k: np.concatenate([r[k] for r in res], axis=0)
    y_prompt = cat("yp")
    y_sample = cat("ys").reshape(32, 4, 1024)
    kv_prompt = cat("kvp").reshape(1, 16, T, 4, 2, 64)
    kv_sample = cat("kvs").reshape(1, 32, 4, 4, 2, 64)
    win_prompt = cat("winp").reshape(1, 16, 512, 2, 2, 64)
    win_sample = cat("wins").reshape(1, 32, 512, 2, 2, 64)
    wkv_prompt = cat("wkvp").reshape(1, 16, 8, 64, 64)
    wkv_sample = cat("wkvs").reshape(1, 32, 8, 64, 64)
    shift_prompt = cat("shp").reshape(1, 16, 1664)
    shift_sample = cat("shs").reshape(1, 32, 1664)
    return (y_prompt, y_sample, kv_prompt, kv_sample, win_prompt, win_sample, wkv_prompt, wkv_sample,
            shift_prompt, shift_sample)
```

```python
import contextlib
import numpy as np
import ml_dtypes
import concourse.bass as bass
import concourse.mybir as mybir
from concourse.bass_utils import run_bass_kernel_spmd

F32 = mybir.dt.float32
BF16 = mybir.dt.bfloat16
I32 = mybir.dt.int32
F32R = mybir.dt.float32r
AF = mybir.ActivationFunctionType
ALU = mybir.AluOpType
AX = mybir.AxisListType

ENGS = ("pe", "act", "dve", "pool", "sp")
NDMASEM = 8
BIG = 30000.0
T = 2048
TB = 256
NTT = TB // 128
NB = T // TB
DW = 3992
EPS = 1e-6
GN_EPS = 64e-5
DEC_C = 0.6065306597126334


class _Stop(Exception):
    pass


import os
KSTOP = float(os.environ.get("KSTOP", "999"))


KSKIP = [int(os.environ.get("KSKIP", "0"))]


def chk(stage):
    if stage >= KSTOP:
        if KSKIP[0] > 0 and stage == KSTOP:
            KSKIP[0] -= 1
            return
        raise _Stop()


class Prog:
    def __init__(self, nc):
        self.nc = nc
        self.ops = []

    def op(self, eng, fn, reads=(), writes=()):
        import sys
        f = sys._getframe(2)
        self.ops.append(dict(eng=eng, fn=fn, reads=tuple(reads), writes=tuple(writes), dma=False, line=f.f_lineno))

    def dma(self, eng, out, in_, reads=(), writes=(), **kw):
        self.ops.append(dict(eng=eng, fn=lambda e: e.dma_start(out=out, in_=in_, **kw),
                             reads=tuple(reads), writes=tuple(writes), dma=True))

    def dma_fn(self, eng, fn, reads=(), writes=()):
        self.ops.append(dict(eng=eng, fn=fn, reads=tuple(reads), writes=tuple(writes), dma=True))

    def build(self, sems, block):
        ops = self.ops
        n = len(ops)
        last_w, readers = {}, {}
        deps = [None] * n
        needed = [False] * n
        for i, o in enumerate(ops):
            d = set()
            for r in o["reads"]:
                if r in last_w:
                    d.add(last_w[r])
            for w in o["writes"]:
                if w in last_w:
                    d.add(last_w[w])
                d.update(readers.get(w, ()))
            d.discard(i)
            d = {j for j in d if not (ops[j]["eng"] == "pe" and o["eng"] == "pe"
                                      and not ops[j]["dma"] and not o["dma"])}
            deps[i] = d
            for j in d:
                needed[j] = True
            for w in o["writes"]:
                last_w[w] = i
                readers[w] = []
            for r in o["reads"]:
                readers.setdefault(r, []).append(i)
        cnt = {e: 0 for e in ENGS}
        dcnt = {e: 0 for e in ENGS}
        ev = [None] * n
        prevdma = [None] * n
        for i, o in enumerate(ops):
            e = o["eng"]
            if o["dma"]:
                k = dcnt[e]
                dcnt[e] += 1
                s = k % NDMASEM
                ev[i] = ((e, s), 16 * (k // NDMASEM + 1))
                if k >= NDMASEM:
                    prevdma[i] = ((e, s), 16 * (k // NDMASEM))
            elif needed[i]:
                cnt[e] += 1
                ev[i] = (e, cnt[e])
        per_eng = {e: [i for i, o in enumerate(ops) if o["eng"] == e] for e in ENGS}
        final_dma = {}
        for i, o in enumerate(ops):
            if o["dma"]:
                final_dma[ev[i][0]] = max(final_dma.get(ev[i][0], 0), ev[i][1])

        def emit(e, eng, idxs, last):
            waited = {}
            for i in idxs:
                o = ops[i]
                want = {}
                for j in deps[i]:
                    sk, v = ev[j]
                    want[sk] = max(want.get(sk, 0), v)
                if prevdma[i] is not None:
                    sk, v = prevdma[i]
                    want[sk] = max(want.get(sk, 0), v)
                for sk, v in want.items():
                    if waited.get(sk, 0) < v:
                        eng.wait_ge(sems[sk], v)
                        waited[sk] = v
                if os.environ.get("KTRACE") and i >= n - 60:
                    print("OP", i, e, "line", o.get("line"), "R", o["reads"], "W", o["writes"], "waits", want, "ev", ev[i], flush=True)
                ins = o["fn"](eng)
                if o["dma"]:
                    ins.then_inc(sems[ev[i][0]], 16)
                elif ev[i] is not None:
                    ins.then_inc(sems[e], 1)
            if last:
                for sk, v in final_dma.items():
                    if waited.get(sk, 0) < v:
                        eng.wait_ge(sems[sk], v)

        block.sync(lambda eng: emit("sp", eng, per_eng["sp"], True))
        block.tensor(lambda eng: emit("pe", eng, per_eng["pe"], False))
        block.scalar(lambda eng: emit("act", eng, per_eng["act"], False))
        block.vector(lambda eng: emit("dve", eng, per_eng["dve"], False))
        block.gpsimd(lambda eng: emit("pool", eng, per_eng["pool"], False))


def _consts():
    bf = ml_dtypes.bfloat16
    c = {}
    p = np.arange(128)[:, None]
    q = np.arange(128)[None, :]
    c["identf"] = np.eye(128, dtype=np.float32)
    c["identb"] = np.eye(128, dtype=np.float32).astype(bf)
    su = (p < q).astype(np.float32)
    ui = (p <= q).astype(np.float32)
    c["mask4"] = np.concatenate([su, ui, su, ui], axis=1)
    c["maskL"] = np.concatenate([(p > q).astype(np.float32)] * 2, axis=1)
    tq = np.arange(TB)[None, :]
    causb = np.zeros((128, NTT, TB), np.float32)
    for r in range(NTT):
        causb[:, r, :] = np.where(p + 128 * r > tq, -BIG, 0.0)
    c["causb"] = causb.astype(bf)
    winlow = np.zeros((128, 2, TB), np.float32)
    winlow[:, 0, :] = np.where(p <= tq, -BIG, 0.0)
    winlow[:, 1, :] = np.where(p <= tq - 128, -BIG, 0.0)
    c["winlow"] = winlow.astype(bf)
    tt = np.arange(T)[None, :]
    c["cmpbias"] = np.where((16 * p + 31 > tt) | (p >= 127), -BIG, 0.0).astype(bf)
    s32 = np.arange(32)[:, None]
    c["zp"] = (tt // 64 == s32).astype(np.float32).astype(bf)
    addc = np.zeros((128, 16, 32), np.float32)
    for t16 in range(16):
        t = 128 * t16 + np.arange(128)[:, None]
        cur = t // 64
        s = np.arange(32)[None, :]
        valid = s <= cur
        forced = (s == 0) | (s == cur) | (s == cur - 1)
        addc[:, t16, :] = np.where(valid, np.where(forced, 1e4, 0.0), -1e4)
    c["addc"] = addc
    j = np.arange(128)[:, None]
    s = np.arange(32)[None, :]
    ov = ((16 * j < 64 * s + 64) & (16 * j + 32 > 64 * s) & (j < 127)).astype(np.float32)
    c["ov32"] = ov
    col = np.arange(248)[None, :]
    c["zs"] = (col - 120 == p // 16).astype(np.float32).astype(bf)
    c["resetm"] = np.tile((np.arange(TB)[None, :] % 128 != 0).astype(np.float32), (128, 1))
    c["bd64"] = ((p // 64) == (q // 64)).astype(np.float32)
    c["hsel"] = np.stack([(np.arange(128) < 64), (np.arange(128) >= 64)], axis=1).astype(np.float32)
    c["zeros"] = np.zeros((128, 512), np.float32).astype(bf)
    p4 = np.arange(4)[:, None]
    q4 = np.arange(4)[None, :]
    su4 = (p4 < q4).astype(np.float32)
    ui4 = (p4 <= q4).astype(np.float32)
    c["mask4s"] = np.concatenate([su4, ui4, su4, ui4], axis=1)
    c["maskLs"] = np.concatenate([(p4 > q4).astype(np.float32)] * 2, axis=1)
    jj = np.arange(128)[:, None]
    s33 = np.arange(33)[None, :]
    c["ovc"] = ((4 * s33 - 1 <= jj) & (jj <= 4 * s33 + 3)).astype(np.float32).astype(bf)
    lastb = np.zeros((128, 16), np.float32)
    lastb[127, :] = -BIG
    c["lastb"] = lastb.astype(bf)
    gsel = np.zeros((16, 4), np.float32)
    for g in range(4):
        for t in range(4):
            gsel[g * 4 + t, t] = 1.0
    c["gsel"] = gsel
    addcs = np.zeros((4, 257), np.float32)
    addcs[:, [0, 255, 256]] = 1e4
    c["addcs"] = addcs
    hl = np.zeros((1, 2, 128), np.float32)
    hl[0, 0, :64] = 1.0
    hl[0, 1, 64:] = 1.0
    c["hl"] = hl.astype(bf)
    tq16 = np.tile(np.arange(4), 4)[None, :]
    c["tailb"] = np.where(np.arange(4)[:, None] > tq16, -BIG, 0.0).astype(np.float32).astype(bf)
    winms = np.ones((128, 4, 2, 4), np.float32)
    winms[:, 0, :, :] = (np.arange(128)[:, None, None] > np.arange(4)[None, None, :]).astype(np.float32)
    c["winms"] = winms.reshape(128, 32).astype(bf)
    c["iota"] = np.stack([2.0 * np.arange(128), 2.0 * np.arange(128) + 1.0], axis=1).astype(np.float32)
    return c


CONST_SPECS = None


def _const_specs():
    global CONST_SPECS
    if CONST_SPECS is None:
        CONST_SPECS = _consts()
    return CONST_SPECS


def _groups():
    g = []
    g.append(("T0", [(1664, 512)], False))
    g.append(("T1", [(2688, 512)], False))
    g.append(("T2", [(3200, 512)], False))
    g.append(("T3", [(3712, 280)], False))
    g.append(("F0", [(1536, 128)], False))
    for fc in range(4):
        g.append((f"R{fc}", [(128 * fc, 128), (512 + 128 * fc, 128), (1024 + 128 * fc, 128)], False))
    g.append(("Q", [(2176, 512)], True))
    g.append(("K", [(3456, 128), (3712, 128)], False))
    return g


GROUPS = _groups()
GIDX = {g[0]: i for i, g in enumerate(GROUPS)}


def build_nc(n_cache_rows, with_sample=True):
    nc = bass.Bass("TRN2", target_bir_lowering=False)
    C = _const_specs()
    dt_of = lambda a: BF16 if a.dtype == ml_dtypes.bfloat16 else F32

    def din(name, shape, dt=F32):
        return nc.dram_tensor(name, list(shape), dt, kind="ExternalInput").ap()

    def dout(name, shape, dt=F32):
        return nc.dram_tensor(name, list(shape), dt, kind="ExternalOutput").ap()

    xp = din("xp", [2, T, 1024])
    w_in = din("w_in", [1024, DW])
    w_out = din("w_out", [1024, 1024])
    norm_g = din("norm_g", [1024])
    final_g = din("final_g", [1024])
    mu = din("mu", [1664])
    vecs = {k: din(k, [512]) for k in ("w0", "a0", "k_k", "k_a", "r_k", "gn_w", "gn_b")}
    w_dup = din("w_dup", [64, 512])
    w_aup = din("w_aup", [64, 512])
    w_cpos = din("w_cpos", [2, 32, 64])
    w_cmix = din("w_cmix", [2, 64, 64])
    cd = {k: din("c_" + k, v.shape, dt_of(v)) for k, v in C.items()}

    yp = dout("yp", [2, T, 1024])
    kvp = dout("kvp", [2, T, 512])
    winp = dout("winp", [2, 512, 256])
    wkvp = dout("wkvp", [2, 8, 64, 64])
    shp = dout("shp", [2, 1664])

    xs = din("xs", [16, 1024])
    cache = din("cache", [n_cache_rows, 512])
    cwin = din("cwin", [4, 512, 256])
    swkv = din("swkv", [4, 8, 64, 64])
    sshift = din("sshift", [4, 1664])
    pt = din("pt", [4, 128], I32)
    ys = dout("ys", [16, 1024])
    kvs = dout("kvs", [16, 512])
    wins = dout("wins", [4, 512, 256])
    wkvs = dout("wkvs", [4, 8, 64, 64])
    shs = dout("shs", [4, 1664])

    wscr = nc.dram_tensor("wscr", [len(GROUPS), 128, 8 * 512], BF16, kind="Internal").ap()

    P = Prog(nc)
    with contextlib.ExitStack() as st:
        def sb(name, shape, dt=F32):
            return st.enter_context(nc.sbuf_tensor(name, list(shape), dt))

        def pst(name, shape, dt=F32):
            return st.enter_context(nc.psum_tensor(name, list(shape), dt))

        def mm(out, lhsT, rhs, start, stop, R, W):
            P.op("pe", lambda e: e.matmul(out, lhsT=lhsT, rhs=rhs, start=start, stop=stop,
                                          skip_group_check=True), R, W)

        def tr(out, in_, ident, R, W):
            P.op("pe", lambda e: e.transpose(out, in_, ident), R, W)

        def act(out, in_, func, R, W, eng="act", **kw):
            P.op("act", lambda e: e.activation(out=out, in_=in_, func=func, **kw), R, W)

        def vop(eng, name, R, W, **kw):
            P.op(eng, lambda e: getattr(e, name)(**kw), R, W)

        def cp(eng, out, in_, R, W):
            if eng == "act":
                P.op("act", lambda e: e.copy(out=out, in_=in_), R, W)
            else:
                P.op(eng, lambda e: e.tensor_copy(out=out, in_=in_), R, W)

        def ld(out, in_, W, R=(), eng="sp", **kw):
            P.dma(eng, out, in_, reads=R, writes=W, **kw)

        def stq(out, in_, R, eng="pool"):
            P.dma(eng, out, in_, reads=R, writes=())

        rr = {"n": 0}

        def evac_eng():
            rr["n"] += 1
            return "act" if rr["n"] % 2 else "dve"

        ct = {}
        for k, v in C.items():
            ct[k] = sb("k_" + k, v.shape, dt_of(v))
            ld(ct[k][:], cd[k], ["k_" + k])
        identf, identb = ct["identf"], ct["identb"]

        yout = sb("yout", [128, 1024])
        tmpf = [sb(f"tmpf{i}", [128, TB]) for i in range(8)]
        stw_t = [sb(f"stw{i}", [128, 512]) for i in range(1)]
        stw = [stw_t[0], stw_t[0]]
        wup_f = stw_t[0]
        g8 = sb("g8", [128, 8])
        gq8 = sb("gq8", [128, 8])
        mu13 = sb("mu13", [128, 13])
        pv = {k: sb("pv_" + k, [128, 4]) for k in ("w0", "a0", "k_k", "k_a", "r_k")}
        omka = sb("omka", [128, 4])
        gnw = sb("gnw", [128, 512])
        gnb = sb("gnb", [128, 512])
        fgb = sb("fgb", [128, 1024])
        wup = sb("wup", [128, 2, 512], BF16)
        wt = sb("wt", [128, 2, 256])
        wmix_f = sb("wmix_f", [128, 2, 64])
        wmix = sb("wmix", [128, 2, 2, 64], BF16)
        xres = sb("xres", [128, 1024])
        wout_f = xres
        wout = sb("wout", [128, 8, 1024], BF16)
        with nc.allow_non_contiguous_dma(reason="small per-feature vectors"):
            ld(g8[:], norm_g.rearrange("(kt p) -> p kt", p=128), ["g8"], allow_slow_non_contiguous=True)
            ld(mu13[:], mu.rearrange("(c p) -> p c", p=128), ["mu13"], allow_slow_non_contiguous=True)
            for k in pv:
                ld(pv[k][:], vecs[k].rearrange("(c p) -> p c", p=128), ["pv_" + k], allow_slow_non_contiguous=True)
        ld(gnw[:], vecs["gn_w"].rearrange("(o f) -> o f", o=1).to_broadcast([128, 512]), ["gnw"])
        ld(gnb[:], vecs["gn_b"].rearrange("(o f) -> o f", o=1).to_broadcast([128, 512]), ["gnb"])
        ld(fgb[:], final_g.rearrange("(o f) -> o f", o=1).to_broadcast([128, 1024]), ["fgb"])
        ld(wup_f[0:64, :], w_dup, ["stw0"])
        ld(wup_f[64:128, :], w_aup, ["stw0"])
        vop("pool", "memset", [], ["wup"], ap=wup[:], constant=0.0)
        cp("dve", wup[0:64, 0, :], wup_f[0:64, :], ["stw0"], ["wup"])
        cp("dve", wup[64:128, 1, :], wup_f[64:128, :], ["stw0"], ["wup"])
        vop("dve", "tensor_scalar", ["g8"], ["gq8"], out=gq8[:], in0=g8[:], scalar1=0.125, scalar2=None, op0=ALU.mult)
        vop("dve", "tensor_scalar", ["pv_k_a"], ["omka"], out=omka[:], in0=pv["k_a"][:], scalar1=-1.0, scalar2=1.0,
            op0=ALU.mult, op1=ALU.add)
        with nc.allow_non_contiguous_dma(reason="compress weights"):
            for m in range(2):
                for e in range(2):
                    for k in range(2):
                        for blk in range(8):
                            ld(wt[16 * blk:16 * blk + 16, m, e * 128 + k * 64: e * 128 + k * 64 + 64],
                               w_cpos[e, 16 * m:16 * m + 16, :], ["wt"])
            for e in range(2):
                ld(wmix_f[0:64, e, :], w_cmix[e], ["wmix_f"])
                ld(wmix_f[64:128, e, :], w_cmix[e], ["wmix_f"])
        vop("pool", "memset", [], ["wmix"], ap=wmix[:], constant=0.0)
        cp("dve", wmix[0:64, 0, :, :], wmix_f[0:64, :, :], ["wmix_f"], ["wmix"])
        cp("dve", wmix[64:128, 1, :, :], wmix_f[64:128, :, :], ["wmix_f"], ["wmix"])
        for kt in range(8):
            ld(wout_f[:], w_out[kt * 128:(kt + 1) * 128, :], ["xres"])
            cp("act" if kt % 2 else "dve", wout[:, kt, :], wout_f[:], ["xres"], ["wout"])

        wbuf = [sb(f"wbuf{i}", [128, 8, 512], BF16) for i in range(2)]
        stg = [(stw_t[0][:, 0:512], "stw0"), (yout[:, 0:512], "yout"), (yout[:, 512:1024], "youtb")]
        sidx = 0
        for gi, (gname, segs, qs) in enumerate(GROUPS):
            sbuf = wbuf[gi % 2]
            key = f"wbuf{gi % 2}"
            for kt in range(8):
                s_, skey = stg[sidx % 3]
                sidx += 1
                off = 0
                for (c0, ncol) in segs:
                    ld(s_[:, off:off + ncol], w_in[kt * 128:(kt + 1) * 128, c0:c0 + ncol], [skey])
                    off += ncol
                gsrc = gq8 if qs else g8
                if sidx % 2:
                    act(sbuf[:, kt, 0:off], s_[:, 0:off], AF.Copy, [skey, "g8", "gq8"], [key], scale=gsrc[:, kt:kt + 1])
                else:
                    vop("dve", "tensor_scalar", [skey, "g8", "gq8"], [key], out=sbuf[:, kt, 0:off], in0=s_[:, 0:off],
                        scalar1=gsrc[:, kt:kt + 1], scalar2=None, op0=ALU.mult)
            P.dma("sp", wscr[gi].rearrange("p (k c) -> p k c", k=8)[:, :, 0:off], sbuf[:, :, 0:off],
                  reads=[key], writes=["wscr"])

        xt = [sb(f"xt{i}", [128, 1024]) for i in range(2)]
        ss = sb("ss", [128, 1])
        rstd = sb("rstd", [128, 1])
        xn = [sb(f"xn{i}", [128, 1024], BF16) for i in range(2)]
        xnT = sb("xnT", [128, 8, TB], BF16)
        gsr = sb("gsr", [128, NTT, 512], BF16)
        gsn = sb("gsn", [128, NTT, 512], BF16)
        kv0 = [sb(f"kv0_{i}", [128, 512]) for i in range(2)]
        kv1 = [sb(f"kv1_{i}", [128, 280]) for i in range(2)]
        gates = sb("gates", [128, NTT, 24])
        ym = [sb(f"ym{i}", [128, 256], BF16) for i in range(2)]
        QT = sb("QT", [64, 8, TB], BF16)
        KTs = sb("KTs", [64, 2, T], BF16)
        KTw = sb("KTw", [64, 2, T], BF16)
        Vs = sb("Vs", [128, 16, 2, 65], BF16)
        Vw = sb("Vw", [128, 16, 2, 65], BF16)
        pooled = sb("pooled", [128, 2, 128])
        pooledb = sb("pooledb", [128, 2, 128], BF16)
        kcT = sb("kcT", [64, 2, 128], BF16)
        vcx = sb("vcx", [128, 2, 97], BF16)
        zsb12 = sb("zsb12", [128, TB + 1])
        zsb = sb("zsb", [128, 3, TB + 1])
        zs12 = sb("zs12", [128, TB])
        zs = sb("zs", [128, 3, TB])
        lora = sb("lora", [128, TB], BF16)
        AR = sb("AR", [128, NTT, 2, 128])
        BK = sb("BK", [128, 2, TB])
        Wc = sb("Wc", [128, NTT * 8])
        prod = sb("prod", [128, TB])
        STz = sb("STz", [128, 4, 2, 64])
        BKz = sb("BKz", [128, 2, 2, TB])
        wkv2 = sb("wkv2", [128, 64])
        Ssb = sb("Ssb", [128, 128])
        tok = sb("tok", [128, 3, 2, 64])
        Mt = [sb(f"Mt{i}", [128, 512]) for i in range(2)]
        PT0 = sb("PT0", [128, 2, 128])
        Pk = [sb(f"Pk{i}", [128, 2, 2, 128]) for i in range(6)]
        Xa = [sb(f"Xa{i}", [128, 2, 64]) for i in range(2)]
        ytok = sb("ytok", [128, NTT, 2, 64])
        vtok = sb("vtok", [128, NTT, 2, 64])
        rk = sb("rk", [128, NTT, 2])
        st4 = sb("st4", [128, 4])
        gtmp = [sb(f"gtmp{i}", [128, 2, 64]) for i in range(3)]
        mix = sb("mix", [128, NTT, 1024], BF16)
        mixT = sb("mixT", [128, 8, 128], BF16)
        sq_junk = mixT[:].rearrange("p k t -> p (k t)")
        PTb = [sb(f"PTb{i}", [128, TB], BF16) for i in range(3)]
        oacc = sb("oacc", [128, NTT, 8, 64])
        otmp = sb("otmp", [128, NTT, 64])
        rden = sb("rden", [128, NTT])
        scg = sb("scg", [128, NTT])
        sc = sb("sc", [128, NTT, 2, 32])
        scw = sb("scw", [128, 32])
        m8a = sb("m8a", [128, 8])
        m8b = sb("m8b", [128, 8])
        thr = sb("thr", [128, 1])
        selb = sb("selb", [128, 32], BF16)
        selbT = sb("selbT", [32, 2, TB], BF16)
        zc = sb("zc", [128, 13])

        Vn = sb("Vn", [4, 2, 2, 65], BF16)
        QTs = sb("QTs", [64, 8, 4], BF16)
        QTz = sb("QTz", [128, 2, 16], BF16)
        KTn = sb("KTn", [64, 2, 2, 4], BF16)
        ptb = sb("ptb", [128, 128], I32)
        ptf = sb("ptf", [128, 128])
        pidx = sb("pidx", [128, 2, 128], I32)
        cache_h = cache.rearrange("n (h c) -> (n h) c", h=2)
        pg4 = [xres[:].rearrange("p (g c) -> p g c", g=4), yout[:].rearrange("p (g c) -> p g c", g=4)]
        pall = Vs[:].rearrange("p a k c -> p (a k c)")[:, 0:2048].rearrange("p (c j) -> p c j", c=2)
        kcTs = Vw[0:64, :, :, :].rearrange("p a k c -> p (a k c)")[:, 0:2048].rearrange("p (c j) -> p c j", c=2)
        vcs = ct["cmpbias"][:, 0:1040].rearrange("p (j k c) -> p j k c", j=8, k=2)
        selflat4 = ct["zp"][0:1, 0:2048].rearrange("o (h g k t) -> o h g k t", h=2, g=128, k=2)
        ocs1 = stw_t[0][0:16, 0:322]
        obr = sb("obr", [16, 3, 2, 64])
        scs = kv1[1][0:4, 0:257]
        scs2 = kv0[1][0:4, 0:257]
        KTpg = sb("KTpg", [128, 4, 128], BF16)
        Vpg = sb("Vpg", [128, 4, 2, 65], BF16)
        mexp = sb("mexp", [128, 32], BF16)

        psA = pst("psA", [128, 512])
        psB = pst("psB", [128, 512])
        psT = pst("psT", [128, 1024], BF16)
        psP = pst("psP", [128, 512])
        psR = [pst(f"psR{i}", [128, 512]) for i in range(4)]

        sems = {}
        for e in ENGS:
            sems[e] = st.enter_context(nc.semaphore("s_" + e))
            for i in range(NDMASEM):
                sems[(e, i)] = st.enter_context(nc.semaphore(f"d_{e}_{i}"))
        block = st.enter_context(nc.Block())

        wb_n = {"n": 0}

        def load_group(gname):
            gi = GIDX[gname]
            i = wb_n["n"] % 2
            wb_n["n"] += 1
            ld(wbuf[i][:], wscr[gi].rearrange("p (k c) -> p k c", k=8), [f"wbuf{i}"], R=["wscr"])
            return wbuf[i], f"wbuf{i}"

        vop("pool", "memset", [], ["Vs"], ap=Vs[:], constant=1.0)
        vop("pool", "memset", [], ["Vw"], ap=Vw[:], constant=1.0)

        identr_t = sb("identr", [128, 128])
        identr = identr_t[:].bitcast(F32R)
        cp("dve", identr, identf[:], ["k_identf"], ["identr"])
        psAB = [(psA, "psA"), (psB, "psB")]
        pn = {"n": 0}

        def nextps():
            pn["n"] += 1
            return psAB[pn["n"] % 2]

        R0, R1, R2, R3 = psR
        sps = [(R1, "psR1"), (R2, "psR2")]
        sn = {"n": 0}

        def rmsnorm_T(xsrc, npart, tt):
            x_ = xt[tt % 2]
            xk = f"xt{tt % 2}"
            pp = slice(0, npart)
            ld(x_[pp, :], xsrc, [xk])
            act(sq_junk[pp, :], x_[pp, :], AF.Square, [xk], ["mixT", "ss"], accum_out=ss[pp, :])
            act(rstd[pp, :], ss[pp, :], AF.Sqrt, ["ss"], ["rstd"], scale=1.0 / 1024, bias=EPS)
            vop("dve", "reciprocal", ["rstd"], ["rstd"], out=rstd[pp, :], in_=rstd[pp, :])
            xn_ = xn[tt % 2]
            xnk = f"xn{tt % 2}"
            vop("dve", "tensor_scalar", [xk, "rstd"], [xnk], out=xn_[pp, :], in0=x_[pp, :], scalar1=rstd[pp, 0:1],
                scalar2=None, op0=ALU.mult)
            for kt in range(8):
                tr(psT[:, kt * 128:kt * 128 + npart], xn_[pp, kt * 128:(kt + 1) * 128], identb[pp, pp],
                   [xnk, "k_identb"], ["psT"])
            cp("act", xnT[:, :, tt * 128:tt * 128 + npart], psT[:].rearrange("p (k t) -> p k t", k=8)[:, :, 0:npart],
               ["psT"], ["xnT"])

        def rwkv_block(n, C, nlev, npart_of_chunk, gate_cols):
            nch = n // C
            use_r = (C == 128)
            Rm = (lambda ap: ap.bitcast(F32R)) if use_r else (lambda ap: ap)
            Ro = lambda ap: ap.bitcast(F32R)
            idm = identr if use_r else identf
            idk = "identr" if use_r else "k_identf"
            wb, wk = load_group("F0")
            ps_, pk = nextps()
            for kt in range(8):
                mm(ps_[:, 0:n], wb[:, kt, 0:128], xnT[:, kt, 0:n], kt == 0, kt == 7, ["xnT", wk], [pk])
            cp("dve", zsb12[:, 0:1], zc[:, 12:13], ["zc"], ["zsb12"])
            cp("act", zsb12[:, 1:n + 1], ps_[:, 0:n], [pk], ["zsb12"])
            cp("dve", zc[:, 12:13], zsb12[:, n:n + 1], ["zsb12"], ["zc"])
            vop("dve", "tensor_tensor", ["zsb12"], ["tmpf5"], out=tmpf[5][:, 0:n], in0=zsb12[:, 0:n],
                in1=zsb12[:, 1:n + 1], op=ALU.subtract)
            vop("dve", "scalar_tensor_tensor", ["tmpf5", "mu13", "zsb12"], ["zs12"], out=zs12[:, 0:n], in0=tmpf[5][:, 0:n],
                scalar=mu13[:, 12:13], in1=zsb12[:, 1:n + 1], op0=ALU.mult, op1=ALU.add)
            act(lora[0:64, 0:n], zs12[0:64, 0:n], AF.Tanh, ["zs12"], ["lora"])
            cp("dve", lora[64:128, 0:n], zs12[64:128, 0:n], ["zs12"], ["lora"])
            m4 = ct["mask4"] if C == 128 else ct["mask4s"]
            mL = ct["maskL"] if C == 128 else ct["maskLs"]
            m4k = "k_mask4" if C == 128 else "k_mask4s"
            mLk = "k_maskL" if C == 128 else "k_maskLs"
            cpp = slice(0, C)
            for fc in range(4):
                wb, wk = load_group(f"R{fc}")
                for j3 in range(3):
                    ps_, pk = nextps()
                    cidx = j3 * 4 + fc
                    for kt in range(8):
                        mm(ps_[:, 0:n], wb[:, kt, j3 * 128:(j3 + 1) * 128], xnT[:, kt, 0:n], kt == 0, kt == 7,
                           ["xnT", wk], [pk])
                    cp("dve", zsb[:, j3, 0:1], zc[:, cidx:cidx + 1], ["zc"], ["zsb"])
                    cp("act", zsb[:, j3, 1:n + 1], ps_[:, 0:n], [pk], ["zsb"])
                    cp("dve", zc[:, cidx:cidx + 1], zsb[:, j3, n:n + 1], ["zsb"], ["zc"])
                    vop("dve", "tensor_tensor", ["zsb"], [f"tmpf{5 + j3}"], out=tmpf[5 + j3][:, 0:n], in0=zsb[:, j3, 0:n],
                        in1=zsb[:, j3, 1:n + 1], op=ALU.subtract)
                    vop("dve", "scalar_tensor_tensor", [f"tmpf{5 + j3}", "mu13", "zsb"], ["zs"], out=zs[:, j3, 0:n],
                        in0=tmpf[5 + j3][:, 0:n], scalar=mu13[:, cidx:cidx + 1], in1=zsb[:, j3, 1:n + 1],
                        op0=ALU.mult, op1=ALU.add)
                r_, k_, v_ = zs[:, 0, 0:n], zs[:, 1, 0:n], zs[:, 2, 0:n]
                sg, al, lw, cum, t4, t5, t6, t7 = [t[:, 0:n] for t in tmpf]
                K = lambda *i: [f"tmpf{j}" for j in i]
                ps_, pk = nextps()
                mm(ps_[:, 0:n], wup[:, 0, fc * 128:(fc + 1) * 128], lora[:, 0:n], True, True, ["wup", "lora"], [pk])
                act(sg, ps_[:, 0:n], AF.Sigmoid, [pk, "pv_w0"], K(0), bias=pv["w0"][:, fc:fc + 1])
                vop("dve", "tensor_scalar", K(0), K(2), out=lw, in0=sg, scalar1=-DEC_C, scalar2=None, op0=ALU.mult)
                ps_, pk = nextps()
                mm(ps_[:, 0:n], wup[:, 1, fc * 128:(fc + 1) * 128], lora[:, 0:n], True, True, ["wup", "lora"], [pk])
                act(al, ps_[:, 0:n], AF.Sigmoid, [pk, "pv_a0"], K(1), bias=pv["a0"][:, fc:fc + 1])
                vop("dve", "tensor_scalar", ["zs", "pv_k_k"], K(4), out=t4, in0=k_, scalar1=pv["k_k"][:, fc:fc + 1],
                    scalar2=None, op0=ALU.mult)
                vop("dve", "tensor_tensor", K(4), K(5), out=t5, in0=t4, in1=t4, op=ALU.mult)
                ps_, pk = nextps()
                mm(ps_[:, 0:n], ct["bd64"][:], t5, True, True, ["k_bd64"] + K(5), [pk])
                vop("dve", "tensor_scalar", [pk], K(5), out=t5, in0=ps_[:, 0:n], scalar1=1e-24, scalar2=None, op0=ALU.max)
                act(t5, t5, AF.Sqrt, K(5), K(5))
                vop("dve", "reciprocal", K(5), K(5), out=t5, in_=t5)
                vop("dve", "tensor_tensor", K(4, 5), K(4), out=t4, in0=t4, in1=t5, op=ALU.mult)
                vop("dve", "tensor_scalar", K(1) + ["pv_k_a", "omka"], K(5), out=t5, in0=al,
                    scalar1=pv["k_a"][:, fc:fc + 1], scalar2=omka[:, fc:fc + 1], op0=ALU.mult, op1=ALU.add)
                vop("dve", "tensor_tensor", ["zs"] + K(5), K(5), out=t5, in0=k_, in1=t5, op=ALU.mult)
                vop("dve", "tensor_tensor_scan", ["k_resetm"] + K(2), K(3), out=cum, data0=ct["resetm"][:, 0:n] if C == 128 else ct["resetm"][:, 0:n],
                    data1=lw, initial=0.0, op0=ALU.mult, op1=ALU.add)
                act(t6, cum, AF.Exp, K(3), K(6))
                for ch in range(nch):
                    cs = slice(ch * C, (ch + 1) * C)
                    cp("dve", Wc[:, 8 * ch:8 * ch + 1], t6[:, ch * C + C - 1: ch * C + C], K(6), ["Wc"])
                    vop("dve", "tensor_tensor", ["zs"] + K(6), ["AR"], out=Ro(AR[:, ch, 1, 0:C]), in0=r_[:, cs], in1=t6[:, cs], op=ALU.mult)
                vop("dve", "scalar_tensor_tensor", ["zs", "pv_r_k"] + K(5), ["prod"], out=prod[:, 0:n], in0=r_,
                    scalar=pv["r_k"][:, fc:fc + 1], in1=t5, op0=ALU.mult, op1=ALU.mult)
                act(t7, cum, AF.Exp, K(3), K(7), scale=-1.0)
                vop("dve", "tensor_tensor", K(5, 7), ["BK"], out=Ro(BK[:, 1, 0:n]), in0=t5, in1=t7, op=ALU.mult)
                for hh in range(2):
                    rws = slice(hh * 64, hh * 64 + 64)
                    cp("act", Ro(BKz[rws, hh, 1, 0:n]), BK[rws, 1, 0:n], ["BK"], ["BKz"])
                vop("dve", "tensor_tensor", K(4, 1), K(5), out=t5, in0=t4, in1=al, op=ALU.mult)
                vop("dve", "tensor_tensor", K(5, 7), ["BK"], out=Ro(BK[:, 0, 0:n]), in0=t5, in1=t7, op=ALU.mult)
                for hh in range(2):
                    rws = slice(hh * 64, hh * 64 + 64)
                    cp("act", Ro(BKz[rws, hh, 0, 0:n]), BK[rws, 0, 0:n], ["BK"], ["BKz"])
                vop("dve", "tensor_tensor", K(3, 2), K(6), out=t6, in0=cum, in1=lw, op=ALU.subtract)
                act(t6, t6, AF.Exp, K(6), K(6))
                for ch in range(nch):
                    cs = slice(ch * C, (ch + 1) * C)
                    vop("dve", "scalar_tensor_tensor", K(4, 6), ["AR"], out=Ro(AR[:, ch, 0, 0:C]), in0=t4[:, cs], scalar=-1.0,
                        in1=t6[:, cs], op0=ALU.mult, op1=ALU.mult)
                chk(3)
                for ch in range(nch):
                    cs = slice(ch * C, (ch + 1) * C)
                    ARf = AR[:, ch, :, 0:C]
                    tr(R0[cpp, 0:128], BK[:, 0, cs], identf[:], ["BK", "k_identf"], ["psR0"])
                    tr(R0[cpp, 128:256], BK[:, 1, cs], identf[:], ["BK", "k_identf"], ["psR0"])
                    tr(R0[cpp, 256:384], zs[:, 2, cs], identf[:], ["zs", "k_identf"], ["psR0"])
                    cp("act", Ro(tok[cpp].rearrange("p a h d -> p (a h d)")), R0[cpp, 0:384], ["psR0"], ["tok"])
                    cp("dve", vtok[cpp, ch, :, :], tok[cpp, 2, :, :], ["tok"], ["vtok"])
                    for hh in range(2):
                        for a2 in range(2):
                            for a3 in range(2):
                                mm((R1, R2)[hh][cpp, (2 * a2 + a3) * C:(2 * a2 + a3 + 1) * C], Rm(BKz[:, hh, a2, cs]), Rm(AR[:, ch, a3, 0:C]), True, True,
                                   ["BKz", "AR"], [("psR1", "psR2")[hh]])
                        mm(R3[cpp, 256 + hh * C: 256 + (hh + 1) * C], Rm(AR[:, ch, 0, 0:C]), Rm(BKz[:, hh, 0, cs]), True, True,
                           ["AR", "BKz"], ["psR3b"])
                    for hh in range(2):
                        vop("dve", "tensor_tensor", [("psR1", "psR2")[hh], m4k], [f"Mt{hh}"], out=Ro(Mt[hh][cpp, 0:4 * C]), in0=(R1, R2)[hh][cpp, 0:4 * C],
                            in1=m4[cpp, 0:4 * C], op=ALU.mult)
                    for hh in range(0):
                        mm(R3[cpp, 256 + hh * C: 256 + (hh + 1) * C], Rm(AR[:, ch, 0, 0:C]), Rm(BKz[:, hh, 0, cs]), True, True,
                           ["AR", "BKz"], ["psR3b"])
                    vop("dve", "tensor_tensor", ["psR3b", mLk], ["PT0"], out=Ro(PT0[cpp, :, 0:C]),
                        in0=R3[cpp, 256:256 + 2 * C].rearrange("p (h t) -> p h t", h=2), in1=mL[cpp, 0:2 * C].rearrange("p (h t) -> p h t", h=2), op=ALU.mult)
                    for lv in range(nlev - 1):
                        for hh in range(2):
                            if lv == 0:
                                Pm, PTm, kk_ = Mt[hh][cpp, 0:C], PT0[cpp, hh, 0:C], [f"Mt{hh}", "PT0"]
                            else:
                                Pm, PTm, kk_ = Pk[lv - 1][cpp, hh, 0, 0:C], Pk[lv - 1][cpp, hh, 1, 0:C], [f"Pk{lv - 1}"]
                            mm(R2[cpp, (2 * hh) * C:(2 * hh + 1) * C], Rm(PTm), Rm(Pm), True, True, kk_, ["psR2"])
                            mm(R2[cpp, (2 * hh + 1) * C:(2 * hh + 2) * C], Rm(Pm), Rm(PTm), True, True, kk_, ["psR2"])
                        cp(evac_eng(), Ro(Pk[lv][cpp, :, :, 0:C]), R2[cpp, 0:4 * C].rearrange("p (h a t) -> p h a t", h=2, a=2), ["psR2"], [f"Pk{lv}"])
                    for hh in range(2):
                        mm(R3[cpp, hh * 64:(hh + 1) * 64], Rm(AR[:, ch, 0, 0:C]), Rm(STz[:, fc, hh, :]), True, False, ["AR", "STz"], ["psR3a"])
                        mm(R3[cpp, hh * 64:(hh + 1) * 64], Rm(Mt[hh][cpp, 2 * C:3 * C]), Rm(tok[cpp, 2, hh, :]), False, True,
                           [f"Mt{hh}", "tok"], ["psR3a"])
                    xi = 0
                    cp("act", Ro(Xa[0][cpp].rearrange("p h d -> p (h d)")), R3[cpp, 0:128], ["psR3a"], ["Xa0"])
                    for lv in range(nlev):
                        for hh in range(2):
                            if lv == 0:
                                Pm, kk_ = Mt[hh][cpp, 0:C], [f"Mt{hh}"]
                            else:
                                Pm, kk_ = Pk[lv - 1][cpp, hh, 0, 0:C], [f"Pk{lv - 1}"]
                            mm(R3[cpp, hh * 64:(hh + 1) * 64], idm[cpp, cpp], Rm(Xa[xi][cpp, hh, :]), True, False,
                               [idk, f"Xa{xi}"], ["psR3a"])
                            mm(R3[cpp, hh * 64:(hh + 1) * 64], Rm(Pm), Rm(Xa[xi][cpp, hh, :]), False, True, kk_ + [f"Xa{xi}"], ["psR3a"])
                        cp(evac_eng(), Ro(Xa[1 - xi][cpp].rearrange("p h d -> p (h d)")), R3[cpp, 0:128], ["psR3a"], [f"Xa{1 - xi}"])
                        xi = 1 - xi
                    E = Xa[xi]
                    ek = f"Xa{xi}"
                    for hh in range(2):
                        mm(R3[cpp, hh * 64:(hh + 1) * 64], Rm(AR[:, ch, 1, 0:C]), Rm(STz[:, fc, hh, :]), True, False, ["AR", "STz"], ["psR3a"])
                        mm(R3[cpp, hh * 64:(hh + 1) * 64], Rm(Mt[hh][cpp, C:2 * C]), Rm(E[cpp, hh, :]), False, False, [f"Mt{hh}", ek], ["psR3a"])
                        mm(R3[cpp, hh * 64:(hh + 1) * 64], Rm(Mt[hh][cpp, 3 * C:4 * C]), Rm(tok[cpp, 2, hh, :]), False, True,
                           [f"Mt{hh}", "tok"], ["psR3a"])
                    cp("act", ytok[cpp, ch, :, :].rearrange("p h d -> p (h d)"), R3[cpp, 0:128], ["psR3a"], ["ytok"])
                    SU = psB[:, 0:128]
                    mm(SU, idm[:], Rm(STz[:, fc, :, :].rearrange("p h d -> p (h d)")), True, False, [idk, "STz"], ["psB"])
                    mm(SU, Rm(tok[cpp, 0, :, :].rearrange("p h d -> p (h d)")), Rm(E[cpp].rearrange("p h d -> p (h d)")), False, False, ["tok", ek], ["psB"])
                    mm(SU, Rm(tok[cpp, 1, :, :].rearrange("p h d -> p (h d)")), Rm(tok[cpp, 2, :, :].rearrange("p h d -> p (h d)")), False, True, ["tok"], ["psB"])
                    cp("dve", Ssb[:], psB[:, 0:128], ["psB"], ["Ssb"])
                    for hh in range(2):
                        rows = slice(hh * 64, hh * 64 + 64)
                        act(Ro(STz[rows, fc, hh, :]), Ssb[rows, hh * 64:(hh + 1) * 64], AF.Copy, ["Ssb", "Wc"], ["STz"],
                            scale=Wc[rows, 8 * ch:8 * ch + 1])
                chk(4)
                for ch in range(nch):
                    cs = slice(ch * C, (ch + 1) * C)
                    mm(psB[cpp, 0:2], prod[:, cs], ct["hsel"][:], True, True, ["prod", "k_hsel"], ["psB"])
                    cp("dve", rk[cpp, ch, :], psB[cpp, 0:2], ["psB"], ["rk"])
                    y2 = ytok[cpp, ch, :, :]
                    g0, g1, g2 = [g[cpp] for g in gtmp]
                    s4 = st4[cpp]
                    vop("dve", "tensor_reduce", ["ytok"], ["st4"], out=s4[:, 0:2], in_=y2, axis=AX.X, op=ALU.add)
                    vop("dve", "tensor_tensor", ["ytok"], ["gtmp0"], out=g0, in0=y2, in1=y2, op=ALU.mult)
                    vop("dve", "tensor_reduce", ["gtmp0"], ["st4"], out=s4[:, 2:4], in_=g0, axis=AX.X, op=ALU.add)
                    vop("dve", "tensor_scalar", ["st4"], ["st4"], out=s4[:, 0:2], in0=s4[:, 0:2], scalar1=1.0 / 64, scalar2=None, op0=ALU.mult)
                    vop("dve", "tensor_tensor", ["st4"], ["gtmp1"], out=g1[:, :, 0], in0=s4[:, 0:2], in1=s4[:, 0:2], op=ALU.mult)
                    vop("dve", "scalar_tensor_tensor", ["st4", "gtmp1"], ["st4"], out=s4[:, 2:4], in0=s4[:, 2:4], scalar=1.0 / 64,
                        in1=g1[:, :, 0], op0=ALU.mult, op1=ALU.subtract)
                    act(s4[:, 2:4], s4[:, 2:4], AF.Sqrt, ["st4"], ["st4"], bias=GN_EPS)
                    vop("dve", "reciprocal", ["st4"], ["st4"], out=s4[:, 2:4], in_=s4[:, 2:4])
                    vop("dve", "tensor_tensor", ["ytok", "st4"], ["gtmp0"], out=g0, in0=y2,
                        in1=s4[:, 0:2].rearrange("p (a o) -> p a o", o=1).to_broadcast([C, 2, 64]), op=ALU.subtract)
                    vop("dve", "tensor_tensor", ["gtmp0", "st4"], ["gtmp0"], out=g0, in0=g0,
                        in1=s4[:, 2:4].rearrange("p (a o) -> p a o", o=1).to_broadcast([C, 2, 64]), op=ALU.mult)
                    gw = gnw[cpp, fc * 128:(fc + 1) * 128].rearrange("p (h d) -> p h d", h=2)
                    gb = gnb[cpp, fc * 128:(fc + 1) * 128].rearrange("p (h d) -> p h d", h=2)
                    vop("dve", "tensor_tensor", ["gtmp0", "gnw"], ["gtmp0"], out=g0, in0=g0, in1=gw, op=ALU.mult)
                    vop("dve", "tensor_tensor", ["gtmp0", "gnb"], ["gtmp0"], out=g0, in0=g0, in1=gb, op=ALU.add)
                    vop("dve", "tensor_tensor", ["vtok", "rk"], ["gtmp1"], out=g1, in0=vtok[cpp, ch, :, :],
                        in1=rk[cpp, ch, :].rearrange("p (a o) -> p a o", o=1).to_broadcast([C, 2, 64]), op=ALU.mult)
                    vop("dve", "tensor_tensor", ["gtmp0", "gtmp1"], ["gtmp0"], out=g0, in0=g0, in1=g1, op=ALU.add)
                    vop("dve", "tensor_tensor", ["gtmp0", "gsr"], ["mix"],
                        out=mix[cpp, ch, fc * 128:(fc + 1) * 128].rearrange("p (h d) -> p h d", h=2), in0=g0,
                        in1=gsr[cpp, ch, fc * 128:(fc + 1) * 128].rearrange("p (h d) -> p h d", h=2), op=ALU.mult)
                chk(4.5)

        def shift_and_state_out(shdst, wkvdst):
            for rnd in range(4):
                ncs = 4 if rnd < 3 else 1
                for ci in range(ncs):
                    c13 = rnd * 4 + ci
                    tr(R0[0:1, ci * 128:(ci + 1) * 128], zc[:, c13:c13 + 1], identf[:], ["zc", "k_identf"], ["psR0"])
                dst = xres if rnd < 2 else yout
                dk = "xres" if rnd < 2 else "yout"
                o_ = (rnd % 2) * 512
                cp("dve", dst[0:1, o_:o_ + ncs * 128], R0[0:1, 0:ncs * 128], ["psR0"], [dk])
            stq(shdst[:, 0:1024], xres[0:1, 0:1024], ["xres"])
            stq(shdst[:, 1024:1664], yout[0:1, 0:640], ["yout"])
            for fc in range(4):
                tr(R0[:, 0:128], STz[:, fc, :, :].rearrange("p h d -> p (h d)"), identf[:], ["STz", "k_identf"], ["psR0"])
                cp("dve", Ssb[:], R0[:, 0:128], ["psR0"], ["Ssb"])
                for hh in range(2):
                    rws = slice(hh * 64, hh * 64 + 64)
                    cp("act", wkv2[rws, :], Ssb[rws, hh * 64:(hh + 1) * 64], ["Ssb"], ["wkv2"])
                stq(wkvdst[2 * fc:2 * fc + 2, :, :].rearrange("h i j -> (h i) j"), wkv2[:], ["wkv2"])

        def proj_qk(n, kdst):
            wb, wk = load_group("Q")
            for h in range(8):
                ps_, pk = nextps()
                for kt in range(8):
                    mm(ps_[0:64, 0:n], wb[:, kt, h * 64:(h + 1) * 64], xnT[:, kt, 0:n], kt == 0, kt == 7, ["xnT", wk], [pk])
                cp(evac_eng(), QT[:, h, 0:n], ps_[0:64, 0:n], [pk], ["QT"])
            wb, wk = load_group("K")
            for si in range(2):
                for kvh in range(2):
                    ps_, pk = nextps()
                    c0 = si * 128 + kvh * 64
                    for kt in range(8):
                        mm(ps_[0:64, 0:n], wb[:, kt, c0:c0 + 64], xnT[:, kt, 0:n], kt == 0, kt == 7, ["xnT", wk], [pk])
                    dst, dk = kdst(si, kvh)
                    cp(evac_eng(), dst, ps_[0:64, 0:n], [pk], [dk])

        def finish(h, br, ps_, stride, first, npart, ntt):
            pp = slice(0, npart)
            view = ps_[pp, 0:ntt * stride].rearrange("p (t c) -> p t c", t=ntt)
            vop("dve", "tensor_scalar", ["psR3a"], ["rden"], out=rden[pp, 0:ntt], in0=view[:, :, 64], scalar1=1e-30, scalar2=None,
                op0=ALU.max)
            vop("dve", "reciprocal", ["rden"], ["rden"], out=rden[pp, 0:ntt], in_=rden[pp, 0:ntt])
            vop("dve", "tensor_tensor", ["rden", "gates"], ["scg"], out=scg[pp, 0:ntt], in0=rden[pp, 0:ntt], in1=gates[pp, 0:ntt, br * 8 + h], op=ALU.mult)
            bc = scg[pp, 0:ntt].rearrange("p (a o) -> p a o", o=1).to_broadcast([npart, ntt, 64])
            if first:
                vop("dve", "tensor_tensor", ["psR3a", "scg"], ["oacc"], out=oacc[pp, 0:ntt, h, :], in0=view[:, :, 0:64], in1=bc, op=ALU.mult)
            else:
                vop("dve", "tensor_tensor", ["psR3a", "scg"], ["otmp"], out=otmp[pp, 0:ntt, :], in0=view[:, :, 0:64], in1=bc, op=ALU.mult)
                vop("dve", "tensor_tensor", ["otmp", "oacc"], ["oacc"], out=oacc[pp, 0:ntt, h, :], in0=oacc[pp, 0:ntt, h, :], in1=otmp[pp, 0:ntt, :], op=ALU.add)

        def out_proj(npart, tt, xsrc, ydst):
            pp = slice(0, npart)
            vop("dve", "tensor_tensor", ["oacc", "gsn"], ["mix"], out=mix[pp, tt, 512:1024],
                in0=oacc[pp, tt, :, :].rearrange("p h d -> p (h d)"), in1=gsn[pp, tt, :], op=ALU.mult)
            for kt in range(8):
                tr(psT[:, kt * 128:kt * 128 + npart], mix[pp, tt, kt * 128:(kt + 1) * 128], identb[pp, pp], ["mix", "k_identb"], ["psT"])
            cp("act", mixT[:, :, 0:npart], psT[:].rearrange("p (k t) -> p k t", k=8)[:, :, 0:npart], ["psT"], ["mixT"])
            ld(xres[pp, :], xsrc, ["xres"])
            for nchunk, (ps_, pk) in enumerate(psAB):
                for kt in range(8):
                    mm(ps_[pp, 0:512], mixT[:, kt, 0:npart], wout[:, kt, nchunk * 512:(nchunk + 1) * 512], kt == 0, kt == 7,
                       ["mixT", "wout"], [pk])
                vop("dve", "tensor_tensor", [pk, "xres"], ["yout"], out=yout[pp, nchunk * 512:(nchunk + 1) * 512], in0=ps_[pp, 0:512],
                    in1=xres[pp, nchunk * 512:(nchunk + 1) * 512], op=ALU.add)
            act(sq_junk[pp, :], yout[pp, :], AF.Square, ["yout"], ["mixT", "ss"], accum_out=ss[pp, :])
            act(rstd[pp, :], ss[pp, :], AF.Sqrt, ["ss"], ["rstd"], scale=1.0 / 1024, bias=EPS)
            vop("dve", "reciprocal", ["rstd"], ["rstd"], out=rstd[pp, :], in_=rstd[pp, :])
            vop("dve", "scalar_tensor_tensor", ["yout", "rstd", "fgb"], ["yout"], out=yout[pp, :], in0=yout[pp, :], scalar=rstd[pp, 0:1],
                in1=fgb[pp, :], op0=ALU.mult, op1=ALU.mult)
            stq(ydst, yout[pp, :], ["yout"])

        def sample_jobs():
            ovc = ct["ovc"]
            vop("pool", "memset", [], ["k_cmpbias"], ap=ct["cmpbias"][:, 0:1040], constant=1.0)
            vop("pool", "memset", [], ["Vn"], ap=Vn[:], constant=1.0)
            vop("pool", "memset", [], ["Vpg"], ap=Vpg[:], constant=1.0)
            vop("pool", "memset", [], ["QTz"], ap=QTz[:], constant=0.0)
            for bs in range(4):
                ld(xres[0:1, 0:1024], sshift[bs:bs + 1, 0:1024], ["xres"])
                ld(yout[0:1, 0:640], sshift[bs:bs + 1, 1024:1664], ["yout"])
                for c13 in range(13):
                    src = xres[0:1, c13 * 128:(c13 + 1) * 128] if c13 < 8 else yout[0:1, (c13 - 8) * 128:(c13 - 7) * 128]
                    tr(R0[:, c13:c13 + 1], src, identf[0:1, 0:1], ["xres", "yout", "k_identf"], ["psR0"])
                cp("dve", zc[:], R0[:, 0:13], ["psR0"], ["zc"])
                cp("dve", STz[:].rearrange("p f h d -> p (f h d)").bitcast(F32R), ct["zeros"][:, 0:512], ["k_zeros"], ["STz"])
                for fc in range(4):
                    ld(tmpf[0][0:64, 0:128].rearrange("i (h j) -> i h j", h=2), swkv[bs, 2 * fc:2 * fc + 2, :, :].rearrange("h i j -> i h j"), ["tmpf0"])
                    tr(R0[:, 0:64], tmpf[0][0:64, 0:128], identf[0:64, 0:64], ["tmpf0", "k_identf"], ["psR0"])
                    cp("dve", Ssb[:, 0:64], R0[:, 0:64], ["psR0"], ["Ssb"])
                    for hh in range(2):
                        rws = slice(hh * 64, hh * 64 + 64)
                        cp("act", STz[rws, fc, hh, :].bitcast(F32R), Ssb[rws, 0:64], ["Ssb"], ["STz"])
                chk(20)
                rmsnorm_T(xs[4 * bs:4 * bs + 4, :], 4, 0)
                p4 = slice(0, 4)
                for gname, ncol in (("T0", 512), ("T1", 512), ("T2", 512), ("T3", 280)):
                    wb, wk = load_group(gname)
                    ps_, pk = nextps()
                    for kt in range(8):
                        mm(ps_[p4, 0:ncol], xnT[:, kt, 0:4], wb[:, kt, 0:ncol], kt == 0, kt == 7, ["xnT", wk], [pk])
                    if gname == "T0":
                        act(gsr[p4, 0, :], ps_[p4, 0:512], AF.Silu, [pk], ["gsr"])
                    elif gname == "T1":
                        act(gsn[p4, 0, :], ps_[p4, 0:512], AF.Silu, [pk], ["gsn"])
                    elif gname == "T2":
                        cp("act", kv0[0][p4, :], ps_[p4, 0:512], [pk], ["kv0_0"])
                        stq(kvs[4 * bs:4 * bs + 4, :], kv0[0][p4, :], ["kv0_0"])
                        cp("dve", Vn[p4, 0, :, 0:64], kv0[0][p4, 384:512].rearrange("p (k d) -> p k d", k=2), ["kv0_0"], ["Vn"])
                    else:
                        cp("act", kv1[0][p4, :], ps_[p4, 0:280], [pk], ["kv1_0"])
                        stq(wins[bs, 508:512, :], kv1[0][p4, 0:256], ["kv1_0"])
                        cp("dve", Vn[p4, 1, :, 0:64], kv1[0][p4, 128:256].rearrange("p (k d) -> p k d", k=2), ["kv1_0"], ["Vn"])
                        act(gates[p4, 0, :], kv1[0][p4, 256:280], AF.Sigmoid, ["kv1_0"], ["gates"])
                chk(21)
                rwkv_block(4, 4, 2, 4, None)
                chk(22)
                shift_and_state_out(shs[bs:bs + 1, :], wkvs[bs])
                chk(23)
                proj_qk(4, lambda si, kvh: (KTn[:, si, kvh, :], "KTn"))
                cp("dve", QTs[:], QT[:, :, 0:4], ["QT"], ["QTs"])
                wb, wk = load_group("Q")
                for g in range(4):
                    h = 4 + g
                    ps_, pk = nextps()
                    for kt in range(8):
                        mm(ps_[:, 0:4], wb[:, kt, (h - 1) * 64:(h + 1) * 64], xnT[:, kt, 0:4], kt == 0, kt == 7, ["xnT", wk], [pk])
                    cp("dve", Ssb[:, g * 4:(g + 1) * 4], ps_[:, 0:4], [pk], ["Ssb"])
                cp("act", QTz[64:128, 1, :], Ssb[64:128, 0:16], ["Ssb"], ["QTz"])
                cp("dve", QTz[0:64, 0, :], QTs[:, 0:4, :].rearrange("p g t -> p (g t)"), ["QTs"], ["QTz"])
                chk(24)
                ld(ptb[:], pt[bs:bs + 1, :].to_broadcast([128, 128]), ["ptb"])
                cp("dve", ptf[:], ptb[:], ["ptb"], ["ptf"])
                vop("dve", "tensor_scalar", ["ptf", "k_iota"], ["pidx"], out=pidx[:, 0, :], in0=ptf[:], scalar1=256.0, scalar2=ct["iota"][:, 0:1],
                    op0=ALU.mult, op1=ALU.add)
                vop("dve", "tensor_scalar", ["ptf", "k_iota"], ["pidx"], out=pidx[:, 1, :], in0=ptf[:], scalar1=256.0, scalar2=ct["iota"][:, 1:2],
                    op0=ALU.mult, op1=ALU.add)

                def gather4(g0, col0, key):
                    dst = pg4[key]
                    kname = ("xres", "yout")[key]
                    for i in range(4):
                        P.dma_fn("pool", (lambda d_, gi: (lambda e: e.indirect_dma_start(
                            out=d_, out_offset=None, in_=cache_h,
                            in_offset=bass.IndirectOffsetOnAxis(ap=pidx[:, col0 // 256, gi:gi + 1], axis=0))))(dst[:, i, :], g0 + i),
                            reads=["pidx"], writes=[kname])
                    return dst, kname

                chk(25)
                gi_ = 0
                for G in range(8):
                    mm(psP[:, 0:258], ct["zeros"][:, 0:128], ct["zeros"][:, 0:258], True, True, ["k_zeros"], ["psP"])
                    for pq in range(4):
                        src, sk = gather4(16 * G + 4 * pq, 0, gi_ % 2)
                        gi_ += 1
                        for i in range(4):
                            ti = 4 * pq + i
                            lo, hi = 8 * ti - 1, 8 * ti + 8
                            for m in range(2):
                                vop("dve", "tensor_tensor", [sk, "wt"], [f"ym{m}"], out=ym[m][:], in0=src[:, i, :],
                                    in1=wt[:, m, :], op=ALU.mult)
                            for cc in range(2):
                                for m in range(2):
                                    zoff = m - 8 * ti + 120
                                    mm(psP[:, cc * 129 + 1 + lo: cc * 129 + 1 + hi], ym[m][:, cc * 128:(cc + 1) * 128],
                                       ct["zs"][:, zoff + lo: zoff + hi], False, True, [f"ym{m}", "k_zs"], ["psP"])
                    for cc in range(2):
                        cp("act", pall[:, cc, 128 * G:128 * G + 128], psP[:, cc * 129 + 1:cc * 129 + 129], ["psP"], ["Vs"])
                        if G > 0:
                            vop("dve", "tensor_tensor", ["psP", "Vs"], ["Vs"], out=pall[:, cc, 128 * G - 1:128 * G],
                                in0=psP[:, cc * 129:cc * 129 + 1], in1=pall[:, cc, 128 * G - 1:128 * G], op=ALU.add)
                chk(26)
                for kvh in range(2):
                    for half in range(2):
                        js = slice(half * 512, (half + 1) * 512)
                        mm(psA[0:64, 0:512], wmix[:, kvh, 0, :], pall[:, 0, js], True, True, ["wmix", "Vs"], ["psA"])
                        cp("act", kcTs[:, kvh, js], psA[0:64, 0:512], ["psA"], ["Vw"])
                    for jt in range(8):
                        mm(psB[:, 0:64], pall[:, 1, jt * 128:(jt + 1) * 128], wmix[:, kvh, 1, :], True, True, ["wmix", "Vs"], ["psB"])
                        cp("dve", vcs[:, jt, kvh, 0:64], psB[:, 0:64], ["psB"], ["k_cmpbias"])
                chk(27)
                p16 = slice(0, 16)
                for kvh in range(2):
                    mm(R3[p16, 0:322], ct["zeros"][:, 0:16], ct["zeros"][:, 0:322], True, False, ["k_zeros"], ["psR3a"])
                    for jt in range(8):
                        sp_, spk = sps[sn["n"] % 2]
                        pt_ = PTb[sn["n"] % 3]
                        ptk = f"PTb{sn['n'] % 3}"
                        sn["n"] += 1
                        mm(sp_[:, 0:16], kcTs[:, kvh, jt * 128:(jt + 1) * 128], QTs[:, 4 * kvh:4 * kvh + 4, :].rearrange("p g t -> p (g t)"),
                           True, jt != 7, ["Vw", "QTs"], [spk])
                        if jt == 7:
                            mm(sp_[:, 0:16], identb[:], ct["lastb"][:], False, True, ["k_identb", "k_lastb"], [spk])
                        act(pt_[:, 0:16], sp_[:, 0:16], AF.Exp, [spk], [ptk])
                        mm(R3[p16, 0:65], pt_[:, 0:16], vcs[:, jt, kvh, :], False, False, [ptk, "k_cmpbias"], ["psR3a"])
                        mm(R3[p16, 65 + 32 * jt:65 + 32 * jt + 33], pt_[:, 0:16], ovc[:, 0:33], False, jt == 7, [ptk, "k_ovc"], ["psR3a"])
                    chk(27.1)
                    cp("act", ocs1[p16, :], R3[p16, 0:322], ["psR3a"], ["stw0"])
                    vop("dve", "reciprocal", ["stw0"], ["rden"], out=rden[p16, 0:1], in_=ocs1[p16, 64:65])
                    vop("dve", "tensor_scalar", ["stw0", "rden"], ["stw0"], out=ocs1[p16, :], in0=ocs1[p16, :], scalar1=rden[p16, 0:1],
                        scalar2=None, op0=ALU.mult)
                    cp("dve", obr[p16, 0, kvh, :], ocs1[p16, 0:64], ["stw0"], ["Mt1"])
                    chk(27.2)
                    mm(psA[p4, 0:257], ct["gsel"][p16, :], ocs1[p16, 65:322], True, True, ["k_gsel", "stw0"], ["psA"])
                    vop("dve", "tensor_tensor", ["psA", "k_addcs"], ["kv1_1"], out=scs[p4, :], in0=psA[p4, 0:257], in1=ct["addcs"][p4, :], op=ALU.add)
                    vop("dve", "max", ["kv1_1"], ["m8a"], out=m8a[p4, :], in_=scs[p4, :])
                    vop("dve", "match_replace", ["kv1_1", "m8a"], ["kv0_1"], out=scs2[p4, :], in_to_replace=m8a[p4, :], in_values=scs[p4, :], imm_value=-3e4)
                    vop("dve", "max", ["kv0_1"], ["m8b"], out=m8b[p4, :], in_=scs2[p4, :])
                    vop("dve", "tensor_scalar", ["kv1_1", "m8b"], ["kv0_1"], out=scs2[p4, :], in0=scs[p4, :], scalar1=m8b[p4, 7:8], scalar2=-BIG,
                        op0=ALU.is_lt, op1=ALU.mult)
                    chk(27.3)
                    for t_ in range(4):
                        mm(psB[0:1, 0:257], ct["identf"][p4, t_:t_ + 1], scs2[p4, :], True, True, ["k_identf", "kv0_1"], ["psB"])
                        cp("dve" if t_ % 2 else "act", selflat4[0:1, :, :, kvh, t_],
                           psB[0:1, 0:256].rearrange("o (pg hf) -> o hf pg", hf=2), ["psB"], ["k_zp"])

                chk(28)
                def page_attn(src, sk, npg, mask_mm, mexp_const, first_group):
                    for i in range(npg):
                        tr(R0[:, i * 128:(i + 1) * 128], src[:, i, 0:128], identf[:], [sk, "k_identf"], ["psR0"])
                    cp("act", KTpg[:, 0:npg, :].rearrange("p g n -> p (g n)"), R0[:, 0:npg * 128], ["psR0"], ["KTpg"])
                    cp("dve", Vpg[:, 0:npg, :, 0:64], src[:, 0:npg, 128:256].rearrange("p g (k d) -> p g k d", k=2), [sk], ["Vpg"])
                    sp_, spk = sps[sn["n"] % 2]
                    pt_ = PTb[sn["n"] % 3]
                    ptk = f"PTb{sn['n'] % 3}"
                    sn["n"] += 1
                    for i in range(npg):
                        for kvh in range(2):
                            c0 = (i * 2 + kvh) * 16
                            mm(sp_[:, c0:c0 + 16], KTpg[:, i, :], QTz[:, kvh, :], True, True, ["KTpg", "QTz"], [spk])
                    if mask_mm is not None:
                        for hf in range(2):
                            mm(sp_[:, 256:256 + npg * 8], ct["hl"][0:1, hf, :], mask_mm(hf), hf == 0, hf == 1, ["k_hl", "k_zp"], [spk])
                        act(mexp[:, 0:npg * 8], sp_[:, 256:256 + npg * 8], AF.Exp, [spk], ["mexp"])
                        mk = mexp[:, 0:npg * 8]
                        mkk = "mexp"
                    else:
                        mk = mexp_const
                        mkk = "k_winms"
                    act(pt_[:, 0:npg * 32], sp_[:, 0:npg * 32], AF.Exp, [spk], [ptk])
                    vop("dve", "tensor_tensor", [ptk, mkk], [ptk], out=pt_[:, 0:npg * 32].rearrange("p (a g t) -> p a g t", g=4, t=4),
                        in0=pt_[:, 0:npg * 32].rearrange("p (a g t) -> p a g t", g=4, t=4),
                        in1=mk.rearrange("p (a o t) -> p a o t", o=1, t=4).to_broadcast([128, npg * 2, 4, 4]), op=ALU.mult)
                    for i in range(npg):
                        for kvh in range(2):
                            c0 = (i * 2 + kvh) * 16
                            mm(R3[p16, kvh * 65:(kvh + 1) * 65], pt_[:, c0:c0 + 16], Vpg[:, i, kvh, :], first_group and i == 0 and kvh == 0,
                               False, [ptk, "Vpg"], ["psR3a"])

                def tail_attn(si, last):
                    sp_, spk = sps[sn["n"] % 2]
                    pt_ = PTb[sn["n"] % 3]
                    ptk = f"PTb{sn['n'] % 3}"
                    sn["n"] += 1
                    for kvh in range(2):
                        mm(sp_[p4, kvh * 16:(kvh + 1) * 16], KTn[:, si, kvh, :], QTs[:, 4 * kvh:4 * kvh + 4, :].rearrange("p g t -> p (g t)"),
                           True, False, ["KTn", "QTs"], [spk])
                        mm(sp_[p4, kvh * 16:(kvh + 1) * 16], identb[p4, p4], ct["tailb"][p4, :], False, True, ["k_identb", "k_tailb"], [spk])
                    act(pt_[p4, 0:32], sp_[p4, 0:32], AF.Exp, [spk], [ptk])
                    for kvh in range(2):
                        mm(R3[p16, kvh * 65:(kvh + 1) * 65], pt_[p4, kvh * 16:(kvh + 1) * 16], Vn[p4, si, kvh, :], False, last and kvh == 1,
                           [ptk, "Vn"], ["psR3a"])

                def fold(br, first):
                    for kvh in range(2):
                        cp("act", obr[p16, br, kvh, :], R3[p16, kvh * 65:kvh * 65 + 64], ["psR3a"], ["Mt1"])
                        cp("act", rden[p16, 1:2], R3[p16, kvh * 65 + 64:kvh * 65 + 65], ["psR3a"], ["rden"])
                        vop("dve", "reciprocal", ["rden"], ["rden"], out=rden[p16, 0:1], in_=rden[p16, 1:2])
                        vop("dve", "tensor_scalar", ["Mt1", "rden"], ["Mt1"], out=obr[p16, br, kvh, :], in0=obr[p16, br, kvh, :],
                            scalar1=rden[p16, 0:1], scalar2=None, op0=ALU.mult)

                chk(29)
                for grp in range(32):
                    src, sk = gather4(4 * grp, 256, gi_ % 2)
                    gi_ += 1
                    page_attn(src, sk, 4, (lambda hf, grp=grp: selflat4[0:1, hf, 4 * grp:4 * grp + 4, :, :].rearrange("o a k t -> o (a k t)")),
                              None, grp == 0)
                tail_attn(0, True)
                fold(1, False)
                chk(30)
                ld(pg4[0][:], cwin[bs].rearrange("(g p) c -> p g c", p=128), ["xres"])
                for tl in range(4):
                    lo_ = 4 if tl == 0 else 0
                    stq(wins[bs, 128 * tl + lo_ - 4:128 * tl + 124, :], pg4[0][lo_:128, tl, :], ["xres"])
                page_attn(pg4[0], "xres", 4, None, ct["winms"][:], True)
                tail_attn(1, True)
                fold(2, False)
                chk(31)
                for kvh in range(2):
                    for g in range(4):
                        h = 4 * kvh + g
                        for br in range(3):
                            mm(psA[p4, br * 64:(br + 1) * 64], ct["identf"][p16, g * 4:g * 4 + 4], obr[p16, br, kvh, :], True, True,
                               ["k_identf", "Mt1"], ["psA"])
                        for br in range(3):
                            if br == 0:
                                vop("dve", "tensor_scalar", ["psA", "gates"], ["oacc"], out=oacc[p4, 0, h, :], in0=psA[p4, 0:64],
                                    scalar1=gates[p4, 0, h:h + 1], scalar2=None, op0=ALU.mult)
                            else:
                                vop("dve", "scalar_tensor_tensor", ["psA", "gates", "oacc"], ["oacc"], out=oacc[p4, 0, h, :], in0=psA[p4, br * 64:(br + 1) * 64],
                                    scalar=gates[p4, 0, br * 8 + h:br * 8 + h + 1], in1=oacc[p4, 0, h, :], op0=ALU.mult, op1=ALU.add)
                chk(32)
                out_proj(4, 0, xs[4 * bs:4 * bs + 4, :], ys[4 * bs:4 * bs + 4, :])

        try:
          chk(0)
          for hh_ in range(2):
              cp("dve", BKz[:, hh_, :, :].rearrange("p a t -> p (a t)").bitcast(F32R), ct["zeros"][:, 0:2 * TB], ["k_zeros"], ["BKz"])
          for b in range(2 if os.environ.get("KNOPROMPT") is None else 0):
            vop("dve", "memset", [], ["zc"], ap=zc[:], constant=0.0)
            cp("dve", STz[:].rearrange("p f h d -> p (f h d)").bitcast(F32R), ct["zeros"][:, 0:512], ["k_zeros"], ["STz"])
            mm(psP[:, 0:256], ct["zeros"][:, 0:128], ct["zeros"][:, 0:256], True, True, ["k_zeros"], ["psP"])
            for tb in range(NB):
                t0 = tb * TB
                for tt in range(NTT):
                    rmsnorm_T(xp[b, t0 + tt * 128:t0 + (tt + 1) * 128, :], 128, tt)
                chk(1)
                for gname, ncol in (("T0", 512), ("T1", 512), ("T2", 512), ("T3", 280)):
                    wb, wk = load_group(gname)
                    for tt in range(NTT):
                        ps_, pk = nextps()
                        for kt in range(8):
                            mm(ps_[:, 0:ncol], xnT[:, kt, tt * 128:(tt + 1) * 128], wb[:, kt, 0:ncol],
                               kt == 0, kt == 7, ["xnT", wk], [pk])
                        tile_i = tb * NTT + tt
                        if gname == "T0":
                            act(gsr[:, tt, :], ps_[:, 0:512], AF.Silu, [pk], ["gsr"])
                        elif gname == "T1":
                            act(gsn[:, tt, :], ps_[:, 0:512], AF.Silu, [pk], ["gsn"])
                        elif gname == "T2":
                            k0 = kv0[tt % 2]
                            kk0 = f"kv0_{tt % 2}"
                            cp("act", k0[:], ps_[:, 0:512], [pk], [kk0])
                            stq(kvp[b, t0 + tt * 128:t0 + (tt + 1) * 128, :], k0[:], [kk0])
                            cp("dve", Vs[:, tile_i, :, 0:64], k0[:, 384:512].rearrange("p (k d) -> p k d", k=2),
                               [kk0], ["Vs"])
                            for m in range(2):
                                vop("pool", "tensor_tensor", [kk0, "wt"], [f"ym{m}"], out=ym[m][:], in0=k0[:, 0:256],
                                    in1=wt[:, m, :], op=ALU.mult)
                            for cc in range(2):
                                for m in range(2):
                                    j0 = 8 * tile_i - 1
                                    lo = max(j0, 0)
                                    hi = min(8 * tile_i + 8, 127)
                                    zoff = m - 8 * tile_i + 120
                                    mm(psP[:, cc * 128 + lo: cc * 128 + hi], ym[m][:, cc * 128:(cc + 1) * 128],
                                       ct["zs"][:, zoff + lo: zoff + hi], False, True, [f"ym{m}", "k_zs"], ["psP"])
                        else:
                            k1 = kv1[tt % 2]
                            kk1 = f"kv1_{tt % 2}"
                            cp("act", k1[:], ps_[:, 0:280], [pk], [kk1])
                            if t0 + tt * 128 >= T - 512:
                                stq(winp[b, t0 + tt * 128 - (T - 512): t0 + (tt + 1) * 128 - (T - 512), :],
                                    k1[:, 0:256], [kk1])
                            cp("dve", Vw[:, tile_i, :, 0:64], k1[:, 128:256].rearrange("p (k d) -> p k d", k=2),
                               [kk1], ["Vw"])
                            act(gates[:, tt, :], k1[:, 256:280], AF.Sigmoid, [kk1], ["gates"])
                chk(2)
                rwkv_block(TB, 128, 7, 128, None)
                chk(5)
                if tb == NB - 1:
                    shift_and_state_out(shp[b:b + 1, :], wkvp[b])
                chk(6)
                proj_qk(TB, lambda si, kvh: ((KTs, KTw)[si][:, kvh, t0:t0 + TB], ("KTs", "KTw")[si]))
                chk(6.2)
                cp("dve", pooled[:].rearrange("p a j -> p (a j)"), psP[:, 0:256], ["psP"], ["pooled"])
                cp("act", pooledb[:], pooled[:], ["pooled"], ["pooledb"])
                chk(6.3)
                for kvh in range(2):
                    mm(psA[0:64, 0:128], wmix[:, kvh, 0, :], pooledb[:, 0, :], True, True, ["wmix", "pooledb"], ["psA"])
                    cp("act", kcT[:, kvh, :], psA[0:64, 0:128], ["psA"], ["kcT"])
                    mm(psB[:, 0:64], pooledb[:, 1, :], wmix[:, kvh, 1, :], True, True, ["wmix", "pooledb"], ["psB"])
                    cp("dve", vcx[:, kvh, 0:64], psB[:, 0:64], ["psB"], ["vcx"])
                chk(6.4)
                if tb == 0 and b == 0:
                    vop("pool", "memset", [], ["vcx"], ap=vcx[:, :, 64:65], constant=1.0)
                    for kvh in range(2):
                        cp("dve", vcx[:, kvh, 65:97], ct["ov32"][:], ["k_ov32"], ["vcx"])

                def attend(h, br, tiles, Kt_, kkey, Vt_, vkey, first):
                    kvh = h // 4
                    nt = len(tiles)
                    for ti, (kt, biases) in enumerate(tiles):
                        sp_, spk = sps[sn["n"] % 2]
                        pt_ = PTb[sn["n"] % 3]
                        ptk = f"PTb{sn['n'] % 3}"
                        sn["n"] += 1
                        mm(sp_[:, 0:TB], Kt_[:, kvh, kt * 128:(kt + 1) * 128], QT[:, h, :], True, len(biases) == 0,
                           [kkey, "QT"], [spk])
                        for bi, (bl, br_, bkeys) in enumerate(biases):
                            mm(sp_[:, 0:TB], bl, br_, False, bi == len(biases) - 1, bkeys, [spk])
                        act(pt_[:], sp_[:, 0:TB], AF.Exp, [spk], [ptk])
                        for tt in range(NTT):
                            mm(R3[:, tt * 65:(tt + 1) * 65], pt_[:, tt * 128:(tt + 1) * 128], Vt_[:, kt, kvh, :],
                               ti == 0 and tt == 0, ti == nt - 1 and tt == NTT - 1, [ptk, vkey], ["psR3a"])
                    finish(h, br, R3, 65, first, 128, NTT)

                chk(7)
                for h in range(8):
                    kvh = h // 4
                    sp_, spk = sps[sn["n"] % 2]
                    pt_ = PTb[sn["n"] % 3]
                    ptk = f"PTb{sn['n'] % 3}"
                    sn["n"] += 1
                    mm(sp_[:, 0:TB], kcT[:, kvh, :], QT[:, h, :], True, False, ["kcT", "QT"], [spk])
                    mm(sp_[:, 0:TB], identb[:], ct["cmpbias"][:, t0:t0 + TB], False, True, ["k_identb", "k_cmpbias"], [spk])
                    act(pt_[:], sp_[:, 0:TB], AF.Exp, [spk], [ptk])
                    for tt in range(NTT):
                        mm(R3[:, tt * 97:(tt + 1) * 97], pt_[:, tt * 128:(tt + 1) * 128], vcx[:, kvh, :], tt == 0, tt == NTT - 1,
                           [ptk, "vcx"], ["psR3a"])
                    finish(h, 0, R3, 97, True, 128, NTT)
                    view = R3[:, 0:NTT * 97].rearrange("p (t c) -> p t c", t=NTT)
                    if h % 4 == 0:
                        vop("dve", "tensor_tensor", ["psR3a", "rden"], ["sc"], out=sc[:, :, kvh, :], in0=view[:, :, 65:97],
                            in1=rden[:].rearrange("p (a o) -> p a o", o=1).to_broadcast([128, NTT, 32]), op=ALU.mult)
                    else:
                        vop("dve", "tensor_tensor", ["psR3a", "rden"], ["otmp"], out=otmp[:, :, 0:32], in0=view[:, :, 65:97],
                            in1=rden[:].rearrange("p (a o) -> p a o", o=1).to_broadcast([128, NTT, 32]), op=ALU.mult)
                        vop("dve", "tensor_tensor", ["otmp", "sc"], ["sc"], out=sc[:, :, kvh, :], in0=sc[:, :, kvh, :], in1=otmp[:, :, 0:32], op=ALU.add)
                chk(8)
                for tt in range(NTT):
                    for kvh in range(2):
                        vop("dve", "tensor_tensor", ["sc", "k_addc"], ["scw"], out=scw[:], in0=sc[:, tt, kvh, :],
                            in1=ct["addc"][:, tb * NTT + tt, :], op=ALU.add)
                        vop("dve", "max", ["scw"], ["m8a"], out=m8a[:], in_=scw[:])
                        vop("dve", "match_replace", ["scw", "m8a"], ["otmp"], out=otmp[:, 0, 0:32], in_to_replace=m8a[:], in_values=scw[:],
                            imm_value=-3e4)
                        vop("dve", "max", ["otmp"], ["m8b"], out=m8b[:], in_=otmp[:, 0, 0:32])
                        vop("dve", "tensor_scalar", ["m8b"], ["thr"], out=thr[:], in0=m8b[:, 7:8], scalar1=-5000.0, scalar2=None, op0=ALU.max)
                        vop("dve", "tensor_scalar", ["scw", "thr"], ["selb"], out=selb[:], in0=scw[:], scalar1=thr[:, 0:1], scalar2=-BIG,
                            op0=ALU.is_lt, op1=ALU.mult)
                        tr(psT[0:32, 0:128], selb[:], identb[:], ["selb", "k_identb"], ["psT"])
                        cp("act", selbT[:, kvh, tt * 128:(tt + 1) * 128], psT[0:32, 0:128], ["psT"], ["selbT"])
                chk(9)
                for h in range(8):
                    kvh = h // 4
                    tiles = []
                    for kt in range(0, tb * NTT + NTT):
                        bs = [(ct["zp"][:, kt * 128:(kt + 1) * 128], selbT[:, kvh, :], ["k_zp", "selbT"])]
                        if kt >= tb * NTT:
                            bs.append((identb[:], ct["causb"][:, kt - tb * NTT, :], ["k_identb", "k_causb"]))
                        tiles.append((kt, bs))
                    attend(h, 1, tiles, KTs, "KTs", Vs, "Vs", False)
                chk(10)
                for h in range(8):
                    tiles = []
                    q0 = tb * NTT
                    for kt in range(max(0, q0 - 4), q0 + NTT):
                        bs = []
                        if kt >= q0:
                            bs.append((identb[:], ct["causb"][:, kt - q0, :], ["k_identb", "k_causb"]))
                        elif kt - q0 + 4 < 2:
                            bs.append((identb[:], ct["winlow"][:, kt - q0 + 4, :], ["k_identb", "k_winlow"]))
                        tiles.append((kt, bs))
                    attend(h, 2, tiles, KTw, "KTw", Vw, "Vw", False)
                chk(11)
                for tt in range(NTT):
                    out_proj(128, tt, xp[b, t0 + tt * 128:t0 + (tt + 1) * 128, :], yp[b, t0 + tt * 128:t0 + (tt + 1) * 128, :])
                chk(12)

          if with_sample:
            sample_jobs()
        except _Stop:
            for _i in range(int(os.environ.get("KPAD", "0"))):
                eng_ = os.environ.get("KPADENG", "act")
                if eng_ == "act":
                    act(thr[:], thr[:], AF.Copy, ["thr"], ["thr"])
                else:
                    cp(eng_, thr[:], m8a[:, 0:1], ["m8a"], ["thr"])
            if os.environ.get("KDBG"):
                dbg = dout("dbg", [128, 4096])
                o = 0
                for nm, tl, ncol in (("STz", STz, 512), ("Mt0", Mt[0], 512), ("Pk5", Pk[5], 512), ("Xa0", Xa[0], 128), ("Xa1", Xa[1], 128),
                                     ("tok", tok, 384), ("AR", AR, 512), ("BK", BK, 512), ("ytok", ytok, 256), ("Wc", Wc, 16)):
                    flat = tl[:]
                    shp_ = list(tl.shape) if hasattr(tl, "shape") else None
                    names = "abcdefg"[:len(shp_) - 1]
                    if len(shp_) > 2:
                        flat = tl[:].rearrange("p " + " ".join(names) + " -> p (" + " ".join(names) + ")")
                    stq(dbg[:, o:o + ncol], flat, [nm if nm not in ("Mt0", "Pk5", "Xa0", "Xa1") else nm])
                    o += ncol
        P.build(sems, block)
    return nc


def _core_inputs(c, inp, consts, cache2d=None, pt_override=None):
    d = {}
    d["xp"] = np.ascontiguousarray(inp["x_prompt"][2 * c:2 * c + 2])
    d["w_in"] = np.ascontiguousarray(inp["w_in"][0])
    d["w_out"] = np.ascontiguousarray(inp["w_out"][0])
    d["norm_g"] = np.ascontiguousarray(inp["norm_g"][0])
    d["final_g"] = np.ascontiguousarray(inp["final_g"])
    d["mu"] = np.ascontiguousarray(inp["mu_shift"][0])
    for k in ("w0", "a0", "k_k", "k_a", "r_k", "gn_w", "gn_b"):
        d[k] = np.ascontiguousarray(inp[k][0])
    d["w_dup"] = np.ascontiguousarray(inp["w_decay_up"][0])
    d["w_aup"] = np.ascontiguousarray(inp["w_aaa_up"][0])
    d["w_cpos"] = np.ascontiguousarray(inp["w_cmp_pos"][0])
    d["w_cmix"] = np.ascontiguousarray(inp["w_cmp_mix"][0])
    d["xs"] = np.ascontiguousarray(inp["x_sample"][4 * c:4 * c + 4]).reshape(16, 1024)
    d["cache"] = cache2d
    d["cwin"] = np.ascontiguousarray(inp["cache_kv_win"][0, 4 * c:4 * c + 4]).reshape(4, 512, 256)
    d["swkv"] = np.ascontiguousarray(inp["state_wkv"][0, 4 * c:4 * c + 4])
    d["sshift"] = np.ascontiguousarray(inp["state_shift"][0, 4 * c:4 * c + 4])
    d["pt"] = np.ascontiguousarray(inp["page_table"][4 * c:4 * c + 4] if pt_override is None else pt_override).astype(np.int32)
    for k, v in consts.items():
        d["c_" + k] = v
    return d


def kernel(**inp):
    inp = {k: np.asarray(v) for k, v in inp.items()}
    consts = _const_specs()
    cache2d = np.ascontiguousarray(inp["cache_kv"][0]).reshape(-1, 512)
    nc = build_nc(cache2d.shape[0])
    in_maps = [_core_inputs(c, inp, consts, cache2d) for c in range(8)]
    res = run_bass_kernel_spmd(nc, in_maps, core_ids=list(range(8))).results
    cat = lambda k: np.concatenate([r[k] for r in res], axis=0)
    y_prompt = cat("yp")
    y_sample = cat("ys").reshape(32, 4, 1024)
    kv_prompt = cat("kvp").reshape(1, 16, T, 4, 2, 64)
    kv_sample = cat("kvs").reshape(1, 32, 4, 4, 2, 64)
    win_prompt = cat("winp").reshape(1, 16, 512, 2, 2, 64)
    win_sample = cat("wins").reshape(1, 32, 512, 2, 2, 64)
    wkv_prompt = cat("wkvp").reshape(1, 16, 8, 64, 64)
    wkv_sample = cat("wkvs").reshape(1, 32, 8, 64, 64)
    shift_prompt = cat("shp").reshape(1, 16, 1664)
    shift_sample = cat("shs").reshape(1, 32, 1664)
    return (y_prompt, y_sample, kv_prompt, kv_sample, win_prompt, win_sample, wkv_prompt, wkv_sample,
            shift_prompt, shift_sample)
```

```python
import contextlib
import numpy as np
import ml_dtypes
import concourse.bass as bass
import concourse.mybir as mybir
from concourse.bass_utils import run_bass_kernel_spmd

F32 = mybir.dt.float32
BF16 = mybir.dt.bfloat16
I32 = mybir.dt.int32
F32R = mybir.dt.float32r
AF = mybir.ActivationFunctionType
ALU = mybir.AluOpType
AX = mybir.AxisListType

ENGS = ("pe", "act", "dve", "pool", "sp")
NDMASEM = 8
BIG = 30000.0
T = 2048
TB = 256
NTT = TB // 128
NB = T // TB
DW = 3992
EPS = 1e-6
GN_EPS = 64e-5
DEC_C = 0.6065306597126334


class _Stop(Exception):
    pass


import os
KSTOP = float(os.environ.get("KSTOP", "999"))


KSKIP = [int(os.environ.get("KSKIP", "0"))]


def chk(stage):
    if stage >= KSTOP:
        if KSKIP[0] > 0 and stage == KSTOP:
            KSKIP[0] -= 1
            return
        raise _Stop()


class Prog:
    def __init__(self, nc):
        self.nc = nc
        self.ops = []

    def op(self, eng, fn, reads=(), writes=()):
        import sys
        f = sys._getframe(2)
        self.ops.append(dict(eng=eng, fn=fn, reads=tuple(reads), writes=tuple(writes), dma=False, line=f.f_lineno))

    def dma(self, eng, out, in_, reads=(), writes=(), **kw):
        self.ops.append(dict(eng=eng, fn=lambda e: e.dma_start(out=out, in_=in_, **kw),
                             reads=tuple(reads), writes=tuple(writes), dma=True))

    def dma_fn(self, eng, fn, reads=(), writes=()):
        self.ops.append(dict(eng=eng, fn=fn, reads=tuple(reads), writes=tuple(writes), dma=True))

    def build(self, sems, block):
        ops = self.ops
        n = len(ops)
        last_w, readers = {}, {}
        deps = [None] * n
        needed = [False] * n
        for i, o in enumerate(ops):
            d = set()
            for r in o["reads"]:
                if r in last_w:
                    d.add(last_w[r])
            for w in o["writes"]:
                if w in last_w:
                    d.add(last_w[w])
                d.update(readers.get(w, ()))
            d.discard(i)
            d = {j for j in d if not (ops[j]["eng"] == "pe" and o["eng"] == "pe"
                                      and not ops[j]["dma"] and not o["dma"])}
            deps[i] = d
            for j in d:
                needed[j] = True
            for w in o["writes"]:
                last_w[w] = i
                readers[w] = []
            for r in o["reads"]:
                readers.setdefault(r, []).append(i)
        cnt = {e: 0 for e in ENGS}
        dcnt = {e: 0 for e in ENGS}
        ev = [None] * n
        prevdma = [None] * n
        for i, o in enumerate(ops):
            e = o["eng"]
            if o["dma"]:
                k = dcnt[e]
                dcnt[e] += 1
                s = k % NDMASEM
                ev[i] = ((e, s), 16 * (k // NDMASEM + 1))
                if k >= NDMASEM:
                    prevdma[i] = ((e, s), 16 * (k // NDMASEM))
            elif needed[i]:
                cnt[e] += 1
                ev[i] = (e, cnt[e])
        per_eng = {e: [i for i, o in enumerate(ops) if o["eng"] == e] for e in ENGS}
        final_dma = {}
        for i, o in enumerate(ops):
            if o["dma"]:
                final_dma[ev[i][0]] = max(final_dma.get(ev[i][0], 0), ev[i][1])

        def emit(e, eng, idxs, last):
            waited = {}
            for i in idxs:
                o = ops[i]
                want = {}
                for j in deps[i]:
                    sk, v = ev[j]
                    want[sk] = max(want.get(sk, 0), v)
                if prevdma[i] is not None:
                    sk, v = prevdma[i]
                    want[sk] = max(want.get(sk, 0), v)
                for sk, v in want.items():
                    if waited.get(sk, 0) < v:
                        eng.wait_ge(sems[sk], v)
                        waited[sk] = v
                if os.environ.get("KTRACE") and i >= n - 60:
                    print("OP", i, e, "line", o.get("line"), "R", o["reads"], "W", o["writes"], "waits", want, "ev", ev[i], flush=True)
                ins = o["fn"](eng)
                if o["dma"]:
                    ins.then_inc(sems[ev[i][0]], 16)
                elif ev[i] is not None:
                    ins.then_inc(sems[e], 1)
            if last:
                for sk, v in final_dma.items():
                    if waited.get(sk, 0) < v:
                        eng.wait_ge(sems[sk], v)

        block.sync(lambda eng: emit("sp", eng, per_eng["sp"], True))
        block.tensor(lambda eng: emit("pe", eng, per_eng["pe"], False))
        block.scalar(lambda eng: emit("act", eng, per_eng["act"], False))
        block.vector(lambda eng: emit("dve", eng, per_eng["dve"], False))
        block.gpsimd(lambda eng: emit("pool", eng, per_eng["pool"], False))


def _consts():
    bf = ml_dtypes.bfloat16
    c = {}
    p = np.arange(128)[:, None]
    q = np.arange(128)[None, :]
    c["identf"] = np.eye(128, dtype=np.float32)
    c["identb"] = np.eye(128, dtype=np.float32).astype(bf)
    su = (p < q).astype(np.float32)
    ui = (p <= q).astype(np.float32)
    c["mask4"] = np.concatenate([su, ui, su, ui], axis=1)
    c["maskL"] = np.concatenate([(p > q).astype(np.float32)] * 2, axis=1)
    tq = np.arange(TB)[None, :]
    causb = np.zeros((128, NTT, TB), np.float32)
    for r in range(NTT):
        causb[:, r, :] = np.where(p + 128 * r > tq, -BIG, 0.0)
    c["causb"] = causb.astype(bf)
    winlow = np.zeros((128, 2, TB), np.float32)
    winlow[:, 0, :] = np.where(p <= tq, -BIG, 0.0)
    winlow[:, 1, :] = np.where(p <= tq - 128, -BIG, 0.0)
    c["winlow"] = winlow.astype(bf)
    tt = np.arange(T)[None, :]
    c["cmpbias"] = np.where((16 * p + 31 > tt) | (p >= 127), -BIG, 0.0).astype(bf)
    s32 = np.arange(32)[:, None]
    c["zp"] = (tt // 64 == s32).astype(np.float32).astype(bf)
    addc = np.zeros((128, 16, 32), np.float32)
    for t16 in range(16):
        t = 128 * t16 + np.arange(128)[:, None]
        cur = t // 64
        s = np.arange(32)[None, :]
        valid = s <= cur
        forced = (s == 0) | (s == cur) | (s == cur - 1)
        addc[:, t16, :] = np.where(valid, np.where(forced, 1e4, 0.0), -1e4)
    c["addc"] = addc
    j = np.arange(128)[:, None]
    s = np.arange(32)[None, :]
    ov = ((16 * j < 64 * s + 64) & (16 * j + 32 > 64 * s) & (j < 127)).astype(np.float32)
    c["ov32"] = ov
    col = np.arange(248)[None, :]
    c["zs"] = (col - 120 == p // 16).astype(np.float32).astype(bf)
    c["resetm"] = np.tile((np.arange(TB)[None, :] % 128 != 0).astype(np.float32), (128, 1))
    c["bd64"] = ((p // 64) == (q // 64)).astype(np.float32)
    c["hsel"] = np.stack([(np.arange(128) < 64), (np.arange(128) >= 64)], axis=1).astype(np.float32)
    c["zeros"] = np.zeros((128, 512), np.float32).astype(bf)
    p4 = np.arange(4)[:, None]
    q4 = np.arange(4)[None, :]
    su4 = (p4 < q4).astype(np.float32)
    ui4 = (p4 <= q4).astype(np.float32)
    c["mask4s"] = np.concatenate([su4, ui4, su4, ui4], axis=1)
    c["maskLs"] = np.concatenate([(p4 > q4).astype(np.float32)] * 2, axis=1)
    jj = np.arange(128)[:, None]
    s33 = np.arange(33)[None, :]
    c["ovc"] = ((4 * s33 - 1 <= jj) & (jj <= 4 * s33 + 3)).astype(np.float32).astype(bf)
    lastb = np.zeros((128, 16), np.float32)
    lastb[127, :] = -BIG
    c["lastb"] = lastb.astype(bf)
    gsel = np.zeros((16, 4), np.float32)
    for g in range(4):
        for t in range(4):
            gsel[g * 4 + t, t] = 1.0
    c["gsel"] = gsel
    addcs = np.zeros((4, 257), np.float32)
    addcs[:, [0, 255, 256]] = 1e4
    c["addcs"] = addcs
    hl = np.zeros((1, 2, 128), np.float32)
    hl[0, 0, :64] = 1.0
    hl[0, 1, 64:] = 1.0
    c["hl"] = hl.astype(bf)
    tq16 = np.tile(np.arange(4), 4)[None, :]
    c["tailb"] = np.where(np.arange(4)[:, None] > tq16, -BIG, 0.0).astype(np.float32).astype(bf)
    winms = np.ones((128, 4, 2, 4), np.float32)
    winms[:, 0, :, :] = (np.arange(128)[:, None, None] > np.arange(4)[None, None, :]).astype(np.float32)
    c["winms"] = winms.reshape(128, 32).astype(bf)
    c["iota"] = np.stack([2.0 * np.arange(128), 2.0 * np.arange(128) + 1.0], axis=1).astype(np.float32)
    return c


CONST_SPECS = None


def _const_specs():
    global CONST_SPECS
    if CONST_SPECS is None:
        CONST_SPECS = _consts()
    return CONST_SPECS


def _groups():
    g = []
    g.append(("T0", [(1664, 512)], False))
    g.append(("T1", [(2688, 512)], False))
    g.append(("T2", [(3200, 512)], False))
    g.append(("T3", [(3712, 280)], False))
    g.append(("F0", [(1536, 128)], False))
    for fc in range(4):
        g.append((f"R{fc}", [(128 * fc, 128), (512 + 128 * fc, 128), (1024 + 128 * fc, 128)], False))
    g.append(("Q", [(2176, 512)], True))
    g.append(("K", [(3456, 128), (3712, 128)], False))
    return g


GROUPS = _groups()
GIDX = {g[0]: i for i, g in enumerate(GROUPS)}


def build_nc(n_cache_rows, with_sample=True):
    nc = bass.Bass("TRN2", target_bir_lowering=False)
    C = _const_specs()
    dt_of = lambda a: BF16 if a.dtype == ml_dtypes.bfloat16 else F32

    def din(name, shape, dt=F32):
        return nc.dram_tensor(name, list(shape), dt, kind="ExternalInput").ap()

    def dout(name, shape, dt=F32):
        return nc.dram_tensor(name, list(shape), dt, kind="ExternalOutput").ap()

    xp = din("xp", [2, T, 1024])
    w_in = din("w_in", [1024, DW])
    w_out = din("w_out", [1024, 1024])
    norm_g = din("norm_g", [1024])
    final_g = din("final_g", [1024])
    mu = din("mu", [1664])
    vecs = {k: din(k, [512]) for k in ("w0", "a0", "k_k", "k_a", "r_k", "gn_w", "gn_b")}
    w_dup = din("w_dup", [64, 512])
    w_aup = din("w_aup", [64, 512])
    w_cpos = din("w_cpos", [2, 32, 64])
    w_cmix = din("w_cmix", [2, 64, 64])
    cd = {k: din("c_" + k, v.shape, dt_of(v)) for k, v in C.items()}

    yp = dout("yp", [2, T, 1024])
    kvp = dout("kvp", [2, T, 512])
    winp = dout("winp", [2, 512, 256])
    wkvp = dout("wkvp", [2, 8, 64, 64])
    shp = dout("shp", [2, 1664])

    xs = din("xs", [16, 1024])
    cache = din("cache", [n_cache_rows, 512])
    cwin = din("cwin", [4, 512, 256])
    swkv = din("swkv", [4, 8, 64, 64])
    sshift = din("sshift", [4, 1664])
    pt = din("pt", [4, 128], I32)
    ys = dout("ys", [16, 1024])
    kvs = dout("kvs", [16, 512])
    wins = dout("wins", [4, 512, 256])
    wkvs = dout("wkvs", [4, 8, 64, 64])
    shs = dout("shs", [4, 1664])

    wscr = nc.dram_tensor("wscr", [len(GROUPS), 128, 8 * 512], BF16, kind="Internal").ap()

    P = Prog(nc)
    with contextlib.ExitStack() as st:
        def sb(name, shape, dt=F32):
            return st.enter_context(nc.sbuf_tensor(name, list(shape), dt))

        def pst(name, shape, dt=F32):
            return st.enter_context(nc.psum_tensor(name, list(shape), dt))

        def mm(out, lhsT, rhs, start, stop, R, W):
            P.op("pe", lambda e: e.matmul(out, lhsT=lhsT, rhs=rhs, start=start, stop=stop,
                                          skip_group_check=True), R, W)

        def tr(out, in_, ident, R, W):
            P.op("pe", lambda e: e.transpose(out, in_, ident), R, W)

        def act(out, in_, func, R, W, eng="act", **kw):
            P.op("act", lambda e: e.activation(out=out, in_=in_, func=func, **kw), R, W)

        def vop(eng, name, R, W, **kw):
            P.op(eng, lambda e: getattr(e, name)(**kw), R, W)

        def cp(eng, out, in_, R, W):
            if eng == "act":
                P.op("act", lambda e: e.copy(out=out, in_=in_), R, W)
            else:
                P.op(eng, lambda e: e.tensor_copy(out=out, in_=in_), R, W)

        def ld(out, in_, W, R=(), eng="sp", **kw):
            P.dma(eng, out, in_, reads=R, writes=W, **kw)

        def stq(out, in_, R, eng="pool"):
            P.dma(eng, out, in_, reads=R, writes=())

        rr = {"n": 0}

        def evac_eng():
            rr["n"] += 1
            return "act" if rr["n"] % 2 else "dve"

        ct = {}
        for k, v in C.items():
            ct[k] = sb("k_" + k, v.shape, dt_of(v))
            ld(ct[k][:], cd[k], ["k_" + k])
        identf, identb = ct["identf"], ct["identb"]

        yout = sb("yout", [128, 1024])
        tmpf = [sb(f"tmpf{i}", [128, TB]) for i in range(8)]
        stw_t = [sb(f"stw{i}", [128, 512]) for i in range(1)]
        stw = [stw_t[0], stw_t[0]]
        wup_f = stw_t[0]
        g8 = sb("g8", [128, 8])
        gq8 = sb("gq8", [128, 8])
        mu13 = sb("mu13", [128, 13])
        pv = {k: sb("pv_" + k, [128, 4]) for k in ("w0", "a0", "k_k", "k_a", "r_k")}
        omka = sb("omka", [128, 4])
        gnw = sb("gnw", [128, 512])
        gnb = sb("gnb", [128, 512])
        fgb = sb("fgb", [128, 1024])
        wup = sb("wup", [128, 2, 512], BF16)
        wt = sb("wt", [128, 2, 256])
        wmix_f = sb("wmix_f", [128, 2, 64])
        wmix = sb("wmix", [128, 2, 2, 64], BF16)
        xres = sb("xres", [128, 1024])
        wout_f = xres
        wout = sb("wout", [128, 8, 1024], BF16)
        with nc.allow_non_contiguous_dma(reason="small per-feature vectors"):
            ld(g8[:], norm_g.rearrange("(kt p) -> p kt", p=128), ["g8"], allow_slow_non_contiguous=True)
            ld(mu13[:], mu.rearrange("(c p) -> p c", p=128), ["mu13"], allow_slow_non_contiguous=True)
            for k in pv:
                ld(pv[k][:], vecs[k].rearrange("(c p) -> p c", p=128), ["pv_" + k], allow_slow_non_contiguous=True)
        ld(gnw[:], vecs["gn_w"].rearrange("(o f) -> o f", o=1).to_broadcast([128, 512]), ["gnw"])
        ld(gnb[:], vecs["gn_b"].rearrange("(o f) -> o f", o=1).to_broadcast([128, 512]), ["gnb"])
        ld(fgb[:], final_g.rearrange("(o f) -> o f", o=1).to_broadcast([128, 1024]), ["fgb"])
        ld(wup_f[0:64, :], w_dup, ["stw0"])
        ld(wup_f[64:128, :], w_aup, ["stw0"])
        vop("pool", "memset", [], ["wup"], ap=wup[:], constant=0.0)
        cp("dve", wup[0:64, 0, :], wup_f[0:64, :], ["stw0"], ["wup"])
        cp("dve", wup[64:128, 1, :], wup_f[64:128, :], ["stw0"], ["wup"])
        vop("dve", "tensor_scalar", ["g8"], ["gq8"], out=gq8[:], in0=g8[:], scalar1=0.125, scalar2=None, op0=ALU.mult)
        vop("dve", "tensor_scalar", ["pv_k_a"], ["omka"], out=omka[:], in0=pv["k_a"][:], scalar1=-1.0, scalar2=1.0,
            op0=ALU.mult, op1=ALU.add)
        with nc.allow_non_contiguous_dma(reason="compress weights"):
            for m in range(2):
                for e in range(2):
                    for k in range(2):
                        for blk in range(8):
                            ld(wt[16 * blk:16 * blk + 16, m, e * 128 + k * 64: e * 128 + k * 64 + 64],
                               w_cpos[e, 16 * m:16 * m + 16, :], ["wt"])
            for e in range(2):
                ld(wmix_f[0:64, e, :], w_cmix[e], ["wmix_f"])
                ld(wmix_f[64:128, e, :], w_cmix[e], ["wmix_f"])
        vop("pool", "memset", [], ["wmix"], ap=wmix[:], constant=0.0)
        cp("dve", wmix[0:64, 0, :, :], wmix_f[0:64, :, :], ["wmix_f"], ["wmix"])
        cp("dve", wmix[64:128, 1, :, :], wmix_f[64:128, :, :], ["wmix_f"], ["wmix"])
        for kt in range(8):
            ld(wout_f[:], w_out[kt * 128:(kt + 1) * 128, :], ["xres"])
            cp("act" if kt % 2 else "dve", wout[:, kt, :], wout_f[:], ["xres"], ["wout"])

        wbuf = [sb(f"wbuf{i}", [128, 8, 512], BF16) for i in range(2)]
        stg = [(stw_t[0][:, 0:512], "stw0"), (yout[:, 0:512], "yout"), (yout[:, 512:1024], "youtb")]
        sidx = 0
        for gi, (gname, segs, qs) in enumerate(GROUPS):
            sbuf = wbuf[gi % 2]
            key = f"wbuf{gi % 2}"
            for kt in range(8):
                s_, skey = stg[sidx % 3]
                sidx += 1
                off = 0
                for (c0, ncol) in segs:
                    ld(s_[:, off:off + ncol], w_in[kt * 128:(kt + 1) * 128, c0:c0 + ncol], [skey])
                    off += ncol
                gsrc = gq8 if qs else g8
                if sidx % 2:
                    act(sbuf[:, kt, 0:off], s_[:, 0:off], AF.Copy, [skey, "g8", "gq8"], [key], scale=gsrc[:, kt:kt + 1])
                else:
                    vop("dve", "tensor_scalar", [skey, "g8", "gq8"], [key], out=sbuf[:, kt, 0:off], in0=s_[:, 0:off],
                        scalar1=gsrc[:, kt:kt + 1], scalar2=None, op0=ALU.mult)
            P.dma("sp", wscr[gi].rearrange("p (k c) -> p k c", k=8)[:, :, 0:off], sbuf[:, :, 0:off],
                  reads=[key], writes=["wscr"])

        xt = [sb(f"xt{i}", [128, 1024]) for i in range(2)]
        ss = sb("ss", [128, 1])
        rstd = sb("rstd", [128, 1])
        xn = [sb(f"xn{i}", [128, 1024], BF16) for i in range(2)]
        xnT = sb("xnT", [128, 8, TB], BF16)
        gsr = sb("gsr", [128, NTT, 512], BF16)
        gsn = sb("gsn", [128, NTT, 512], BF16)
        kv0 = [sb(f"kv0_{i}", [128, 512]) for i in range(2)]
        kv1 = [sb(f"kv1_{i}", [128, 280]) for i in range(2)]
        gates = sb("gates", [128, NTT, 24])
        ym = [sb(f"ym{i}", [128, 256], BF16) for i in range(2)]
        QT = sb("QT", [64, 8, TB], BF16)
        KTs = sb("KTs", [64, 2, T], BF16)
        KTw = sb("KTw", [64, 2, T], BF16)
        Vs = sb("Vs", [128, 16, 2, 65], BF16)
        Vw = sb("Vw", [128, 16, 2, 65], BF16)
        pooled = sb("pooled", [128, 2, 128])
        pooledb = sb("pooledb", [128, 2, 128], BF16)
        kcT = sb("kcT", [64, 2, 128], BF16)
        vcx = sb("vcx", [128, 2, 97], BF16)
        zsb12 = sb("zsb12", [128, TB + 1])
        zsb = sb("zsb", [128, 3, TB + 1])
        zs12 = sb("zs12", [128, TB])
        zs = sb("zs", [128, 3, TB])
        lora = sb("lora", [128, TB], BF16)
        AR = sb("AR", [128, NTT, 2, 128])
        BK = sb("BK", [128, 2, TB])
        Wc = sb("Wc", [128, NTT * 8])
        prod = sb("prod", [128, TB])
        STz = sb("STz", [128, 4, 2, 64])
        BKz = sb("BKz", [128, 2, 2, TB])
        wkv2 = sb("wkv2", [128, 64])
        Ssb = sb("Ssb", [128, 128])
        tok = sb("tok", [128, 3, 2, 64])
        Mt = [sb(f"Mt{i}", [128, 512]) for i in range(2)]
        PT0 = sb("PT0", [128, 2, 128])
        Pk = [sb(f"Pk{i}", [128, 2, 2, 128]) for i in range(6)]
        Xa = [sb(f"Xa{i}", [128, 2, 64]) for i in range(2)]
        ytok = sb("ytok", [128, NTT, 2, 64])
        vtok = sb("vtok", [128, NTT, 2, 64])
        rk = sb("rk", [128, NTT, 2])
        st4 = sb("st4", [128, 4])
        gtmp = [sb(f"gtmp{i}", [128, 2, 64]) for i in range(3)]
        mix = sb("mix", [128, NTT, 1024], BF16)
        mixT = sb("mixT", [128, 8, 128], BF16)
        sq_junk = mixT[:].rearrange("p k t -> p (k t)")
        PTb = [sb(f"PTb{i}", [128, TB], BF16) for i in range(3)]
        oacc = sb("oacc", [128, NTT, 8, 64])
        otmp = sb("otmp", [128, NTT, 64])
        rden = sb("rden", [128, NTT])
        scg = sb("scg", [128, NTT])
        sc = sb("sc", [128, NTT, 2, 32])
        scw = sb("scw", [128, 32])
        m8a = sb("m8a", [128, 8])
        m8b = sb("m8b", [128, 8])
        thr = sb("thr", [128, 1])
        selb = sb("selb", [128, 32], BF16)
        selbT = sb("selbT", [32, 2, TB], BF16)
        zc = sb("zc", [128, 13])

        Vn = sb("Vn", [4, 2, 2, 65], BF16)
        QTs = sb("QTs", [64, 8, 4], BF16)
        QTz = sb("QTz", [128, 2, 16], BF16)
        KTn = sb("KTn", [64, 2, 2, 4], BF16)
        ptb = sb("ptb", [128, 128], I32)
        ptf = sb("ptf", [128, 128])
        pidx = sb("pidx", [128, 2, 128], I32)
        cache_h = cache.rearrange("n (h c) -> (n h) c", h=2)
        pg4 = [xres[:].rearrange("p (g c) -> p g c", g=4), yout[:].rearrange("p (g c) -> p g c", g=4)]
        pall = Vs[:].rearrange("p a k c -> p (a k c)")[:, 0:2048].rearrange("p (c j) -> p c j", c=2)
        kcTs = Vw[0:64, :, :, :].rearrange("p a k c -> p (a k c)")[:, 0:2048].rearrange("p (c j) -> p c j", c=2)
        vcs = ct["cmpbias"][:, 0:1040].rearrange("p (j k c) -> p j k c", j=8, k=2)
        selflat4 = ct["zp"][0:1, 0:2048].rearrange("o (h g k t) -> o h g k t", h=2, g=128, k=2)
        ocs1 = stw_t[0][0:16, 0:322]
        obr = sb("obr", [16, 3, 2, 64])
        scs = kv1[1][0:4, 0:257]
        scs2 = kv0[1][0:4, 0:257]
        KTpg = sb("KTpg", [128, 4, 128], BF16)
        Vpg = sb("Vpg", [128, 4, 2, 65], BF16)
        mexp = sb("mexp", [128, 32], BF16)

        psA = pst("psA", [128, 512])
        psB = pst("psB", [128, 512])
        psT = pst("psT", [128, 1024], BF16)
        psP = pst("psP", [128, 512])
        psR = [pst(f"psR{i}", [128, 512]) for i in range(4)]

        sems = {}
        for e in ENGS:
            sems[e] = st.enter_context(nc.semaphore("s_" + e))
            for i in range(NDMASEM):
                sems[(e, i)] = st.enter_context(nc.semaphore(f"d_{e}_{i}"))
        block = st.enter_context(nc.Block())

        wb_n = {"n": 0}

        def load_group(gname):
            gi = GIDX[gname]
            i = wb_n["n"] % 2
            wb_n["n"] += 1
            ld(wbuf[i][:], wscr[gi].rearrange("p (k c) -> p k c", k=8), [f"wbuf{i}"], R=["wscr"])
            return wbuf[i], f"wbuf{i}"

        vop("pool", "memset", [], ["Vs"], ap=Vs[:], constant=1.0)
        vop("pool", "memset", [], ["Vw"], ap=Vw[:], constant=1.0)

        identr_t = sb("identr", [128, 128])
        identr = identr_t[:].bitcast(F32R)
        cp("dve", identr, identf[:], ["k_identf"], ["identr"])
        psAB = [(psA, "psA"), (psB, "psB")]
        pn = {"n": 0}

        def nextps():
            pn["n"] += 1
            return psAB[pn["n"] % 2]

        R0, R1, R2, R3 = psR
        sps = [(R1, "psR1"), (R2, "psR2")]
        sn = {"n": 0}

        def rmsnorm_T(xsrc, npart, tt):
            x_ = xt[tt % 2]
            xk = f"xt{tt % 2}"
            pp = slice(0, npart)
            ld(x_[pp, :], xsrc, [xk])
            act(sq_junk[pp, :], x_[pp, :], AF.Square, [xk], ["mixT", "ss"], accum_out=ss[pp, :])
            act(rstd[pp, :], ss[pp, :], AF.Sqrt, ["ss"], ["rstd"], scale=1.0 / 1024, bias=EPS)
            vop("dve", "reciprocal", ["rstd"], ["rstd"], out=rstd[pp, :], in_=rstd[pp, :])
            xn_ = xn[tt % 2]
            xnk = f"xn{tt % 2}"
            vop("dve", "tensor_scalar", [xk, "rstd"], [xnk], out=xn_[pp, :], in0=x_[pp, :], scalar1=rstd[pp, 0:1],
                scalar2=None, op0=ALU.mult)
            for kt in range(8):
                tr(psT[:, kt * 128:kt * 128 + npart], xn_[pp, kt * 128:(kt + 1) * 128], identb[pp, pp],
                   [xnk, "k_identb"], ["psT"])
            cp("act", xnT[:, :, tt * 128:tt * 128 + npart], psT[:].rearrange("p (k t) -> p k t", k=8)[:, :, 0:npart],
               ["psT"], ["xnT"])

        def rwkv_block(n, C, nlev, npart_of_chunk, gate_cols):
            nch = n // C
            use_r = (C == 128)
            Rm = (lambda ap: ap.bitcast(F32R)) if use_r else (lambda ap: ap)
            Ro = lambda ap: ap.bitcast(F32R)
            idm = identr if use_r else identf
            idk = "identr" if use_r else "k_identf"
            wb, wk = load_group("F0")
            ps_, pk = nextps()
            for kt in range(8):
                mm(ps_[:, 0:n], wb[:, kt, 0:128], xnT[:, kt, 0:n], kt == 0, kt == 7, ["xnT", wk], [pk])
            cp("dve", zsb12[:, 0:1], zc[:, 12:13], ["zc"], ["zsb12"])
            cp("act", zsb12[:, 1:n + 1], ps_[:, 0:n], [pk], ["zsb12"])
            cp("dve", zc[:, 12:13], zsb12[:, n:n + 1], ["zsb12"], ["zc"])
            vop("dve", "tensor_tensor", ["zsb12"], ["tmpf5"], out=tmpf[5][:, 0:n], in0=zsb12[:, 0:n],
                in1=zsb12[:, 1:n + 1], op=ALU.subtract)
            vop("dve", "scalar_tensor_tensor", ["tmpf5", "mu13", "zsb12"], ["zs12"], out=zs12[:, 0:n], in0=tmpf[5][:, 0:n],
                scalar=mu13[:, 12:13], in1=zsb12[:, 1:n + 1], op0=ALU.mult, op1=ALU.add)
            act(lora[0:64, 0:n], zs12[0:64, 0:n], AF.Tanh, ["zs12"], ["lora"])
            cp("dve", lora[64:128, 0:n], zs12[64:128, 0:n], ["zs12"], ["lora"])
            m4 = ct["mask4"] if C == 128 else ct["mask4s"]
            mL = ct["maskL"] if C == 128 else ct["maskLs"]
            m4k = "k_mask4" if C == 128 else "k_mask4s"
            mLk = "k_maskL" if C == 128 else "k_maskLs"
            cpp = slice(0, C)
            for fc in range(4):
                wb, wk = load_group(f"R{fc}")
                for j3 in range(3):
                    ps_, pk = nextps()
                    cidx = j3 * 4 + fc
                    for kt in range(8):
                        mm(ps_[:, 0:n], wb[:, kt, j3 * 128:(j3 + 1) * 128], xnT[:, kt, 0:n], kt == 0, kt == 7,
                           ["xnT", wk], [pk])
                    cp("dve", zsb[:, j3, 0:1], zc[:, cidx:cidx + 1], ["zc"], ["zsb"])
                    cp("act", zsb[:, j3, 1:n + 1], ps_[:, 0:n], [pk], ["zsb"])
                    cp("dve", zc[:, cidx:cidx + 1], zsb[:, j3, n:n + 1], ["zsb"], ["zc"])
                    vop("dve", "tensor_tensor", ["zsb"], [f"tmpf{5 + j3}"], out=tmpf[5 + j3][:, 0:n], in0=zsb[:, j3, 0:n],
                        in1=zsb[:, j3, 1:n + 1], op=ALU.subtract)
                    vop("dve", "scalar_tensor_tensor", [f"tmpf{5 + j3}", "mu13", "zsb"], ["zs"], out=zs[:, j3, 0:n],
                        in0=tmpf[5 + j3][:, 0:n], scalar=mu13[:, cidx:cidx + 1], in1=zsb[:, j3, 1:n + 1],
                        op0=ALU.mult, op1=ALU.add)
                r_, k_, v_ = zs[:, 0, 0:n], zs[:, 1, 0:n], zs[:, 2, 0:n]
                sg, al, lw, cum, t4, t5, t6, t7 = [t[:, 0:n] for t in tmpf]
                K = lambda *i: [f"tmpf{j}" for j in i]
                ps_, pk = nextps()
                mm(ps_[:, 0:n], wup[:, 0, fc * 128:(fc + 1) * 128], lora[:, 0:n], True, True, ["wup", "lora"], [pk])
                act(sg, ps_[:, 0:n], AF.Sigmoid, [pk, "pv_w0"], K(0), bias=pv["w0"][:, fc:fc + 1])
                vop("dve", "tensor_scalar", K(0), K(2), out=lw, in0=sg, scalar1=-DEC_C, scalar2=None, op0=ALU.mult)
                ps_, pk = nextps()
                mm(ps_[:, 0:n], wup[:, 1, fc * 128:(fc + 1) * 128], lora[:, 0:n], True, True, ["wup", "lora"], [pk])
                act(al, ps_[:, 0:n], AF.Sigmoid, [pk, "pv_a0"], K(1), bias=pv["a0"][:, fc:fc + 1])
                vop("dve", "tensor_scalar", ["zs", "pv_k_k"], K(4), out=t4, in0=k_, scalar1=pv["k_k"][:, fc:fc + 1],
                    scalar2=None, op0=ALU.mult)
                vop("dve", "tensor_tensor", K(4), K(5), out=t5, in0=t4, in1=t4, op=ALU.mult)
                ps_, pk = nextps()
                mm(ps_[:, 0:n], ct["bd64"][:], t5, True, True, ["k_bd64"] + K(5), [pk])
                vop("dve", "tensor_scalar", [pk], K(5), out=t5, in0=ps_[:, 0:n], scalar1=1e-24, scalar2=None, op0=ALU.max)
                act(t5, t5, AF.Sqrt, K(5), K(5))
                vop("dve", "reciprocal", K(5), K(5), out=t5, in_=t5)
                vop("dve", "tensor_tensor", K(4, 5), K(4), out=t4, in0=t4, in1=t5, op=ALU.mult)
                vop("dve", "tensor_scalar", K(1) + ["pv_k_a", "omka"], K(5), out=t5, in0=al,
                    scalar1=pv["k_a"][:, fc:fc + 1], scalar2=omka[:, fc:fc + 1], op0=ALU.mult, op1=ALU.add)
                vop("dve", "tensor_tensor", ["zs"] + K(5), K(5), out=t5, in0=k_, in1=t5, op=ALU.mult)
                vop("dve", "tensor_tensor_scan", ["k_resetm"] + K(2), K(3), out=cum, data0=ct["resetm"][:, 0:n] if C == 128 else ct["resetm"][:, 0:n],
                    data1=lw, initial=0.0, op0=ALU.mult, op1=ALU.add)
                act(t6, cum, AF.Exp, K(3), K(6))
                for ch in range(nch):
                    cs = slice(ch * C, (ch + 1) * C)
                    cp("dve", Wc[:, 8 * ch:8 * ch + 1], t6[:, ch * C + C - 1: ch * C + C], K(6), ["Wc"])
                    vop("dve", "tensor_tensor", ["zs"] + K(6), ["AR"], out=Ro(AR[:, ch, 1, 0:C]), in0=r_[:, cs], in1=t6[:, cs], op=ALU.mult)
                vop("dve", "scalar_tensor_tensor", ["zs", "pv_r_k"] + K(5), ["prod"], out=prod[:, 0:n], in0=r_,
                    scalar=pv["r_k"][:, fc:fc + 1], in1=t5, op0=ALU.mult, op1=ALU.mult)
                act(t7, cum, AF.Exp, K(3), K(7), scale=-1.0)
                vop("dve", "tensor_tensor", K(5, 7), ["BK"], out=Ro(BK[:, 1, 0:n]), in0=t5, in1=t7, op=ALU.mult)
                for hh in range(2):
                    rws = slice(hh * 64, hh * 64 + 64)
                    cp("act", Ro(BKz[rws, hh, 1, 0:n]), BK[rws, 1, 0:n], ["BK"], ["BKz"])
                vop("dve", "tensor_tensor", K(4, 1), K(5), out=t5, in0=t4, in1=al, op=ALU.mult)
                vop("dve", "tensor_tensor", K(5, 7), ["BK"], out=Ro(BK[:, 0, 0:n]), in0=t5, in1=t7, op=ALU.mult)
                for hh in range(2):
                    rws = slice(hh * 64, hh * 64 + 64)
                    cp("act", Ro(BKz[rws, hh, 0, 0:n]), BK[rws, 0, 0:n], ["BK"], ["BKz"])
                vop("dve", "tensor_tensor", K(3, 2), K(6), out=t6, in0=cum, in1=lw, op=ALU.subtract)
                act(t6, t6, AF.Exp, K(6), K(6))
                for ch in range(nch):
                    cs = slice(ch * C, (ch + 1) * C)
                    vop("dve", "scalar_tensor_tensor", K(4, 6), ["AR"], out=Ro(AR[:, ch, 0, 0:C]), in0=t4[:, cs], scalar=-1.0,
                        in1=t6[:, cs], op0=ALU.mult, op1=ALU.mult)
                chk(3)
                for ch in range(nch):
                    cs = slice(ch * C, (ch + 1) * C)
                    ARf = AR[:, ch, :, 0:C]
                    tr(R0[cpp, 0:128], BK[:, 0, cs], identf[:], ["BK", "k_identf"], ["psR0"])
                    tr(R0[cpp, 128:256], BK[:, 1, cs], identf[:], ["BK", "k_identf"], ["psR0"])
                    tr(R0[cpp, 256:384], zs[:, 2, cs], identf[:], ["zs", "k_identf"], ["psR0"])
                    cp("act", Ro(tok[cpp].rearrange("p a h d -> p (a h d)")), R0[cpp, 0:384], ["psR0"], ["tok"])
                    cp("dve", vtok[cpp, ch, :, :], tok[cpp, 2, :, :], ["tok"], ["vtok"])
                    for hh in range(2):
                        for a2 in range(2):
                            for a3 in range(2):
                                mm((R1, R2)[hh][cpp, (2 * a2 + a3) * C:(2 * a2 + a3 + 1) * C], Rm(BKz[:, hh, a2, cs]), Rm(AR[:, ch, a3, 0:C]), True, True,
                                   ["BKz", "AR"], [("psR1", "psR2")[hh]])
                        mm(R3[cpp, 256 + hh * C: 256 + (hh + 1) * C], Rm(AR[:, ch, 0, 0:C]), Rm(BKz[:, hh, 0, cs]), True, True,
                           ["AR", "BKz"], ["psR3b"])
                    for hh in range(2):
                        vop("dve", "tensor_tensor", [("psR1", "psR2")[hh], m4k], [f"Mt{hh}"], out=Ro(Mt[hh][cpp, 0:4 * C]), in0=(R1, R2)[hh][cpp, 0:4 * C],
                            in1=m4[cpp, 0:4 * C], op=ALU.mult)
                    for hh in range(0):
                        mm(R3[cpp, 256 + hh * C: 256 + (hh + 1) * C], Rm(AR[:, ch, 0, 0:C]), Rm(BKz[:, hh, 0, cs]), True, True,
                           ["AR", "BKz"], ["psR3b"])
                    vop("dve", "tensor_tensor", ["psR3b", mLk], ["PT0"], out=Ro(PT0[cpp, :, 0:C]),
                        in0=R3[cpp, 256:256 + 2 * C].rearrange("p (h t) -> p h t", h=2), in1=mL[cpp, 0:2 * C].rearrange("p (h t) -> p h t", h=2), op=ALU.mult)
                    for lv in range(nlev - 1):
                        for hh in range(2):
                            if lv == 0:
                                Pm, PTm, kk_ = Mt[hh][cpp, 0:C], PT0[cpp, hh, 0:C], [f"Mt{hh}", "PT0"]
                            else:
                                Pm, PTm, kk_ = Pk[lv - 1][cpp, hh, 0, 0:C], Pk[lv - 1][cpp, hh, 1, 0:C], [f"Pk{lv - 1}"]
                            mm(R2[cpp, (2 * hh) * C:(2 * hh + 1) * C], Rm(PTm), Rm(Pm), True, True, kk_, ["psR2"])
                            mm(R2[cpp, (2 * hh + 1) * C:(2 * hh + 2) * C], Rm(Pm), Rm(PTm), True, True, kk_, ["psR2"])
                        cp(evac_eng(), Ro(Pk[lv][cpp, :, :, 0:C]), R2[cpp, 0:4 * C].rearrange("p (h a t) -> p h a t", h=2, a=2), ["psR2"], [f"Pk{lv}"])
                    for hh in range(2):
                        mm(R3[cpp, hh * 64:(hh + 1) * 64], Rm(AR[:, ch, 0, 0:C]), Rm(STz[:, fc, hh, :]), True, False, ["AR", "STz"], ["psR3a"])
                        mm(R3[cpp, hh * 64:(hh + 1) * 64], Rm(Mt[hh][cpp, 2 * C:3 * C]), Rm(tok[cpp, 2, hh, :]), False, True,
                           [f"Mt{hh}", "tok"], ["psR3a"])
                    xi = 0
                    cp("act", Ro(Xa[0][cpp].rearrange("p h d -> p (h d)")), R3[cpp, 0:128], ["psR3a"], ["Xa0"])
                    for lv in range(nlev):
                        for hh in range(2):
                            if lv == 0:
                                Pm, kk_ = Mt[hh][cpp, 0:C], [f"Mt{hh}"]
                            else:
                                Pm, kk_ = Pk[lv - 1][cpp, hh, 0, 0:C], [f"Pk{lv - 1}"]
                            mm(R3[cpp, hh * 64:(hh + 1) * 64], Rm(Pm), Rm(Xa[xi][cpp, hh, :]), True, True, kk_ + [f"Xa{xi}"], ["psR3a"])
                        vop("dve", "tensor_tensor", ["psR3a", f"Xa{xi}"], [f"Xa{1 - xi}"], out=Ro(Xa[1 - xi][cpp].rearrange("p h d -> p (h d)")),
                            in0=R3[cpp, 0:128], in1=Xa[xi][cpp].rearrange("p h d -> p (h d)"), op=ALU.add)
                        xi = 1 - xi
                    E = Xa[xi]
                    ek = f"Xa{xi}"
                    for hh in range(2):
                        mm(R3[cpp, hh * 64:(hh + 1) * 64], Rm(AR[:, ch, 1, 0:C]), Rm(STz[:, fc, hh, :]), True, False, ["AR", "STz"], ["psR3a"])
                        mm(R3[cpp, hh * 64:(hh + 1) * 64], Rm(Mt[hh][cpp, C:2 * C]), Rm(E[cpp, hh, :]), False, False, [f"Mt{hh}", ek], ["psR3a"])
                        mm(R3[cpp, hh * 64:(hh + 1) * 64], Rm(Mt[hh][cpp, 3 * C:4 * C]), Rm(tok[cpp, 2, hh, :]), False, True,
                           [f"Mt{hh}", "tok"], ["psR3a"])
                    cp("act", ytok[cpp, ch, :, :].rearrange("p h d -> p (h d)"), R3[cpp, 0:128], ["psR3a"], ["ytok"])
                    SU = psB[:, 0:128]
                    mm(SU, idm[:], Rm(STz[:, fc, :, :].rearrange("p h d -> p (h d)")), True, False, [idk, "STz"], ["psB"])
                    mm(SU, Rm(tok[cpp, 0, :, :].rearrange("p h d -> p (h d)")), Rm(E[cpp].rearrange("p h d -> p (h d)")), False, False, ["tok", ek], ["psB"])
                    mm(SU, Rm(tok[cpp, 1, :, :].rearrange("p h d -> p (h d)")), Rm(tok[cpp, 2, :, :].rearrange("p h d -> p (h d)")), False, True, ["tok"], ["psB"])
                    cp("dve", Ssb[:], psB[:, 0:128], ["psB"], ["Ssb"])
                    for hh in range(2):
                        rows = slice(hh * 64, hh * 64 + 64)
                        act(Ro(STz[rows, fc, hh, :]), Ssb[rows, hh * 64:(hh + 1) * 64], AF.Copy, ["Ssb", "Wc"], ["STz"],
                            scale=Wc[rows, 8 * ch:8 * ch + 1])
                chk(4)
                for ch in range(nch):
                    cs = slice(ch * C, (ch + 1) * C)
                    mm(psB[cpp, 0:2], prod[:, cs], ct["hsel"][:], True, True, ["prod", "k_hsel"], ["psB"])
                    cp("dve", rk[cpp, ch, :], psB[cpp, 0:2], ["psB"], ["rk"])
                    y2 = ytok[cpp, ch, :, :]
                    g0, g1, g2 = [g[cpp] for g in gtmp]
                    s4 = st4[cpp]
                    vop("dve", "tensor_reduce", ["ytok"], ["st4"], out=s4[:, 0:2], in_=y2, axis=AX.X, op=ALU.add)
                    vop("dve", "tensor_tensor", ["ytok"], ["gtmp0"], out=g0, in0=y2, in1=y2, op=ALU.mult)
                    vop("dve", "tensor_reduce", ["gtmp0"], ["st4"], out=s4[:, 2:4], in_=g0, axis=AX.X, op=ALU.add)
                    vop("dve", "tensor_scalar", ["st4"], ["st4"], out=s4[:, 0:2], in0=s4[:, 0:2], scalar1=1.0 / 64, scalar2=None, op0=ALU.mult)
                    vop("dve", "tensor_tensor", ["st4"], ["gtmp1"], out=g1[:, :, 0], in0=s4[:, 0:2], in1=s4[:, 0:2], op=ALU.mult)
                    vop("dve", "scalar_tensor_tensor", ["st4", "gtmp1"], ["st4"], out=s4[:, 2:4], in0=s4[:, 2:4], scalar=1.0 / 64,
                        in1=g1[:, :, 0], op0=ALU.mult, op1=ALU.subtract)
                    act(s4[:, 2:4], s4[:, 2:4], AF.Sqrt, ["st4"], ["st4"], bias=GN_EPS)
                    vop("dve", "reciprocal", ["st4"], ["st4"], out=s4[:, 2:4], in_=s4[:, 2:4])
                    vop("dve", "tensor_tensor", ["ytok", "st4"], ["gtmp0"], out=g0, in0=y2,
                        in1=s4[:, 0:2].rearrange("p (a o) -> p a o", o=1).to_broadcast([C, 2, 64]), op=ALU.subtract)
                    vop("dve", "tensor_tensor", ["gtmp0", "st4"], ["gtmp0"], out=g0, in0=g0,
                        in1=s4[:, 2:4].rearrange("p (a o) -> p a o", o=1).to_broadcast([C, 2, 64]), op=ALU.mult)
                    gw = gnw[cpp, fc * 128:(fc + 1) * 128].rearrange("p (h d) -> p h d", h=2)
                    gb = gnb[cpp, fc * 128:(fc + 1) * 128].rearrange("p (h d) -> p h d", h=2)
                    vop("dve", "tensor_tensor", ["gtmp0", "gnw"], ["gtmp0"], out=g0, in0=g0, in1=gw, op=ALU.mult)
                    vop("dve", "tensor_tensor", ["gtmp0", "gnb"], ["gtmp0"], out=g0, in0=g0, in1=gb, op=ALU.add)
                    vop("dve", "tensor_tensor", ["vtok", "rk"], ["gtmp1"], out=g1, in0=vtok[cpp, ch, :, :],
                        in1=rk[cpp, ch, :].rearrange("p (a o) -> p a o", o=1).to_broadcast([C, 2, 64]), op=ALU.mult)
                    vop("dve", "tensor_tensor", ["gtmp0", "gtmp1"], ["gtmp0"], out=g0, in0=g0, in1=g1, op=ALU.add)
                    vop("dve", "tensor_tensor", ["gtmp0", "gsr"], ["mix"],
                        out=mix[cpp, ch, fc * 128:(fc + 1) * 128].rearrange("p (h d) -> p h d", h=2), in0=g0,
                        in1=gsr[cpp, ch, fc * 128:(fc + 1) * 128].rearrange("p (h d) -> p h d", h=2), op=ALU.mult)
                chk(4.5)

        def shift_and_state_out(shdst, wkvdst):
            for rnd in range(4):
                ncs = 4 if rnd < 3 else 1
                for ci in range(ncs):
                    c13 = rnd * 4 + ci
                    tr(R0[0:1, ci * 128:(ci + 1) * 128], zc[:, c13:c13 + 1], identf[:], ["zc", "k_identf"], ["psR0"])
                dst = xres if rnd < 2 else yout
                dk = "xres" if rnd < 2 else "yout"
                o_ = (rnd % 2) * 512
                cp("dve", dst[0:1, o_:o_ + ncs * 128], R0[0:1, 0:ncs * 128], ["psR0"], [dk])
            stq(shdst[:, 0:1024], xres[0:1, 0:1024], ["xres"])
            stq(shdst[:, 1024:1664], yout[0:1, 0:640], ["yout"])
            for fc in range(4):
                tr(R0[:, 0:128], STz[:, fc, :, :].rearrange("p h d -> p (h d)"), identf[:], ["STz", "k_identf"], ["psR0"])
                cp("dve", Ssb[:], R0[:, 0:128], ["psR0"], ["Ssb"])
                for hh in range(2):
                    rws = slice(hh * 64, hh * 64 + 64)
                    cp("act", wkv2[rws, :], Ssb[rws, hh * 64:(hh + 1) * 64], ["Ssb"], ["wkv2"])
                stq(wkvdst[2 * fc:2 * fc + 2, :, :].rearrange("h i j -> (h i) j"), wkv2[:], ["wkv2"])

        def proj_qk(n, kdst):
            wb, wk = load_group("Q")
            for h in range(8):
                ps_, pk = nextps()
                for kt in range(8):
                    mm(ps_[0:64, 0:n], wb[:, kt, h * 64:(h + 1) * 64], xnT[:, kt, 0:n], kt == 0, kt == 7, ["xnT", wk], [pk])
                cp(evac_eng(), QT[:, h, 0:n], ps_[0:64, 0:n], [pk], ["QT"])
            wb, wk = load_group("K")
            for si in range(2):
                for kvh in range(2):
                    ps_, pk = nextps()
                    c0 = si * 128 + kvh * 64
                    for kt in range(8):
                        mm(ps_[0:64, 0:n], wb[:, kt, c0:c0 + 64], xnT[:, kt, 0:n], kt == 0, kt == 7, ["xnT", wk], [pk])
                    dst, dk = kdst(si, kvh)
                    cp(evac_eng(), dst, ps_[0:64, 0:n], [pk], [dk])

        def finish(h, br, ps_, stride, first, npart, ntt):
            pp = slice(0, npart)
            view = ps_[pp, 0:ntt * stride].rearrange("p (t c) -> p t c", t=ntt)
            vop("dve", "tensor_scalar", ["psR3a"], ["rden"], out=rden[pp, 0:ntt], in0=view[:, :, 64], scalar1=1e-30, scalar2=None,
                op0=ALU.max)
            vop("dve", "reciprocal", ["rden"], ["rden"], out=rden[pp, 0:ntt], in_=rden[pp, 0:ntt])
            vop("dve", "tensor_tensor", ["rden", "gates"], ["scg"], out=scg[pp, 0:ntt], in0=rden[pp, 0:ntt], in1=gates[pp, 0:ntt, br * 8 + h], op=ALU.mult)
            bc = scg[pp, 0:ntt].rearrange("p (a o) -> p a o", o=1).to_broadcast([npart, ntt, 64])
            if first:
                vop("dve", "tensor_tensor", ["psR3a", "scg"], ["oacc"], out=oacc[pp, 0:ntt, h, :], in0=view[:, :, 0:64], in1=bc, op=ALU.mult)
            else:
                vop("dve", "tensor_tensor", ["psR3a", "scg"], ["otmp"], out=otmp[pp, 0:ntt, :], in0=view[:, :, 0:64], in1=bc, op=ALU.mult)
                vop("dve", "tensor_tensor", ["otmp", "oacc"], ["oacc"], out=oacc[pp, 0:ntt, h, :], in0=oacc[pp, 0:ntt, h, :], in1=otmp[pp, 0:ntt, :], op=ALU.add)

        def out_proj(npart, tt, xsrc, ydst):
            pp = slice(0, npart)
            vop("dve", "tensor_tensor", ["oacc", "gsn"], ["mix"], out=mix[pp, tt, 512:1024],
                in0=oacc[pp, tt, :, :].rearrange("p h d -> p (h d)"), in1=gsn[pp, tt, :], op=ALU.mult)
            for kt in range(8):
                tr(psT[:, kt * 128:kt * 128 + npart], mix[pp, tt, kt * 128:(kt + 1) * 128], identb[pp, pp], ["mix", "k_identb"], ["psT"])
            cp("act", mixT[:, :, 0:npart], psT[:].rearrange("p (k t) -> p k t", k=8)[:, :, 0:npart], ["psT"], ["mixT"])
            ld(xres[pp, :], xsrc, ["xres"])
            for nchunk, (ps_, pk) in enumerate(psAB):
                for kt in range(8):
                    mm(ps_[pp, 0:512], mixT[:, kt, 0:npart], wout[:, kt, nchunk * 512:(nchunk + 1) * 512], kt == 0, kt == 7,
                       ["mixT", "wout"], [pk])
                vop("dve", "tensor_tensor", [pk, "xres"], ["yout"], out=yout[pp, nchunk * 512:(nchunk + 1) * 512], in0=ps_[pp, 0:512],
                    in1=xres[pp, nchunk * 512:(nchunk + 1) * 512], op=ALU.add)
            act(sq_junk[pp, :], yout[pp, :], AF.Square, ["yout"], ["mixT", "ss"], accum_out=ss[pp, :])
            act(rstd[pp, :], ss[pp, :], AF.Sqrt, ["ss"], ["rstd"], scale=1.0 / 1024, bias=EPS)
            vop("dve", "reciprocal", ["rstd"], ["rstd"], out=rstd[pp, :], in_=rstd[pp, :])
            vop("dve", "scalar_tensor_tensor", ["yout", "rstd", "fgb"], ["yout"], out=yout[pp, :], in0=yout[pp, :], scalar=rstd[pp, 0:1],
                in1=fgb[pp, :], op0=ALU.mult, op1=ALU.mult)
            stq(ydst, yout[pp, :], ["yout"])

        def sample_jobs():
            ovc = ct["ovc"]
            vop("pool", "memset", [], ["k_cmpbias"], ap=ct["cmpbias"][:, 0:1040], constant=1.0)
            vop("pool", "memset", [], ["Vn"], ap=Vn[:], constant=1.0)
            vop("pool", "memset", [], ["Vpg"], ap=Vpg[:], constant=1.0)
            vop("pool", "memset", [], ["QTz"], ap=QTz[:], constant=0.0)
            for bs in range(4):
                ld(xres[0:1, 0:1024], sshift[bs:bs + 1, 0:1024], ["xres"])
                ld(yout[0:1, 0:640], sshift[bs:bs + 1, 1024:1664], ["yout"])
                for c13 in range(13):
                    src = xres[0:1, c13 * 128:(c13 + 1) * 128] if c13 < 8 else yout[0:1, (c13 - 8) * 128:(c13 - 7) * 128]
                    tr(R0[:, c13:c13 + 1], src, identf[0:1, 0:1], ["xres", "yout", "k_identf"], ["psR0"])
                cp("dve", zc[:], R0[:, 0:13], ["psR0"], ["zc"])
                cp("dve", STz[:].rearrange("p f h d -> p (f h d)").bitcast(F32R), ct["zeros"][:, 0:512], ["k_zeros"], ["STz"])
                for fc in range(4):
                    ld(tmpf[0][0:64, 0:128].rearrange("i (h j) -> i h j", h=2), swkv[bs, 2 * fc:2 * fc + 2, :, :].rearrange("h i j -> i h j"), ["tmpf0"])
                    tr(R0[:, 0:64], tmpf[0][0:64, 0:128], identf[0:64, 0:64], ["tmpf0", "k_identf"], ["psR0"])
                    cp("dve", Ssb[:, 0:64], R0[:, 0:64], ["psR0"], ["Ssb"])
                    for hh in range(2):
                        rws = slice(hh * 64, hh * 64 + 64)
                        cp("act", STz[rws, fc, hh, :].bitcast(F32R), Ssb[rws, 0:64], ["Ssb"], ["STz"])
                chk(20)
                rmsnorm_T(xs[4 * bs:4 * bs + 4, :], 4, 0)
                p4 = slice(0, 4)
                for gname, ncol in (("T0", 512), ("T1", 512), ("T2", 512), ("T3", 280)):
                    wb, wk = load_group(gname)
                    ps_, pk = nextps()
                    for kt in range(8):
                        mm(ps_[p4, 0:ncol], xnT[:, kt, 0:4], wb[:, kt, 0:ncol], kt == 0, kt == 7, ["xnT", wk], [pk])
                    if gname == "T0":
                        act(gsr[p4, 0, :], ps_[p4, 0:512], AF.Silu, [pk], ["gsr"])
                    elif gname == "T1":
                        act(gsn[p4, 0, :], ps_[p4, 0:512], AF.Silu, [pk], ["gsn"])
                    elif gname == "T2":
                        cp("act", kv0[0][p4, :], ps_[p4, 0:512], [pk], ["kv0_0"])
                        stq(kvs[4 * bs:4 * bs + 4, :], kv0[0][p4, :], ["kv0_0"])
                        cp("dve", Vn[p4, 0, :, 0:64], kv0[0][p4, 384:512].rearrange("p (k d) -> p k d", k=2), ["kv0_0"], ["Vn"])
                    else:
                        cp("act", kv1[0][p4, :], ps_[p4, 0:280], [pk], ["kv1_0"])
                        stq(wins[bs, 508:512, :], kv1[0][p4, 0:256], ["kv1_0"])
                        cp("dve", Vn[p4, 1, :, 0:64], kv1[0][p4, 128:256].rearrange("p (k d) -> p k d", k=2), ["kv1_0"], ["Vn"])
                        act(gates[p4, 0, :], kv1[0][p4, 256:280], AF.Sigmoid, ["kv1_0"], ["gates"])
                chk(21)
                rwkv_block(4, 4, 2, 4, None)
                chk(22)
                shift_and_state_out(shs[bs:bs + 1, :], wkvs[bs])
                chk(23)
                proj_qk(4, lambda si, kvh: (KTn[:, si, kvh, :], "KTn"))
                cp("dve", QTs[:], QT[:, :, 0:4], ["QT"], ["QTs"])
                wb, wk = load_group("Q")
                for g in range(4):
                    h = 4 + g
                    ps_, pk = nextps()
                    for kt in range(8):
                        mm(ps_[:, 0:4], wb[:, kt, (h - 1) * 64:(h + 1) * 64], xnT[:, kt, 0:4], kt == 0, kt == 7, ["xnT", wk], [pk])
                    cp("dve", Ssb[:, g * 4:(g + 1) * 4], ps_[:, 0:4], [pk], ["Ssb"])
                cp("act", QTz[64:128, 1, :], Ssb[64:128, 0:16], ["Ssb"], ["QTz"])
                cp("dve", QTz[0:64, 0, :], QTs[:, 0:4, :].rearrange("p g t -> p (g t)"), ["QTs"], ["QTz"])
                chk(24)
                ld(ptb[:], pt[bs:bs + 1, :].to_broadcast([128, 128]), ["ptb"])
                cp("dve", ptf[:], ptb[:], ["ptb"], ["ptf"])
                vop("dve", "tensor_scalar", ["ptf", "k_iota"], ["pidx"], out=pidx[:, 0, :], in0=ptf[:], scalar1=256.0, scalar2=ct["iota"][:, 0:1],
                    op0=ALU.mult, op1=ALU.add)
                vop("dve", "tensor_scalar", ["ptf", "k_iota"], ["pidx"], out=pidx[:, 1, :], in0=ptf[:], scalar1=256.0, scalar2=ct["iota"][:, 1:2],
                    op0=ALU.mult, op1=ALU.add)

                def gather4(g0, col0, key):
                    dst = pg4[key]
                    kname = ("xres", "yout")[key]
                    for i in range(4):
                        P.dma_fn("pool", (lambda d_, gi: (lambda e: e.indirect_dma_start(
                            out=d_, out_offset=None, in_=cache_h,
                            in_offset=bass.IndirectOffsetOnAxis(ap=pidx[:, col0 // 256, gi:gi + 1], axis=0))))(dst[:, i, :], g0 + i),
                            reads=["pidx"], writes=[kname])
                    return dst, kname

                chk(25)
                gi_ = 0
                for G in range(8):
                    mm(psP[:, 0:258], ct["zeros"][:, 0:128], ct["zeros"][:, 0:258], True, True, ["k_zeros"], ["psP"])
                    for pq in range(4):
                        src, sk = gather4(16 * G + 4 * pq, 0, gi_ % 2)
                        gi_ += 1
                        for i in range(4):
                            ti = 4 * pq + i
                            lo, hi = 8 * ti - 1, 8 * ti + 8
                            for m in range(2):
                                vop("dve", "tensor_tensor", [sk, "wt"], [f"ym{m}"], out=ym[m][:], in0=src[:, i, :],
                                    in1=wt[:, m, :], op=ALU.mult)
                            for cc in range(2):
                                for m in range(2):
                                    zoff = m - 8 * ti + 120
                                    mm(psP[:, cc * 129 + 1 + lo: cc * 129 + 1 + hi], ym[m][:, cc * 128:(cc + 1) * 128],
                                       ct["zs"][:, zoff + lo: zoff + hi], False, True, [f"ym{m}", "k_zs"], ["psP"])
                    for cc in range(2):
                        cp("act", pall[:, cc, 128 * G:128 * G + 128], psP[:, cc * 129 + 1:cc * 129 + 129], ["psP"], ["Vs"])
                        if G > 0:
                            vop("dve", "tensor_tensor", ["psP", "Vs"], ["Vs"], out=pall[:, cc, 128 * G - 1:128 * G],
                                in0=psP[:, cc * 129:cc * 129 + 1], in1=pall[:, cc, 128 * G - 1:128 * G], op=ALU.add)
                chk(26)
                for kvh in range(2):
                    for half in range(2):
                        js = slice(half * 512, (half + 1) * 512)
                        mm(psA[0:64, 0:512], wmix[:, kvh, 0, :], pall[:, 0, js], True, True, ["wmix", "Vs"], ["psA"])
                        cp("act", kcTs[:, kvh, js], psA[0:64, 0:512], ["psA"], ["Vw"])
                    for jt in range(8):
                        mm(psB[:, 0:64], pall[:, 1, jt * 128:(jt + 1) * 128], wmix[:, kvh, 1, :], True, True, ["wmix", "Vs"], ["psB"])
                        cp("dve", vcs[:, jt, kvh, 0:64], psB[:, 0:64], ["psB"], ["k_cmpbias"])
                chk(27)
                p16 = slice(0, 16)
                for kvh in range(2):
                    mm(R3[p16, 0:322], ct["zeros"][:, 0:16], ct["zeros"][:, 0:322], True, False, ["k_zeros"], ["psR3a"])
                    for jt in range(8):
                        sp_, spk = sps[sn["n"] % 2]
                        pt_ = PTb[sn["n"] % 3]
                        ptk = f"PTb{sn['n'] % 3}"
                        sn["n"] += 1
                        mm(sp_[:, 0:16], kcTs[:, kvh, jt * 128:(jt + 1) * 128], QTs[:, 4 * kvh:4 * kvh + 4, :].rearrange("p g t -> p (g t)"),
                           True, jt != 7, ["Vw", "QTs"], [spk])
                        if jt == 7:
                            mm(sp_[:, 0:16], identb[:], ct["lastb"][:], False, True, ["k_identb", "k_lastb"], [spk])
                        act(pt_[:, 0:16], sp_[:, 0:16], AF.Exp, [spk], [ptk])
                        mm(R3[p16, 0:65], pt_[:, 0:16], vcs[:, jt, kvh, :], False, False, [ptk, "k_cmpbias"], ["psR3a"])
                        mm(R3[p16, 65 + 32 * jt:65 + 32 * jt + 33], pt_[:, 0:16], ovc[:, 0:33], False, jt == 7, [ptk, "k_ovc"], ["psR3a"])
                    chk(27.1)
                    cp("act", ocs1[p16, :], R3[p16, 0:322], ["psR3a"], ["stw0"])
                    vop("dve", "reciprocal", ["stw0"], ["rden"], out=rden[p16, 0:1], in_=ocs1[p16, 64:65])
                    vop("dve", "tensor_scalar", ["stw0", "rden"], ["stw0"], out=ocs1[p16, :], in0=ocs1[p16, :], scalar1=rden[p16, 0:1],
                        scalar2=None, op0=ALU.mult)
                    cp("dve", obr[p16, 0, kvh, :], ocs1[p16, 0:64], ["stw0"], ["Mt1"])
                    chk(27.2)
                    mm(psA[p4, 0:257], ct["gsel"][p16, :], ocs1[p16, 65:322], True, True, ["k_gsel", "stw0"], ["psA"])
                    vop("dve", "tensor_tensor", ["psA", "k_addcs"], ["kv1_1"], out=scs[p4, :], in0=psA[p4, 0:257], in1=ct["addcs"][p4, :], op=ALU.add)
                    vop("dve", "max", ["kv1_1"], ["m8a"], out=m8a[p4, :], in_=scs[p4, :])
                    vop("dve", "match_replace", ["kv1_1", "m8a"], ["kv0_1"], out=scs2[p4, :], in_to_replace=m8a[p4, :], in_values=scs[p4, :], imm_value=-3e4)
                    vop("dve", "max", ["kv0_1"], ["m8b"], out=m8b[p4, :], in_=scs2[p4, :])
                    vop("dve", "tensor_scalar", ["kv1_1", "m8b"], ["kv0_1"], out=scs2[p4, :], in0=scs[p4, :], scalar1=m8b[p4, 7:8], scalar2=-BIG,
                        op0=ALU.is_lt, op1=ALU.mult)
                    chk(27.3)
                    for t_ in range(4):
                        mm(psB[0:1, 0:257], ct["identf"][p4, t_:t_ + 1], scs2[p4, :], True, True, ["k_identf", "kv0_1"], ["psB"])
                        cp("dve" if t_ % 2 else "act", selflat4[0:1, :, :, kvh, t_],
                           psB[0:1, 0:256].rearrange("o (pg hf) -> o hf pg", hf=2), ["psB"], ["k_zp"])

                chk(28)
                def page_attn(src, sk, npg, mask_mm, mexp_const, first_group):
                    for i in range(npg):
                        tr(R0[:, i * 128:(i + 1) * 128], src[:, i, 0:128], identf[:], [sk, "k_identf"], ["psR0"])
                    cp("act", KTpg[:, 0:npg, :].rearrange("p g n -> p (g n)"), R0[:, 0:npg * 128], ["psR0"], ["KTpg"])
                    cp("dve", Vpg[:, 0:npg, :, 0:64], src[:, 0:npg, 128:256].rearrange("p g (k d) -> p g k d", k=2), [sk], ["Vpg"])
                    sp_, spk = sps[sn["n"] % 2]
                    pt_ = PTb[sn["n"] % 3]
                    ptk = f"PTb{sn['n'] % 3}"
                    sn["n"] += 1
                    for i in range(npg):
                        for kvh in range(2):
                            c0 = (i * 2 + kvh) * 16
                            mm(sp_[:, c0:c0 + 16], KTpg[:, i, :], QTz[:, kvh, :], True, True, ["KTpg", "QTz"], [spk])
                    if mask_mm is not None:
                        for hf in range(2):
                            mm(sp_[:, 256:256 + npg * 8], ct["hl"][0:1, hf, :], mask_mm(hf), hf == 0, hf == 1, ["k_hl", "k_zp"], [spk])
                        act(mexp[:, 0:npg * 8], sp_[:, 256:256 + npg * 8], AF.Exp, [spk], ["mexp"])
                        mk = mexp[:, 0:npg * 8]
                        mkk = "mexp"
                    else:
                        mk = mexp_const
                        mkk = "k_winms"
                    act(pt_[:, 0:npg * 32], sp_[:, 0:npg * 32], AF.Exp, [spk], [ptk])
                    vop("dve", "tensor_tensor", [ptk, mkk], [ptk], out=pt_[:, 0:npg * 32].rearrange("p (a g t) -> p a g t", g=4, t=4),
                        in0=pt_[:, 0:npg * 32].rearrange("p (a g t) -> p a g t", g=4, t=4),
                        in1=mk.rearrange("p (a o t) -> p a o t", o=1, t=4).to_broadcast([128, npg * 2, 4, 4]), op=ALU.mult)
                    for i in range(npg):
                        for kvh in range(2):
                            c0 = (i * 2 + kvh) * 16
                            mm(R3[p16, kvh * 65:(kvh + 1) * 65], pt_[:, c0:c0 + 16], Vpg[:, i, kvh, :], first_group and i == 0 and kvh == 0,
                               False, [ptk, "Vpg"], ["psR3a"])

                def tail_attn(si, last):
                    sp_, spk = sps[sn["n"] % 2]
                    pt_ = PTb[sn["n"] % 3]
                    ptk = f"PTb{sn['n'] % 3}"
                    sn["n"] += 1
                    for kvh in range(2):
                        mm(sp_[p4, kvh * 16:(kvh + 1) * 16], KTn[:, si, kvh, :], QTs[:, 4 * kvh:4 * kvh + 4, :].rearrange("p g t -> p (g t)"),
                           True, False, ["KTn", "QTs"], [spk])
                        mm(sp_[p4, kvh * 16:(kvh + 1) * 16], identb[p4, p4], ct["tailb"][p4, :], False, True, ["k_identb", "k_tailb"], [spk])
                    act(pt_[p4, 0:32], sp_[p4, 0:32], AF.Exp, [spk], [ptk])
                    for kvh in range(2):
                        mm(R3[p16, kvh * 65:(kvh + 1) * 65], pt_[p4, kvh * 16:(kvh + 1) * 16], Vn[p4, si, kvh, :], False, last and kvh == 1,
                           [ptk, "Vn"], ["psR3a"])

                def fold(br, first):
                    for kvh in range(2):
                        cp("act", obr[p16, br, kvh, :], R3[p16, kvh * 65:kvh * 65 + 64], ["psR3a"], ["Mt1"])
                        cp("act", rden[p16, 1:2], R3[p16, kvh * 65 + 64:kvh * 65 + 65], ["psR3a"], ["rden"])
                        vop("dve", "reciprocal", ["rden"], ["rden"], out=rden[p16, 0:1], in_=rden[p16, 1:2])
                        vop("dve", "tensor_scalar", ["Mt1", "rden"], ["Mt1"], out=obr[p16, br, kvh, :], in0=obr[p16, br, kvh, :],
                            scalar1=rden[p16, 0:1], scalar2=None, op0=ALU.mult)

                chk(29)
                for grp in range(32):
                    src, sk = gather4(4 * grp, 256, gi_ % 2)
                    gi_ += 1
                    page_attn(src, sk, 4, (lambda hf, grp=grp: selflat4[0:1, hf, 4 * grp:4 * grp + 4, :, :].rearrange("o a k t -> o (a k t)")),
                              None, grp == 0)
                tail_attn(0, True)
                fold(1, False)
                chk(30)
                ld(pg4[0][:], cwin[bs].rearrange("(g p) c -> p g c", p=128), ["xres"])
                for tl in range(4):
                    lo_ = 4 if tl == 0 else 0
                    stq(wins[bs, 128 * tl + lo_ - 4:128 * tl + 124, :], pg4[0][lo_:128, tl, :], ["xres"])
                page_attn(pg4[0], "xres", 4, None, ct["winms"][:], True)
                tail_attn(1, True)
                fold(2, False)
                chk(31)
                for kvh in range(2):
                    for g in range(4):
                        h = 4 * kvh + g
                        for br in range(3):
                            mm(psA[p4, br * 64:(br + 1) * 64], ct["identf"][p16, g * 4:g * 4 + 4], obr[p16, br, kvh, :], True, True,
                               ["k_identf", "Mt1"], ["psA"])
                        for br in range(3):
                            if br == 0:
                                vop("dve", "tensor_scalar", ["psA", "gates"], ["oacc"], out=oacc[p4, 0, h, :], in0=psA[p4, 0:64],
                                    scalar1=gates[p4, 0, h:h + 1], scalar2=None, op0=ALU.mult)
                            else:
                                vop("dve", "scalar_tensor_tensor", ["psA", "gates", "oacc"], ["oacc"], out=oacc[p4, 0, h, :], in0=psA[p4, br * 64:(br + 1) * 64],
                                    scalar=gates[p4, 0, br * 8 + h:br * 8 + h + 1], in1=oacc[p4, 0, h, :], op0=ALU.mult, op1=ALU.add)
                chk(32)
                out_proj(4, 0, xs[4 * bs:4 * bs + 4, :], ys[4 * bs:4 * bs + 4, :])

        try:
          chk(0)
          for hh_ in range(2):
              cp("dve", BKz[:, hh_, :, :].rearrange("p a t -> p (a t)").bitcast(F32R), ct["zeros"][:, 0:2 * TB], ["k_zeros"], ["BKz"])
          for b in range(2 if os.environ.get("KNOPROMPT") is None else 0):
            vop("dve", "memset", [], ["zc"], ap=zc[:], constant=0.0)
            cp("dve", STz[:].rearrange("p f h d -> p (f h d)").bitcast(F32R), ct["zeros"][:, 0:512], ["k_zeros"], ["STz"])
            mm(psP[:, 0:256], ct["zeros"][:, 0:128], ct["zeros"][:, 0:256], True, True, ["k_zeros"], ["psP"])
            for tb in range(NB):
                t0 = tb * TB
                for tt in range(NTT):
                    rmsnorm_T(xp[b, t0 + tt * 128:t0 + (tt + 1) * 128, :], 128, tt)
                chk(1)
                for gname, ncol in (("T0", 512), ("T1", 512), ("T2", 512), ("T3", 280)):
                    wb, wk = load_group(gname)
                    for tt in range(NTT):
                        ps_, pk = nextps()
                        for kt in range(8):
                            mm(ps_[:, 0:ncol], xnT[:, kt, tt * 128:(tt + 1) * 128], wb[:, kt, 0:ncol],
                               kt == 0, kt == 7, ["xnT", wk], [pk])
                        tile_i = tb * NTT + tt
                        if gname == "T0":
                            act(gsr[:, tt, :], ps_[:, 0:512], AF.Silu, [pk], ["gsr"])
                        elif gname == "T1":
                            act(gsn[:, tt, :], ps_[:, 0:512], AF.Silu, [pk], ["gsn"])
                        elif gname == "T2":
                            k0 = kv0[tt % 2]
                            kk0 = f"kv0_{tt % 2}"
                            cp("act", k0[:], ps_[:, 0:512], [pk], [kk0])
                            stq(kvp[b, t0 + tt * 128:t0 + (tt + 1) * 128, :], k0[:], [kk0])
                            cp("dve", Vs[:, tile_i, :, 0:64], k0[:, 384:512].rearrange("p (k d) -> p k d", k=2),
                               [kk0], ["Vs"])
                            for m in range(2):
                                vop("pool", "tensor_tensor", [kk0, "wt"], [f"ym{m}"], out=ym[m][:], in0=k0[:, 0:256],
                                    in1=wt[:, m, :], op=ALU.mult)
                            for cc in range(2):
                                for m in range(2):
                                    j0 = 8 * tile_i - 1
                                    lo = max(j0, 0)
                                    hi = min(8 * tile_i + 8, 127)
                                    zoff = m - 8 * tile_i + 120
                                    mm(psP[:, cc * 128 + lo: cc * 128 + hi], ym[m][:, cc * 128:(cc + 1) * 128],
                                       ct["zs"][:, zoff + lo: zoff + hi], False, True, [f"ym{m}", "k_zs"], ["psP"])
                        else:
                            k1 = kv1[tt % 2]
                            kk1 = f"kv1_{tt % 2}"
                            cp("act", k1[:], ps_[:, 0:280], [pk], [kk1])
                            if t0 + tt * 128 >= T - 512:
                                stq(winp[b, t0 + tt * 128 - (T - 512): t0 + (tt + 1) * 128 - (T - 512), :],
                                    k1[:, 0:256], [kk1])
                            cp("dve", Vw[:, tile_i, :, 0:64], k1[:, 128:256].rearrange("p (k d) -> p k d", k=2),
                               [kk1], ["Vw"])
                            act(gates[:, tt, :], k1[:, 256:280], AF.Sigmoid, [kk1], ["gates"])
                chk(2)
                rwkv_block(TB, 128, 7, 128, None)
                chk(5)
                if tb == NB - 1:
                    shift_and_state_out(shp[b:b + 1, :], wkvp[b])
                chk(6)
                proj_qk(TB, lambda si, kvh: ((KTs, KTw)[si][:, kvh, t0:t0 + TB], ("KTs", "KTw")[si]))
                chk(6.2)
                cp("dve", pooled[:].rearrange("p a j -> p (a j)"), psP[:, 0:256], ["psP"], ["pooled"])
                cp("act", pooledb[:], pooled[:], ["pooled"], ["pooledb"])
                chk(6.3)
                for kvh in range(2):
                    mm(psA[0:64, 0:128], wmix[:, kvh, 0, :], pooledb[:, 0, :], True, True, ["wmix", "pooledb"], ["psA"])
                    cp("act", kcT[:, kvh, :], psA[0:64, 0:128], ["psA"], ["kcT"])
                    mm(psB[:, 0:64], pooledb[:, 1, :], wmix[:, kvh, 1, :], True, True, ["wmix", "pooledb"], ["psB"])
                    cp("dve", vcx[:, kvh, 0:64], psB[:, 0:64], ["psB"], ["vcx"])
                chk(6.4)
                if tb == 0 and b == 0:
                    vop("pool", "memset", [], ["vcx"], ap=vcx[:, :, 64:65], constant=1.0)
                    for kvh in range(2):
                        cp("dve", vcx[:, kvh, 65:97], ct["ov32"][:], ["k_ov32"], ["vcx"])

                def attend(h, br, tiles, Kt_, kkey, Vt_, vkey, first):
                    kvh = h // 4
                    nt = len(tiles)
                    for ti, (kt, biases) in enumerate(tiles):
                        sp_, spk = sps[sn["n"] % 2]
                        pt_ = PTb[sn["n"] % 3]
                        ptk = f"PTb{sn['n'] % 3}"
                        sn["n"] += 1
                        mm(sp_[:, 0:TB], Kt_[:, kvh, kt * 128:(kt + 1) * 128], QT[:, h, :], True, len(biases) == 0,
                           [kkey, "QT"], [spk])
                        for bi, (bl, br_, bkeys) in enumerate(biases):
                            mm(sp_[:, 0:TB], bl, br_, False, bi == len(biases) - 1, bkeys, [spk])
                        act(pt_[:], sp_[:, 0:TB], AF.Exp, [spk], [ptk])
                        for tt in range(NTT):
                            mm(R3[:, tt * 65:(tt + 1) * 65], pt_[:, tt * 128:(tt + 1) * 128], Vt_[:, kt, kvh, :],
                               ti == 0 and tt == 0, ti == nt - 1 and tt == NTT - 1, [ptk, vkey], ["psR3a"])
                    finish(h, br, R3, 65, first, 128, NTT)

                chk(7)
                for h in range(8):
                    kvh = h // 4
                    sp_, spk = sps[sn["n"] % 2]
                    pt_ = PTb[sn["n"] % 3]
                    ptk = f"PTb{sn['n'] % 3}"
                    sn["n"] += 1
                    mm(sp_[:, 0:TB], kcT[:, kvh, :], QT[:, h, :], True, False, ["kcT", "QT"], [spk])
                    mm(sp_[:, 0:TB], identb[:], ct["cmpbias"][:, t0:t0 + TB], False, True, ["k_identb", "k_cmpbias"], [spk])
                    act(pt_[:], sp_[:, 0:TB], AF.Exp, [spk], [ptk])
                    for tt in range(NTT):
                        mm(R3[:, tt * 97:(tt + 1) * 97], pt_[:, tt * 128:(tt + 1) * 128], vcx[:, kvh, :], tt == 0, tt == NTT - 1,
                           [ptk, "vcx"], ["psR3a"])
                    finish(h, 0, R3, 97, True, 128, NTT)
                    view = R3[:, 0:NTT * 97].rearrange("p (t c) -> p t c", t=NTT)
                    if h % 4 == 0:
                        vop("dve", "tensor_tensor", ["psR3a", "rden"], ["sc"], out=sc[:, :, kvh, :], in0=view[:, :, 65:97],
                            in1=rden[:].rearrange("p (a o) -> p a o", o=1).to_broadcast([128, NTT, 32]), op=ALU.mult)
                    else:
                        vop("dve", "tensor_tensor", ["psR3a", "rden"], ["otmp"], out=otmp[:, :, 0:32], in0=view[:, :, 65:97],
                            in1=rden[:].rearrange("p (a o) -> p a o", o=1).to_broadcast([128, NTT, 32]), op=ALU.mult)
                        vop("dve", "tensor_tensor", ["otmp", "sc"], ["sc"], out=sc[:, :, kvh, :], in0=sc[:, :, kvh, :], in1=otmp[:, :, 0:32], op=ALU.add)
                chk(8)
                for tt in range(NTT):
                    for kvh in range(2):
                        vop("dve", "tensor_tensor", ["sc", "k_addc"], ["scw"], out=scw[:], in0=sc[:, tt, kvh, :],
                            in1=ct["addc"][:, tb * NTT + tt, :], op=ALU.add)
                        vop("dve", "max", ["scw"], ["m8a"], out=m8a[:], in_=scw[:])
                        vop("dve", "match_replace", ["scw", "m8a"], ["otmp"], out=otmp[:, 0, 0:32], in_to_replace=m8a[:], in_values=scw[:],
                            imm_value=-3e4)
                        vop("dve", "max", ["otmp"], ["m8b"], out=m8b[:], in_=otmp[:, 0, 0:32])
                        vop("dve", "tensor_scalar", ["m8b"], ["thr"], out=thr[:], in0=m8b[:, 7:8], scalar1=-5000.0, scalar2=None, op0=ALU.max)
                        vop("dve", "tensor_scalar", ["scw", "thr"], ["selb"], out=selb[:], in0=scw[:], scalar1=thr[:, 0:1], scalar2=-BIG,
                            op0=ALU.is_lt, op1=ALU.mult)
                        tr(psT[0:32, 0:128], selb[:], identb[:], ["selb", "k_identb"], ["psT"])
                        cp("act", selbT[:, kvh, tt * 128:(tt + 1) * 128], psT[0:32, 0:128], ["psT"], ["selbT"])
                chk(9)
                for h in range(8):
                    kvh = h // 4
                    tiles = []
                    for kt in range(0, tb * NTT + NTT):
                        bs = [(ct["zp"][:, kt * 128:(kt + 1) * 128], selbT[:, kvh, :], ["k_zp", "selbT"])]
                        if kt >= tb * NTT:
                            bs.append((identb[:], ct["causb"][:, kt - tb * NTT, :], ["k_identb", "k_causb"]))
                        tiles.append((kt, bs))
                    attend(h, 1, tiles, KTs, "KTs", Vs, "Vs", False)
                chk(10)
                for h in range(8):
                    tiles = []
                    q0 = tb * NTT
                    for kt in range(max(0, q0 - 4), q0 + NTT):
                        bs = []
                        if kt >= q0:
                            bs.append((identb[:], ct["causb"][:, kt - q0, :], ["k_identb", "k_causb"]))
                        elif kt - q0 + 4 < 2:
                            bs.append((identb[:], ct["winlow"][:, kt - q0 + 4, :], ["k_identb", "k_winlow"]))
                        tiles.append((kt, bs))
                    attend(h, 2, tiles, KTw, "KTw", Vw, "Vw", False)
                chk(11)
                for tt in range(NTT):
                    out_proj(128, tt, xp[b, t0 + tt * 128:t0 + (tt + 1) * 128, :], yp[b, t0 + tt * 128:t0 + (tt + 1) * 128, :])
                chk(12)

          if with_sample:
            sample_jobs()
        except _Stop:
            for _i in range(int(os.environ.get("KPAD", "0"))):
                eng_ = os.environ.get("KPADENG", "act")
                if eng_ == "act":
                    act(thr[:], thr[:], AF.Copy, ["thr"], ["thr"])
                else:
                    cp(eng_, thr[:], m8a[:, 0:1], ["m8a"], ["thr"])
            if os.environ.get("KDBG"):
                dbg = dout("dbg", [128, 4096])
                o = 0
                for nm, tl, ncol in (("STz", STz, 512), ("Mt0", Mt[0], 512), ("Pk5", Pk[5], 512), ("Xa0", Xa[0], 128), ("Xa1", Xa[1], 128),
                                     ("tok", tok, 384), ("AR", AR, 512), ("BK", BK, 512), ("ytok", ytok, 256), ("Wc", Wc, 16)):
                    flat = tl[:]
                    shp_ = list(tl.shape) if hasattr(tl, "shape") else None
                    names = "abcdefg"[:len(shp_) - 1]
                    if len(shp_) > 2:
                        flat = tl[:].rearrange("p " + " ".join(names) + " -> p (" + " ".join(names) + ")")
                    stq(dbg[:, o:o + ncol], flat, [nm if nm not in ("Mt0", "Pk5", "Xa0", "Xa1") else nm])
                    o += ncol
        P.build(sems, block)
    return nc


def _core_inputs(c, inp, consts, cache2d=None, pt_override=None):
    d = {}
    d["xp"] = np.ascontiguousarray(inp["x_prompt"][2 * c:2 * c + 2])
    d["w_in"] = np.ascontiguousarray(inp["w_in"][0])
    d["w_out"] = np.ascontiguousarray(inp["w_out"][0])
    d["norm_g"] = np.ascontiguousarray(inp["norm_g"][0])
    d["final_g"] = np.ascontiguousarray(inp["final_g"])
    d["mu"] = np.ascontiguousarray(inp["mu_shift"][0])
    for k in ("w0", "a0", "k_k", "k_a", "r_k", "gn_w", "gn_b"):
        d[k] = np.ascontiguousarray(inp[k][0])
    d["w_dup"] = np.ascontiguousarray(inp["w_decay_up"][0])
    d["w_aup"] = np.ascontiguousarray(inp["w_aaa_up"][0])
    d["w_cpos"] = np.ascontiguousarray(inp["w_cmp_pos"][0])
    d["w_cmix"] = np.ascontiguousarray(inp["w_cmp_mix"][0])
    d["xs"] = np.ascontiguousarray(inp["x_sample"][4 * c:4 * c + 4]).reshape(16, 1024)
    d["cache"] = cache2d
    d["cwin"] = np.ascontiguousarray(inp["cache_kv_win"][0, 4 * c:4 * c + 4]).reshape(4, 512, 256)
    d["swkv"] = np.ascontiguousarray(inp["state_wkv"][0, 4 * c:4 * c + 4])
    d["sshift"] = np.ascontiguousarray(inp["state_shift"][0, 4 * c:4 * c + 4])
    d["pt"] = np.ascontiguousarray(inp["page_table"][4 * c:4 * c + 4] if pt_override is None else pt_override).astype(np.int32)
    for k, v in consts.items():
        d["c_" + k] = v
    return d


def kernel(**inp):
    inp = {k: np.asarray(v) for k, v in inp.items()}
    consts = _const_specs()
    cache2d = np.ascontiguousarray(inp["cache_kv"][0]).reshape(-1, 512)
    nc = build_nc(cache2d.shape[0])
    in_maps = [_core_inputs(c, inp, consts, cache2d) for c in range(8)]
    res = run_bass_kernel_spmd(nc, in_maps, core_ids=list(range(8))).results
    cat = lambda k: np.concatenate([r[k] for r in res], axis=0)
    y_prompt = cat("yp")
    y_sample = cat("ys").reshape(32, 4, 1024)
    kv_prompt = cat("kvp").reshape(1, 16, T, 4, 2, 64)
    kv_sample = cat("kvs").reshape(1, 32, 4, 4, 2, 64)
    win_prompt = cat("winp").reshape(1, 16, 512, 2, 2, 64)
    win_sample = cat("wins").reshape(1, 32, 512, 2, 2, 64)
    wkv_prompt = cat("wkvp").reshape(1, 16, 8, 64, 64)
    wkv_sample = cat("wkvs").reshape(1, 32, 8, 64, 64)
    shift_prompt = cat("shp").reshape(1, 16, 1664)
    shift_sample = cat("shs").reshape(1, 32, 1664)
    return (y_prompt, y_sample, kv_prompt, kv_sample, win_prompt, win_sample, wkv_prompt, wkv_sample,
            shift_prompt, shift_sample)
```

```python
import contextlib
import numpy as np
import ml_dtypes
import concourse.bass as bass
import concourse.mybir as mybir
from concourse.bass_utils import run_bass_kernel_spmd

F32 = mybir.dt.float32
BF16 = mybir.dt.bfloat16
I32 = mybir.dt.int32
F32R = mybir.dt.float32r
AF = mybir.ActivationFunctionType
ALU = mybir.AluOpType
AX = mybir.AxisListType

ENGS = ("pe", "act", "dve", "pool", "sp")
NDMASEM = 8
BIG = 30000.0
T = 2048
TB = 256
NTT = TB // 128
NB = T // TB
DW = 3992
EPS = 1e-6
GN_EPS = 64e-5
DEC_C = 0.6065306597126334


class _Stop(Exception):
    pass


import os
KSTOP = float(os.environ.get("KSTOP", "999"))


KSKIP = [int(os.environ.get("KSKIP", "0"))]


def chk(stage):
    if stage >= KSTOP:
        if KSKIP[0] > 0 and stage == KSTOP:
            KSKIP[0] -= 1
            return
        raise _Stop()


class Prog:
    def __init__(self, nc):
        self.nc = nc
        self.ops = []

    def op(self, eng, fn, reads=(), writes=()):
        import sys
        f = sys._getframe(2)
        self.ops.append(dict(eng=eng, fn=fn, reads=tuple(reads), writes=tuple(writes), dma=False, line=f.f_lineno))

    def dma(self, eng, out, in_, reads=(), writes=(), **kw):
        self.ops.append(dict(eng=eng, fn=lambda e: e.dma_start(out=out, in_=in_, **kw),
                             reads=tuple(reads), writes=tuple(writes), dma=True))

    def dma_fn(self, eng, fn, reads=(), writes=()):
        self.ops.append(dict(eng=eng, fn=fn, reads=tuple(reads), writes=tuple(writes), dma=True))

    def build(self, sems, block):
        ops = self.ops
        n = len(ops)
        last_w, readers = {}, {}
        deps = [None] * n
        needed = [False] * n
        for i, o in enumerate(ops):
            d = set()
            for r in o["reads"]:
                if r in last_w:
                    d.add(last_w[r])
            for w in o["writes"]:
                if w in last_w:
                    d.add(last_w[w])
                d.update(readers.get(w, ()))
            d.discard(i)
            d = {j for j in d if not (ops[j]["eng"] == "pe" and o["eng"] == "pe"
                                      and not ops[j]["dma"] and not o["dma"])}
            deps[i] = d
            for j in d:
                needed[j] = True
            for w in o["writes"]:
                last_w[w] = i
                readers[w] = []
            for r in o["reads"]:
                readers.setdefault(r, []).append(i)
        cnt = {e: 0 for e in ENGS}
        dcnt = {e: 0 for e in ENGS}
        ev = [None] * n
        prevdma = [None] * n
        for i, o in enumerate(ops):
            e = o["eng"]
            if o["dma"]:
                k = dcnt[e]
                dcnt[e] += 1
                s = k % NDMASEM
                ev[i] = ((e, s), 16 * (k // NDMASEM + 1))
                if k >= NDMASEM:
                    prevdma[i] = ((e, s), 16 * (k // NDMASEM))
            elif needed[i]:
                cnt[e] += 1
                ev[i] = (e, cnt[e])
        per_eng = {e: [i for i, o in enumerate(ops) if o["eng"] == e] for e in ENGS}
        final_dma = {}
        for i, o in enumerate(ops):
            if o["dma"]:
                final_dma[ev[i][0]] = max(final_dma.get(ev[i][0], 0), ev[i][1])

        def emit(e, eng, idxs, last):
            waited = {}
            for i in idxs:
                o = ops[i]
                want = {}
                for j in deps[i]:
                    sk, v = ev[j]
                    want[sk] = max(want.get(sk, 0), v)
                if prevdma[i] is not None:
                    sk, v = prevdma[i]
                    want[sk] = max(want.get(sk, 0), v)
                for sk, v in want.items():
                    if waited.get(sk, 0) < v:
                        eng.wait_ge(sems[sk], v)
                        waited[sk] = v
                if os.environ.get("KTRACE") and i >= n - 60:
                    print("OP", i, e, "line", o.get("line"), "R", o["reads"], "W", o["writes"], "waits", want, "ev", ev[i], flush=True)
                ins = o["fn"](eng)
                if o["dma"]:
                    ins.then_inc(sems[ev[i][0]], 16)
                elif ev[i] is not None:
                    ins.then_inc(sems[e], 1)
            if last:
                for sk, v in final_dma.items():
                    if waited.get(sk, 0) < v:
                        eng.wait_ge(sems[sk], v)

        block.sync(lambda eng: emit("sp", eng, per_eng["sp"], True))
        block.tensor(lambda eng: emit("pe", eng, per_eng["pe"], False))
        block.scalar(lambda eng: emit("act", eng, per_eng["act"], False))
        block.vector(lambda eng: emit("dve", eng, per_eng["dve"], False))
        block.gpsimd(lambda eng: emit("pool", eng, per_eng["pool"], False))


def _consts():
    bf = ml_dtypes.bfloat16
    c = {}
    p = np.arange(128)[:, None]
    q = np.arange(128)[None, :]
    c["identf"] = np.eye(128, dtype=np.float32)
    c["identb"] = np.eye(128, dtype=np.float32).astype(bf)
    su = (p < q).astype(np.float32)
    ui = (p <= q).astype(np.float32)
    c["mask4"] = np.concatenate([su, ui, su, ui], axis=1)
    c["maskL"] = np.concatenate([(p > q).astype(np.float32)] * 2, axis=1)
    tq = np.arange(TB)[None, :]
    causb = np.zeros((128, NTT, TB), np.float32)
    for r in range(NTT):
        causb[:, r, :] = np.where(p + 128 * r > tq, -BIG, 0.0)
    c["causb"] = causb.astype(bf)
    winlow = np.zeros((128, 2, TB), np.float32)
    winlow[:, 0, :] = np.where(p <= tq, -BIG, 0.0)
    winlow[:, 1, :] = np.where(p <= tq - 128, -BIG, 0.0)
    c["winlow"] = winlow.astype(bf)
    tt = np.arange(T)[None, :]
    c["cmpbias"] = np.where((16 * p + 31 > tt) | (p >= 127), -BIG, 0.0).astype(bf)
    s32 = np.arange(32)[:, None]
    c["zp"] = (tt // 64 == s32).astype(np.float32).astype(bf)
    addc = np.zeros((128, 16, 32), np.float32)
    for t16 in range(16):
        t = 128 * t16 + np.arange(128)[:, None]
        cur = t // 64
        s = np.arange(32)[None, :]
        valid = s <= cur
        forced = (s == 0) | (s == cur) | (s == cur - 1)
        addc[:, t16, :] = np.where(valid, np.where(forced, 1e4, 0.0), -1e4)
    c["addc"] = addc
    j = np.arange(128)[:, None]
    s = np.arange(32)[None, :]
    ov = ((16 * j < 64 * s + 64) & (16 * j + 32 > 64 * s) & (j < 127)).astype(np.float32)
    c["ov32"] = ov
    col = np.arange(248)[None, :]
    c["zs"] = (col - 120 == p // 16).astype(np.float32).astype(bf)
    c["resetm"] = np.tile((np.arange(TB)[None, :] % 128 != 0).astype(np.float32), (128, 1))
    c["bd64"] = ((p // 64) == (q // 64)).astype(np.float32)
    c["hsel"] = np.stack([(np.arange(128) < 64), (np.arange(128) >= 64)], axis=1).astype(np.float32)
    c["zeros"] = np.zeros((128, 512), np.float32).astype(bf)
    p4 = np.arange(4)[:, None]
    q4 = np.arange(4)[None, :]
    su4 = (p4 < q4).astype(np.float32)
    ui4 = (p4 <= q4).astype(np.float32)
    c["mask4s"] = np.concatenate([su4, ui4, su4, ui4], axis=1)
    c["maskLs"] = np.concatenate([(p4 > q4).astype(np.float32)] * 2, axis=1)
    jj = np.arange(128)[:, None]
    s33 = np.arange(33)[None, :]
    c["ovc"] = ((4 * s33 - 1 <= jj) & (jj <= 4 * s33 + 3)).astype(np.float32).astype(bf)
    lastb = np.zeros((128, 16), np.float32)
    lastb[127, :] = -BIG
    c["lastb"] = lastb.astype(bf)
    gsel = np.zeros((16, 4), np.float32)
    for g in range(4):
        for t in range(4):
            gsel[g * 4 + t, t] = 1.0
    c["gsel"] = gsel
    addcs = np.zeros((4, 257), np.float32)
    addcs[:, [0, 255, 256]] = 1e4
    c["addcs"] = addcs
    hl = np.zeros((1, 2, 128), np.float32)
    hl[0, 0, :64] = 1.0
    hl[0, 1, 64:] = 1.0
    c["hl"] = hl.astype(bf)
    tq16 = np.tile(np.arange(4), 4)[None, :]
    c["tailb"] = np.where(np.arange(4)[:, None] > tq16, -BIG, 0.0).astype(np.float32).astype(bf)
    winms = np.ones((128, 4, 2, 4), np.float32)
    winms[:, 0, :, :] = (np.arange(128)[:, None, None] > np.arange(4)[None, None, :]).astype(np.float32)
    c["winms"] = winms.reshape(128, 32).astype(bf)
    c["iota"] = np.stack([2.0 * np.arange(128), 2.0 * np.arange(128) + 1.0], axis=1).astype(np.float32)
    return c


CONST_SPECS = None


def _const_specs():
    global CONST_SPECS
    if CONST_SPECS is None:
        CONST_SPECS = _consts()
    return CONST_SPECS


def _groups():
    g = []
    g.append(("T0", [(1664, 512)], False))
    g.append(("T1", [(2688, 512)], False))
    g.append(("T2", [(3200, 512)], False))
    g.append(("T3", [(3712, 280)], False))
    g.append(("F0", [(1536, 128)], False))
    for fc in range(4):
        g.append((f"R{fc}", [(128 * fc, 128), (512 + 128 * fc, 128), (1024 + 128 * fc, 128)], False))
    g.append(("Q", [(2176, 512)], True))
    g.append(("K", [(3456, 128), (3712, 128)], False))
    return g


GROUPS = _groups()
GIDX = {g[0]: i for i, g in enumerate(GROUPS)}


def build_nc(n_cache_rows, with_sample=True):
    nc = bass.Bass("TRN2", target_bir_lowering=False)
    C = _const_specs()
    dt_of = lambda a: BF16 if a.dtype == ml_dtypes.bfloat16 else F32

    def din(name, shape, dt=F32):
        return nc.dram_tensor(name, list(shape), dt, kind="ExternalInput").ap()

    def dout(name, shape, dt=F32):
        return nc.dram_tensor(name, list(shape), dt, kind="ExternalOutput").ap()

    xp = din("xp", [2, T, 1024])
    w_in = din("w_in", [1024, DW])
    w_out = din("w_out", [1024, 1024])
    norm_g = din("norm_g", [1024])
    final_g = din("final_g", [1024])
    mu = din("mu", [1664])
    vecs = {k: din(k, [512]) for k in ("w0", "a0", "k_k", "k_a", "r_k", "gn_w", "gn_b")}
    w_dup = din("w_dup", [64, 512])
    w_aup = din("w_aup", [64, 512])
    w_cpos = din("w_cpos", [2, 32, 64])
    w_cmix = din("w_cmix", [2, 64, 64])
    cd = {k: din("c_" + k, v.shape, dt_of(v)) for k, v in C.items()}

    yp = dout("yp", [2, T, 1024])
    kvp = dout("kvp", [2, T, 512])
    winp = dout("winp", [2, 512, 256])
    wkvp = dout("wkvp", [2, 8, 64, 64])
    shp = dout("shp", [2, 1664])

    xs = din("xs", [16, 1024])
    cache = din("cache", [n_cache_rows, 512])
    cwin = din("cwin", [4, 512, 256])
    swkv = din("swkv", [4, 8, 64, 64])
    sshift = din("sshift", [4, 1664])
    pt = din("pt", [4, 128], I32)
    ys = dout("ys", [16, 1024])
    kvs = dout("kvs", [16, 512])
    wins = dout("wins", [4, 512, 256])
    wkvs = dout("wkvs", [4, 8, 64, 64])
    shs = dout("shs", [4, 1664])

    wscr = nc.dram_tensor("wscr", [len(GROUPS), 128, 8 * 512], BF16, kind="Internal").ap()

    P = Prog(nc)
    with contextlib.ExitStack() as st:
        def sb(name, shape, dt=F32):
            return st.enter_context(nc.sbuf_tensor(name, list(shape), dt))

        def pst(name, shape, dt=F32):
            return st.enter_context(nc.psum_tensor(name, list(shape), dt))

        def mm(out, lhsT, rhs, start, stop, R, W):
            P.op("pe", lambda e: e.matmul(out, lhsT=lhsT, rhs=rhs, start=start, stop=stop,
                                          skip_group_check=True), R, W)

        def tr(out, in_, ident, R, W):
            P.op("pe", lambda e: e.transpose(out, in_, ident), R, W)

        def act(out, in_, func, R, W, eng="act", **kw):
            P.op("act", lambda e: e.activation(out=out, in_=in_, func=func, **kw), R, W)

        def vop(eng, name, R, W, **kw):
            P.op(eng, lambda e: getattr(e, name)(**kw), R, W)

        def cp(eng, out, in_, R, W):
            if eng == "act":
                P.op("act", lambda e: e.copy(out=out, in_=in_), R, W)
            else:
                P.op(eng, lambda e: e.tensor_copy(out=out, in_=in_), R, W)

        def ld(out, in_, W, R=(), eng="sp", **kw):
            P.dma(eng, out, in_, reads=R, writes=W, **kw)

        def stq(out, in_, R, eng="pool"):
            P.dma(eng, out, in_, reads=R, writes=())

        rr = {"n": 0}

        def evac_eng():
            rr["n"] += 1
            return "act" if rr["n"] % 2 else "dve"

        ct = {}
        for k, v in C.items():
            ct[k] = sb("k_" + k, v.shape, dt_of(v))
            ld(ct[k][:], cd[k], ["k_" + k])
        identf, identb = ct["identf"], ct["identb"]

        yout = sb("yout", [128, 1024])
        tmpf = [sb(f"tmpf{i}", [128, TB]) for i in range(8)]
        stw_t = [sb(f"stw{i}", [128, 512]) for i in range(1)]
        stw = [stw_t[0], stw_t[0]]
        wup_f = stw_t[0]
        g8 = sb("g8", [128, 8])
        gq8 = sb("gq8", [128, 8])
        mu13 = sb("mu13", [128, 13])
        pv = {k: sb("pv_" + k, [128, 4]) for k in ("w0", "a0", "k_k", "k_a", "r_k")}
        omka = sb("omka", [128, 4])
        gnw = sb("gnw", [128, 512])
        gnb = sb("gnb", [128, 512])
        fgb = sb("fgb", [128, 1024])
        wup = sb("wup", [128, 2, 512], BF16)
        wt = sb("wt", [128, 2, 256])
        wmix_f = sb("wmix_f", [128, 2, 64])
        wmix = sb("wmix", [128, 2, 2, 64], BF16)
        xres = sb("xres", [128, 1024])
        wout_f = xres
        wout = sb("wout", [128, 8, 1024], BF16)
        with nc.allow_non_contiguous_dma(reason="small per-feature vectors"):
            ld(g8[:], norm_g.rearrange("(kt p) -> p kt", p=128), ["g8"], allow_slow_non_contiguous=True)
            ld(mu13[:], mu.rearrange("(c p) -> p c", p=128), ["mu13"], allow_slow_non_contiguous=True)
            for k in pv:
                ld(pv[k][:], vecs[k].rearrange("(c p) -> p c", p=128), ["pv_" + k], allow_slow_non_contiguous=True)
        ld(gnw[:], vecs["gn_w"].rearrange("(o f) -> o f", o=1).to_broadcast([128, 512]), ["gnw"])
        ld(gnb[:], vecs["gn_b"].rearrange("(o f) -> o f", o=1).to_broadcast([128, 512]), ["gnb"])
        ld(fgb[:], final_g.rearrange("(o f) -> o f", o=1).to_broadcast([128, 1024]), ["fgb"])
        ld(wup_f[0:64, :], w_dup, ["stw0"])
        ld(wup_f[64:128, :], w_aup, ["stw0"])
        vop("pool", "memset", [], ["wup"], ap=wup[:], constant=0.0)
        cp("dve", wup[0:64, 0, :], wup_f[0:64, :], ["stw0"], ["wup"])
        cp("dve", wup[64:128, 1, :], wup_f[64:128, :], ["stw0"], ["wup"])
        vop("dve", "tensor_scalar", ["g8"], ["gq8"], out=gq8[:], in0=g8[:], scalar1=0.125, scalar2=None, op0=ALU.mult)
        vop("dve", "tensor_scalar", ["pv_k_a"], ["omka"], out=omka[:], in0=pv["k_a"][:], scalar1=-1.0, scalar2=1.0,
            op0=ALU.mult, op1=ALU.add)
        with nc.allow_non_contiguous_dma(reason="compress weights"):
            for m in range(2):
                for e in range(2):
                    for k in range(2):
                        for blk in range(8):
                            ld(wt[16 * blk:16 * blk + 16, m, e * 128 + k * 64: e * 128 + k * 64 + 64],
                               w_cpos[e, 16 * m:16 * m + 16, :], ["wt"])
            for e in range(2):
                ld(wmix_f[0:64, e, :], w_cmix[e], ["wmix_f"])
                ld(wmix_f[64:128, e, :], w_cmix[e], ["wmix_f"])
        vop("pool", "memset", [], ["wmix"], ap=wmix[:], constant=0.0)
        cp("dve", wmix[0:64, 0, :, :], wmix_f[0:64, :, :], ["wmix_f"], ["wmix"])
        cp("dve", wmix[64:128, 1, :, :], wmix_f[64:128, :, :], ["wmix_f"], ["wmix"])
        for kt in range(8):
            ld(wout_f[:], w_out[kt * 128:(kt + 1) * 128, :], ["xres"])
            cp("act" if kt % 2 else "dve", wout[:, kt, :], wout_f[:], ["xres"], ["wout"])

        wbuf = [sb(f"wbuf{i}", [128, 8, 512], BF16) for i in range(2)]
        stg = [(stw_t[0][:, 0:512], "stw0"), (yout[:, 0:512], "yout"), (yout[:, 512:1024], "youtb")]
        sidx = 0
        for gi, (gname, segs, qs) in enumerate(GROUPS):
            sbuf = wbuf[gi % 2]
            key = f"wbuf{gi % 2}"
            for kt in range(8):
                s_, skey = stg[sidx % 3]
                sidx += 1
                off = 0
                for (c0, ncol) in segs:
                    ld(s_[:, off:off + ncol], w_in[kt * 128:(kt + 1) * 128, c0:c0 + ncol], [skey])
                    off += ncol
                gsrc = gq8 if qs else g8
                if sidx % 2:
                    act(sbuf[:, kt, 0:off], s_[:, 0:off], AF.Copy, [skey, "g8", "gq8"], [key], scale=gsrc[:, kt:kt + 1])
                else:
                    vop("dve", "tensor_scalar", [skey, "g8", "gq8"], [key], out=sbuf[:, kt, 0:off], in0=s_[:, 0:off],
                        scalar1=gsrc[:, kt:kt + 1], scalar2=None, op0=ALU.mult)
            P.dma("sp", wscr[gi].rearrange("p (k c) -> p k c", k=8)[:, :, 0:off], sbuf[:, :, 0:off],
                  reads=[key], writes=["wscr"])

        xt = [sb(f"xt{i}", [128, 1024]) for i in range(2)]
        ss = sb("ss", [128, 1])
        rstd = sb("rstd", [128, 1])
        xn = [sb(f"xn{i}", [128, 1024], BF16) for i in range(2)]
        xnT = sb("xnT", [128, 8, TB], BF16)
        gsr = sb("gsr", [128, NTT, 512], BF16)
        gsn = sb("gsn", [128, NTT, 512], BF16)
        kv0 = [sb(f"kv0_{i}", [128, 512]) for i in range(2)]
        kv1 = [sb(f"kv1_{i}", [128, 280]) for i in range(2)]
        gates = sb("gates", [128, NTT, 24])
        ym = [sb(f"ym{i}", [128, 256], BF16) for i in range(2)]
        QT = sb("QT", [64, 8, TB], BF16)
        KTs = sb("KTs", [64, 2, T], BF16)
        KTw = sb("KTw", [64, 2, T], BF16)
        Vs = sb("Vs", [128, 16, 2, 65], BF16)
        Vw = sb("Vw", [128, 16, 2, 65], BF16)
        pooled = sb("pooled", [128, 2, 128])
        pooledb = sb("pooledb", [128, 2, 128], BF16)
        kcT = sb("kcT", [64, 2, 128], BF16)
        vcx = sb("vcx", [128, 2, 97], BF16)
        zsb12 = sb("zsb12", [128, TB + 1])
        zsb = sb("zsb", [128, 3, TB + 1])
        zs12 = sb("zs12", [128, TB])
        zs = sb("zs", [128, 3, TB])
        lora = sb("lora", [128, TB], BF16)
        AR = sb("AR", [128, NTT, 2, 128])
        BK = sb("BK", [128, 2, TB])
        Wc = sb("Wc", [128, NTT * 8])
        prod = sb("prod", [128, TB])
        STz = sb("STz", [128, 4, 2, 64])
        BKz = sb("BKz", [128, 2, 2, TB])
        wkv2 = sb("wkv2", [128, 64])
        Ssb = sb("Ssb", [128, 128])
        tok = sb("tok", [128, 3, 2, 64])
        Mt = [sb(f"Mt{i}", [128, 512]) for i in range(2)]
        PT0 = sb("PT0", [128, 2, 128])
        Pk = [sb(f"Pk{i}", [128, 2, 2, 128]) for i in range(6)]
        Xa = [sb(f"Xa{i}", [128, 2, 64]) for i in range(2)]
        ytok = sb("ytok", [128, NTT, 2, 64])
        vtok = sb("vtok", [128, NTT, 2, 64])
        rk = sb("rk", [128, NTT, 2])
        st4 = sb("st4", [128, 4])
        gtmp = [sb(f"gtmp{i}", [128, 2, 64]) for i in range(3)]
        mix = sb("mix", [128, NTT, 1024], BF16)
        mixT = sb("mixT", [128, 8, 128], BF16)
        sq_junk = mixT[:].rearrange("p k t -> p (k t)")
        PTb = [sb(f"PTb{i}", [128, TB], BF16) for i in range(3)]
        oacc = sb("oacc", [128, NTT, 8, 64])
        otmp = sb("otmp", [128, NTT, 64])
        rden = sb("rden", [128, NTT])
        scg = sb("scg", [128, NTT])
        sc = sb("sc", [128, NTT, 2, 32])
        scw = sb("scw", [128, 32])
        m8a = sb("m8a", [128, 8])
        m8b = sb("m8b", [128, 8])
        thr = sb("thr", [128, 1])
        selb = sb("selb", [128, 32], BF16)
        selbT = sb("selbT", [32, 2, TB], BF16)
        zc = sb("zc", [128, 13])

        Vn = sb("Vn", [4, 2, 2, 65], BF16)
        QTs = sb("QTs", [64, 8, 4], BF16)
        QTz = sb("QTz", [128, 2, 16], BF16)
        KTn = sb("KTn", [64, 2, 2, 4], BF16)
        ptb = sb("ptb", [128, 128], I32)
        ptf = sb("ptf", [128, 128])
        pidx = sb("pidx", [128, 2, 128], I32)
        cache_h = cache.rearrange("n (h c) -> (n h) c", h=2)
        pg4 = [xres[:].rearrange("p (g c) -> p g c", g=4), yout[:].rearrange("p (g c) -> p g c", g=4)]
        pall = Vs[:].rearrange("p a k c -> p (a k c)")[:, 0:2048].rearrange("p (c j) -> p c j", c=2)
        kcTs = Vw[0:64, :, :, :].rearrange("p a k c -> p (a k c)")[:, 0:2048].rearrange("p (c j) -> p c j", c=2)
        vcs = ct["cmpbias"][:, 0:1040].rearrange("p (j k c) -> p j k c", j=8, k=2)
        selflat4 = ct["zp"][0:1, 0:2048].rearrange("o (h g k t) -> o h g k t", h=2, g=128, k=2)
        ocs1 = stw_t[0][0:16, 0:322]
        obr = sb("obr", [16, 3, 2, 64])
        scs = kv1[1][0:4, 0:257]
        scs2 = kv0[1][0:4, 0:257]
        KTpg = sb("KTpg", [128, 4, 128], BF16)
        Vpg = sb("Vpg", [128, 4, 2, 65], BF16)
        mexp = sb("mexp", [128, 32], BF16)

        psA = pst("psA", [128, 512])
        psB = pst("psB", [128, 512])
        psT = pst("psT", [128, 1024], BF16)
        psP = pst("psP", [128, 512])
        psR = [pst(f"psR{i}", [128, 512]) for i in range(4)]

        sems = {}
        for e in ENGS:
            sems[e] = st.enter_context(nc.semaphore("s_" + e))
            for i in range(NDMASEM):
                sems[(e, i)] = st.enter_context(nc.semaphore(f"d_{e}_{i}"))
        block = st.enter_context(nc.Block())

        wb_n = {"n": 0}

        def load_group(gname):
            gi = GIDX[gname]
            i = wb_n["n"] % 2
            wb_n["n"] += 1
            ld(wbuf[i][:], wscr[gi].rearrange("p (k c) -> p k c", k=8), [f"wbuf{i}"], R=["wscr"])
            return wbuf[i], f"wbuf{i}"

        vop("pool", "memset", [], ["Vs"], ap=Vs[:], constant=1.0)
        vop("pool", "memset", [], ["Vw"], ap=Vw[:], constant=1.0)

        identr_t = sb("identr", [128, 128])
        identr = identr_t[:].bitcast(F32R)
        cp("dve", identr, identf[:], ["k_identf"], ["identr"])
        psAB = [(psA, "psA"), (psB, "psB")]
        pn = {"n": 0}

        def nextps():
            pn["n"] += 1
            return psAB[pn["n"] % 2]

        R0, R1, R2, R3 = psR
        sps = [(R1, "psR1"), (R2, "psR2")]
        sn = {"n": 0}
        accs = [(R3, "psR3a"), (psA, "psA")]
        an = {"n": 0}

        def rmsnorm_T(xsrc, npart, tt):
            x_ = xt[tt % 2]
            xk = f"xt{tt % 2}"
            pp = slice(0, npart)
            ld(x_[pp, :], xsrc, [xk])
            act(sq_junk[pp, :], x_[pp, :], AF.Square, [xk], ["mixT", "ss"], accum_out=ss[pp, :])
            act(rstd[pp, :], ss[pp, :], AF.Sqrt, ["ss"], ["rstd"], scale=1.0 / 1024, bias=EPS)
            vop("dve", "reciprocal", ["rstd"], ["rstd"], out=rstd[pp, :], in_=rstd[pp, :])
            xn_ = xn[tt % 2]
            xnk = f"xn{tt % 2}"
            vop("dve", "tensor_scalar", [xk, "rstd"], [xnk], out=xn_[pp, :], in0=x_[pp, :], scalar1=rstd[pp, 0:1],
                scalar2=None, op0=ALU.mult)
            for kt in range(8):
                tr(psT[:, kt * 128:kt * 128 + npart], xn_[pp, kt * 128:(kt + 1) * 128], identb[pp, pp],
                   [xnk, "k_identb"], ["psT"])
            cp("act", xnT[:, :, tt * 128:tt * 128 + npart], psT[:].rearrange("p (k t) -> p k t", k=8)[:, :, 0:npart],
               ["psT"], ["xnT"])

        def rwkv_block(n, C, nlev, npart_of_chunk, gate_cols):
            nch = n // C
            use_r = (C == 128)
            Rm = (lambda ap: ap.bitcast(F32R)) if use_r else (lambda ap: ap)
            Ro = lambda ap: ap.bitcast(F32R)
            idm = identr if use_r else identf
            idk = "identr" if use_r else "k_identf"
            wb, wk = load_group("F0")
            ps_, pk = nextps()
            for kt in range(8):
                mm(ps_[:, 0:n], wb[:, kt, 0:128], xnT[:, kt, 0:n], kt == 0, kt == 7, ["xnT", wk], [pk])
            cp("dve", zsb12[:, 0:1], zc[:, 12:13], ["zc"], ["zsb12"])
            cp("act", zsb12[:, 1:n + 1], ps_[:, 0:n], [pk], ["zsb12"])
            cp("dve", zc[:, 12:13], zsb12[:, n:n + 1], ["zsb12"], ["zc"])
            vop("dve", "tensor_tensor", ["zsb12"], ["tmpf5"], out=tmpf[5][:, 0:n], in0=zsb12[:, 0:n],
                in1=zsb12[:, 1:n + 1], op=ALU.subtract)
            vop("dve", "scalar_tensor_tensor", ["tmpf5", "mu13", "zsb12"], ["zs12"], out=zs12[:, 0:n], in0=tmpf[5][:, 0:n],
                scalar=mu13[:, 12:13], in1=zsb12[:, 1:n + 1], op0=ALU.mult, op1=ALU.add)
            act(lora[0:64, 0:n], zs12[0:64, 0:n], AF.Tanh, ["zs12"], ["lora"])
            cp("dve", lora[64:128, 0:n], zs12[64:128, 0:n], ["zs12"], ["lora"])
            m4 = ct["mask4"] if C == 128 else ct["mask4s"]
            mL = ct["maskL"] if C == 128 else ct["maskLs"]
            m4k = "k_mask4" if C == 128 else "k_mask4s"
            mLk = "k_maskL" if C == 128 else "k_maskLs"
            cpp = slice(0, C)
            for fc in range(4):
                wb, wk = load_group(f"R{fc}")
                for j3 in range(3):
                    ps_, pk = nextps()
                    cidx = j3 * 4 + fc
                    for kt in range(8):
                        mm(ps_[:, 0:n], wb[:, kt, j3 * 128:(j3 + 1) * 128], xnT[:, kt, 0:n], kt == 0, kt == 7,
                           ["xnT", wk], [pk])
                    cp("dve", zsb[:, j3, 0:1], zc[:, cidx:cidx + 1], ["zc"], ["zsb"])
                    cp("act", zsb[:, j3, 1:n + 1], ps_[:, 0:n], [pk], ["zsb"])
                    cp("dve", zc[:, cidx:cidx + 1], zsb[:, j3, n:n + 1], ["zsb"], ["zc"])
                    vop("dve", "tensor_tensor", ["zsb"], [f"tmpf{5 + j3}"], out=tmpf[5 + j3][:, 0:n], in0=zsb[:, j3, 0:n],
                        in1=zsb[:, j3, 1:n + 1], op=ALU.subtract)
                    vop("dve", "scalar_tensor_tensor", [f"tmpf{5 + j3}", "mu13", "zsb"], ["zs"], out=zs[:, j3, 0:n],
                        in0=tmpf[5 + j3][:, 0:n], scalar=mu13[:, cidx:cidx + 1], in1=zsb[:, j3, 1:n + 1],
                        op0=ALU.mult, op1=ALU.add)
                r_, k_, v_ = zs[:, 0, 0:n], zs[:, 1, 0:n], zs[:, 2, 0:n]
                sg, al, lw, cum, t4, t5, t6, t7 = [t[:, 0:n] for t in tmpf]
                K = lambda *i: [f"tmpf{j}" for j in i]
                ps_, pk = nextps()
                mm(ps_[:, 0:n], wup[:, 0, fc * 128:(fc + 1) * 128], lora[:, 0:n], True, True, ["wup", "lora"], [pk])
                act(sg, ps_[:, 0:n], AF.Sigmoid, [pk, "pv_w0"], K(0), bias=pv["w0"][:, fc:fc + 1])
                vop("dve", "tensor_scalar", K(0), K(2), out=lw, in0=sg, scalar1=-DEC_C, scalar2=None, op0=ALU.mult)
                ps_, pk = nextps()
                mm(ps_[:, 0:n], wup[:, 1, fc * 128:(fc + 1) * 128], lora[:, 0:n], True, True, ["wup", "lora"], [pk])
                act(al, ps_[:, 0:n], AF.Sigmoid, [pk, "pv_a0"], K(1), bias=pv["a0"][:, fc:fc + 1])
                vop("dve", "tensor_scalar", ["zs", "pv_k_k"], K(4), out=t4, in0=k_, scalar1=pv["k_k"][:, fc:fc + 1],
                    scalar2=None, op0=ALU.mult)
                vop("dve", "tensor_tensor", K(4), K(5), out=t5, in0=t4, in1=t4, op=ALU.mult)
                ps_, pk = nextps()
                mm(ps_[:, 0:n], ct["bd64"][:], t5, True, True, ["k_bd64"] + K(5), [pk])
                vop("dve", "tensor_scalar", [pk], K(5), out=t5, in0=ps_[:, 0:n], scalar1=1e-24, scalar2=None, op0=ALU.max)
                act(t5, t5, AF.Sqrt, K(5), K(5))
                vop("dve", "reciprocal", K(5), K(5), out=t5, in_=t5)
                vop("dve", "tensor_tensor", K(4, 5), K(4), out=t4, in0=t4, in1=t5, op=ALU.mult)
                vop("dve", "tensor_scalar", K(1) + ["pv_k_a", "omka"], K(5), out=t5, in0=al,
                    scalar1=pv["k_a"][:, fc:fc + 1], scalar2=omka[:, fc:fc + 1], op0=ALU.mult, op1=ALU.add)
                vop("dve", "tensor_tensor", ["zs"] + K(5), K(5), out=t5, in0=k_, in1=t5, op=ALU.mult)
                vop("dve", "tensor_tensor_scan", ["k_resetm"] + K(2), K(3), out=cum, data0=ct["resetm"][:, 0:n] if C == 128 else ct["resetm"][:, 0:n],
                    data1=lw, initial=0.0, op0=ALU.mult, op1=ALU.add)
                act(t6, cum, AF.Exp, K(3), K(6))
                for ch in range(nch):
                    cs = slice(ch * C, (ch + 1) * C)
                    cp("dve", Wc[:, 8 * ch:8 * ch + 1], t6[:, ch * C + C - 1: ch * C + C], K(6), ["Wc"])
                    vop("dve", "tensor_tensor", ["zs"] + K(6), ["AR"], out=Ro(AR[:, ch, 1, 0:C]), in0=r_[:, cs], in1=t6[:, cs], op=ALU.mult)
                vop("dve", "scalar_tensor_tensor", ["zs", "pv_r_k"] + K(5), ["prod"], out=prod[:, 0:n], in0=r_,
                    scalar=pv["r_k"][:, fc:fc + 1], in1=t5, op0=ALU.mult, op1=ALU.mult)
                act(t7, cum, AF.Exp, K(3), K(7), scale=-1.0)
                vop("dve", "tensor_tensor", K(5, 7), ["BK"], out=Ro(BK[:, 1, 0:n]), in0=t5, in1=t7, op=ALU.mult)
                for hh in range(2):
                    rws = slice(hh * 64, hh * 64 + 64)
                    cp("act", Ro(BKz[rws, hh, 1, 0:n]), BK[rws, 1, 0:n], ["BK"], ["BKz"])
                vop("dve", "tensor_tensor", K(4, 1), K(5), out=t5, in0=t4, in1=al, op=ALU.mult)
                vop("dve", "tensor_tensor", K(5, 7), ["BK"], out=Ro(BK[:, 0, 0:n]), in0=t5, in1=t7, op=ALU.mult)
                for hh in range(2):
                    rws = slice(hh * 64, hh * 64 + 64)
                    cp("act", Ro(BKz[rws, hh, 0, 0:n]), BK[rws, 0, 0:n], ["BK"], ["BKz"])
                vop("dve", "tensor_tensor", K(3, 2), K(6), out=t6, in0=cum, in1=lw, op=ALU.subtract)
                act(t6, t6, AF.Exp, K(6), K(6))
                for ch in range(nch):
                    cs = slice(ch * C, (ch + 1) * C)
                    vop("dve", "scalar_tensor_tensor", K(4, 6), ["AR"], out=Ro(AR[:, ch, 0, 0:C]), in0=t4[:, cs], scalar=-1.0,
                        in1=t6[:, cs], op0=ALU.mult, op1=ALU.mult)
                chk(3)
                for ch in range(nch):
                    cs = slice(ch * C, (ch + 1) * C)
                    ARf = AR[:, ch, :, 0:C]
                    tr(R0[cpp, 0:128], BK[:, 0, cs], identf[:], ["BK", "k_identf"], ["psR0"])
                    tr(R0[cpp, 128:256], BK[:, 1, cs], identf[:], ["BK", "k_identf"], ["psR0"])
                    tr(R0[cpp, 256:384], zs[:, 2, cs], identf[:], ["zs", "k_identf"], ["psR0"])
                    cp("act", Ro(tok[cpp].rearrange("p a h d -> p (a h d)")), R0[cpp, 0:384], ["psR0"], ["tok"])
                    cp("dve", vtok[cpp, ch, :, :], tok[cpp, 2, :, :], ["tok"], ["vtok"])
                    for hh in range(2):
                        for a2 in range(2):
                            for a3 in range(2):
                                mm((R1, R2)[hh][cpp, (2 * a2 + a3) * C:(2 * a2 + a3 + 1) * C], Rm(BKz[:, hh, a2, cs]), Rm(AR[:, ch, a3, 0:C]), True, True,
                                   ["BKz", "AR"], [("psR1", "psR2")[hh]])
                        mm(R3[cpp, 256 + hh * C: 256 + (hh + 1) * C], Rm(AR[:, ch, 0, 0:C]), Rm(BKz[:, hh, 0, cs]), True, True,
                           ["AR", "BKz"], ["psR3b"])
                    for hh in range(2):
                        vop("dve", "tensor_tensor", [("psR1", "psR2")[hh], m4k], [f"Mt{hh}"], out=Ro(Mt[hh][cpp, 0:4 * C]), in0=(R1, R2)[hh][cpp, 0:4 * C],
                            in1=m4[cpp, 0:4 * C], op=ALU.mult)
                    for hh in range(0):
                        mm(R3[cpp, 256 + hh * C: 256 + (hh + 1) * C], Rm(AR[:, ch, 0, 0:C]), Rm(BKz[:, hh, 0, cs]), True, True,
                           ["AR", "BKz"], ["psR3b"])
                    vop("dve", "tensor_tensor", ["psR3b", mLk], ["PT0"], out=Ro(PT0[cpp, :, 0:C]),
                        in0=R3[cpp, 256:256 + 2 * C].rearrange("p (h t) -> p h t", h=2), in1=mL[cpp, 0:2 * C].rearrange("p (h t) -> p h t", h=2), op=ALU.mult)
                    for lv in range(nlev - 1):
                        for hh in range(2):
                            if lv == 0:
                                Pm, PTm, kk_ = Mt[hh][cpp, 0:C], PT0[cpp, hh, 0:C], [f"Mt{hh}", "PT0"]
                            else:
                                Pm, PTm, kk_ = Pk[lv - 1][cpp, hh, 0, 0:C], Pk[lv - 1][cpp, hh, 1, 0:C], [f"Pk{lv - 1}"]
                            mm(R2[cpp, (2 * hh) * C:(2 * hh + 1) * C], Rm(PTm), Rm(Pm), True, True, kk_, ["psR2"])
                            mm(R2[cpp, (2 * hh + 1) * C:(2 * hh + 2) * C], Rm(Pm), Rm(PTm), True, True, kk_, ["psR2"])
                        cp(evac_eng(), Ro(Pk[lv][cpp, :, :, 0:C]), R2[cpp, 0:4 * C].rearrange("p (h a t) -> p h a t", h=2, a=2), ["psR2"], [f"Pk{lv}"])
                    for hh in range(2):
                        mm(R3[cpp, hh * 64:(hh + 1) * 64], Rm(AR[:, ch, 0, 0:C]), Rm(STz[:, fc, hh, :]), True, False, ["AR", "STz"], ["psR3a"])
                        mm(R3[cpp, hh * 64:(hh + 1) * 64], Rm(Mt[hh][cpp, 2 * C:3 * C]), Rm(tok[cpp, 2, hh, :]), False, True,
                           [f"Mt{hh}", "tok"], ["psR3a"])
                    xi = 0
                    cp("act", Ro(Xa[0][cpp].rearrange("p h d -> p (h d)")), R3[cpp, 0:128], ["psR3a"], ["Xa0"])
                    for lv in range(nlev):
                        for hh in range(2):
                            if lv == 0:
                                Pm, kk_ = Mt[hh][cpp, 0:C], [f"Mt{hh}"]
                            else:
                                Pm, kk_ = Pk[lv - 1][cpp, hh, 0, 0:C], [f"Pk{lv - 1}"]
                            mm(R3[cpp, hh * 64:(hh + 1) * 64], Rm(Pm), Rm(Xa[xi][cpp, hh, :]), True, True, kk_ + [f"Xa{xi}"], ["psR3a"])
                        vop("dve", "tensor_tensor", ["psR3a", f"Xa{xi}"], [f"Xa{1 - xi}"], out=Ro(Xa[1 - xi][cpp].rearrange("p h d -> p (h d)")),
                            in0=R3[cpp, 0:128], in1=Xa[xi][cpp].rearrange("p h d -> p (h d)"), op=ALU.add)
                        xi = 1 - xi
                    E = Xa[xi]
                    ek = f"Xa{xi}"
                    for hh in range(2):
                        mm(R3[cpp, hh * 64:(hh + 1) * 64], Rm(AR[:, ch, 1, 0:C]), Rm(STz[:, fc, hh, :]), True, False, ["AR", "STz"], ["psR3a"])
                        mm(R3[cpp, hh * 64:(hh + 1) * 64], Rm(Mt[hh][cpp, C:2 * C]), Rm(E[cpp, hh, :]), False, False, [f"Mt{hh}", ek], ["psR3a"])
                        mm(R3[cpp, hh * 64:(hh + 1) * 64], Rm(Mt[hh][cpp, 3 * C:4 * C]), Rm(tok[cpp, 2, hh, :]), False, True,
                           [f"Mt{hh}", "tok"], ["psR3a"])
                    cp("act", ytok[cpp, ch, :, :].rearrange("p h d -> p (h d)"), R3[cpp, 0:128], ["psR3a"], ["ytok"])
                    SU = psB[:, 0:128]
                    mm(SU, idm[:], Rm(STz[:, fc, :, :].rearrange("p h d -> p (h d)")), True, False, [idk, "STz"], ["psB"])
                    mm(SU, Rm(tok[cpp, 0, :, :].rearrange("p h d -> p (h d)")), Rm(E[cpp].rearrange("p h d -> p (h d)")), False, False, ["tok", ek], ["psB"])
                    mm(SU, Rm(tok[cpp, 1, :, :].rearrange("p h d -> p (h d)")), Rm(tok[cpp, 2, :, :].rearrange("p h d -> p (h d)")), False, True, ["tok"], ["psB"])
                    cp("dve", Ssb[:], psB[:, 0:128], ["psB"], ["Ssb"])
                    for hh in range(2):
                        rows = slice(hh * 64, hh * 64 + 64)
                        act(Ro(STz[rows, fc, hh, :]), Ssb[rows, hh * 64:(hh + 1) * 64], AF.Copy, ["Ssb", "Wc"], ["STz"],
                            scale=Wc[rows, 8 * ch:8 * ch + 1])
                chk(4)
                for ch in range(nch):
                    cs = slice(ch * C, (ch + 1) * C)
                    mm(psB[cpp, 0:2], prod[:, cs], ct["hsel"][:], True, True, ["prod", "k_hsel"], ["psB"])
                    cp("dve", rk[cpp, ch, :], psB[cpp, 0:2], ["psB"], ["rk"])
                    y2 = ytok[cpp, ch, :, :]
                    g0, g1, g2 = [g[cpp] for g in gtmp]
                    s4 = st4[cpp]
                    vop("dve", "tensor_reduce", ["ytok"], ["st4"], out=s4[:, 0:2], in_=y2, axis=AX.X, op=ALU.add)
                    vop("dve", "tensor_tensor", ["ytok"], ["gtmp0"], out=g0, in0=y2, in1=y2, op=ALU.mult)
                    vop("dve", "tensor_reduce", ["gtmp0"], ["st4"], out=s4[:, 2:4], in_=g0, axis=AX.X, op=ALU.add)
                    vop("dve", "tensor_scalar", ["st4"], ["st4"], out=s4[:, 0:2], in0=s4[:, 0:2], scalar1=1.0 / 64, scalar2=None, op0=ALU.mult)
                    vop("dve", "tensor_tensor", ["st4"], ["gtmp1"], out=g1[:, :, 0], in0=s4[:, 0:2], in1=s4[:, 0:2], op=ALU.mult)
                    vop("dve", "scalar_tensor_tensor", ["st4", "gtmp1"], ["st4"], out=s4[:, 2:4], in0=s4[:, 2:4], scalar=1.0 / 64,
                        in1=g1[:, :, 0], op0=ALU.mult, op1=ALU.subtract)
                    act(s4[:, 2:4], s4[:, 2:4], AF.Sqrt, ["st4"], ["st4"], bias=GN_EPS)
                    vop("dve", "reciprocal", ["st4"], ["st4"], out=s4[:, 2:4], in_=s4[:, 2:4])
                    vop("dve", "tensor_tensor", ["ytok", "st4"], ["gtmp0"], out=g0, in0=y2,
                        in1=s4[:, 0:2].rearrange("p (a o) -> p a o", o=1).to_broadcast([C, 2, 64]), op=ALU.subtract)
                    vop("dve", "tensor_tensor", ["gtmp0", "st4"], ["gtmp0"], out=g0, in0=g0,
                        in1=s4[:, 2:4].rearrange("p (a o) -> p a o", o=1).to_broadcast([C, 2, 64]), op=ALU.mult)
                    gw = gnw[cpp, fc * 128:(fc + 1) * 128].rearrange("p (h d) -> p h d", h=2)
                    gb = gnb[cpp, fc * 128:(fc + 1) * 128].rearrange("p (h d) -> p h d", h=2)
                    vop("dve", "tensor_tensor", ["gtmp0", "gnw"], ["gtmp0"], out=g0, in0=g0, in1=gw, op=ALU.mult)
                    vop("dve", "tensor_tensor", ["gtmp0", "gnb"], ["gtmp0"], out=g0, in0=g0, in1=gb, op=ALU.add)
                    vop("dve", "tensor_tensor", ["vtok", "rk"], ["gtmp1"], out=g1, in0=vtok[cpp, ch, :, :],
                        in1=rk[cpp, ch, :].rearrange("p (a o) -> p a o", o=1).to_broadcast([C, 2, 64]), op=ALU.mult)
                    vop("dve", "tensor_tensor", ["gtmp0", "gtmp1"], ["gtmp0"], out=g0, in0=g0, in1=g1, op=ALU.add)
                    vop("dve", "tensor_tensor", ["gtmp0", "gsr"], ["mix"],
                        out=mix[cpp, ch, fc * 128:(fc + 1) * 128].rearrange("p (h d) -> p h d", h=2), in0=g0,
                        in1=gsr[cpp, ch, fc * 128:(fc + 1) * 128].rearrange("p (h d) -> p h d", h=2), op=ALU.mult)
                chk(4.5)

        def shift_and_state_out(shdst, wkvdst):
            for rnd in range(4):
                ncs = 4 if rnd < 3 else 1
                for ci in range(ncs):
                    c13 = rnd * 4 + ci
                    tr(R0[0:1, ci * 128:(ci + 1) * 128], zc[:, c13:c13 + 1], identf[:], ["zc", "k_identf"], ["psR0"])
                dst = xres if rnd < 2 else yout
                dk = "xres" if rnd < 2 else "yout"
                o_ = (rnd % 2) * 512
                cp("dve", dst[0:1, o_:o_ + ncs * 128], R0[0:1, 0:ncs * 128], ["psR0"], [dk])
            stq(shdst[:, 0:1024], xres[0:1, 0:1024], ["xres"])
            stq(shdst[:, 1024:1664], yout[0:1, 0:640], ["yout"])
            for fc in range(4):
                tr(R0[:, 0:128], STz[:, fc, :, :].rearrange("p h d -> p (h d)"), identf[:], ["STz", "k_identf"], ["psR0"])
                cp("dve", Ssb[:], R0[:, 0:128], ["psR0"], ["Ssb"])
                for hh in range(2):
                    rws = slice(hh * 64, hh * 64 + 64)
                    cp("act", wkv2[rws, :], Ssb[rws, hh * 64:(hh + 1) * 64], ["Ssb"], ["wkv2"])
                stq(wkvdst[2 * fc:2 * fc + 2, :, :].rearrange("h i j -> (h i) j"), wkv2[:], ["wkv2"])

        def proj_qk(n, kdst):
            wb, wk = load_group("Q")
            for h in range(8):
                ps_, pk = nextps()
                for kt in range(8):
                    mm(ps_[0:64, 0:n], wb[:, kt, h * 64:(h + 1) * 64], xnT[:, kt, 0:n], kt == 0, kt == 7, ["xnT", wk], [pk])
                cp(evac_eng(), QT[:, h, 0:n], ps_[0:64, 0:n], [pk], ["QT"])
            wb, wk = load_group("K")
            for si in range(2):
                for kvh in range(2):
                    ps_, pk = nextps()
                    c0 = si * 128 + kvh * 64
                    for kt in range(8):
                        mm(ps_[0:64, 0:n], wb[:, kt, c0:c0 + 64], xnT[:, kt, 0:n], kt == 0, kt == 7, ["xnT", wk], [pk])
                    dst, dk = kdst(si, kvh)
                    cp(evac_eng(), dst, ps_[0:64, 0:n], [pk], [dk])

        def finish(h, br, ps_, stride, first, npart, ntt, pkey="psR3a"):
            pp = slice(0, npart)
            view = ps_[pp, 0:ntt * stride].rearrange("p (t c) -> p t c", t=ntt)
            vop("dve", "tensor_scalar", [pkey], ["rden"], out=rden[pp, 0:ntt], in0=view[:, :, 64], scalar1=1e-30, scalar2=None,
                op0=ALU.max)
            vop("dve", "reciprocal", ["rden"], ["rden"], out=rden[pp, 0:ntt], in_=rden[pp, 0:ntt])
            vop("dve", "tensor_tensor", ["rden", "gates"], ["scg"], out=scg[pp, 0:ntt], in0=rden[pp, 0:ntt], in1=gates[pp, 0:ntt, br * 8 + h], op=ALU.mult)
            bc = scg[pp, 0:ntt].rearrange("p (a o) -> p a o", o=1).to_broadcast([npart, ntt, 64])
            if first:
                vop("dve", "tensor_tensor", [pkey, "scg"], ["oacc"], out=oacc[pp, 0:ntt, h, :], in0=view[:, :, 0:64], in1=bc, op=ALU.mult)
            else:
                vop("dve", "tensor_tensor", [pkey, "scg"], ["otmp"], out=otmp[pp, 0:ntt, :], in0=view[:, :, 0:64], in1=bc, op=ALU.mult)
                vop("dve", "tensor_tensor", ["otmp", "oacc"], ["oacc"], out=oacc[pp, 0:ntt, h, :], in0=oacc[pp, 0:ntt, h, :], in1=otmp[pp, 0:ntt, :], op=ALU.add)

        def out_proj(npart, tt, xsrc, ydst):
            pp = slice(0, npart)
            vop("dve", "tensor_tensor", ["oacc", "gsn"], ["mix"], out=mix[pp, tt, 512:1024],
                in0=oacc[pp, tt, :, :].rearrange("p h d -> p (h d)"), in1=gsn[pp, tt, :], op=ALU.mult)
            for kt in range(8):
                tr(psT[:, kt * 128:kt * 128 + npart], mix[pp, tt, kt * 128:(kt + 1) * 128], identb[pp, pp], ["mix", "k_identb"], ["psT"])
            cp("act", mixT[:, :, 0:npart], psT[:].rearrange("p (k t) -> p k t", k=8)[:, :, 0:npart], ["psT"], ["mixT"])
            ld(xres[pp, :], xsrc, ["xres"])
            for nchunk, (ps_, pk) in enumerate(psAB):
                for kt in range(8):
                    mm(ps_[pp, 0:512], mixT[:, kt, 0:npart], wout[:, kt, nchunk * 512:(nchunk + 1) * 512], kt == 0, kt == 7,
                       ["mixT", "wout"], [pk])
                vop("dve", "tensor_tensor", [pk, "xres"], ["yout"], out=yout[pp, nchunk * 512:(nchunk + 1) * 512], in0=ps_[pp, 0:512],
                    in1=xres[pp, nchunk * 512:(nchunk + 1) * 512], op=ALU.add)
            act(sq_junk[pp, :], yout[pp, :], AF.Square, ["yout"], ["mixT", "ss"], accum_out=ss[pp, :])
            act(rstd[pp, :], ss[pp, :], AF.Sqrt, ["ss"], ["rstd"], scale=1.0 / 1024, bias=EPS)
            vop("dve", "reciprocal", ["rstd"], ["rstd"], out=rstd[pp, :], in_=rstd[pp, :])
            vop("dve", "scalar_tensor_tensor", ["yout", "rstd", "fgb"], ["yout"], out=yout[pp, :], in0=yout[pp, :], scalar=rstd[pp, 0:1],
                in1=fgb[pp, :], op0=ALU.mult, op1=ALU.mult)
            stq(ydst, yout[pp, :], ["yout"])

        def sample_jobs():
            ovc = ct["ovc"]
            vop("pool", "memset", [], ["k_cmpbias"], ap=ct["cmpbias"][:, 0:1040], constant=1.0)
            vop("pool", "memset", [], ["Vn"], ap=Vn[:], constant=1.0)
            vop("pool", "memset", [], ["Vpg"], ap=Vpg[:], constant=1.0)
            vop("pool", "memset", [], ["QTz"], ap=QTz[:], constant=0.0)
            for bs in range(4):
                ld(xres[0:1, 0:1024], sshift[bs:bs + 1, 0:1024], ["xres"])
                ld(yout[0:1, 0:640], sshift[bs:bs + 1, 1024:1664], ["yout"])
                for c13 in range(13):
                    src = xres[0:1, c13 * 128:(c13 + 1) * 128] if c13 < 8 else yout[0:1, (c13 - 8) * 128:(c13 - 7) * 128]
                    tr(R0[:, c13:c13 + 1], src, identf[0:1, 0:1], ["xres", "yout", "k_identf"], ["psR0"])
                cp("dve", zc[:], R0[:, 0:13], ["psR0"], ["zc"])
                cp("dve", STz[:].rearrange("p f h d -> p (f h d)").bitcast(F32R), ct["zeros"][:, 0:512], ["k_zeros"], ["STz"])
                for fc in range(4):
                    ld(tmpf[0][0:64, 0:128].rearrange("i (h j) -> i h j", h=2), swkv[bs, 2 * fc:2 * fc + 2, :, :].rearrange("h i j -> i h j"), ["tmpf0"])
                    tr(R0[:, 0:64], tmpf[0][0:64, 0:128], identf[0:64, 0:64], ["tmpf0", "k_identf"], ["psR0"])
                    cp("dve", Ssb[:, 0:64], R0[:, 0:64], ["psR0"], ["Ssb"])
                    for hh in range(2):
                        rws = slice(hh * 64, hh * 64 + 64)
                        cp("act", STz[rws, fc, hh, :].bitcast(F32R), Ssb[rws, 0:64], ["Ssb"], ["STz"])
                chk(20)
                rmsnorm_T(xs[4 * bs:4 * bs + 4, :], 4, 0)
                p4 = slice(0, 4)
                for gname, ncol in (("T0", 512), ("T1", 512), ("T2", 512), ("T3", 280)):
                    wb, wk = load_group(gname)
                    ps_, pk = nextps()
                    for kt in range(8):
                        mm(ps_[p4, 0:ncol], xnT[:, kt, 0:4], wb[:, kt, 0:ncol], kt == 0, kt == 7, ["xnT", wk], [pk])
                    if gname == "T0":
                        act(gsr[p4, 0, :], ps_[p4, 0:512], AF.Silu, [pk], ["gsr"])
                    elif gname == "T1":
                        act(gsn[p4, 0, :], ps_[p4, 0:512], AF.Silu, [pk], ["gsn"])
                    elif gname == "T2":
                        cp("act", kv0[0][p4, :], ps_[p4, 0:512], [pk], ["kv0_0"])
                        stq(kvs[4 * bs:4 * bs + 4, :], kv0[0][p4, :], ["kv0_0"])
                        cp("dve", Vn[p4, 0, :, 0:64], kv0[0][p4, 384:512].rearrange("p (k d) -> p k d", k=2), ["kv0_0"], ["Vn"])
                    else:
                        cp("act", kv1[0][p4, :], ps_[p4, 0:280], [pk], ["kv1_0"])
                        stq(wins[bs, 508:512, :], kv1[0][p4, 0:256], ["kv1_0"])
                        cp("dve", Vn[p4, 1, :, 0:64], kv1[0][p4, 128:256].rearrange("p (k d) -> p k d", k=2), ["kv1_0"], ["Vn"])
                        act(gates[p4, 0, :], kv1[0][p4, 256:280], AF.Sigmoid, ["kv1_0"], ["gates"])
                chk(21)
                rwkv_block(4, 4, 2, 4, None)
                chk(22)
                shift_and_state_out(shs[bs:bs + 1, :], wkvs[bs])
                chk(23)
                proj_qk(4, lambda si, kvh: (KTn[:, si, kvh, :], "KTn"))
                cp("dve", QTs[:], QT[:, :, 0:4], ["QT"], ["QTs"])
                wb, wk = load_group("Q")
                for g in range(4):
                    h = 4 + g
                    ps_, pk = nextps()
                    for kt in range(8):
                        mm(ps_[:, 0:4], wb[:, kt, (h - 1) * 64:(h + 1) * 64], xnT[:, kt, 0:4], kt == 0, kt == 7, ["xnT", wk], [pk])
                    cp("dve", Ssb[:, g * 4:(g + 1) * 4], ps_[:, 0:4], [pk], ["Ssb"])
                cp("act", QTz[64:128, 1, :], Ssb[64:128, 0:16], ["Ssb"], ["QTz"])
                cp("dve", QTz[0:64, 0, :], QTs[:, 0:4, :].rearrange("p g t -> p (g t)"), ["QTs"], ["QTz"])
                chk(24)
                ld(ptb[:], pt[bs:bs + 1, :].to_broadcast([128, 128]), ["ptb"])
                cp("dve", ptf[:], ptb[:], ["ptb"], ["ptf"])
                vop("dve", "tensor_scalar", ["ptf", "k_iota"], ["pidx"], out=pidx[:, 0, :], in0=ptf[:], scalar1=256.0, scalar2=ct["iota"][:, 0:1],
                    op0=ALU.mult, op1=ALU.add)
                vop("dve", "tensor_scalar", ["ptf", "k_iota"], ["pidx"], out=pidx[:, 1, :], in0=ptf[:], scalar1=256.0, scalar2=ct["iota"][:, 1:2],
                    op0=ALU.mult, op1=ALU.add)

                def gather4(g0, col0, key):
                    dst = pg4[key]
                    kname = ("xres", "yout")[key]
                    for i in range(4):
                        P.dma_fn("pool", (lambda d_, gi: (lambda e: e.indirect_dma_start(
                            out=d_, out_offset=None, in_=cache_h,
                            in_offset=bass.IndirectOffsetOnAxis(ap=pidx[:, col0 // 256, gi:gi + 1], axis=0))))(dst[:, i, :], g0 + i),
                            reads=["pidx"], writes=[kname])
                    return dst, kname

                chk(25)
                gi_ = 0
                for G in range(8):
                    mm(psP[:, 0:258], ct["zeros"][:, 0:128], ct["zeros"][:, 0:258], True, True, ["k_zeros"], ["psP"])
                    for pq in range(4):
                        src, sk = gather4(16 * G + 4 * pq, 0, gi_ % 2)
                        gi_ += 1
                        for i in range(4):
                            ti = 4 * pq + i
                            lo, hi = 8 * ti - 1, 8 * ti + 8
                            for m in range(2):
                                vop("dve", "tensor_tensor", [sk, "wt"], [f"ym{m}"], out=ym[m][:], in0=src[:, i, :],
                                    in1=wt[:, m, :], op=ALU.mult)
                            for cc in range(2):
                                for m in range(2):
                                    zoff = m - 8 * ti + 120
                                    mm(psP[:, cc * 129 + 1 + lo: cc * 129 + 1 + hi], ym[m][:, cc * 128:(cc + 1) * 128],
                                       ct["zs"][:, zoff + lo: zoff + hi], False, True, [f"ym{m}", "k_zs"], ["psP"])
                    for cc in range(2):
                        cp("act", pall[:, cc, 128 * G:128 * G + 128], psP[:, cc * 129 + 1:cc * 129 + 129], ["psP"], ["Vs"])
                        if G > 0:
                            vop("dve", "tensor_tensor", ["psP", "Vs"], ["Vs"], out=pall[:, cc, 128 * G - 1:128 * G],
                                in0=psP[:, cc * 129:cc * 129 + 1], in1=pall[:, cc, 128 * G - 1:128 * G], op=ALU.add)
                chk(26)
                for kvh in range(2):
                    for half in range(2):
                        js = slice(half * 512, (half + 1) * 512)
                        mm(psA[0:64, 0:512], wmix[:, kvh, 0, :], pall[:, 0, js], True, True, ["wmix", "Vs"], ["psA"])
                        cp("act", kcTs[:, kvh, js], psA[0:64, 0:512], ["psA"], ["Vw"])
                    for jt in range(8):
                        mm(psB[:, 0:64], pall[:, 1, jt * 128:(jt + 1) * 128], wmix[:, kvh, 1, :], True, True, ["wmix", "Vs"], ["psB"])
                        cp("dve", vcs[:, jt, kvh, 0:64], psB[:, 0:64], ["psB"], ["k_cmpbias"])
                chk(27)
                p16 = slice(0, 16)
                for kvh in range(2):
                    mm(R3[p16, 0:322], ct["zeros"][:, 0:16], ct["zeros"][:, 0:322], True, False, ["k_zeros"], ["psR3a"])
                    for jt in range(8):
                        sp_, spk = sps[sn["n"] % 2]
                        pt_ = PTb[sn["n"] % 3]
                        ptk = f"PTb{sn['n'] % 3}"
                        sn["n"] += 1
                        mm(sp_[:, 0:16], kcTs[:, kvh, jt * 128:(jt + 1) * 128], QTs[:, 4 * kvh:4 * kvh + 4, :].rearrange("p g t -> p (g t)"),
                           True, jt != 7, ["Vw", "QTs"], [spk])
                        if jt == 7:
                            mm(sp_[:, 0:16], identb[:], ct["lastb"][:], False, True, ["k_identb", "k_lastb"], [spk])
                        act(pt_[:, 0:16], sp_[:, 0:16], AF.Exp, [spk], [ptk])
                        mm(R3[p16, 0:65], pt_[:, 0:16], vcs[:, jt, kvh, :], False, False, [ptk, "k_cmpbias"], ["psR3a"])
                        mm(R3[p16, 65 + 32 * jt:65 + 32 * jt + 33], pt_[:, 0:16], ovc[:, 0:33], False, jt == 7, [ptk, "k_ovc"], ["psR3a"])
                    chk(27.1)
                    cp("act", ocs1[p16, :], R3[p16, 0:322], ["psR3a"], ["stw0"])
                    vop("dve", "reciprocal", ["stw0"], ["rden"], out=rden[p16, 0:1], in_=ocs1[p16, 64:65])
                    vop("dve", "tensor_scalar", ["stw0", "rden"], ["stw0"], out=ocs1[p16, :], in0=ocs1[p16, :], scalar1=rden[p16, 0:1],
                        scalar2=None, op0=ALU.mult)
                    cp("dve", obr[p16, 0, kvh, :], ocs1[p16, 0:64], ["stw0"], ["Mt1"])
                    chk(27.2)
                    mm(psA[p4, 0:257], ct["gsel"][p16, :], ocs1[p16, 65:322], True, True, ["k_gsel", "stw0"], ["psA"])
                    vop("dve", "tensor_tensor", ["psA", "k_addcs"], ["kv1_1"], out=scs[p4, :], in0=psA[p4, 0:257], in1=ct["addcs"][p4, :], op=ALU.add)
                    vop("dve", "max", ["kv1_1"], ["m8a"], out=m8a[p4, :], in_=scs[p4, :])
                    vop("dve", "match_replace", ["kv1_1", "m8a"], ["kv0_1"], out=scs2[p4, :], in_to_replace=m8a[p4, :], in_values=scs[p4, :], imm_value=-3e4)
                    vop("dve", "max", ["kv0_1"], ["m8b"], out=m8b[p4, :], in_=scs2[p4, :])
                    vop("dve", "tensor_scalar", ["kv1_1", "m8b"], ["kv0_1"], out=scs2[p4, :], in0=scs[p4, :], scalar1=m8b[p4, 7:8], scalar2=-BIG,
                        op0=ALU.is_lt, op1=ALU.mult)
                    chk(27.3)
                    for t_ in range(4):
                        mm(psB[0:1, 0:257], ct["identf"][p4, t_:t_ + 1], scs2[p4, :], True, True, ["k_identf", "kv0_1"], ["psB"])
                        cp("dve" if t_ % 2 else "act", selflat4[0:1, :, :, kvh, t_],
                           psB[0:1, 0:256].rearrange("o (pg hf) -> o hf pg", hf=2), ["psB"], ["k_zp"])

                chk(28)
                def page_attn(src, sk, npg, mask_mm, mexp_const, first_group):
                    for i in range(npg):
                        tr(R0[:, i * 128:(i + 1) * 128], src[:, i, 0:128], identf[:], [sk, "k_identf"], ["psR0"])
                    cp("act", KTpg[:, 0:npg, :].rearrange("p g n -> p (g n)"), R0[:, 0:npg * 128], ["psR0"], ["KTpg"])
                    cp("dve", Vpg[:, 0:npg, :, 0:64], src[:, 0:npg, 128:256].rearrange("p g (k d) -> p g k d", k=2), [sk], ["Vpg"])
                    sp_, spk = sps[sn["n"] % 2]
                    pt_ = PTb[sn["n"] % 3]
                    ptk = f"PTb{sn['n'] % 3}"
                    sn["n"] += 1
                    for i in range(npg):
                        for kvh in range(2):
                            c0 = (i * 2 + kvh) * 16
                            mm(sp_[:, c0:c0 + 16], KTpg[:, i, :], QTz[:, kvh, :], True, True, ["KTpg", "QTz"], [spk])
                    if mask_mm is not None:
                        for hf in range(2):
                            mm(sp_[:, 256:256 + npg * 8], ct["hl"][0:1, hf, :], mask_mm(hf), hf == 0, hf == 1, ["k_hl", "k_zp"], [spk])
                        act(mexp[:, 0:npg * 8], sp_[:, 256:256 + npg * 8], AF.Exp, [spk], ["mexp"])
                        mk = mexp[:, 0:npg * 8]
                        mkk = "mexp"
                    else:
                        mk = mexp_const
                        mkk = "k_winms"
                    act(pt_[:, 0:npg * 32], sp_[:, 0:npg * 32], AF.Exp, [spk], [ptk])
                    vop("dve", "tensor_tensor", [ptk, mkk], [ptk], out=pt_[:, 0:npg * 32].rearrange("p (a g t) -> p a g t", g=4, t=4),
                        in0=pt_[:, 0:npg * 32].rearrange("p (a g t) -> p a g t", g=4, t=4),
                        in1=mk.rearrange("p (a o t) -> p a o t", o=1, t=4).to_broadcast([128, npg * 2, 4, 4]), op=ALU.mult)
                    for i in range(npg):
                        for kvh in range(2):
                            c0 = (i * 2 + kvh) * 16
                            mm(R3[p16, kvh * 65:(kvh + 1) * 65], pt_[:, c0:c0 + 16], Vpg[:, i, kvh, :], first_group and i == 0 and kvh == 0,
                               False, [ptk, "Vpg"], ["psR3a"])

                def tail_attn(si, last):
                    sp_, spk = sps[sn["n"] % 2]
                    pt_ = PTb[sn["n"] % 3]
                    ptk = f"PTb{sn['n'] % 3}"
                    sn["n"] += 1
                    for kvh in range(2):
                        mm(sp_[p4, kvh * 16:(kvh + 1) * 16], KTn[:, si, kvh, :], QTs[:, 4 * kvh:4 * kvh + 4, :].rearrange("p g t -> p (g t)"),
                           True, False, ["KTn", "QTs"], [spk])
                        mm(sp_[p4, kvh * 16:(kvh + 1) * 16], identb[p4, p4], ct["tailb"][p4, :], False, True, ["k_identb", "k_tailb"], [spk])
                    act(pt_[p4, 0:32], sp_[p4, 0:32], AF.Exp, [spk], [ptk])
                    for kvh in range(2):
                        mm(R3[p16, kvh * 65:(kvh + 1) * 65], pt_[p4, kvh * 16:(kvh + 1) * 16], Vn[p4, si, kvh, :], False, last and kvh == 1,
                           [ptk, "Vn"], ["psR3a"])

                def fold(br, first):
                    for kvh in range(2):
                        cp("act", obr[p16, br, kvh, :], R3[p16, kvh * 65:kvh * 65 + 64], ["psR3a"], ["Mt1"])
                        cp("act", rden[p16, 1:2], R3[p16, kvh * 65 + 64:kvh * 65 + 65], ["psR3a"], ["rden"])
                        vop("dve", "reciprocal", ["rden"], ["rden"], out=rden[p16, 0:1], in_=rden[p16, 1:2])
                        vop("dve", "tensor_scalar", ["Mt1", "rden"], ["Mt1"], out=obr[p16, br, kvh, :], in0=obr[p16, br, kvh, :],
                            scalar1=rden[p16, 0:1], scalar2=None, op0=ALU.mult)

                chk(29)
                for grp in range(32):
                    src, sk = gather4(4 * grp, 256, gi_ % 2)
                    gi_ += 1
                    page_attn(src, sk, 4, (lambda hf, grp=grp: selflat4[0:1, hf, 4 * grp:4 * grp + 4, :, :].rearrange("o a k t -> o (a k t)")),
                              None, grp == 0)
                tail_attn(0, True)
                fold(1, False)
                chk(30)
                ld(pg4[0][:], cwin[bs].rearrange("(g p) c -> p g c", p=128), ["xres"])
                for tl in range(4):
                    lo_ = 4 if tl == 0 else 0
                    stq(wins[bs, 128 * tl + lo_ - 4:128 * tl + 124, :], pg4[0][lo_:128, tl, :], ["xres"])
                page_attn(pg4[0], "xres", 4, None, ct["winms"][:], True)
                tail_attn(1, True)
                fold(2, False)
                chk(31)
                for kvh in range(2):
                    for g in range(4):
                        h = 4 * kvh + g
                        for br in range(3):
                            mm(psA[p4, br * 64:(br + 1) * 64], ct["identf"][p16, g * 4:g * 4 + 4], obr[p16, br, kvh, :], True, True,
                               ["k_identf", "Mt1"], ["psA"])
                        for br in range(3):
                            if br == 0:
                                vop("dve", "tensor_scalar", ["psA", "gates"], ["oacc"], out=oacc[p4, 0, h, :], in0=psA[p4, 0:64],
                                    scalar1=gates[p4, 0, h:h + 1], scalar2=None, op0=ALU.mult)
                            else:
                                vop("dve", "scalar_tensor_tensor", ["psA", "gates", "oacc"], ["oacc"], out=oacc[p4, 0, h, :], in0=psA[p4, br * 64:(br + 1) * 64],
                                    scalar=gates[p4, 0, br * 8 + h:br * 8 + h + 1], in1=oacc[p4, 0, h, :], op0=ALU.mult, op1=ALU.add)
                chk(32)
                out_proj(4, 0, xs[4 * bs:4 * bs + 4, :], ys[4 * bs:4 * bs + 4, :])

        try:
          chk(0)
          for hh_ in range(2):
              cp("dve", BKz[:, hh_, :, :].rearrange("p a t -> p (a t)").bitcast(F32R), ct["zeros"][:, 0:2 * TB], ["k_zeros"], ["BKz"])
          for b in range(2 if os.environ.get("KNOPROMPT") is None else 0):
            vop("dve", "memset", [], ["zc"], ap=zc[:], constant=0.0)
            cp("dve", STz[:].rearrange("p f h d -> p (f h d)").bitcast(F32R), ct["zeros"][:, 0:512], ["k_zeros"], ["STz"])
            mm(psP[:, 0:256], ct["zeros"][:, 0:128], ct["zeros"][:, 0:256], True, True, ["k_zeros"], ["psP"])
            for tb in range(NB):
                t0 = tb * TB
                for tt in range(NTT):
                    rmsnorm_T(xp[b, t0 + tt * 128:t0 + (tt + 1) * 128, :], 128, tt)
                chk(1)
                for gname, ncol in (("T0", 512), ("T1", 512), ("T2", 512), ("T3", 280)):
                    wb, wk = load_group(gname)
                    for tt in range(NTT):
                        ps_, pk = nextps()
                        for kt in range(8):
                            mm(ps_[:, 0:ncol], xnT[:, kt, tt * 128:(tt + 1) * 128], wb[:, kt, 0:ncol],
                               kt == 0, kt == 7, ["xnT", wk], [pk])
                        tile_i = tb * NTT + tt
                        if gname == "T0":
                            act(gsr[:, tt, :], ps_[:, 0:512], AF.Silu, [pk], ["gsr"])
                        elif gname == "T1":
                            act(gsn[:, tt, :], ps_[:, 0:512], AF.Silu, [pk], ["gsn"])
                        elif gname == "T2":
                            k0 = kv0[tt % 2]
                            kk0 = f"kv0_{tt % 2}"
                            cp("act", k0[:], ps_[:, 0:512], [pk], [kk0])
                            stq(kvp[b, t0 + tt * 128:t0 + (tt + 1) * 128, :], k0[:], [kk0])
                            cp("dve", Vs[:, tile_i, :, 0:64], k0[:, 384:512].rearrange("p (k d) -> p k d", k=2),
                               [kk0], ["Vs"])
                            for m in range(2):
                                vop("pool", "tensor_tensor", [kk0, "wt"], [f"ym{m}"], out=ym[m][:], in0=k0[:, 0:256],
                                    in1=wt[:, m, :], op=ALU.mult)
                            for cc in range(2):
                                for m in range(2):
                                    j0 = 8 * tile_i - 1
                                    lo = max(j0, 0)
                                    hi = min(8 * tile_i + 8, 127)
                                    zoff = m - 8 * tile_i + 120
                                    mm(psP[:, cc * 128 + lo: cc * 128 + hi], ym[m][:, cc * 128:(cc + 1) * 128],
                                       ct["zs"][:, zoff + lo: zoff + hi], False, True, [f"ym{m}", "k_zs"], ["psP"])
                        else:
                            k1 = kv1[tt % 2]
                            kk1 = f"kv1_{tt % 2}"
                            cp("act", k1[:], ps_[:, 0:280], [pk], [kk1])
                            if t0 + tt * 128 >= T - 512:
                                stq(winp[b, t0 + tt * 128 - (T - 512): t0 + (tt + 1) * 128 - (T - 512), :],
                                    k1[:, 0:256], [kk1])
                            cp("dve", Vw[:, tile_i, :, 0:64], k1[:, 128:256].rearrange("p (k d) -> p k d", k=2),
                               [kk1], ["Vw"])
                            act(gates[:, tt, :], k1[:, 256:280], AF.Sigmoid, [kk1], ["gates"])
                chk(2)
                rwkv_block(TB, 128, 7, 128, None)
                chk(5)
                if tb == NB - 1:
                    shift_and_state_out(shp[b:b + 1, :], wkvp[b])
                chk(6)
                proj_qk(TB, lambda si, kvh: ((KTs, KTw)[si][:, kvh, t0:t0 + TB], ("KTs", "KTw")[si]))
                chk(6.2)
                cp("dve", pooled[:].rearrange("p a j -> p (a j)"), psP[:, 0:256], ["psP"], ["pooled"])
                cp("act", pooledb[:], pooled[:], ["pooled"], ["pooledb"])
                chk(6.3)
                for kvh in range(2):
                    mm(psA[0:64, 0:128], wmix[:, kvh, 0, :], pooledb[:, 0, :], True, True, ["wmix", "pooledb"], ["psA"])
                    cp("act", kcT[:, kvh, :], psA[0:64, 0:128], ["psA"], ["kcT"])
                    mm(psB[:, 0:64], pooledb[:, 1, :], wmix[:, kvh, 1, :], True, True, ["wmix", "pooledb"], ["psB"])
                    cp("dve", vcx[:, kvh, 0:64], psB[:, 0:64], ["psB"], ["vcx"])
                chk(6.4)
                if tb == 0 and b == 0:
                    vop("pool", "memset", [], ["vcx"], ap=vcx[:, :, 64:65], constant=1.0)
                    for kvh in range(2):
                        cp("dve", vcx[:, kvh, 65:97], ct["ov32"][:], ["k_ov32"], ["vcx"])

                def attend(h, br, tiles, Kt_, kkey, Vt_, vkey, first):
                    kvh = h // 4
                    nt = len(tiles)
                    acc, acck = accs[an["n"] % 2]
                    an["n"] += 1
                    for ti, (kt, biases) in enumerate(tiles):
                        sp_, spk = sps[sn["n"] % 2]
                        pt_ = PTb[sn["n"] % 3]
                        ptk = f"PTb{sn['n'] % 3}"
                        sn["n"] += 1
                        mm(sp_[:, 0:TB], Kt_[:, kvh, kt * 128:(kt + 1) * 128], QT[:, h, :], True, len(biases) == 0,
                           [kkey, "QT"], [spk])
                        for bi, (bl, br_, bkeys) in enumerate(biases):
                            mm(sp_[:, 0:TB], bl, br_, False, bi == len(biases) - 1, bkeys, [spk])
                        act(pt_[:], sp_[:, 0:TB], AF.Exp, [spk], [ptk])
                        for tt in range(NTT):
                            mm(acc[:, tt * 65:(tt + 1) * 65], pt_[:, tt * 128:(tt + 1) * 128], Vt_[:, kt, kvh, :],
                               ti == 0 and tt == 0, ti == nt - 1 and tt == NTT - 1, [ptk, vkey], [acck])
                    finish(h, br, acc, 65, first, 128, NTT, acck)

                chk(7)
                for h in range(8):
                    kvh = h // 4
                    sp_, spk = sps[sn["n"] % 2]
                    pt_ = PTb[sn["n"] % 3]
                    ptk = f"PTb{sn['n'] % 3}"
                    sn["n"] += 1
                    mm(sp_[:, 0:TB], kcT[:, kvh, :], QT[:, h, :], True, False, ["kcT", "QT"], [spk])
                    mm(sp_[:, 0:TB], identb[:], ct["cmpbias"][:, t0:t0 + TB], False, True, ["k_identb", "k_cmpbias"], [spk])
                    act(pt_[:], sp_[:, 0:TB], AF.Exp, [spk], [ptk])
                    acc, acck = accs[an["n"] % 2]
                    an["n"] += 1
                    for tt in range(NTT):
                        mm(acc[:, tt * 97:(tt + 1) * 97], pt_[:, tt * 128:(tt + 1) * 128], vcx[:, kvh, :], tt == 0, tt == NTT - 1,
                           [ptk, "vcx"], [acck])
                    finish(h, 0, acc, 97, True, 128, NTT, acck)
                    view = acc[:, 0:NTT * 97].rearrange("p (t c) -> p t c", t=NTT)
                    if h % 4 == 0:
                        vop("dve", "tensor_tensor", [acck, "rden"], ["sc"], out=sc[:, :, kvh, :], in0=view[:, :, 65:97],
                            in1=rden[:].rearrange("p (a o) -> p a o", o=1).to_broadcast([128, NTT, 32]), op=ALU.mult)
                    else:
                        vop("dve", "tensor_tensor", [acck, "rden"], ["otmp"], out=otmp[:, :, 0:32], in0=view[:, :, 65:97],
                            in1=rden[:].rearrange("p (a o) -> p a o", o=1).to_broadcast([128, NTT, 32]), op=ALU.mult)
                        vop("dve", "tensor_tensor", ["otmp", "sc"], ["sc"], out=sc[:, :, kvh, :], in0=sc[:, :, kvh, :], in1=otmp[:, :, 0:32], op=ALU.add)
                chk(8)
                for tt in range(NTT):
                    for kvh in range(2):
                        vop("dve", "tensor_tensor", ["sc", "k_addc"], ["scw"], out=scw[:], in0=sc[:, tt, kvh, :],
                            in1=ct["addc"][:, tb * NTT + tt, :], op=ALU.add)
                        vop("dve", "max", ["scw"], ["m8a"], out=m8a[:], in_=scw[:])
                        vop("dve", "match_replace", ["scw", "m8a"], ["otmp"], out=otmp[:, 0, 0:32], in_to_replace=m8a[:], in_values=scw[:],
                            imm_value=-3e4)
                        vop("dve", "max", ["otmp"], ["m8b"], out=m8b[:], in_=otmp[:, 0, 0:32])
                        vop("dve", "tensor_scalar", ["m8b"], ["thr"], out=thr[:], in0=m8b[:, 7:8], scalar1=-5000.0, scalar2=None, op0=ALU.max)
                        vop("dve", "tensor_scalar", ["scw", "thr"], ["selb"], out=selb[:], in0=scw[:], scalar1=thr[:, 0:1], scalar2=-BIG,
                            op0=ALU.is_lt, op1=ALU.mult)
                        tr(psT[0:32, 0:128], selb[:], identb[:], ["selb", "k_identb"], ["psT"])
                        cp("act", selbT[:, kvh, tt * 128:(tt + 1) * 128], psT[0:32, 0:128], ["psT"], ["selbT"])
                chk(9)
                for h in range(8):
                    kvh = h // 4
                    tiles = []
                    for kt in range(0, tb * NTT + NTT):
                        bs = [(ct["zp"][:, kt * 128:(kt + 1) * 128], selbT[:, kvh, :], ["k_zp", "selbT"])]
                        if kt >= tb * NTT:
                            bs.append((identb[:], ct["causb"][:, kt - tb * NTT, :], ["k_identb", "k_causb"]))
                        tiles.append((kt, bs))
                    attend(h, 1, tiles, KTs, "KTs", Vs, "Vs", False)
                chk(10)
                for h in range(8):
                    tiles = []
                    q0 = tb * NTT
                    for kt in range(max(0, q0 - 4), q0 + NTT):
                        bs = []
                        if kt >= q0:
                            bs.append((identb[:], ct["causb"][:, kt - q0, :], ["k_identb", "k_causb"]))
                        elif kt - q0 + 4 < 2:
                            bs.append((identb[:], ct["winlow"][:, kt - q0 + 4, :], ["k_identb", "k_winlow"]))
                        tiles.append((kt, bs))
                    attend(h, 2, tiles, KTw, "KTw", Vw, "Vw", False)
                chk(11)
                for tt in range(NTT):
                    out_proj(128, tt, xp[b, t0 + tt * 128:t0 + (tt + 1) * 128, :], yp[b, t0 + tt * 128:t0 + (tt + 1) * 128, :])
                chk(12)

          if with_sample:
            sample_jobs()
        except _Stop:
            for _i in range(int(os.environ.get("KPAD", "0"))):
                eng_ = os.environ.get("KPADENG", "act")
                if eng_ == "act":
                    act(thr[:], thr[:], AF.Copy, ["thr"], ["thr"])
                else:
                    cp(eng_, thr[:], m8a[:, 0:1], ["m8a"], ["thr"])
            if os.environ.get("KDBG"):
                dbg = dout("dbg", [128, 4096])
                o = 0
                for nm, tl, ncol in (("STz", STz, 512), ("Mt0", Mt[0], 512), ("Pk5", Pk[5], 512), ("Xa0", Xa[0], 128), ("Xa1", Xa[1], 128),
                                     ("tok", tok, 384), ("AR", AR, 512), ("BK", BK, 512), ("ytok", ytok, 256), ("Wc", Wc, 16)):
                    flat = tl[:]
                    shp_ = list(tl.shape) if hasattr(tl, "shape") else None
                    names = "abcdefg"[:len(shp_) - 1]
                    if len(shp_) > 2:
                        flat = tl[:].rearrange("p " + " ".join(names) + " -> p (" + " ".join(names) + ")")
                    stq(dbg[:, o:o + ncol], flat, [nm if nm not in ("Mt0", "Pk5", "Xa0", "Xa1") else nm])
                    o += ncol
        P.build(sems, block)
    return nc


def _core_inputs(c, inp, consts, cache2d=None, pt_override=None):
    d = {}
    d["xp"] = np.ascontiguousarray(inp["x_prompt"][2 * c:2 * c + 2])
    d["w_in"] = np.ascontiguousarray(inp["w_in"][0])
    d["w_out"] = np.ascontiguousarray(inp["w_out"][0])
    d["norm_g"] = np.ascontiguousarray(inp["norm_g"][0])
    d["final_g"] = np.ascontiguousarray(inp["final_g"])
    d["mu"] = np.ascontiguousarray(inp["mu_shift"][0])
    for k in ("w0", "a0", "k_k", "k_a", "r_k", "gn_w", "gn_b"):
        d[k] = np.ascontiguousarray(inp[k][0])
    d["w_dup"] = np.ascontiguousarray(inp["w_decay_up"][0])
    d["w_aup"] = np.ascontiguousarray(inp["w_aaa_up"][0])
    d["w_cpos"] = np.ascontiguousarray(inp["w_cmp_pos"][0])
    d["w_cmix"] = np.ascontiguousarray(inp["w_cmp_mix"][0])
    d["xs"] = np.ascontiguousarray(inp["x_sample"][4 * c:4 * c + 4]).reshape(16, 1024)
    d["cache"] = cache2d
    d["cwin"] = np.ascontiguousarray(inp["cache_kv_win"][0, 4 * c:4 * c + 4]).reshape(4, 512, 256)
    d["swkv"] = np.ascontiguousarray(inp["state_wkv"][0, 4 * c:4 * c + 4])
    d["sshift"] = np.ascontiguousarray(inp["state_shift"][0, 4 * c:4 * c + 4])
    d["pt"] = np.ascontiguousarray(inp["page_table"][4 * c:4 * c + 4] if pt_override is None else pt_override).astype(np.int32)
    for k, v in consts.items():
        d["c_" + k] = v
    return d


def kernel(**inp):
    inp = {k: np.asarray(v) for k, v in inp.items()}
    consts = _const_specs()
    cache2d = np.ascontiguousarray(inp["cache_kv"][0]).reshape(-1, 512)
    nc = build_nc(cache2d.shape[0])
    in_maps = [_core_inputs(c, inp, consts, cache2d) for c in range(8)]
    res = run_bass_kernel_spmd(nc, in_maps, core_ids=list(range(8))).results
    cat = lambda k: np.concatenate([r[k] for r in res], axis=0)
    y_prompt = cat("yp")
    y_sample = cat("ys").reshape(32, 4, 1024)
    kv_prompt = cat("kvp").reshape(1, 16, T, 4, 2, 64)
    kv_sample = cat("kvs").reshape(1, 32, 4, 4, 2, 64)
    win_prompt = cat("winp").reshape(1, 16, 512, 2, 2, 64)
    win_sample = cat("wins").reshape(1, 32, 512, 2, 2, 64)
    wkv_prompt = cat("wkvp").reshape(1, 16, 8, 64, 64)
    wkv_sample = cat("wkvs").reshape(1, 32, 8, 64, 64)
    shift_prompt = cat("shp").reshape(1, 16, 1664)
    shift_sample = cat("shs").reshape(1, 32, 1664)
    return (y_prompt, y_sample, kv_prompt, kv_sample, win_prompt, win_sample, wkv_prompt, wkv_sample,
            shift_prompt, shift_sample)
```
